# Optimizing a Trainium2 kernel written in Bass

```python
import math
import jax, jax.numpy as jnp
from jax import lax
import numpy as np

D_MODEL = 1024
BATCH = 8
SEQ = 8192
DEPTH = 2
DEC_BATCH = 32
DEC_SEQ = 32
PAST_LEN = 4096

CHUNK = 64
N_A_LAYERS = DEPTH // 2
N_B_LAYERS = DEPTH - N_A_LAYERS
NH_A = 4
DK_A = 128
DV_A = 256
QK_A = NH_A * DK_A
V_A = NH_A * DV_A
CONV_W = 4
MLSTM_CHUNK = CHUNK
IN_A = 2 * QK_A + 2 * V_A + 2 * NH_A
NH_B = 16
NKV_B = 4
HD_B = 64
GQ_B = NH_B // NKV_B
KV_W = NKV_B * HD_B
N_PREV_CHUNKS = 8
BAND = N_PREV_CHUNKS * CHUNK
MAX_REL = 128
N_REL = 2 * MAX_REL + 1
D_FF = 4 * D_MODEL
ALPHA = (2 * DEPTH) ** 0.25
BETA = (8 * DEPTH) ** -0.25
LN_EPS = 1e-5
HEAD_NORM_EPS = 1e-6

kernel_name = 'yoco_mlstm_chunkband_stream_step'


def layer_norm(x, g, b):
    xf = x.astype(jnp.float32)
    mu = xf.mean(-1, keepdims=True)
    var = jnp.square(xf - mu).mean(-1, keepdims=True)
    return ((xf - mu) * lax.rsqrt(var + LN_EPS) * g.astype(jnp.float32) + b.astype(jnp.float32)).astype(x.dtype)


def ada_mod(c, w, b):
    return jax.nn.silu(c) @ w + b


def causal_conv(x, prev, w, b):
    S = x.shape[1]
    xp = jnp.concatenate([prev.astype(x.dtype), x], axis=1)
    y = xp[:, 0:S] * w[0]
    for j in range(1, CONV_W):
        y = y + xp[:, j:j + S] * w[j]
    return y + b, xp[:, -(CONV_W - 1):]


def mlstm_chunk(carry, xs):
    C, n, m = carry
    q, k, v, ig, lf = xs
    L = q.shape[1]
    b = jnp.cumsum(lf, axis=1)
    dmat = b[:, :, None, :] - b[:, None, :, :] + ig[:, None, :, :]
    causal = jnp.tril(jnp.ones((L, L), dtype=bool))
    dmat = jnp.where(causal[None, :, :, None], dmat, -jnp.inf)
    inter = b + m[:, None, :]
    m_t = jnp.maximum(inter, dmat.max(axis=2))
    wts = jnp.exp(dmat - m_t[:, :, None, :])
    a = jnp.exp(inter - m_t)
    s = jnp.einsum('bthd,bshd->btsh', q, k) * wts
    num = jnp.einsum('btsh,bshv->bthv', s, v) + a[..., None] * jnp.einsum('bhvd,bthd->bthv', C, q)
    den = s.sum(axis=2) + a * jnp.einsum('bhd,bthd->bth', n, q)
    h = num / jnp.maximum(jnp.abs(den), jnp.exp(-m_t))[..., None]
    b_last = b[:, -1]
    m_new = m_t[:, -1]
    w_s = jnp.exp(b_last[:, None, :] - b + ig - m_new[:, None, :])
    a_last = jnp.exp(b_last + m - m_new)
    C_new = a_last[..., None, None] * C + jnp.einsum('bsh,bshv,bshd->bhvd', w_s, v, k)
    n_new = a_last[..., None] * n + jnp.einsum('bsh,bshd->bhd', w_s, k)
    return (C_new, n_new, m_new), h


def mlstm_scan(q, k, v, ig, lf, C0, n0, m0, chunk_len):
    Bsz, S = q.shape[:2]
    nck = S // chunk_len

    def to_chunks(a):
        return jnp.moveaxis(a.reshape((Bsz, nck, chunk_len) + a.shape[2:]), 1, 0)

    (C, n, m), hs = lax.scan(mlstm_chunk, (C0, n0, m0),
                             (to_chunks(q), to_chunks(k), to_chunks(v), to_chunks(ig), to_chunks(lf)))
    h = jnp.moveaxis(hs, 0, 1).reshape((Bsz, S) + hs.shape[3:])
    return h, C, n, m


def mlstm_mixer(u, conv_prev, C0, n0, m0, w_in, b_if, conv_w, conv_b, mhn_g, w_out, chunk_len):
    Bsz, S, _ = u.shape
    proj = u @ w_in
    qk_pre = proj[..., :2 * QK_A]
    v = proj[..., 2 * QK_A:2 * QK_A + V_A]
    o_pre = proj[..., 2 * QK_A + V_A:2 * QK_A + 2 * V_A]
    gates = (proj[..., 2 * QK_A + 2 * V_A:] + b_if).astype(jnp.float32)
    qk, conv_new = causal_conv(qk_pre, conv_prev, conv_w, conv_b)
    qk = jax.nn.silu(qk).astype(jnp.float32)
    q = qk[..., :QK_A].reshape(Bsz, S, NH_A, DK_A) * (DK_A ** -0.5)
    k = qk[..., QK_A:].reshape(Bsz, S, NH_A, DK_A)
    vh = v.astype(jnp.float32).reshape(Bsz, S, NH_A, DV_A)
    ig = gates[..., :NH_A]
    lf = jax.nn.log_sigmoid(gates[..., NH_A:])
    h, C, n, m = mlstm_scan(q, k, vh, ig, lf, C0.astype(jnp.float32), n0.astype(jnp.float32),
                            m0.astype(jnp.float32), chunk_len)
    mu = h.mean(-1, keepdims=True)
    var = jnp.square(h - mu).mean(-1, keepdims=True)
    h = (h - mu) * lax.rsqrt(var + HEAD_NORM_EPS) * mhn_g.astype(jnp.float32)
    h = (h.reshape(Bsz, S, V_A) * jax.nn.sigmoid(o_pre.astype(jnp.float32))).astype(u.dtype)
    return h @ w_out, conv_new, C, n, m


def rel_bias(table, q_off, tq, tk):
    rel = q_off + jnp.arange(tq)[:, None] - jnp.arange(tk)[None, :]
    idx = jnp.clip(rel, -MAX_REL, MAX_REL) + MAX_REL
    return table[:, idx].astype(jnp.float32)


def band_attn_prompt(q, k, v, table):
    Bsz, S = q.shape[:2]
    nc = S // CHUNK
    span = BAND + CHUNK
    kp = jnp.pad(k, ((0, 0), (BAND, 0), (0, 0), (0, 0)))
    vp = jnp.pad(v, ((0, 0), (BAND, 0), (0, 0), (0, 0)))
    bias = rel_bias(table, BAND, CHUNK, span).reshape(NKV_B, GQ_B, CHUNK, span)
    qc = jnp.moveaxis(q.reshape(Bsz, nc, CHUNK, NKV_B, GQ_B, HD_B), 1, 0)

    def one_chunk(args):
        ci, qb = args
        kb = lax.dynamic_slice_in_dim(kp, ci * CHUNK, span, axis=1)
        vb = lax.dynamic_slice_in_dim(vp, ci * CHUNK, span, axis=1)
        s = jnp.einsum('bqkgd,bskd->bkgqs', qb, kb).astype(jnp.float32) * (HD_B ** -0.5) + bias
        valid = (ci * CHUNK - BAND + jnp.arange(span)) >= 0
        s = jnp.where(valid, s, -jnp.inf)
        p = jax.nn.softmax(s, axis=-1).astype(vb.dtype)
        return jnp.einsum('bkgqs,bskd->bqkgd', p, vb)

    o = lax.map(one_chunk, (jnp.arange(nc), qc))
    return jnp.moveaxis(o, 0, 1).reshape(Bsz, S, NH_B * HD_B)


def band_attn_sample(q, k_new, v_new, k_buf, v_buf, table):
    Bsz, T = q.shape[:2]
    W = k_buf.shape[1]
    k_all = jnp.concatenate([k_buf.astype(k_new.dtype), k_new], axis=1)
    v_all = jnp.concatenate([v_buf.astype(v_new.dtype), v_new], axis=1)
    bias = rel_bias(table, W, T, W + T).reshape(NKV_B, GQ_B, T, W + T)
    qg = q.reshape(Bsz, T, NKV_B, GQ_B, HD_B)
    s = jnp.einsum('bqkgd,bskd->bkgqs', qg, k_all).astype(jnp.float32) * (HD_B ** -0.5) + bias
    p = jax.nn.softmax(s, axis=-1).astype(v_all.dtype)
    o = jnp.einsum('bkgqs,bskd->bqkgd', p, v_all)
    return o.reshape(Bsz, T, NH_B * HD_B)


def trunk(x, c, conv_st, C_st, n_st, m_st, k_buf, v_buf, p, chunk_len):
    Bsz, S, _ = x.shape
    convs, Cs, ns, ms = [], [], [], []
    k_sh = None
    v_sh = None
    for l in range(DEPTH):
        sh1, sc1, g1, sh2, sc2, g2 = jnp.split(ada_mod(c, p['w_ada'][l], p['b_ada'][l])[:, None, :], 6, axis=-1)
        u = x * (1 + sc1) + sh1
        if l < N_A_LAYERS:
            y, cv, C, n, m = mlstm_mixer(u, conv_st[l], C_st[l], n_st[l], m_st[l], p['w_in_a'][l], p['b_if_a'][l],
                                         p['conv_w_a'][l], p['conv_b_a'][l], p['mhn_g_a'][l], p['w_out_a'][l],
                                         chunk_len)
            convs.append(cv)
            Cs.append(C)
            ns.append(n)
            ms.append(m)
        else:
            lb = l - N_A_LAYERS
            if k_sh is None:
                kv_shift, kv_scale = jnp.split(ada_mod(c, p['w_ada_kv'], p['b_ada_kv'])[:, None, :], 2, axis=-1)
                kv = (x * (1 + kv_scale) + kv_shift) @ p['w_kv']
                k_sh = kv[..., :KV_W].reshape(Bsz, S, NKV_B, HD_B)
                v_sh = kv[..., KV_W:].reshape(Bsz, S, NKV_B, HD_B)
            q = (u @ p['w_q_b'][lb]).reshape(Bsz, S, NH_B, HD_B)
            if k_buf is None:
                o = band_attn_prompt(q, k_sh, v_sh, p['rel_bias_b'][lb])
            else:
                o = band_attn_sample(q, k_sh, v_sh, k_buf, v_buf, p['rel_bias_b'][lb])
            y = o @ p['w_out_b'][lb]
        x = layer_norm(ALPHA * x + (1 + g1) * y, p['ln_g'][l, 0], p['ln_b'][l, 0])
        u = x * (1 + sc2) + sh2
        f = jnp.square(jax.nn.relu(u @ p['w_up'][l])) @ p['w_down'][l]
        x = layer_norm(ALPHA * x + (1 + g2) * f, p['ln_g'][l, 1], p['ln_b'][l, 1])
    dt = x.dtype
    return (x, jnp.stack(convs).astype(dt), jnp.stack(Cs).astype(dt), jnp.stack(ns).astype(dt),
            jnp.stack(ms).astype(dt), k_sh, v_sh)


def setup_inputs(seed: int = 0) -> dict:
    key = jax.random.key(seed)
    ks = jax.random.split(key, 32)
    f32 = jnp.float32

    def nrm(k, shape, scale):
        return jax.random.normal(k, shape, f32) * scale

    W = min(BAND, PAST_LEN)
    return {
        'x_prompt': nrm(ks[0], (BATCH, SEQ, D_MODEL), 1.0),
        'x_sample': nrm(ks[1], (DEC_BATCH, DEC_SEQ, D_MODEL), 1.0),
        'c_prompt': nrm(ks[2], (BATCH, D_MODEL), 1.0),
        'c_sample': nrm(ks[3], (DEC_BATCH, D_MODEL), 1.0),
        'state_conv': nrm(ks[4], (N_A_LAYERS, DEC_BATCH, CONV_W - 1, 2 * QK_A), 1.0),
        'state_C': nrm(ks[5], (N_A_LAYERS, DEC_BATCH, NH_A, DV_A, DK_A), 0.5),
        'state_n': nrm(ks[6], (N_A_LAYERS, DEC_BATCH, NH_A, DK_A), 0.5),
        'state_m': nrm(ks[7], (N_A_LAYERS, DEC_BATCH, NH_A), 1.0),
        'cache_k': nrm(ks[8], (DEC_BATCH, W, NKV_B, HD_B), 1.0),
        'cache_v': nrm(ks[9], (DEC_BATCH, W, NKV_B, HD_B), 1.0),
        'w_ada': nrm(ks[10], (DEPTH, D_MODEL, 6 * D_MODEL), 0.1 * D_MODEL ** -0.5),
        'b_ada': nrm(ks[11], (DEPTH, 6 * D_MODEL), 0.02),
        'ln_g': 1.0 + nrm(ks[12], (DEPTH, 2, D_MODEL), 0.02),
        'ln_b': nrm(ks[13], (DEPTH, 2, D_MODEL), 0.02),
        'w_in_a': nrm(ks[14], (N_A_LAYERS, D_MODEL, IN_A), D_MODEL ** -0.5),
        'b_if_a': jnp.concatenate([nrm(ks[15], (N_A_LAYERS, NH_A), 0.1),
                                   3.0 + nrm(ks[16], (N_A_LAYERS, NH_A), 0.5)], axis=-1),
        'conv_w_a': nrm(ks[17], (N_A_LAYERS, CONV_W, 2 * QK_A), CONV_W ** -0.5),
        'conv_b_a': nrm(ks[18], (N_A_LAYERS, 2 * QK_A), 0.02),
        'mhn_g_a': 1.0 + nrm(ks[19], (N_A_LAYERS, NH_A, DV_A), 0.02),
        'w_out_a': nrm(ks[20], (N_A_LAYERS, V_A, D_MODEL), BETA * V_A ** -0.5),
        'w_ada_kv': nrm(ks[21], (D_MODEL, 2 * D_MODEL), 0.1 * D_MODEL ** -0.5),
        'b_ada_kv': nrm(ks[22], (2 * D_MODEL,), 0.02),
        'w_kv': nrm(ks[23], (D_MODEL, 2 * KV_W), D_MODEL ** -0.5),
        'w_q_b': nrm(ks[24], (N_B_LAYERS, D_MODEL, NH_B * HD_B), D_MODEL ** -0.5),
        'rel_bias_b': nrm(ks[25], (N_B_LAYERS, NH_B, N_REL), 0.5),
        'w_out_b': nrm(ks[26], (N_B_LAYERS, NH_B * HD_B, D_MODEL), BETA * (NH_B * HD_B) ** -0.5),
        'w_up': nrm(ks[27], (DEPTH, D_MODEL, D_FF), D_MODEL ** -0.5),
        'w_down': nrm(ks[28], (DEPTH, D_FF, D_MODEL), BETA * D_FF ** -0.5),
    }


def reference(x_prompt, x_sample, c_prompt, c_sample, state_conv, state_C, state_n, state_m, cache_k, cache_v,
              w_ada, b_ada, ln_g, ln_b, w_in_a, b_if_a, conv_w_a, conv_b_a, mhn_g_a, w_out_a,
              w_ada_kv, b_ada_kv, w_kv, w_q_b, rel_bias_b, w_out_b, w_up, w_down):
    p = dict(w_ada=w_ada, b_ada=b_ada, ln_g=ln_g, ln_b=ln_b, w_in_a=w_in_a, b_if_a=b_if_a,
             conv_w_a=conv_w_a, conv_b_a=conv_b_a, mhn_g_a=mhn_g_a, w_out_a=w_out_a,
             w_ada_kv=w_ada_kv, b_ada_kv=b_ada_kv, w_kv=w_kv, w_q_b=w_q_b, rel_bias_b=rel_bias_b,
             w_out_b=w_out_b, w_up=w_up, w_down=w_down)
    bp, sp = x_prompt.shape[:2]
    conv0 = jnp.zeros((N_A_LAYERS, bp, CONV_W - 1, 2 * QK_A), x_prompt.dtype)
    C0 = jnp.zeros((N_A_LAYERS, bp, NH_A, DV_A, DK_A), jnp.float32)
    n0 = jnp.zeros((N_A_LAYERS, bp, NH_A, DK_A), jnp.float32)
    m0 = jnp.zeros((N_A_LAYERS, bp, NH_A), jnp.float32)
    y_prompt, conv_p, C_p, n_p, m_p, k_p, v_p = trunk(x_prompt, c_prompt, conv0, C0, n0, m0, None, None, p,
                                                      MLSTM_CHUNK)
    wp = min(BAND, sp)
    k_p = k_p[:, -wp:]
    v_p = v_p[:, -wp:]
    y_sample, conv_s, C_s, n_s, m_s, k_s, v_s = trunk(x_sample, c_sample, state_conv, state_C, state_n, state_m,
                                                      cache_k, cache_v, p, x_sample.shape[1])
    return (y_prompt, y_sample, conv_p, C_p, n_p, m_p, k_p, v_p, conv_s, C_s, n_s, m_s, k_s, v_s)
```

```python
import contextlib
import numpy as np
import concourse.bass as bass
import concourse.mybir as mybir
from concourse.bass_utils import run_bass_kernel_spmd

F32 = mybir.dt.float32
BF16 = mybir.dt.bfloat16
AF = mybir.ActivationFunctionType
ALU = mybir.AluOpType
AX = mybir.AxisListType

ALPHA = 4.0 ** 0.25
LN_EPS_P = 1e-5 / (ALPHA * ALPHA)
HN_EPS = 1e-6
NSLOT = 3
PDEPTH = 2
NEG = -30000.0
import os
XQ = os.environ.get("XQ", "act")
SQE = os.environ.get("SQE", "pool")


class Res:
    __slots__ = ("name", "w", "r", "sem", "excl")

    def __init__(self, name, excl=False):
        self.name = name
        self.w = None
        self.r = []
        self.sem = None
        self.excl = excl


class Sched:
    ENGS = ("pe", "act", "dve", "pool", "sp")

    def __init__(self):
        self.ops = {e: [] for e in self.ENGS}
        self.nsig = {e: 0 for e in self.ENGS}
        self.waited = {e: {} for e in self.ENGS}
        self.semkeys = ["E_" + e for e in self.ENGS]
        self.dma_sems = {}
        self.nres = 0

    def res(self, name=None):
        self.nres += 1
        return Res(name or f"r{self.nres}")

    def _need(self, eng, ev, waits, same_ok):
        if ev is None:
            return
        key, val, weng = ev
        if same_ok and weng == eng:
            return
        if self.waited[eng].get(key, 0) >= val:
            return
        if waits.get(key, 0) < val:
            waits[key] = val

    def _deps(self, eng, reads, writes):
        waits = {}
        for r in reads:
            self._need(eng, r.w, waits, eng == "pe")
            if r.excl:
                for ev in r.r:
                    self._need(eng, ev, waits, True)
        for w in writes:
            self._need(eng, w.w, waits, True)
            for ev in w.r:
                self._need(eng, ev, waits, True)
        for k, v in waits.items():
            self.waited[eng][k] = v
        return list(waits.items())

    def _record(self, ev, reads, writes):
        for r in reads:
            r.r.append(ev)
        for w in writes:
            w.w = ev
            w.r = []

    def op(self, eng, fn, reads=(), writes=(), sig=True):
        waits = self._deps(eng, reads, writes)
        key = "E_" + eng
        if sig:
            self.nsig[eng] += 1
            val = self.nsig[eng]
        else:
            val = self.nsig[eng] + 1
        self._record((key, val, eng), reads, writes)
        self.ops[eng].append((waits, fn, (key, 1) if sig else None))

    def dma(self, q, out, in_, reads=(), writes=(), **kw):
        waits = self._deps(q, reads, writes)
        tgt = writes[0]
        if tgt.sem is None:
            tgt.sem = f"D{len(self.dma_sems)}_{tgt.name}"
            self.semkeys.append(tgt.sem)
            self.dma_sems[tgt.sem] = 0
        self.dma_sems[tgt.sem] += 16
        self._record((tgt.sem, self.dma_sems[tgt.sem], "dma"), reads, writes)
        self.ops[q].append((waits, (lambda e, o=out, i=in_, k=kw: e.dma_start(out=o, in_=i, **k)),
                            (tgt.sem, 16)))

    def barrier(self):
        for e in self.ENGS:
            waits = {}
            for e2 in self.ENGS:
                k = "E_" + e2
                if self.nsig[e2] > 0 and e2 != e and self.waited[e].get(k, 0) < self.nsig[e2]:
                    waits[k] = self.nsig[e2]
            for k, v in self.dma_sems.items():
                if v > 0 and self.waited[e].get(k, 0) < v:
                    waits[k] = v
            for k, v in waits.items():
                self.waited[e][k] = v
            if waits:
                self.ops[e].append((list(waits.items()), None, None))

    def final_drain(self, eng="sp"):
        waits = {}
        for k, v in self.dma_sems.items():
            if v > 0 and self.waited[eng].get(k, 0) < v:
                waits[k] = v
        for e2 in self.ENGS:
            if self.nsig[e2] > 0 and e2 != eng:
                waits["E_" + e2] = self.nsig[e2]
        self.ops[eng].append((list(waits.items()), None, None))

    def emit(self, nc):
        with contextlib.ExitStack() as st:
            sems = {k: st.enter_context(nc.semaphore(k)) for k in self.semkeys}
            block = st.enter_context(nc.Block())

            def replay(engobj, name):
                for waits, fn, sig in self.ops[name]:
                    for k, v in waits:
                        engobj.wait_ge(sems[k], v)
                    if fn is None:
                        continue
                    ins = fn(engobj)
                    if sig is not None:
                        ins.then_inc(sems[sig[0]], sig[1])

            @block.sync
            def _(e):
                replay(e, "sp")

            @block.tensor
            def _(e):
                replay(e, "pe")

            @block.scalar
            def _(e):
                replay(e, "act")

            @block.vector
            def _(e):
                replay(e, "dve")

            @block.gpsimd
            def _(e):
                replay(e, "pool")


def panel_list():
    pl = []
    for j in range(6):
        pl.append(("w_in", 0, j * 512, 3080, None))
    for j in range(2):
        pl.append(("w_out_a", 0, j * 512, 1024, None))
    for j in range(8):
        pl.append(("w_up0", 0, j * 512, 4096, None))
    for nh in range(2):
        for kg in range(4):
            pl.append(("w_down0", kg * 1024, nh * 512, 1024, None))
    pl.append(("w_kv", 0, 0, 512, None))
    for j in range(2):
        pl.append(("w_q", 0, j * 512, 1024, "qperm"))
    for j in range(2):
        pl.append(("w_out_b", 0, j * 512, 1024, None))
    for j in range(8):
        pl.append(("w_up1", 0, j * 512, 4096, None))
    for nh in range(2):
        for kg in range(4):
            pl.append(("w_down1", kg * 1024, nh * 512, 1024, None))
    return pl


NPANEL = 45


class _Stop(Exception):
    pass


def build(NT, do_sample=True, stage=99):
    nc = bass.Bass("TRN2", target_bir_lowering=False)
    S = Sched()
    st = contextlib.ExitStack()
    SEQ = NT * 512

    def din(name, shape, dt=F32):
        return nc.dram_tensor(name, list(shape), dt, kind="ExternalInput").ap()

    def dout(name, shape, dt=F32):
        return nc.dram_tensor(name, list(shape), dt, kind="ExternalOutput").ap()

    def dap(t, offset, dims):
        return bass.AP(tensor=t.tensor, offset=offset, ap=[list(d) for d in dims])

    x_p = din("x_p", [SEQ, 1024])
    x_s = din("x_s", [128, 1024])
    c_all = din("c_all", [5, 1024])
    sconv = din("sconv", [4, 3, 1024])
    sC = din("sC", [4, 4, 256, 128])
    sn = din("sn", [4, 4, 128])
    smT = din("smT", [4, 4])
    ck = din("ck", [4, 512, 256])
    cv = din("cv", [4, 512, 256])
    W = {
        "w_in": din("w_in", [1024, 3080]),
        "w_out_a": din("w_out_a", [1024, 1024]),
        "w_up0": din("w_up0", [1024, 4096]),
        "w_up1": din("w_up1", [1024, 4096]),
        "w_down0": din("w_down0", [4096, 1024]),
        "w_down1": din("w_down1", [4096, 1024]),
        "w_kv": din("w_kv", [1024, 512]),
        "w_q": din("w_q", [1024, 1024]),
        "w_out_b": din("w_out_b", [1024, 1024]),
    }
    w_ada = din("w_ada", [2, 1024, 6144])
    w_ada_kv = din("w_ada_kv", [1024, 2048])
    vec1 = din("vec1", [112, 128])
    vec2 = din("vec2", [112, 128])
    b_if = din("b_if", [8, 1])
    lnf = din("lnf", [2, 1024])
    rel_rev = din("rel_rev", [16, 385])
    cst = din("cst", [128, 3, 128])

    y_p = dout("y_p", [SEQ, 1024])
    y_s = dout("y_s", [128, 1024])
    o_convp = dout("o_convp", [3, 1024])
    o_Cp = dout("o_Cp", [4, 256, 128])
    o_np = dout("o_np", [4, 128])
    o_mp = dout("o_mp", [4, 1])
    o_kp = dout("o_kp", [512, 256])
    o_vp = dout("o_vp", [512, 256])
    o_convs = dout("o_convs", [4, 3, 1024])
    o_Cs = dout("o_Cs", [4, 4, 256, 128])
    o_ns = dout("o_ns", [4, 4, 128])
    o_ms = dout("o_ms", [4, 4, 1])
    o_ks = dout("o_ks", [128, 256])
    o_vs = dout("o_vs", [128, 256])
    wsc = nc.dram_tensor("wsc", [NPANEL, 128, 8, 512], BF16).ap()

    def sb(name, shape, dt=F32):
        return st.enter_context(nc.sbuf_tensor(name, list(shape), dt))

    PP = [st.enter_context(nc.psum_tensor(f"pp{i}", [128, 1024], F32)) for i in range(4)]
    bankres = [Res(f"bank{i}", excl=True) for i in range(8)]

    def bank(i):
        return PP[i // 2][:, (i % 2) * 512:(i % 2) * 512 + 512]

    def bankb(i):
        return PP[i // 2][:, (i % 2) * 512:(i % 2) * 512 + 512].bitcast(BF16)

    def pair(i):
        return PP[i][:, :]

    wslot = [sb(f"wslot{i}", [128, 8, 512], BF16) for i in range(NSLOT)]
    wslot_r = [S.res(f"wslot{i}") for i in range(NSLOT)]
    xT = sb("xT", [128, 8, 512])
    scr = [sb(f"scr{i}", [128, 1024]) for i in range(3)]
    scr_r = [S.res(f"scr{i}") for i in range(3)]
    uT = sb("uT", [128, 8, 512], BF16)
    bufA = sb("bufA", [128, 8, 512], BF16)
    bufB = sb("bufB", [128, 8, 512], BF16)
    bufC = sb("bufC", [128, 8, 512], BF16)
    cb = sb("cb", [128, 8, 4, 131], BF16)
    vaug = sb("vaug", [128, 4, 4, 257], BF16)
    hT = sb("hT", [128, 32, 512], BF16)
    rtmp = [sb(f"rtmp{i}", [128, 512]) for i in range(2)]
    rtmp_r = [S.res(f"rtmp{i}") for i in range(2)]
    C_sb = sb("C_sb", [128, 4, 257])
    Cb = sb("Cb", [128, 4, 257], BF16)
    sTs = sb("sTs", [128, 4, 128], BF16)
    kw = sb("kw", [128, 4, 128], BF16)
    hn = sb("hn", [128, 4, 256], BF16)
    KTr = sb("KTr", [128, 2, 1024], BF16)
    Vr = sb("Vr", [128, 8, 4, 65], BF16)
    KTc = sb("KTc", [128, 2, 512], BF16)
    Vc = sb("Vc", [128, 4, 4, 65], BF16)
    NPT = 5
    PT = [sb(f"PT{i}", [128, 512], BF16) for i in range(NPT)]
    PT_r = [S.res(f"PT{i}") for i in range(NPT)]
    PT0 = sb("PTm", [128, 512], BF16)
    PT0_r = S.res("PT0")
    expin = [sb(f"expin{i}", [128, 512]) for i in range(2)]
    expin_r = [S.res(f"expin{i}") for i in range(2)]
    On = sb("On", [128, 16, 64], BF16)
    bias3 = sb("bias3", [128, 16, 128], BF16)
    bias4 = sb("bias4", [128, 16, 128], BF16)
    lnfbc = sb("lnfbc", [128, 2, 1024])
    ident = sb("ident", [128, 128])
    identb = sb("identb", [128, 128], BF16)
    trimask = sb("trimask", [128, 128])
    mask4 = sb("mask4", [128, 128])
    diagc = sb("diagc", [128, 8, 4, 128], BF16)
    vecT = sb("vecT", [128, 112])
    biasT = sb("biasT", [128, 112])
    TAB = sb("TAB", [128, 14, 8, 5])
    cT = sb("cT", [128, 8, 5])
    wg = sb("wg", [128, 8, 8], BF16)
    wg32 = sb("wg32", [128, 8, 8])
    bg = sb("bg", [4, 2])
    bg8 = sb("bg8", [8, 1])
    ones4 = sb("ones4", [4, 128])
    chv = sb("chv", [128, 16])
    cprev = sb("cprev", [128, 8, 3], BF16)
    cstage = sb("cstage", [128, 3, 8])
    mcol = sb("mcol", [4, 8])
    mout = sb("mout", [4, 4])
    G_l = sb("G_l", [4, 128]); G_nb = sb("G_nb", [4, 128]); G_g = sb("G_g", [4, 128])
    G_wk = sb("G_wk", [4, 128]); G_thr = sb("G_thr", [4, 128])
    G_s = sb("G_s", [4, 8])
    G_d4 = sb("G_d4", [4, 4])
    gsc = sb("gsc", [128, 4, 16])
    st6 = sb("st6", [128, 4, 6])
    mv = sb("mv", [128, 4, 2])
    hsm = sb("hsm", [128, 8, 4])
    lst = sb("lst", [128, 4, 2, 6])
    lmv = sb("lmv", [128, 4, 8])
    rden = sb("rden", [128, 16])
    modT = scr[0][:, 0:560].rearrange("p (a s) -> p a s", a=112)
    mod1 = scr[1][:, 0:560].rearrange("p (a s) -> p a s", a=112)
    crow = scr[2][0:5, :]
    rowc = expin[0][0:5, :]
    G_ig = expin[1][0:4, :]
    G_fg = expin[0][0:4, :]

    R = {"modT": scr_r[0], "mod1": scr_r[1], "crow": scr_r[2], "rowc": expin_r[0]}

    def r(name):
        if name not in R:
            R[name] = S.res(name)
        return R[name]

    def grid(name):
        return [[S.res(f"{name}_{c}_{b}") for b in range(4)] for c in range(8)]

    g_xT = grid("xT"); g_uT = grid("uT"); g_A = grid("bufA"); g_B = grid("bufB"); g_C = grid("bufC")
    g_cb = grid("cb")
    r_vaug = [S.res(f"vaug{b}") for b in range(4)]
    r_hT = [S.res(f"hT{c}") for c in range(32)]
    r_Csb = [S.res(f"Csb{h}") for h in range(4)]
    r_Cb = [S.res(f"Cb{h}") for h in range(4)]
    r_KTr = [S.res(f"KTr{i}") for i in range(8)]
    r_Vr = [S.res(f"Vr{i}") for i in range(8)]

    def gsel(g, cs, bs):
        return [g[c][b] for c in cs for b in bs]

    ALLC = list(range(8))

    def mm(out, lhsT, rhs, start, stop, rd, wr, sig=True):
        S.op("pe", lambda e: e.matmul(out, lhsT=lhsT, rhs=rhs, start=start, stop=stop), rd, wr, sig)

    def tr(out, in_, idn, rd, wr, sig=True):
        S.op("pe", lambda e: e.transpose(out, in_, idn), rd, wr, sig)

    def act(out, in_, func, rd, wr, bias=None, scale=None):
        kw_ = {}
        if bias is not None:
            kw_["bias"] = bias
        if scale is not None:
            kw_["scale"] = scale
        S.op("act", lambda e: e.activation(out=out, in_=in_, func=func, **kw_), rd, wr)

    def ts(eng, out, in0, s1, s2, op0, op1, rd, wr):
        if s2 is None:
            S.op(eng, lambda e: e.tensor_scalar(out=out, in0=in0, scalar1=s1, scalar2=None, op0=op0), rd, wr)
        else:
            S.op(eng, lambda e: e.tensor_scalar(out=out, in0=in0, scalar1=s1, scalar2=s2, op0=op0, op1=op1), rd, wr)

    def tt(eng, out, in0, in1, op, rd, wr):
        S.op(eng, lambda e: e.tensor_tensor(out=out, in0=in0, in1=in1, op=op), rd, wr)

    def stt(out, in0, scalar, in1, op0, op1, rd, wr):
        S.op("dve", lambda e: e.scalar_tensor_tensor(out=out, in0=in0, scalar=scalar, in1=in1, op0=op0, op1=op1), rd, wr)

    def cp(eng, out, in_, rd, wr):
        if eng == "act":
            if "i" in os.environ.get("DBG", ""):
                S.op("act", lambda e: e.activation(out=out, in_=in_, func=AF.Identity), rd, wr)
            else:
                S.op("act", lambda e: e.copy(out=out, in_=in_), rd, wr)
        else:
            S.op(eng, lambda e: e.tensor_copy(out=out, in_=in_), rd, wr)

    def recip(out, in_, rd, wr):
        S.op("dve", lambda e: e.reciprocal(out=out, in_=in_), rd, wr)

    affctr = [0]

    def aff(eng, out, in_, A, B, rd, wr):
        if eng == "dve":
            ts("dve", out, in_, A, B, ALU.mult, ALU.add, rd, wr)
        else:
            act(out, in_, AF.Identity, rd, wr, bias=B, scale=A)

    cpctr = [0]

    def cpa(out, in_, rd, wr):
        cpctr[0] += 1
        cp("dve" if cpctr[0] % 2 == 0 else "act", out, in_, rd, wr)

    S.dma("sp", ident[:], cst[:, 0, :], writes=[r("ident")])
    S.dma("sp", trimask[:], cst[:, 1, :], writes=[r("trimask")])
    S.dma("sp", mask4[:], cst[:, 2, :], writes=[r("mask4")])
    cp("dve", identb[:], ident[:], [r("ident")], [r("identb")])
    S.op("dve", lambda e: e.memset(ones4[:], 1.0), [], [r("ones4")])
    S.dma("sp", lnfbc[:, 0, :], dap(lnf, 0, [[0, 128], [1, 1024]]), writes=[r("lnfbc")])
    S.dma("sp", lnfbc[:, 1, :], dap(lnf, 1024, [[0, 128], [1, 1024]]), writes=[r("lnfbc")])

    S.dma("sp", scr[0][0:112, 0:128], vec1[:, :], writes=[scr_r[0]])
    S.dma("sp", scr[1][0:112, 0:128], vec2[:, :], writes=[scr_r[1]])
    tr(bank(0)[:, 0:112], scr[0][0:112, 0:128], ident[0:112, 0:112], [scr_r[0], r("ident")], [bankres[0]])
    tr(bank(1)[:, 0:112], scr[1][0:112, 0:128], ident[0:112, 0:112], [scr_r[1], r("ident")], [bankres[1]])
    cp("dve", vecT[:], bank(0)[:, 0:112], [bankres[0]], [r("vecT")])
    cp("dve", biasT[:], bank(1)[:, 0:112], [bankres[1]], [r("biasT")])

    for c in range(8):
        for j in range(4):
            ts("dve", diagc[:, c, j, :], ident[:], vecT[:, j * 8 + c:j * 8 + c + 1], None, ALU.mult, None,
               [r("ident"), r("vecT")], [r("diagc")])

    S.dma("sp", wg32[:], dap(W["w_in"], 3072, [[3080, 128], [128 * 3080, 8], [1, 8]]), writes=[r("wg32")])
    cp("dve", wg[:], wg32[:], [r("wg32")], [r("wg")])
    S.dma("sp", bg[:, 0:1], b_if[0:4, :], writes=[r("bg")])
    S.dma("sp", bg[:, 1:2], b_if[4:8, :], writes=[r("bg")])
    ts("dve", bg[:, 1:2], bg[:, 1:2], -1.0, None, ALU.mult, None, [r("bg")], [r("bg")])

    S.dma("sp", crow[:], c_all[:, :], writes=[r("crow")])
    act(crow[:], crow[:], AF.Silu, [r("crow")], [r("crow")])
    for kc in range(8):
        tr(bank(2)[:, kc * 8:kc * 8 + 5], crow[:, kc * 128:(kc + 1) * 128], ident[0:5, 0:5],
           [r("crow"), r("ident")], [bankres[2]])
    cp("dve", cT[:], bank(2)[:, 0:64].rearrange("p (k s) -> p k s", k=8)[:, :, 0:5], [bankres[2]], [r("cT")])

    hT32 = hT[:].rearrange("p a b -> p (a b)").bitcast(F32)
    stg32 = [hT32[:, i * 4096:(i + 1) * 4096].rearrange("p (k n) -> p k n", k=8) for i in range(2)]
    stg32_r = [S.res("stg32_0"), S.res("stg32_1")]
    npan = 0
    ada_srcs = []
    for l in range(2):
        for j in range(12):
            ada_srcs.append((w_ada, l * 1024 * 6144 + j * 512, 6144))
    for j in range(4):
        ada_srcs.append((w_ada_kv, j * 512, 2048))
    for pi, (wt, off, ncols) in enumerate(ada_srcs):
        sl = pi % 2
        S.dma("sp", stg32[sl], dap(wt, off, [[ncols, 128], [128 * ncols, 8], [1, 512]]), writes=[stg32_r[sl]])
        pb = 3 + (pi % 2)
        for kc in range(8):
            mm(bank(pb)[0:5, :], cT[:, kc, :], stg32[sl][:, kc, :], kc == 0, kc == 7,
               [r("cT"), stg32_r[sl]], [bankres[pb]], sig=(kc == 7))
        cp("act", rowc[:], bank(pb)[0:5, :], [bankres[pb]], [r("rowc")])
        tb = 5 + (pi % 2)
        for q in range(4):
            tr(bank(tb)[:, q * 8:q * 8 + 5], rowc[:, q * 128:(q + 1) * 128], ident[0:5, 0:5],
               [r("rowc"), r("ident")], [bankres[tb]])
        cp("dve", modT[:, pi * 4:pi * 4 + 4, :], bank(tb)[:, 0:32].rearrange("p (q s) -> p q s", q=4)[:, :, 0:5],
           [bankres[tb]], [r("modT")])
    tt("dve", modT[:], modT[:], biasT[:].unsqueeze(2).broadcast_to([128, 112, 5]), ALU.add,
       [r("modT"), r("biasT")], [r("modT")])
    ts("dve", mod1[:], modT[:], 1.0, None, ALU.add, None, [r("modT")], [r("mod1")])

    def mchunk(l, j):
        return l * 48 + j * 8

    def gam(l, i):
        o = 40 + (l * 2 + i) * 8
        return vecT[:, o:o + 8]

    def bet(l, i):
        o = 72 + (l * 2 + i) * 8
        return vecT[:, o:o + 8]

    def bc5(a):
        return a.unsqueeze(2).broadcast_to([128, 8, 5])

    rT = [r("modT"), r("mod1"), r("vecT")]
    cp("dve", TAB[:, 0], mod1[:, mchunk(0, 1):mchunk(0, 1) + 8, :], rT, [r("TAB")])
    cp("dve", TAB[:, 1], modT[:, mchunk(0, 0):mchunk(0, 0) + 8, :], rT, [r("TAB")])
    for kind, l, j in ((2, 0, 2), (5, 0, 5), (10, 1, 2), (13, 1, 5)):
        ts("dve", TAB[:, kind], mod1[:, mchunk(l, j):mchunk(l, j) + 8, :], 1.0 / ALPHA, None, ALU.mult, None, rT, [r("TAB")])

    def mkAB(kA, kB, g_, b_, sc_off, sh_off):
        tt("dve", TAB[:, kA], mod1[:, sc_off:sc_off + 8, :], bc5(g_), ALU.mult, rT + [r("TAB")], [r("TAB")])
        tt("dve", TAB[:, kB], mod1[:, sc_off:sc_off + 8, :], bc5(b_), ALU.mult, rT + [r("TAB")], [r("TAB")])
        tt("dve", TAB[:, kB], TAB[:, kB], modT[:, sh_off:sh_off + 8, :], ALU.add, rT + [r("TAB")], [r("TAB")])

    mkAB(3, 4, gam(0, 0), bet(0, 0), mchunk(0, 4), mchunk(0, 3))
    mkAB(6, 7, gam(0, 1), bet(0, 1), mchunk(1, 1), mchunk(1, 0))
    mkAB(8, 9, gam(0, 1), bet(0, 1), 96 + 8, 96)
    mkAB(11, 12, gam(1, 0), bet(1, 0), mchunk(1, 4), mchunk(1, 3))

    def tab(kind, c, seq):
        return TAB[:, kind, c, seq:seq + 1]

    bst = [hT32[:, 0:2048].rearrange("p (h q) -> p h q", h=16), hT32[:, 2048:4096].rearrange("p (h q) -> p h q", h=16)]
    S.barrier()
    S.dma("sp", bst[0], dap(rel_rev, 1, [[1, 128], [385, 16], [1, 128]]), writes=[stg32_r[0]])
    S.dma("sp", bst[1], dap(rel_rev, 129, [[1, 128], [385, 16], [1, 128]]), writes=[stg32_r[0]])
    S.dma("sp", chv[:], dap(rel_rev, 128, [[0, 128], [385, 16]]), writes=[r("chv")], allow_slow_non_contiguous=True)

    def flipped(a):
        return bass.AP(tensor=a.tensor, offset=a.offset + 127, ap=[list(a.ap[0]), list(a.ap[1]), [-1, 128]])

    chb = chv[:].unsqueeze(2).broadcast_to([128, 16, 128])
    tt("dve", bias3[:], flipped(bst[0]), chb, ALU.subtract, [stg32_r[0], r("chv")], [r("bias3")])
    tt("dve", bst[1], bst[1], flipped(chb) if False else chb, ALU.subtract, [stg32_r[0], r("chv")], [stg32_r[0]])
    m4b = bass.AP(tensor=mask4[:].tensor, offset=mask4[:].offset, ap=[list(mask4[:].ap[0]), [0, 16], [1, 128]])
    tt("dve", bias4[:], flipped(bst[1]), m4b, ALU.add, [stg32_r[0], r("mask4")], [r("bias4")])
    S.barrier()

    def ckpt(n):
        if stage <= n:
            raise _Stop()

    plist = panel_list()
    r_wsc = S.res("wsc")
    casteng = ["dve", "act", "pool"]
    for pi, (wn, row0, col0, ncols, kind) in enumerate(plist):
        sl = pi % 2
        ws = pi % NSLOT
        S.dma("sp", stg32[sl], dap(W[wn], row0 * ncols + col0, [[ncols, 128], [128 * ncols, 8], [1, 512]]),
              writes=[stg32_r[sl]])
        eng = casteng[pi % 3]
        if kind == "qperm":
            for k in range(2):
                o_ = wslot[ws][:].rearrange("p c (g k d) -> p c g k d", g=4, k=2)[:, :, :, k, :]
                i_ = stg32[sl].rearrange("p c (k g d) -> p c k g d", k=2, g=4)[:, :, k, :, :]
                cp("dve", o_, i_, [stg32_r[sl]], [wslot_r[ws]])
        else:
            cp(eng, wslot[ws][:], stg32[sl], [stg32_r[sl]], [wslot_r[ws]])
        S.dma("pool", wsc[pi], wslot[ws][:], reads=[wslot_r[ws]], writes=[r_wsc])
    S.barrier()

    S.op("dve", lambda e: e.memset(vaug[:], 1.0), [], r_vaug)
    S.op("dve", lambda e: e.memset(Vr[:], 1.0), [], r_Vr)
    S.op("dve", lambda e: e.memset(Vc[:], 1.0), [], [r("Vc")])
    S.op("dve", lambda e: e.memset(PT0[:], 0.0), [], [PT0_r])
    S.op("dve", lambda e: e.memset(gsc[:], 0.0), [], [r("gsc")])

    tiles = []
    if do_sample:
        tiles.append(("s", 0))
    for ti in range(NT):
        tiles.append(("p", ti))
    uses = [pi for _ in tiles for pi in range(NPANEL)]
    wstate = {"next": 0, "cur": -1}

    def wnext():
        wstate["cur"] += 1
        i = wstate["cur"]
        while wstate["next"] <= min(i + PDEPTH, len(uses) - 1):
            n = wstate["next"]
            S.dma("sp", wslot[n % NSLOT][:], wsc[uses[n]], reads=[r_wsc], writes=[wslot_r[n % NSLOT]])
            wstate["next"] += 1
        return wslot[i % NSLOT], wslot_r[i % NSLOT]

    bctr = [0]

    def rot(lst):
        bctr[0] += 1
        return lst[bctr[0] % len(lst)]

    scrctr = [0]

    def nscr():
        scrctr[0] += 1
        i = scrctr[0] % 3
        return scr[i], scr_r[i]

    def proj_a(ws, wr_, nchunk, rhs_buf, g_rhs, T, evac, banks):
        for j in range(nchunk):
            bi = rot(banks)
            for kc in range(8):
                mm(bank(bi)[:, 0:T], ws[:, kc, j * 128:(j + 1) * 128], rhs_buf[:, kc, 0:T], kc == 0, kc == 7,
                   [wr_] + gsel(g_rhs, [kc], range(4)), [bankres[bi]], sig=(kc == 7))
            evac(j, bi)

    def ln_block(L, b, cs, eps, targets, final_dst=None, nbk=4):
        pz = (b % 2)
        pbk = 2 + (b % 2)
        zp = pair(pz)
        zres = [bankres[2 * pz], bankres[2 * pz + 1]]
        for c in range(8):
            tr(zp[0:L, c * 128:(c + 1) * 128], xT[:, c, cs], ident[:, :], [g_xT[c][b], r("ident")], zres, sig=(c == 7))
        S.op("dve", lambda e: e.bn_stats(out=lst[0:L, 0, :], in_=zp[0:L, 0:512]), zres, [r("lst")])
        S.op("dve", lambda e: e.bn_stats(out=lst[0:L, 1, :], in_=zp[0:L, 512:1024]), zres, [r("lst")])
        S.op("dve", lambda e: e.bn_aggr(out=lmv[0:L, 0:2], in_=lst[0:L].rearrange("p a b -> p (a b)")), [r("lst")], [r("lmv")])
        ts("dve", lmv[0:L, 2:3], lmv[0:L, 1:2], eps, None, ALU.add, None, [r("lmv")], [r("lmv")])
        act(lmv[0:L, 3:4], lmv[0:L, 2:3], AF.Sqrt, [r("lmv")], [r("lmv")])
        recip(lmv[0:L, 4:5], lmv[0:L, 3:4], [r("lmv")], [r("lmv")])
        stt(lmv[0:L, 5:6], lmv[0:L, 0:1], -1.0, lmv[0:L, 4:5], ALU.mult, ALU.mult, [r("lmv")], [r("lmv")])
        xh, xh_r = nscr()
        act(xh[0:L, :], zp[0:L, :], AF.Identity, zres + [r("lmv")], [xh_r], bias=lmv[0:L, 5:6], scale=lmv[0:L, 4:5])
        return xh, xh_r

    def back_block(L, b, cs, xh, xh_r, gb, targets):
        pbk = 2 + (b % 2)
        bp = pair(pbk)
        bres = [bankres[2 * pbk], bankres[2 * pbk + 1]]
        for c in range(8):
            tr(bp[:, c * 128:c * 128 + L], xh[0:L, c * 128:(c + 1) * 128], ident[0:L, 0:L], [xh_r, r("ident")], bres,
               sig=(c == 7))
        for c in range(8):
            src = bp[:, c * 128:c * 128 + L]
            eng = "dve" if c < 4 else "act"
            br1 = [bres[c // 4]]
            if gb is None:
                cp(eng, xT[:, c, cs], src, br1, [g_xT[c][b]])
            else:
                aff(eng, xT[:, c, cs], src, gb[0][:, c:c + 1], gb[1][:, c:c + 1], br1 + [r("vecT")], [g_xT[c][b]])
            for (buf, g_, kA, kB, seq) in targets:
                aff(eng, buf[:, c, cs], src, tab(kA, c, seq), tab(kB, c, seq), br1 + [r("TAB")], [g_[c][b]])

    def ln_stage(L, nb, seqs, prompt, ti, mode, gb, targets):
        for g0 in (0, 2):
            grp = [g0, g0 + 1]
            xh = {}
            if mode == "xload":
                for b in grp:
                    xs, xs_r = nscr()
                    src = x_p[ti * 512 + b * 128: ti * 512 + (b + 1) * 128, :] if prompt else x_s[b * 32:(b + 1) * 32, :]
                    S.dma(XQ, xs[0:L, :], src, writes=[xs_r])
                    xh[b] = (xs, xs_r)
            else:
                zps = {}
                for b in grp:
                    cs = slice(b * L, (b + 1) * L)
                    zp = pair(b % 2)
                    zres = [bankres[2 * (b % 2)], bankres[2 * (b % 2) + 1]]
                    zps[b] = (zp, zres)
                    for c in range(8):
                        tr(zp[0:L, c * 128:(c + 1) * 128], xT[:, c, cs], ident[:, :], [g_xT[c][b], r("ident")], zres, sig=(c == 7))
                for b in grp:
                    zp, zres = zps[b]
                    rl = [r(f"lmv{b}")]
                    S.op("dve", lambda e, b=b, zp=zp: e.bn_stats(out=lst[0:L, b, 0, :], in_=zp[0:L, 0:512]), zres, rl)
                    S.op("dve", lambda e, b=b, zp=zp: e.bn_stats(out=lst[0:L, b, 1, :], in_=zp[0:L, 512:1024]), zres, rl)
                    S.op("dve", lambda e, b=b: e.bn_aggr(out=lmv[0:L, b, 0:2], in_=lst[0:L, b].rearrange("p a b -> p (a b)")), rl, rl)
                    ts("dve", lmv[0:L, b, 2:3], lmv[0:L, b, 1:2], LN_EPS_P, None, ALU.add, None, rl, rl)
                for b in grp:
                    rl = [r(f"lmv{b}")]
                    act(lmv[0:L, b, 3:4], lmv[0:L, b, 2:3], AF.Sqrt, rl, rl)
                for b in grp:
                    rl = [r(f"lmv{b}")]
                    recip(lmv[0:L, b, 4:5], lmv[0:L, b, 3:4], rl, rl)
                    stt(lmv[0:L, b, 5:6], lmv[0:L, b, 0:1], -1.0, lmv[0:L, b, 4:5], ALU.mult, ALU.mult, rl, rl)
                for b in grp:
                    zp, zres = zps[b]
                    xs, xs_r = nscr()
                    act(xs[0:L, :], zp[0:L, :], AF.Identity, zres + [r(f"lmv{b}")], [xs_r], bias=lmv[0:L, b, 5:6], scale=lmv[0:L, b, 4:5])
                    xh[b] = (xs, xs_r)
            if mode == "final":
                for b in grp:
                    xs, xs_r = xh[b]
                    tt("pool", xs[0:L, :], xs[0:L, :], lnfbc[0:L, 0, :], ALU.mult, [xs_r, r("lnfbc")], [xs_r])
                    tt("dve", xs[0:L, :], xs[0:L, :], lnfbc[0:L, 1, :], ALU.add, [xs_r, r("lnfbc")], [xs_r])
                    dst = y_p[ti * 512 + b * 128: ti * 512 + (b + 1) * 128, :] if prompt else y_s[b * 32:(b + 1) * 32, :]
                    S.dma("pool", dst, xs[0:L, :], reads=[xs_r], writes=[r("o_y")])
                continue
            for b in grp:
                xs, xs_r = xh[b]
                for c in range(8):
                    o_ = (c % 4) * 256 + (b % 2) * 128
                    tr(PP[2 + c // 4][:, o_:o_ + L], xs[0:L, c * 128:(c + 1) * 128], ident[0:L, 0:L], [xs_r, r("ident")],
                       [bankres[4 + c // 2]], sig=(c == 7))
            for c in range(8):
                bk = 4 + c // 2
                eng = "dve" if (c // 2) % 2 == 0 else "act"
                if prompt:
                    src = PP[2 + c // 4][:, (c % 4) * 256:(c % 4) * 256 + 256]
                    cs2 = slice(g0 * 128, g0 * 128 + 256)
                    wr_x = gsel(g_xT, [c], grp)
                    if gb is None:
                        cp(eng, xT[:, c, cs2], src, [bankres[bk]], wr_x)
                    else:
                        aff(eng, xT[:, c, cs2], src, gb[0][:, c:c + 1], gb[1][:, c:c + 1], [bankres[bk], r("vecT")], wr_x)
                    for (buf, g_, kA, kB) in targets:
                        aff(eng, buf[:, c, cs2], src, tab(kA, c, 0), tab(kB, c, 0), [bankres[bk], r("TAB")], gsel(g_, [c], grp))
                else:
                    for b in grp:
                        o_ = (c % 4) * 256 + (b % 2) * 128
                        src = PP[2 + c // 4][:, o_:o_ + L]
                        cs = slice(b * L, (b + 1) * L)
                        if gb is None:
                            cp(eng, xT[:, c, cs], src, [bankres[bk]], [g_xT[c][b]])
                        else:
                            aff(eng, xT[:, c, cs], src, gb[0][:, c:c + 1], gb[1][:, c:c + 1], [bankres[bk], r("vecT")], [g_xT[c][b]])
                        for (buf, g_, kA, kB) in targets:
                            aff(eng, buf[:, c, cs], src, tab(kA, c, seqs[b]), tab(kB, c, seqs[b]), [bankres[bk], r("TAB")], [g_[c][b]])

    def resid_evac(L, nb, seqs, kind, same):
        def ev(c, bi):
            if same:
                T = nb * L
                stt(xT[:, c, 0:T], bank(bi)[:, 0:T], tab(kind, c, seqs[0]), xT[:, c, 0:T], ALU.mult, ALU.add,
                    [bankres[bi], r("TAB")] + gsel(g_xT, [c], range(nb)), gsel(g_xT, [c], range(nb)))
            else:
                for b in range(nb):
                    cs = slice(b * L, (b + 1) * L)
                    stt(xT[:, c, cs], bank(bi)[:, cs], tab(kind, c, seqs[b]), xT[:, c, cs], ALU.mult, ALU.add,
                        [bankres[bi], r("TAB"), g_xT[c][b]], [g_xT[c][b]])
        return ev

    def do_tile(kind, ti):
        prompt = (kind == "p")
        nb = 4
        L = 128 if prompt else 32
        T = nb * L
        seqs = [0, 0, 0, 0] if prompt else [1, 2, 3, 4]
        last = prompt and ti == NT - 1
        first = prompt and ti == 0
        BL = range(nb)
        CS = [slice(b * L, (b + 1) * L) for b in BL]

        ln_stage(L, nb, seqs, prompt, ti, "xload", None, [(uT, g_uT, 0, 1)])

        ckpt(2)
        def ev_qk(base):
            def ev(j, bi):
                c = base + j
                cpa(cb[:, c, 0:nb, 3:3 + L], bank(bi)[:, 0:T].rearrange("p (b t) -> p b t", b=nb),
                    [bankres[bi]], gsel(g_cb, [c], BL))
            return ev
        for pj in range(2):
            ws, wr_ = wnext()
            proj_a(ws, wr_, 4, uT, g_uT, T, ev_qk(pj * 4), [0, 1, 2, 3])
        for pj in range(2):
            ws, wr_ = wnext()
            for b in BL:
                bi = rot([4, 5, 6, 7])
                for kc in range(8):
                    mm(bank(bi)[0:L, :], uT[:, kc, CS[b]], ws[:, kc, :], kc == 0, kc == 7,
                       [wr_, g_uT[kc][b]], [bankres[bi]], sig=(kc == 7))
                cpa(vaug[0:L, b, pj * 2:pj * 2 + 2, 0:256], bank(bi)[0:L, :].rearrange("p (h v) -> p h v", h=2),
                    [bankres[bi]], [r_vaug[b]])

        def ev_o(base):
            def ev(j, bi):
                c = base + j
                act(bufB[:, c, 0:T], bank(bi)[:, 0:T], AF.Sigmoid, [bankres[bi]], gsel(g_B, [c], BL))
            return ev
        for pj in range(2):
            ws, wr_ = wnext()
            proj_a(ws, wr_, 4, uT, g_uT, T, ev_o(pj * 4), [0, 1, 2, 3])
        for gi in range(2):
            for kc in range(8):
                mm(bank(6 + gi)[0:4, 0:T], wg[:, kc, gi * 4:gi * 4 + 4], uT[:, kc, 0:T], kc == 0, kc == 7,
                   [r("wg")] + gsel(g_uT, [kc], BL), [bankres[6 + gi]], sig=(kc == 7))
        cp("act", G_ig[:, 0:T], bank(6)[0:4, 0:T], [bankres[6]], [expin_r[1]])
        cp("act", G_fg[:, 0:T], bank(7)[0:4, 0:T], [bankres[7]], [expin_r[0]])

        ckpt(3)
        for b in BL:
            if prompt:
                if b == 0:
                    if first:
                        S.op("dve", lambda e: e.memset(cb[:, :, 0, 0:3], 0.0), [], gsel(g_cb, ALLC, [0]))
                    else:
                        cp("dve", cb[:, :, 0, 0:3], cprev[:], [r("cprev")], gsel(g_cb, ALLC, [0]))
                else:
                    cp("dve", cb[:, :, b, 0:3], cb[:, :, b - 1, L:L + 3], gsel(g_cb, ALLC, [b - 1]), gsel(g_cb, ALLC, [b]))
            else:
                S.dma("act", cstage[:].rearrange("p j c -> p (j c)"),
                      dap(sconv, b * 3072, [[1, 128], [128, 24]]), writes=[r("cstage")],
                      allow_slow_non_contiguous=True)
                cp("dve", cb[:, :, b, 0:3], cstage[:].rearrange("p j c -> p c j"), [r("cstage")], gsel(g_cb, ALLC, [b]))
        for c in range(8):
            bi = rot([0, 1, 2, 3])
            for j in range(4):
                mm(bank(bi)[:, 0:T], diagc[:, c, j, :], cb[:, c, 0:nb, j:j + L], j == 0, j == 3,
                   [r("diagc")] + gsel(g_cb, [c], BL), [bankres[bi]], sig=(j == 3))
            act(bufC[:, c, 0:T], bank(bi)[:, 0:T], AF.Silu, [bankres[bi], r("vecT")], gsel(g_C, [c], BL),
                bias=vecT[:, 32 + c:33 + c])
        cp("dve", cprev[:], cb[:, :, nb - 1, L:L + 3], gsel(g_cb, ALLC, [nb - 1]), [r("cprev")])
        if last or not prompt:
            for b in ([nb - 1] if prompt else BL):
                cp("dve", cstage[:].rearrange("p j c -> p c j"), cb[:, :, b, L:L + 3], gsel(g_cb, ALLC, [b]), [r("cstage")])
                dst = dap(o_convp, 0, [[1, 128], [128, 24]]) if prompt else \
                    dap(o_convs, b * 3072, [[1, 128], [128, 24]])
                S.dma("pool", dst, cstage[:].rearrange("p j c -> p (j c)"), reads=[r("cstage")], writes=[r("o_conv")],
                      allow_slow_non_contiguous=True)

        ckpt(4)
        if not prompt:
            S.dma("act", mcol[:, 0:4], smT[:, :], writes=[r("mcol")])
        elif first:
            S.op("dve", lambda e: e.memset(mcol[:], 0.0), [], [r("mcol")])
            S.op("dve", lambda e: e.memset(C_sb[:], 0.0), [], r_Csb)
        for b in BL:
            cs = CS[b]
            rg = [r("G")]
            mprev = mcol[:, 0:1] if prompt else mcol[:, b:b + 1]
            act(G_l[:, 0:L], G_fg[:, cs], AF.Exp, [expin_r[0], r("bg")], rg, bias=bg[:, 1:2], scale=-1.0)
            act(G_l[:, 0:L], G_l[:, 0:L], AF.Ln, rg, rg, bias=1.0)
            S.op("dve", lambda e: e.tensor_tensor_scan(out=G_nb[:, 0:L], data0=ones4[:, 0:L], data1=G_l[:, 0:L],
                                                       initial=0.0, op0=ALU.mult, op1=ALU.add), rg + [r("ones4")], rg)
            stt(G_g[:, 0:L], G_ig[:, cs], bg[:, 0:1], G_nb[:, 0:L], ALU.add, ALU.add, [expin_r[1], r("bg")] + rg, rg)
            S.op("dve", lambda e: e.tensor_reduce(out=G_s[:, 0:1], in_=G_g[:, 0:L], axis=AX.X, op=ALU.max), rg, rg)
            tt("dve", G_s[:, 1:2], G_s[:, 0:1], mprev, ALU.max, rg + [r("mcol")], rg)
            ts("dve", G_s[:, 2:3], G_s[:, 1:2], -1.0, None, ALU.mult, None, rg, rg)
            act(G_s[:, 3:4], mprev, AF.Exp, rg + [r("mcol")], rg, bias=G_s[:, 2:3])
            act(G_wk[:, 0:L], G_g[:, 0:L], AF.Exp, rg, rg, bias=G_s[:, 2:3])
            act(G_thr[:, 0:L], G_nb[:, 0:L], AF.Exp, rg, rg, bias=G_s[:, 2:3])
            if prompt:
                tt("dve", mcol[:, 0:1], G_s[:, 1:2], G_nb[:, L - 1:L], ALU.subtract, rg + [r("mcol")], [r("mcol")])
            else:
                tt("dve", mout[:, b:b + 1], G_s[:, 1:2], G_nb[:, L - 1:L], ALU.subtract, rg, [r("mout")])
            ts("dve", G_d4[:], ident[0:4, 0:4], G_s[:, 3:4], None, ALU.mult, None, rg + [r("ident")], rg)
            gb_ = 5
            tr(bank(gb_)[0:L, 0:4], G_wk[:, 0:L], ident[0:4, 0:4], rg + [r("ident")], [bankres[gb_]])
            tr(bank(gb_)[0:L, 4:8], G_thr[:, 0:L], ident[0:4, 0:4], rg + [r("ident")], [bankres[gb_]])
            mm(bank(gb_)[:, 8:12], ones4[:, :], G_d4[:], True, True, rg + [r("ones4")], [bankres[gb_]])
            cp("dve", gsc[0:L, b, 0:8], bank(gb_)[0:L, 0:8], [bankres[gb_]], [r("gsc")])
            cp("dve", gsc[:, b, 8:12], bank(gb_)[:, 8:12], [bankres[gb_]], [r("gsc")])
            ts("dve", gsc[:, b, 12:16], gsc[:, b, 8:12], 128.0 ** -0.5, None, ALU.mult, None, [r("gsc")], [r("gsc")])

            if not prompt:
                stg, stg_r = nscr()
                S.dma("act", stg[:, :].rearrange("p (a d) -> p a d", a=8),
                      dap(sC, b * 4 * 256 * 128, [[128, 128], [128 * 128, 8], [1, 128]]), writes=[stg_r])
                for a in range(8):
                    tr(pair(0)[:, a * 128:(a + 1) * 128], stg[:, a * 128:(a + 1) * 128], ident[:, :],
                       [stg_r, r("ident")], [bankres[0], bankres[1]], sig=(a == 7))
                for h in range(4):
                    cp("dve" if h < 2 else "act", C_sb[:, h, 0:256], pair(0)[:, h * 256:(h + 1) * 256], [bankres[h // 2]], [r_Csb[h]])
                S.dma("act", C_sb[:, :, 256:257], dap(sn, b * 512, [[1, 128], [128, 4], [1, 1]]), writes=[r("Cn")],
                      reads=[], allow_slow_non_contiguous=True)
            rCn = [] if prompt else [r("Cn")]

            for h in range(4):
                mm(bank(0)[0:L, h * 128:h * 128 + L], bufC[:, 4 + h, cs], bufC[:, h, cs], True, True,
                   [g_C[4 + h][b], g_C[h][b]], [bankres[0]], sig=(h == 3))
            for h in range(4):
                stt(sTs[0:L, h, 0:L], bank(0)[0:L, h * 128:h * 128 + L], gsc[0:L, b, h:h + 1], trimask[0:L, 0:L],
                    ALU.mult, ALU.mult, [bankres[0], r("gsc"), r("trimask")], [r("sTs")])
            for h in range(4):
                tr(bankb(1)[0:L, h * 128:(h + 1) * 128], bufC[:, 4 + h, cs], identb[:, :], [g_C[4 + h][b], r("identb")],
                   [bankres[1]], sig=(h == 3))
            tt("dve", kw[0:L, :, :], bankb(1)[0:L, 0:512].rearrange("p (h d) -> p h d", h=4),
               gsc[0:L, b, 0:4].unsqueeze(2).broadcast_to([L, 4, 128]), ALU.mult, [bankres[1], r("gsc")], [r("kw")])
            for h in range(4):
                act(Cb[:, h, :], C_sb[:, h, :], AF.Identity, [r_Csb[h], r("gsc")] + rCn, [r_Cb[h]], scale=gsc[:, b, 12 + h:13 + h])
            for h in range(4):
                nbk = 2 + h
                mm(bank(nbk)[0:L, 0:257], sTs[0:L, h, 0:L], vaug[0:L, b, h, :], True, False,
                   [r("sTs"), r_vaug[b]], [bankres[nbk]], sig=False)
                mm(bank(nbk)[0:L, 0:257], bufC[:, h, cs], Cb[:, h, :], False, True,
                   [g_C[h][b], r_Cb[h]], [bankres[nbk]])
            for h in range(4):
                dbk = 6 + (h % 2)
                mm(bank(dbk)[:, 0:257], kw[0:L, h, :], vaug[0:L, b, h, :], True, True, [r("kw"), r_vaug[b]], [bankres[dbk]])
                stt(C_sb[:, h, :], C_sb[:, h, :], gsc[:, b, 8 + h:9 + h], bank(dbk)[:, 0:257], ALU.mult, ALU.add,
                    [r_Csb[h], r("gsc"), bankres[dbk]] + rCn, [r_Csb[h]])
            rh = [r("hsm")]
            for h in range(4):
                nbk = 2 + h
                S.op("dve", lambda e, h=h, nbk=nbk: e.bn_stats(out=st6[0:L, h, :], in_=bank(nbk)[0:L, 0:256]),
                     [bankres[nbk]], [r("st6")])
                S.op("dve", lambda e, h=h: e.bn_aggr(out=mv[0:L, h, :], in_=st6[0:L, h, :]), [r("st6")], [r("mv")])
                cp("dve", hsm[0:L, 0, h:h + 1], bank(nbk)[0:L, 256:257], [bankres[nbk]], rh)
            stt(hsm[0:L, 1, :], hsm[0:L, 0, :], -1.0, hsm[0:L, 0, :], ALU.mult, ALU.max, rh, rh)
            tt("dve", hsm[0:L, 1, :], hsm[0:L, 1, :], gsc[0:L, b, 4:8], ALU.max, rh + [r("gsc")], rh)
            recip(hsm[0:L, 2, :], hsm[0:L, 1, :], rh, rh)
            tt("dve", hsm[0:L, 3, :], hsm[0:L, 2, :], hsm[0:L, 2, :], ALU.mult, rh, rh)
            tt("dve", hsm[0:L, 3, :], hsm[0:L, 3, :], mv[0:L, :, 1], ALU.mult, rh + [r("mv")], rh)
            ts("dve", hsm[0:L, 3, :], hsm[0:L, 3, :], HN_EPS, None, ALU.add, None, rh, rh)
            act(hsm[0:L, 4, :], hsm[0:L, 3, :], AF.Sqrt, rh, rh)
            recip(hsm[0:L, 5, :], hsm[0:L, 4, :], rh, rh)
            tt("dve", hsm[0:L, 6, :], hsm[0:L, 2, :], hsm[0:L, 5, :], ALU.mult, rh, rh)
            stt(hsm[0:L, 7, :], mv[0:L, :, 0], -1.0, hsm[0:L, 6, :], ALU.mult, ALU.mult, rh + [r("mv")], rh)
            for h in range(4):
                nbk = 2 + h
                act(hn[0:L, h, :], bank(nbk)[0:L, 0:256], AF.Identity, [bankres[nbk]] + rh, [r("hn")],
                    bias=hsm[0:L, 7, h:h + 1], scale=hsm[0:L, 6, h:h + 1])
            for j in range(8):
                tr(bankb(1)[:, j * 128:j * 128 + L], hn[0:L, j // 2, (j % 2) * 128:(j % 2) * 128 + 128], identb[0:L, 0:L],
                   [r("hn"), r("identb")], [bankres[1]], sig=(j == 7))
            for j in range(8):
                stt(bufA[:, j, cs], bankb(1)[:, j * 128:j * 128 + L], vecT[:, 104 + j:105 + j], bufB[:, j, cs],
                    ALU.mult, ALU.mult, [bankres[1], r("vecT"), g_B[j][b]], [g_A[j][b]])

            if (last and b == nb - 1) or not prompt:
                for h in range(4):
                    for vh in range(2):
                        a = h * 2 + vh
                        tr(pair(0)[:, a * 128:(a + 1) * 128], C_sb[:, h, vh * 128:(vh + 1) * 128], ident[:, :],
                           [r_Csb[h], r("ident")] + rCn, [bankres[0], bankres[1]], sig=(a == 7))
                stg, stg_r = nscr()
                cp("dve", stg[:, :], pair(0)[:, :], [bankres[0], bankres[1]], [stg_r])
                dC = dap(o_Cp, 0, [[128, 128], [128 * 128, 8], [1, 128]]) if prompt else \
                    dap(o_Cs, b * 4 * 256 * 128, [[128, 128], [128 * 128, 8], [1, 128]])
                S.dma("pool", dC, stg[:, :].rearrange("p (a d) -> p a d", a=8), reads=[stg_r], writes=[r("o_C")])
                dn = dap(o_np, 0, [[1, 128], [128, 4], [1, 1]]) if prompt else dap(o_ns, b * 512, [[1, 128], [128, 4], [1, 1]])
                S.dma("pool", dn, C_sb[:, :, 256:257], reads=r_Csb + rCn, writes=[r("o_n")], allow_slow_non_contiguous=True)
                if prompt:
                    S.dma("pool", o_mp[:, :], mcol[:, 0:1], reads=[r("mcol")], writes=[r("o_m")])
                else:
                    S.dma("pool", o_ms[b], mout[:, b:b + 1], reads=[r("mout")], writes=[r("o_m")])

        ckpt(5)
        same = prompt
        for pj in range(2):
            ws, wr_ = wnext()
            evr = resid_evac(L, nb, seqs, 2, same)
            proj_a(ws, wr_, 4, bufA, g_A, T, (lambda j, bi, pj=pj, evr=evr: evr(pj * 4 + j, bi)), [0, 1, 2, 3])
        ln_stage(L, nb, seqs, prompt, ti, "ln", (gam(0, 0), bet(0, 0)), [(uT, g_uT, 3, 4)])

        def mlp(l, kind_s):
            for pj in range(8):
                ws, wr_ = wnext()

                def ev(j, bi, pj=pj):
                    c = pj * 4 + j
                    rt, rt_r = rot(list(zip(rtmp, rtmp_r)))
                    act(rt[:, 0:T], bank(bi)[:, 0:T], AF.Relu, [bankres[bi]], [rt_r])
                    tt(SQE, hT[:, c, 0:T], rt[:, 0:T], rt[:, 0:T], ALU.mult, [rt_r], [r_hT[c]])
                proj_a(ws, wr_, 4, uT, g_uT, T, ev, [0, 1, 2, 3, 4, 5, 6, 7])
            evr = resid_evac(L, nb, seqs, kind_s, same)
            for nh in range(2):
                bks = [nh * 4 + j for j in range(4)]
                for kg in range(4):
                    ws, wr_ = wnext()
                    for j in range(4):
                        for kc in range(8):
                            mm(bank(bks[j])[:, 0:T], ws[:, kc, j * 128:(j + 1) * 128], hT[:, kg * 8 + kc, 0:T],
                               kg == 0 and kc == 0, kg == 3 and kc == 7, [wr_, r_hT[kg * 8 + kc]], [bankres[bks[j]]],
                               sig=(kc == 7))
                for j in range(4):
                    evr(nh * 4 + j, bks[j])

        ckpt(6)
        mlp(0, 5)
        ckpt(7)
        ln_stage(L, nb, seqs, prompt, ti, "ln", (gam(0, 1), bet(0, 1)), [(uT, g_uT, 6, 7), (bufA, g_A, 8, 9)])

        ckpt(8)
        ws, wr_ = wnext()
        if prompt:
            kcol0 = (ti % 2) * 512
            ringb = [(ti % 2) * 4 + b for b in BL]
        else:
            kcol0 = 0
            ringb = [0, 1, 2, 3]

        def ev_k(j, bi):
            cpa(KTr[:, j, kcol0:kcol0 + T], bank(bi)[:, 0:T], [bankres[bi]],
                [r_KTr[x] for x in (ringb if prompt else [0])])
        proj_a(ws, wr_, 2, bufA, g_A, T, ev_k, [0, 1])
        for b in BL:
            bi = rot([2, 3])
            for kc in range(8):
                mm(bank(bi)[0:L, :], bufA[:, kc, CS[b]], ws[:, kc, :], kc == 0, kc == 7, [wr_, g_A[kc][b]], [bankres[bi]],
                   sig=(kc == 7))
            cpa(Vr[0:L, ringb[b], :, 0:64], bank(bi)[0:L, 256:512].rearrange("p (h d) -> p h d", h=4),
                [bankres[bi]], [r_Vr[ringb[b]]])
            if last or not prompt:
                stg, stg_r = nscr()
                cp("act", stg[0:L, 0:512], bank(bi)[0:L, :], [bankres[bi]], [stg_r])
                if prompt:
                    dk_, dv_ = o_kp[b * 128:(b + 1) * 128, :], o_vp[b * 128:(b + 1) * 128, :]
                else:
                    dk_, dv_ = o_ks[b * 32:(b + 1) * 32, :], o_vs[b * 32:(b + 1) * 32, :]
                S.dma("pool", dk_, stg[0:L, 0:256], reads=[stg_r], writes=[r("o_k")])
                S.dma("pool", dv_, stg[0:L, 256:512], reads=[stg_r], writes=[r("o_v")])

        def ev_q(base):
            def ev(j, bi):
                cpa(bufC[:, base + j, 0:T], bank(bi)[:, 0:T], [bankres[bi]], gsel(g_C, [base + j], BL))
            return ev
        for pj in range(2):
            ws, wr_ = wnext()
            proj_a(ws, wr_, 4, uT, g_uT, T, ev_q(pj * 4), [4, 5, 6, 7])

        ckpt(9)
        for b in BL:
            cs = CS[b]
            Lq = L
            keyblocks = []
            if prompt:
                Bg = ti * 4 + b
                for j in range(5):
                    KB = Bg - 4 + j
                    if KB < 0:
                        continue
                    pos = KB % 8
                    kd = {0: "mask0", 1: "plain", 2: "plain", 3: "b3", 4: "b4"}[j]
                    keyblocks.append((
                        (lambda kk, rows, pos=pos: KTr[rows, kk, pos * 128:(pos + 1) * 128]),
                        (lambda kh, pos=pos: Vr[:, pos, kh, :]), 128, kd, [r_KTr[pos], r_Vr[pos]]))
            else:
                stg, stg_r = nscr()
                S.dma("act", stg[:, :].rearrange("p (j f) -> p j f", j=4),
                      dap(ck, b * 512 * 256, [[256, 128], [128 * 256, 4], [1, 256]]), writes=[stg_r])
                for j in range(4):
                    for kk in range(2):
                        a = j * 2 + kk
                        tr(pair(0)[:, a * 128:(a + 1) * 128], stg[:, j * 256 + kk * 128:j * 256 + (kk + 1) * 128], ident[:, :],
                           [stg_r, r("ident")], [bankres[0], bankres[1]], sig=(a == 7))
                for kk in range(2):
                    cp("dve", KTc[:, kk, :].rearrange("p (j s) -> p j s", j=4),
                        pair(0)[:, :].rearrange("p (j k s) -> p j k s", j=4, k=2)[:, :, kk, :],
                        [bankres[0], bankres[1]], [r("KTc")])
                stg2, stg2_r = nscr()
                S.dma("act", stg2[:, :].rearrange("p (j f) -> p j f", j=4),
                      dap(cv, b * 512 * 256, [[256, 128], [128 * 256, 4], [1, 256]]), writes=[stg2_r])
                cp("dve", Vc[:, :, :, 0:64], stg2[:, :].rearrange("p (j h d) -> p j h d", j=4, h=4), [stg2_r], [r("Vc")])
                for j in range(4):
                    keyblocks.append((
                        (lambda kk, rows, j=j: KTc[rows, kk, j * 128:(j + 1) * 128]),
                        (lambda kh, j=j: Vc[:, j, kh, :]), 128, "b3" if j == 3 else "plain", [r("KTc"), r("Vc")]))
                keyblocks.append((
                    (lambda kk, rows, b=b: KTr[rows, kk, b * 32:(b + 1) * 32]),
                    (lambda kh, b=b: Vr[0:32, b, kh, :]), 32, "b4", [r_KTr[0], r_Vr[b]]))
            nkb = len(keyblocks)
            for kh in range(4):
                kk, e_ = kh // 2, kh % 2
                rows = slice(e_ * 64, (e_ + 1) * 64)
                obk = 4 + kh
                pts = []
                for (ktf, vf, nk, kd, rds) in keyblocks:
                    sbk = rot([0, 1, 2, 3])
                    sps = bank(sbk)[0:nk, 0:4 * Lq]
                    mm(sps, ktf(kk, rows), bufC[rows, kk * 4:(kk + 1) * 4, cs], True, True,
                       rds + gsel(g_C, range(kk * 4, kk * 4 + 4), [b]), [bankres[sbk]])
                    if kd == "mask0":
                        pt, pt_r = PT0, PT0_r
                        s3 = sps.rearrange("p (g q) -> p g q", g=4)
                        p3 = pt[0:nk, 0:4 * Lq].rearrange("p (g q) -> p g q", g=4)
                        act(p3[:, :, 0:64], s3[:, :, 0:64], AF.Exp, [bankres[sbk]], [pt_r], scale=0.125)
                        act(p3[64:128, :, 64:128], s3[64:128, :, 64:128], AF.Exp, [bankres[sbk]], [pt_r], scale=0.125)
                    else:
                        pt, pt_r = rot(list(zip(PT, PT_r)))
                        if kd == "plain":
                            act(pt[0:nk, 0:4 * Lq], sps, AF.Exp, [bankres[sbk]], [pt_r], scale=0.125)
                        else:
                            bt_ = bias3 if kd == "b3" else bias4
                            ei, ei_r = rot(list(zip(expin, expin_r)))
                            stt(ei[0:nk, 0:4 * Lq].rearrange("p (g q) -> p g q", g=4),
                                sps.rearrange("p (g q) -> p g q", g=4), 0.125,
                                bt_[0:nk, kh * 4:(kh + 1) * 4, 0:Lq], ALU.mult, ALU.add,
                                [bankres[sbk], r("bias3"), r("bias4")], [ei_r])
                            act(pt[0:nk, 0:4 * Lq], ei[0:nk, 0:4 * Lq], AF.Exp, [ei_r], [pt_r])
                    pts.append((pt, pt_r))
                for g in range(4):
                    for ki, (ktf, vf, nk, kd, rds) in enumerate(keyblocks):
                        pt, pt_r = pts[ki]
                        mm(bank(obk)[0:Lq, g * 65:(g + 1) * 65], pt[0:nk, g * Lq:(g + 1) * Lq], vf(kh)[0:nk, :],
                           ki == 0, ki == nkb - 1, [pt_r] + rds, [bankres[obk]], sig=(ki == nkb - 1))
            for kh in range(4):
                obk = 4 + kh
                o3 = bank(obk)[0:Lq, 0:260].rearrange("p (g d) -> p g d", g=4)
                recip(rden[0:Lq, kh * 4:(kh + 1) * 4], o3[:, :, 64], [bankres[obk]], [r("rden")])
                tt("dve", On[0:Lq, kh * 4:(kh + 1) * 4, :], o3[:, :, 0:64],
                   rden[0:Lq, kh * 4:(kh + 1) * 4].unsqueeze(2).broadcast_to([Lq, 4, 64]), ALU.mult,
                   [bankres[obk], r("rden")], [r("On")])
            for c in range(8):
                tr(bankb(0)[:, c * 128:c * 128 + Lq], On[0:Lq, 2 * c:2 * c + 2, :].rearrange("p h d -> p (h d)"),
                   identb[0:Lq, 0:Lq], [r("On"), r("identb")], [bankres[0]], sig=(c == 7))
            cpa(bufB[:, :, cs], bankb(0)[:, 0:1024].rearrange("p (c q) -> p c q", c=8)[:, :, 0:Lq], [bankres[0]],
                gsel(g_B, ALLC, [b]))

        ckpt(10)
        for pj in range(2):
            ws, wr_ = wnext()
            evr = resid_evac(L, nb, seqs, 10, same)
            proj_a(ws, wr_, 4, bufB, g_B, T, (lambda j, bi, pj=pj, evr=evr: evr(pj * 4 + j, bi)), [0, 1, 2, 3])
        ln_stage(L, nb, seqs, prompt, ti, "ln", (gam(1, 0), bet(1, 0)), [(uT, g_uT, 11, 12)])
        mlp(1, 13)
        ln_stage(L, nb, seqs, prompt, ti, "final", None, [])

    try:
        ckpt(1)
        for (kind, ti) in tiles:
            do_tile(kind, ti)
    except _Stop:
        pass

    S.final_drain("sp")
    S.emit(nc)
    st.close()
    return nc


_CACHE = {}


def _consts():
    ident = np.eye(128, dtype=np.float32)
    tri = (np.arange(128)[:, None] <= np.arange(128)[None, :]).astype(np.float32) * np.float32(128.0 ** -0.5)
    m4 = np.zeros((128, 128), np.float32)
    m4[64:, :64] = NEG
    return np.ascontiguousarray(np.stack([ident, tri, m4], axis=1))


def make_in_maps(inp, NT, ncores=8):
    f = lambda a: np.ascontiguousarray(np.asarray(a, dtype=np.float32))
    rel = f(inp["rel_bias_b"])[0]
    ext = np.concatenate([rel, np.repeat(rel[:, 256:257], 128, axis=1)], axis=1)
    rel_rev = np.ascontiguousarray(ext[:, ::-1])
    vec1 = np.concatenate([f(inp["conv_w_a"])[0].reshape(32, 128), f(inp["conv_b_a"])[0].reshape(8, 128),
                           f(inp["ln_g"]).reshape(32, 128), f(inp["ln_b"]).reshape(32, 128),
                           f(inp["mhn_g_a"])[0].reshape(8, 128)], axis=0)
    vec2 = np.concatenate([f(inp["b_ada"]).reshape(96, 128), f(inp["b_ada_kv"]).reshape(16, 128)], axis=0)
    lnf = np.stack([f(inp["ln_g"])[1, 1], f(inp["ln_b"])[1, 1]], axis=0)
    shared = {
        "w_in": f(inp["w_in_a"])[0], "w_out_a": f(inp["w_out_a"])[0],
        "w_up0": f(inp["w_up"])[0], "w_up1": f(inp["w_up"])[1],
        "w_down0": f(inp["w_down"])[0], "w_down1": f(inp["w_down"])[1],
        "w_kv": f(inp["w_kv"]), "w_q": f(inp["w_q_b"])[0], "w_out_b": f(inp["w_out_b"])[0],
        "w_ada": f(inp["w_ada"]), "w_ada_kv": f(inp["w_ada_kv"]),
        "vec1": np.ascontiguousarray(vec1), "vec2": np.ascontiguousarray(vec2),
        "b_if": f(inp["b_if_a"]).reshape(8, 1), "lnf": np.ascontiguousarray(lnf),
        "rel_rev": rel_rev, "cst": _consts(),
    }
    maps = []
    xp, xs = f(inp["x_prompt"]), f(inp["x_sample"])
    for i in range(ncores):
        s0 = 4 * i
        m = dict(shared)
        m["x_p"] = np.ascontiguousarray(xp[i, :NT * 512])
        m["x_s"] = np.ascontiguousarray(xs[s0:s0 + 4].reshape(128, 1024))
        m["c_all"] = np.ascontiguousarray(np.concatenate([f(inp["c_prompt"])[i:i + 1], f(inp["c_sample"])[s0:s0 + 4]], axis=0))
        m["sconv"] = np.ascontiguousarray(f(inp["state_conv"])[0, s0:s0 + 4])
        m["sC"] = np.ascontiguousarray(f(inp["state_C"])[0, s0:s0 + 4])
        m["sn"] = np.ascontiguousarray(f(inp["state_n"])[0, s0:s0 + 4])
        m["smT"] = np.ascontiguousarray(f(inp["state_m"])[0, s0:s0 + 4].T)
        m["ck"] = np.ascontiguousarray(f(inp["cache_k"])[s0:s0 + 4].reshape(4, 512, 256))
        m["cv"] = np.ascontiguousarray(f(inp["cache_v"])[s0:s0 + 4].reshape(4, 512, 256))
        maps.append(m)
    return maps


def assemble(results, NT, ncores=8):
    g = lambda k: np.stack([np.asarray(results[i][k], dtype=np.float32) for i in range(ncores)], axis=0)
    y_p = g("y_p")
    y_s = g("y_s").reshape(ncores * 4, 32, 1024)
    conv_p = g("o_convp")[None]
    C_p = g("o_Cp")[None]
    n_p = g("o_np")[None]
    m_p = g("o_mp").reshape(ncores, 4)[None]
    k_p = g("o_kp").reshape(ncores, 512, 4, 64)
    v_p = g("o_vp").reshape(ncores, 512, 4, 64)
    conv_s = g("o_convs").reshape(ncores * 4, 3, 1024)[None]
    C_s = g("o_Cs").reshape(ncores * 4, 4, 256, 128)[None]
    n_s = g("o_ns").reshape(ncores * 4, 4, 128)[None]
    m_s = g("o_ms").reshape(ncores * 4, 4)[None]
    k_s = g("o_ks").reshape(ncores * 4, 32, 4, 64)
    v_s = g("o_vs").reshape(ncores * 4, 32, 4, 64)
    return (y_p, y_s, conv_p, C_p, n_p, m_p, k_p, v_p, conv_s, C_s, n_s, m_s, k_s, v_s)


def kernel(**inputs):
    NT = 16
    if NT not in _CACHE:
        _CACHE[NT] = build(NT)
    nc = _CACHE[NT]
    in_maps = make_in_maps(inputs, NT)
    res = run_bass_kernel_spmd(nc, in_maps, core_ids=list(range(8)))
    return assemble(res.results, NT)
```

```python
import contextlib
import numpy as np
import concourse.bass as bass
import concourse.mybir as mybir
from concourse.bass_utils import run_bass_kernel_spmd

F32 = mybir.dt.float32
BF16 = mybir.dt.bfloat16
AF = mybir.ActivationFunctionType
ALU = mybir.AluOpType
AX = mybir.AxisListType

ALPHA = 4.0 ** 0.25
LN_EPS_P = 1e-5 / (ALPHA * ALPHA)
HN_EPS = 1e-6
NSLOT = 3
PDEPTH = 2
NEG = -30000.0
import os
XQ = os.environ.get("XQ", "act")
SQE = os.environ.get("SQE", "pool")


class Res:
    __slots__ = ("name", "w", "r", "sem", "excl")

    def __init__(self, name, excl=False):
        self.name = name
        self.w = None
        self.r = []
        self.sem = None
        self.excl = excl


class Sched:
    ENGS = ("pe", "act", "dve", "pool", "sp")

    def __init__(self):
        self.ops = {e: [] for e in self.ENGS}
        self.nsig = {e: 0 for e in self.ENGS}
        self.waited = {e: {} for e in self.ENGS}
        self.semkeys = ["E_" + e for e in self.ENGS]
        self.dma_sems = {}
        self.nres = 0

    def res(self, name=None):
        self.nres += 1
        return Res(name or f"r{self.nres}")

    def _need(self, eng, ev, waits, same_ok):
        if ev is None:
            return
        key, val, weng = ev
        if same_ok and weng == eng:
            return
        if self.waited[eng].get(key, 0) >= val:
            return
        if waits.get(key, 0) < val:
            waits[key] = val

    def _deps(self, eng, reads, writes):
        waits = {}
        for r in reads:
            self._need(eng, r.w, waits, eng == "pe")
            if r.excl:
                for ev in r.r:
                    self._need(eng, ev, waits, True)
        for w in writes:
            self._need(eng, w.w, waits, True)
            for ev in w.r:
                self._need(eng, ev, waits, True)
        for k, v in waits.items():
            self.waited[eng][k] = v
        return list(waits.items())

    def _record(self, ev, reads, writes):
        for r in reads:
            r.r.append(ev)
        for w in writes:
            w.w = ev
            w.r = []

    def op(self, eng, fn, reads=(), writes=(), sig=True):
        waits = self._deps(eng, reads, writes)
        key = "E_" + eng
        if sig:
            self.nsig[eng] += 1
            val = self.nsig[eng]
        else:
            val = self.nsig[eng] + 1
        self._record((key, val, eng), reads, writes)
        self.ops[eng].append((waits, fn, (key, 1) if sig else None))

    def dma(self, q, out, in_, reads=(), writes=(), **kw):
        waits = self._deps(q, reads, writes)
        tgt = writes[0]
        if tgt.sem is None:
            tgt.sem = f"D{len(self.dma_sems)}_{tgt.name}"
            self.semkeys.append(tgt.sem)
            self.dma_sems[tgt.sem] = 0
        self.dma_sems[tgt.sem] += 16
        self._record((tgt.sem, self.dma_sems[tgt.sem], "dma"), reads, writes)
        self.ops[q].append((waits, (lambda e, o=out, i=in_, k=kw: e.dma_start(out=o, in_=i, **k)),
                            (tgt.sem, 16)))

    def barrier(self):
        for e in self.ENGS:
            waits = {}
            for e2 in self.ENGS:
                k = "E_" + e2
                if self.nsig[e2] > 0 and e2 != e and self.waited[e].get(k, 0) < self.nsig[e2]:
                    waits[k] = self.nsig[e2]
            for k, v in self.dma_sems.items():
                if v > 0 and self.waited[e].get(k, 0) < v:
                    waits[k] = v
            for k, v in waits.items():
                self.waited[e][k] = v
            if waits:
                self.ops[e].append((list(waits.items()), None, None))

    def final_drain(self, eng="sp"):
        waits = {}
        for k, v in self.dma_sems.items():
            if v > 0 and self.waited[eng].get(k, 0) < v:
                waits[k] = v
        for e2 in self.ENGS:
            if self.nsig[e2] > 0 and e2 != eng:
                waits["E_" + e2] = self.nsig[e2]
        self.ops[eng].append((list(waits.items()), None, None))

    def emit(self, nc):
        with contextlib.ExitStack() as st:
            sems = {k: st.enter_context(nc.semaphore(k)) for k in self.semkeys}
            block = st.enter_context(nc.Block())

            def replay(engobj, name):
                for waits, fn, sig in self.ops[name]:
                    for k, v in waits:
                        engobj.wait_ge(sems[k], v)
                    if fn is None:
                        continue
                    ins = fn(engobj)
                    if sig is not None:
                        ins.then_inc(sems[sig[0]], sig[1])

            @block.sync
            def _(e):
                replay(e, "sp")

            @block.tensor
            def _(e):
                replay(e, "pe")

            @block.scalar
            def _(e):
                replay(e, "act")

            @block.vector
            def _(e):
                replay(e, "dve")

            @block.gpsimd
            def _(e):
                replay(e, "pool")


def panel_list():
    pl = []
    for j in range(6):
        pl.append(("w_in", 0, j * 512, 3080, None))
    for j in range(2):
        pl.append(("w_out_a", 0, j * 512, 1024, None))
    for j in range(8):
        pl.append(("w_up0", 0, j * 512, 4096, None))
    for nh in range(2):
        for kg in range(4):
            pl.append(("w_down0", kg * 1024, nh * 512, 1024, None))
    pl.append(("w_kv", 0, 0, 512, None))
    for j in range(2):
        pl.append(("w_q", 0, j * 512, 1024, "qperm"))
    for j in range(2):
        pl.append(("w_out_b", 0, j * 512, 1024, None))
    for j in range(8):
        pl.append(("w_up1", 0, j * 512, 4096, None))
    for nh in range(2):
        for kg in range(4):
            pl.append(("w_down1", kg * 1024, nh * 512, 1024, None))
    return pl


NPANEL = 45


class _Stop(Exception):
    pass


def build(NT, do_sample=True, stage=99):
    nc = bass.Bass("TRN2", target_bir_lowering=False)
    S = Sched()
    st = contextlib.ExitStack()
    SEQ = NT * 512

    def din(name, shape, dt=F32):
        return nc.dram_tensor(name, list(shape), dt, kind="ExternalInput").ap()

    def dout(name, shape, dt=F32):
        return nc.dram_tensor(name, list(shape), dt, kind="ExternalOutput").ap()

    def dap(t, offset, dims):
        return bass.AP(tensor=t.tensor, offset=offset, ap=[list(d) for d in dims])

    x_p = din("x_p", [SEQ, 1024])
    x_s = din("x_s", [128, 1024])
    c_all = din("c_all", [5, 1024])
    sconv = din("sconv", [4, 3, 1024])
    sC = din("sC", [4, 4, 256, 128])
    sn = din("sn", [4, 4, 128])
    smT = din("smT", [4, 4])
    ck = din("ck", [4, 512, 256])
    cv = din("cv", [4, 512, 256])
    W = {
        "w_in": din("w_in", [1024, 3080]),
        "w_out_a": din("w_out_a", [1024, 1024]),
        "w_up0": din("w_up0", [1024, 4096]),
        "w_up1": din("w_up1", [1024, 4096]),
        "w_down0": din("w_down0", [4096, 1024]),
        "w_down1": din("w_down1", [4096, 1024]),
        "w_kv": din("w_kv", [1024, 512]),
        "w_q": din("w_q", [1024, 1024]),
        "w_out_b": din("w_out_b", [1024, 1024]),
    }
    w_ada = din("w_ada", [2, 1024, 6144])
    w_ada_kv = din("w_ada_kv", [1024, 2048])
    vec1 = din("vec1", [112, 128])
    vec2 = din("vec2", [112, 128])
    b_if = din("b_if", [8, 1])
    lnf = din("lnf", [2, 1024])
    rel_rev = din("rel_rev", [16, 385])
    cst = din("cst", [128, 3, 128])

    y_p = dout("y_p", [SEQ, 1024])
    y_s = dout("y_s", [128, 1024])
    o_convp = dout("o_convp", [3, 1024])
    o_Cp = dout("o_Cp", [4, 256, 128])
    o_np = dout("o_np", [4, 128])
    o_mp = dout("o_mp", [4, 1])
    o_kp = dout("o_kp", [512, 256])
    o_vp = dout("o_vp", [512, 256])
    o_convs = dout("o_convs", [4, 3, 1024])
    o_Cs = dout("o_Cs", [4, 4, 256, 128])
    o_ns = dout("o_ns", [4, 4, 128])
    o_ms = dout("o_ms", [4, 4, 1])
    o_ks = dout("o_ks", [128, 256])
    o_vs = dout("o_vs", [128, 256])
    wsc = nc.dram_tensor("wsc", [NPANEL, 128, 8, 512], BF16).ap()

    def sb(name, shape, dt=F32):
        return st.enter_context(nc.sbuf_tensor(name, list(shape), dt))

    PP = [st.enter_context(nc.psum_tensor(f"pp{i}", [128, 1024], F32)) for i in range(4)]
    bankres = [Res(f"bank{i}", excl=True) for i in range(8)]

    def bank(i):
        return PP[i // 2][:, (i % 2) * 512:(i % 2) * 512 + 512]

    def bankb(i):
        return PP[i // 2][:, (i % 2) * 512:(i % 2) * 512 + 512].bitcast(BF16)

    def pair(i):
        return PP[i][:, :]

    wslot = [sb(f"wslot{i}", [128, 8, 512], BF16) for i in range(NSLOT)]
    wslot_r = [S.res(f"wslot{i}") for i in range(NSLOT)]
    xT = sb("xT", [128, 8, 512])
    scr = [sb(f"scr{i}", [128, 1024]) for i in range(3)]
    scr_r = [S.res(f"scr{i}") for i in range(3)]
    uT = sb("uT", [128, 8, 512], BF16)
    bufA = sb("bufA", [128, 8, 512], BF16)
    bufB = sb("bufB", [128, 8, 512], BF16)
    bufC = sb("bufC", [128, 8, 512], BF16)
    cb = sb("cb", [128, 8, 4, 131], BF16)
    vaug = sb("vaug", [128, 4, 4, 257], BF16)
    hT = sb("hT", [128, 32, 512], BF16)
    rtmp = [sb(f"rtmp{i}", [128, 512]) for i in range(2)]
    rtmp_r = [S.res(f"rtmp{i}") for i in range(2)]
    C_sb = sb("C_sb", [128, 4, 257])
    Cb = sb("Cb", [128, 4, 257], BF16)
    sTs = sb("sTs", [128, 4, 128], BF16)
    kw = sb("kw", [128, 4, 128], BF16)
    hn = sb("hn", [128, 4, 256], BF16)
    KTr = sb("KTr", [128, 2, 1024], BF16)
    Vr = sb("Vr", [128, 8, 4, 65], BF16)
    KTc = KTr[:, :, 512:1024]
    Vc = Vr[:, 4:8, :, :]
    NPT = 8
    PT = [sb(f"PT{i}", [128, 512], BF16) for i in range(NPT)]
    PT_r = [S.res(f"PT{i}") for i in range(NPT)]
    PT0 = [sb(f"PTm{i}", [128, 512], BF16) for i in range(2)]
    PT0_r = [S.res(f"PTm{i}") for i in range(2)]
    expin = [sb(f"expin{i}", [128, 512]) for i in range(2)]
    expin_r = [S.res(f"expin{i}") for i in range(2)]
    On = sb("On", [128, 16, 64], BF16)
    bias3 = sb("bias3", [128, 16, 128], BF16)
    bias4 = sb("bias4", [128, 16, 128], BF16)
    lnfbc = sb("lnfbc", [128, 2, 1024])
    ident = sb("ident", [128, 128])
    identb = sb("identb", [128, 128], BF16)
    trimask = sb("trimask", [128, 128])
    mask4 = sb("mask4", [128, 128])
    diagc = sb("diagc", [128, 8, 4, 128], BF16)
    vecT = sb("vecT", [128, 112])
    biasT = sb("biasT", [128, 112])
    TAB = sb("TAB", [128, 14, 8, 5])
    cT = sb("cT", [128, 8, 5])
    wg = sb("wg", [128, 8, 8], BF16)
    wg32 = sb("wg32", [128, 8, 8])
    bg = sb("bg", [4, 2])
    bg8 = sb("bg8", [8, 1])
    ones4 = sb("ones4", [4, 128])
    chv = sb("chv", [128, 16])
    cprev = sb("cprev", [128, 8, 3], BF16)
    cstage = sb("cstage", [128, 3, 8])
    mcol = sb("mcol", [4, 8])
    mout = sb("mout", [4, 4])
    G_l = sb("G_l", [4, 128]); G_nb = sb("G_nb", [4, 128]); G_g = sb("G_g", [4, 128])
    G_wk = sb("G_wk", [4, 128]); G_thr = sb("G_thr", [4, 128])
    G_s = sb("G_s", [4, 8])
    G_d4 = sb("G_d4", [4, 4])
    gsc = sb("gsc", [128, 4, 16])
    st6 = sb("st6", [128, 4, 6])
    mv = sb("mv", [128, 4, 2])
    hsm = sb("hsm", [128, 8, 4])
    lst = sb("lst", [128, 4, 2, 6])
    lmv = sb("lmv", [128, 4, 8])
    rden = sb("rden", [128, 16])
    modT = scr[0][:, 0:560].rearrange("p (a s) -> p a s", a=112)
    mod1 = scr[1][:, 0:560].rearrange("p (a s) -> p a s", a=112)
    crow = scr[2][0:5, :]
    rowc = expin[0][0:5, :]
    G_ig = expin[1][0:4, :]
    G_fg = expin[0][0:4, :]

    R = {"modT": scr_r[0], "mod1": scr_r[1], "crow": scr_r[2], "rowc": expin_r[0]}

    def r(name):
        if name not in R:
            R[name] = S.res(name)
        return R[name]

    def grid(name):
        return [[S.res(f"{name}_{c}_{b}") for b in range(4)] for c in range(8)]

    g_xT = grid("xT"); g_uT = grid("uT"); g_A = grid("bufA"); g_B = grid("bufB"); g_C = grid("bufC")
    g_cb = grid("cb")
    r_vaug = [S.res(f"vaug{b}") for b in range(4)]
    r_hT = [S.res(f"hT{c}") for c in range(32)]
    r_Csb = [S.res(f"Csb{h}") for h in range(4)]
    r_Cb = [S.res(f"Cb{h}") for h in range(4)]
    r_KTr = [S.res(f"KTr{i}") for i in range(8)]
    r_Vr = [S.res(f"Vr{i}") for i in range(8)]

    def gsel(g, cs, bs):
        return [g[c][b] for c in cs for b in bs]

    ALLC = list(range(8))

    def mm(out, lhsT, rhs, start, stop, rd, wr, sig=True):
        S.op("pe", lambda e: e.matmul(out, lhsT=lhsT, rhs=rhs, start=start, stop=stop), rd, wr, sig)

    def tr(out, in_, idn, rd, wr, sig=True):
        S.op("pe", lambda e: e.transpose(out, in_, idn), rd, wr, sig)

    def act(out, in_, func, rd, wr, bias=None, scale=None):
        kw_ = {}
        if bias is not None:
            kw_["bias"] = bias
        if scale is not None:
            kw_["scale"] = scale
        S.op("act", lambda e: e.activation(out=out, in_=in_, func=func, **kw_), rd, wr)

    def ts(eng, out, in0, s1, s2, op0, op1, rd, wr):
        if s2 is None:
            S.op(eng, lambda e: e.tensor_scalar(out=out, in0=in0, scalar1=s1, scalar2=None, op0=op0), rd, wr)
        else:
            S.op(eng, lambda e: e.tensor_scalar(out=out, in0=in0, scalar1=s1, scalar2=s2, op0=op0, op1=op1), rd, wr)

    def tt(eng, out, in0, in1, op, rd, wr):
        S.op(eng, lambda e: e.tensor_tensor(out=out, in0=in0, in1=in1, op=op), rd, wr)

    def stt(out, in0, scalar, in1, op0, op1, rd, wr):
        S.op("dve", lambda e: e.scalar_tensor_tensor(out=out, in0=in0, scalar=scalar, in1=in1, op0=op0, op1=op1), rd, wr)

    def cp(eng, out, in_, rd, wr):
        if eng == "act":
            if "i" in os.environ.get("DBG", ""):
                S.op("act", lambda e: e.activation(out=out, in_=in_, func=AF.Identity), rd, wr)
            else:
                S.op("act", lambda e: e.copy(out=out, in_=in_), rd, wr)
        else:
            S.op(eng, lambda e: e.tensor_copy(out=out, in_=in_), rd, wr)

    def recip(out, in_, rd, wr):
        S.op("dve", lambda e: e.reciprocal(out=out, in_=in_), rd, wr)

    affctr = [0]

    def aff(eng, out, in_, A, B, rd, wr):
        if eng == "dve":
            ts("dve", out, in_, A, B, ALU.mult, ALU.add, rd, wr)
        else:
            act(out, in_, AF.Identity, rd, wr, bias=B, scale=A)

    cpctr = [0]

    def cpa(out, in_, rd, wr):
        cpctr[0] += 1
        cp("dve" if cpctr[0] % 2 == 0 else "act", out, in_, rd, wr)

    S.dma("sp", ident[:], cst[:, 0, :], writes=[r("ident")])
    S.dma("sp", trimask[:], cst[:, 1, :], writes=[r("trimask")])
    S.dma("sp", mask4[:], cst[:, 2, :], writes=[r("mask4")])
    cp("dve", identb[:], ident[:], [r("ident")], [r("identb")])
    S.op("dve", lambda e: e.memset(ones4[:], 1.0), [], [r("ones4")])
    S.dma("sp", lnfbc[:, 0, :], dap(lnf, 0, [[0, 128], [1, 1024]]), writes=[r("lnfbc")])
    S.dma("sp", lnfbc[:, 1, :], dap(lnf, 1024, [[0, 128], [1, 1024]]), writes=[r("lnfbc")])

    S.dma("sp", scr[0][0:112, 0:128], vec1[:, :], writes=[scr_r[0]])
    S.dma("sp", scr[1][0:112, 0:128], vec2[:, :], writes=[scr_r[1]])
    tr(bank(0)[:, 0:112], scr[0][0:112, 0:128], ident[0:112, 0:112], [scr_r[0], r("ident")], [bankres[0]])
    tr(bank(1)[:, 0:112], scr[1][0:112, 0:128], ident[0:112, 0:112], [scr_r[1], r("ident")], [bankres[1]])
    cp("dve", vecT[:], bank(0)[:, 0:112], [bankres[0]], [r("vecT")])
    cp("dve", biasT[:], bank(1)[:, 0:112], [bankres[1]], [r("biasT")])

    for c in range(8):
        for j in range(4):
            ts("dve", diagc[:, c, j, :], ident[:], vecT[:, j * 8 + c:j * 8 + c + 1], None, ALU.mult, None,
               [r("ident"), r("vecT")], [r("diagc")])

    S.dma("sp", wg32[:], dap(W["w_in"], 3072, [[3080, 128], [128 * 3080, 8], [1, 8]]), writes=[r("wg32")])
    cp("dve", wg[:], wg32[:], [r("wg32")], [r("wg")])
    S.dma("sp", bg[:, 0:1], b_if[0:4, :], writes=[r("bg")])
    S.dma("sp", bg[:, 1:2], b_if[4:8, :], writes=[r("bg")])
    ts("dve", bg[:, 1:2], bg[:, 1:2], -1.0, None, ALU.mult, None, [r("bg")], [r("bg")])

    S.dma("sp", crow[:], c_all[:, :], writes=[r("crow")])
    act(crow[:], crow[:], AF.Silu, [r("crow")], [r("crow")])
    for kc in range(8):
        tr(bank(2)[:, kc * 8:kc * 8 + 5], crow[:, kc * 128:(kc + 1) * 128], ident[0:5, 0:5],
           [r("crow"), r("ident")], [bankres[2]])
    cp("dve", cT[:], bank(2)[:, 0:64].rearrange("p (k s) -> p k s", k=8)[:, :, 0:5], [bankres[2]], [r("cT")])

    hT32 = hT[:].rearrange("p a b -> p (a b)").bitcast(F32)
    stg32 = [hT32[:, i * 4096:(i + 1) * 4096].rearrange("p (k n) -> p k n", k=8) for i in range(2)]
    stg32_r = [S.res("stg32_0"), S.res("stg32_1")]
    npan = 0
    ada_srcs = []
    for l in range(2):
        for j in range(12):
            ada_srcs.append((w_ada, l * 1024 * 6144 + j * 512, 6144))
    for j in range(4):
        ada_srcs.append((w_ada_kv, j * 512, 2048))
    for pi, (wt, off, ncols) in enumerate(ada_srcs):
        sl = pi % 2
        S.dma("sp", stg32[sl], dap(wt, off, [[ncols, 128], [128 * ncols, 8], [1, 512]]), writes=[stg32_r[sl]])
        pb = 3 + (pi % 2)
        for kc in range(8):
            mm(bank(pb)[0:5, :], cT[:, kc, :], stg32[sl][:, kc, :], kc == 0, kc == 7,
               [r("cT"), stg32_r[sl]], [bankres[pb]], sig=(kc == 7))
        cp("act", rowc[:], bank(pb)[0:5, :], [bankres[pb]], [r("rowc")])
        tb = 5 + (pi % 2)
        for q in range(4):
            tr(bank(tb)[:, q * 8:q * 8 + 5], rowc[:, q * 128:(q + 1) * 128], ident[0:5, 0:5],
               [r("rowc"), r("ident")], [bankres[tb]])
        cp("dve", modT[:, pi * 4:pi * 4 + 4, :], bank(tb)[:, 0:32].rearrange("p (q s) -> p q s", q=4)[:, :, 0:5],
           [bankres[tb]], [r("modT")])
    tt("dve", modT[:], modT[:], biasT[:].unsqueeze(2).broadcast_to([128, 112, 5]), ALU.add,
       [r("modT"), r("biasT")], [r("modT")])
    ts("dve", mod1[:], modT[:], 1.0, None, ALU.add, None, [r("modT")], [r("mod1")])

    def mchunk(l, j):
        return l * 48 + j * 8

    def gam(l, i):
        o = 40 + (l * 2 + i) * 8
        return vecT[:, o:o + 8]

    def bet(l, i):
        o = 72 + (l * 2 + i) * 8
        return vecT[:, o:o + 8]

    def bc5(a):
        return a.unsqueeze(2).broadcast_to([128, 8, 5])

    rT = [r("modT"), r("mod1"), r("vecT")]
    cp("dve", TAB[:, 0], mod1[:, mchunk(0, 1):mchunk(0, 1) + 8, :], rT, [r("TAB")])
    cp("dve", TAB[:, 1], modT[:, mchunk(0, 0):mchunk(0, 0) + 8, :], rT, [r("TAB")])
    for kind, l, j in ((2, 0, 2), (5, 0, 5), (10, 1, 2), (13, 1, 5)):
        ts("dve", TAB[:, kind], mod1[:, mchunk(l, j):mchunk(l, j) + 8, :], 1.0 / ALPHA, None, ALU.mult, None, rT, [r("TAB")])

    def mkAB(kA, kB, g_, b_, sc_off, sh_off):
        tt("dve", TAB[:, kA], mod1[:, sc_off:sc_off + 8, :], bc5(g_), ALU.mult, rT + [r("TAB")], [r("TAB")])
        tt("dve", TAB[:, kB], mod1[:, sc_off:sc_off + 8, :], bc5(b_), ALU.mult, rT + [r("TAB")], [r("TAB")])
        tt("dve", TAB[:, kB], TAB[:, kB], modT[:, sh_off:sh_off + 8, :], ALU.add, rT + [r("TAB")], [r("TAB")])

    mkAB(3, 4, gam(0, 0), bet(0, 0), mchunk(0, 4), mchunk(0, 3))
    mkAB(6, 7, gam(0, 1), bet(0, 1), mchunk(1, 1), mchunk(1, 0))
    mkAB(8, 9, gam(0, 1), bet(0, 1), 96 + 8, 96)
    mkAB(11, 12, gam(1, 0), bet(1, 0), mchunk(1, 4), mchunk(1, 3))

    def tab(kind, c, seq):
        return TAB[:, kind, c, seq:seq + 1]

    bst = [hT32[:, 0:2048].rearrange("p (h q) -> p h q", h=16), hT32[:, 2048:4096].rearrange("p (h q) -> p h q", h=16)]
    S.barrier()
    S.dma("sp", bst[0], dap(rel_rev, 1, [[1, 128], [385, 16], [1, 128]]), writes=[stg32_r[0]])
    S.dma("sp", bst[1], dap(rel_rev, 129, [[1, 128], [385, 16], [1, 128]]), writes=[stg32_r[0]])
    S.dma("sp", chv[:], dap(rel_rev, 128, [[0, 128], [385, 16]]), writes=[r("chv")], allow_slow_non_contiguous=True)

    def flipped(a):
        return bass.AP(tensor=a.tensor, offset=a.offset + 127, ap=[list(a.ap[0]), list(a.ap[1]), [-1, 128]])

    chb = chv[:].unsqueeze(2).broadcast_to([128, 16, 128])
    tt("dve", bias3[:], flipped(bst[0]), chb, ALU.subtract, [stg32_r[0], r("chv")], [r("bias3")])
    tt("dve", bst[1], bst[1], flipped(chb) if False else chb, ALU.subtract, [stg32_r[0], r("chv")], [stg32_r[0]])
    m4b = bass.AP(tensor=mask4[:].tensor, offset=mask4[:].offset, ap=[list(mask4[:].ap[0]), [0, 16], [1, 128]])
    tt("dve", bias4[:], flipped(bst[1]), m4b, ALU.add, [stg32_r[0], r("mask4")], [r("bias4")])
    S.barrier()

    def ckpt(n):
        if stage <= n:
            raise _Stop()

    plist = panel_list()
    r_wsc = S.res("wsc")
    casteng = ["dve", "act", "pool"]
    for pi, (wn, row0, col0, ncols, kind) in enumerate(plist):
        sl = pi % 2
        ws = pi % NSLOT
        S.dma("sp", stg32[sl], dap(W[wn], row0 * ncols + col0, [[ncols, 128], [128 * ncols, 8], [1, 512]]),
              writes=[stg32_r[sl]])
        eng = casteng[pi % 3]
        if kind == "qperm":
            for k in range(2):
                o_ = wslot[ws][:].rearrange("p c (g k d) -> p c g k d", g=4, k=2)[:, :, :, k, :]
                i_ = stg32[sl].rearrange("p c (k g d) -> p c k g d", k=2, g=4)[:, :, k, :, :]
                cp("dve", o_, i_, [stg32_r[sl]], [wslot_r[ws]])
        else:
            cp(eng, wslot[ws][:], stg32[sl], [stg32_r[sl]], [wslot_r[ws]])
        S.dma("pool", wsc[pi], wslot[ws][:], reads=[wslot_r[ws]], writes=[r_wsc])
    S.barrier()

    S.op("dve", lambda e: e.memset(vaug[:], 1.0), [], r_vaug)
    S.op("dve", lambda e: e.memset(Vr[:], 1.0), [], r_Vr)
    S.op("dve", lambda e: e.memset(PT0[0][:], 0.0), [], [PT0_r[0]])
    S.op("dve", lambda e: e.memset(PT0[1][:], 0.0), [], [PT0_r[1]])
    S.op("dve", lambda e: e.memset(gsc[:], 0.0), [], [r("gsc")])

    tiles = []
    if do_sample:
        tiles.append(("s", 0))
    for ti in range(NT):
        tiles.append(("p", ti))
    uses = [pi for _ in tiles for pi in range(NPANEL)]
    wstate = {"next": 0, "cur": -1}

    def wnext():
        wstate["cur"] += 1
        i = wstate["cur"]
        while wstate["next"] <= min(i + PDEPTH, len(uses) - 1):
            n = wstate["next"]
            S.dma("sp", wslot[n % NSLOT][:], wsc[uses[n]], reads=[r_wsc], writes=[wslot_r[n % NSLOT]])
            wstate["next"] += 1
        return wslot[i % NSLOT], wslot_r[i % NSLOT]

    bctr = {}

    def rot(lst, key=None):
        key = key or "k%d_%s" % (len(lst), str(lst[0])[:24])
        bctr[key] = bctr.get(key, -1) + 1
        return lst[bctr[key] % len(lst)]

    scrctr = [0]

    def nscr():
        scrctr[0] += 1
        i = scrctr[0] % 3
        return scr[i], scr_r[i]

    def proj_a(ws, wr_, nchunk, rhs_buf, g_rhs, T, evac, banks):
        for j in range(nchunk):
            bi = rot(banks)
            for kc in range(8):
                mm(bank(bi)[:, 0:T], ws[:, kc, j * 128:(j + 1) * 128], rhs_buf[:, kc, 0:T], kc == 0, kc == 7,
                   [wr_] + gsel(g_rhs, [kc], range(4)), [bankres[bi]], sig=(kc == 7))
            evac(j, bi)

    def ln_block(L, b, cs, eps, targets, final_dst=None, nbk=4):
        pz = (b % 2)
        pbk = 2 + (b % 2)
        zp = pair(pz)
        zres = [bankres[2 * pz], bankres[2 * pz + 1]]
        for c in range(8):
            tr(zp[0:L, c * 128:(c + 1) * 128], xT[:, c, cs], ident[:, :], [g_xT[c][b], r("ident")], zres, sig=(c == 7))
        S.op("dve", lambda e: e.bn_stats(out=lst[0:L, 0, :], in_=zp[0:L, 0:512]), zres, [r("lst")])
        S.op("dve", lambda e: e.bn_stats(out=lst[0:L, 1, :], in_=zp[0:L, 512:1024]), zres, [r("lst")])
        S.op("dve", lambda e: e.bn_aggr(out=lmv[0:L, 0:2], in_=lst[0:L].rearrange("p a b -> p (a b)")), [r("lst")], [r("lmv")])
        ts("dve", lmv[0:L, 2:3], lmv[0:L, 1:2], eps, None, ALU.add, None, [r("lmv")], [r("lmv")])
        act(lmv[0:L, 3:4], lmv[0:L, 2:3], AF.Sqrt, [r("lmv")], [r("lmv")])
        recip(lmv[0:L, 4:5], lmv[0:L, 3:4], [r("lmv")], [r("lmv")])
        stt(lmv[0:L, 5:6], lmv[0:L, 0:1], -1.0, lmv[0:L, 4:5], ALU.mult, ALU.mult, [r("lmv")], [r("lmv")])
        xh, xh_r = nscr()
        act(xh[0:L, :], zp[0:L, :], AF.Identity, zres + [r("lmv")], [xh_r], bias=lmv[0:L, 5:6], scale=lmv[0:L, 4:5])
        return xh, xh_r

    def back_block(L, b, cs, xh, xh_r, gb, targets):
        pbk = 2 + (b % 2)
        bp = pair(pbk)
        bres = [bankres[2 * pbk], bankres[2 * pbk + 1]]
        for c in range(8):
            tr(bp[:, c * 128:c * 128 + L], xh[0:L, c * 128:(c + 1) * 128], ident[0:L, 0:L], [xh_r, r("ident")], bres,
               sig=(c == 7))
        for c in range(8):
            src = bp[:, c * 128:c * 128 + L]
            eng = "dve" if c < 4 else "act"
            br1 = [bres[c // 4]]
            if gb is None:
                cp(eng, xT[:, c, cs], src, br1, [g_xT[c][b]])
            else:
                aff(eng, xT[:, c, cs], src, gb[0][:, c:c + 1], gb[1][:, c:c + 1], br1 + [r("vecT")], [g_xT[c][b]])
            for (buf, g_, kA, kB, seq) in targets:
                aff(eng, buf[:, c, cs], src, tab(kA, c, seq), tab(kB, c, seq), br1 + [r("TAB")], [g_[c][b]])

    def ln_stage(L, nb, seqs, prompt, ti, mode, gb, targets):
        for g0 in (0, 2):
            grp = [g0, g0 + 1]
            xh = {}
            if mode == "xload":
                for b in grp:
                    xs, xs_r = nscr()
                    src = x_p[ti * 512 + b * 128: ti * 512 + (b + 1) * 128, :] if prompt else x_s[b * 32:(b + 1) * 32, :]
                    S.dma(XQ, xs[0:L, :], src, writes=[xs_r])
                    xh[b] = (xs, xs_r)
            else:
                zps = {}
                for b in grp:
                    cs = slice(b * L, (b + 1) * L)
                    zp = pair(b % 2)
                    zres = [bankres[2 * (b % 2)], bankres[2 * (b % 2) + 1]]
                    zps[b] = (zp, zres)
                    for c in range(8):
                        tr(zp[0:L, c * 128:(c + 1) * 128], xT[:, c, cs], ident[:, :], [g_xT[c][b], r("ident")], zres, sig=(c == 7))
                for b in grp:
                    zp, zres = zps[b]
                    rl = [r(f"lmv{b}")]
                    S.op("dve", lambda e, b=b, zp=zp: e.bn_stats(out=lst[0:L, b, 0, :], in_=zp[0:L, 0:512]), zres, rl)
                    S.op("dve", lambda e, b=b, zp=zp: e.bn_stats(out=lst[0:L, b, 1, :], in_=zp[0:L, 512:1024]), zres, rl)
                    S.op("dve", lambda e, b=b: e.bn_aggr(out=lmv[0:L, b, 0:2], in_=lst[0:L, b].rearrange("p a b -> p (a b)")), rl, rl)
                    ts("dve", lmv[0:L, b, 2:3], lmv[0:L, b, 1:2], LN_EPS_P, None, ALU.add, None, rl, rl)
                for b in grp:
                    rl = [r(f"lmv{b}")]
                    act(lmv[0:L, b, 3:4], lmv[0:L, b, 2:3], AF.Sqrt, rl, rl)
                for b in grp:
                    rl = [r(f"lmv{b}")]
                    recip(lmv[0:L, b, 4:5], lmv[0:L, b, 3:4], rl, rl)
                    stt(lmv[0:L, b, 5:6], lmv[0:L, b, 0:1], -1.0, lmv[0:L, b, 4:5], ALU.mult, ALU.mult, rl, rl)
                for b in grp:
                    zp, zres = zps[b]
                    xs, xs_r = nscr()
                    act(xs[0:L, :], zp[0:L, :], AF.Identity, zres + [r(f"lmv{b}")], [xs_r], bias=lmv[0:L, b, 5:6], scale=lmv[0:L, b, 4:5])
                    xh[b] = (xs, xs_r)
            if mode == "final":
                for b in grp:
                    xs, xs_r = xh[b]
                    tt("pool", xs[0:L, :], xs[0:L, :], lnfbc[0:L, 0, :], ALU.mult, [xs_r, r("lnfbc")], [xs_r])
                    tt("dve", xs[0:L, :], xs[0:L, :], lnfbc[0:L, 1, :], ALU.add, [xs_r, r("lnfbc")], [xs_r])
                    dst = y_p[ti * 512 + b * 128: ti * 512 + (b + 1) * 128, :] if prompt else y_s[b * 32:(b + 1) * 32, :]
                    S.dma("pool", dst, xs[0:L, :], reads=[xs_r], writes=[r("o_y")])
                continue
            for b in grp:
                xs, xs_r = xh[b]
                for c in range(8):
                    o_ = (c % 4) * 256 + (b % 2) * 128
                    tr(PP[2 + c // 4][:, o_:o_ + L], xs[0:L, c * 128:(c + 1) * 128], ident[0:L, 0:L], [xs_r, r("ident")],
                       [bankres[4 + c // 2]], sig=(c == 7))
            for c in range(8):
                bk = 4 + c // 2
                eng = "dve" if (c // 2) % 2 == 0 else "act"
                if prompt:
                    src = PP[2 + c // 4][:, (c % 4) * 256:(c % 4) * 256 + 256]
                    cs2 = slice(g0 * 128, g0 * 128 + 256)
                    wr_x = gsel(g_xT, [c], grp)
                    if gb is None:
                        cp(eng, xT[:, c, cs2], src, [bankres[bk]], wr_x)
                    else:
                        aff(eng, xT[:, c, cs2], src, gb[0][:, c:c + 1], gb[1][:, c:c + 1], [bankres[bk], r("vecT")], wr_x)
                    for (buf, g_, kA, kB) in targets:
                        aff(eng, buf[:, c, cs2], src, tab(kA, c, 0), tab(kB, c, 0), [bankres[bk], r("TAB")], gsel(g_, [c], grp))
                else:
                    for b in grp:
                        o_ = (c % 4) * 256 + (b % 2) * 128
                        src = PP[2 + c // 4][:, o_:o_ + L]
                        cs = slice(b * L, (b + 1) * L)
                        if gb is None:
                            cp(eng, xT[:, c, cs], src, [bankres[bk]], [g_xT[c][b]])
                        else:
                            aff(eng, xT[:, c, cs], src, gb[0][:, c:c + 1], gb[1][:, c:c + 1], [bankres[bk], r("vecT")], [g_xT[c][b]])
                        for (buf, g_, kA, kB) in targets:
                            aff(eng, buf[:, c, cs], src, tab(kA, c, seqs[b]), tab(kB, c, seqs[b]), [bankres[bk], r("TAB")], [g_[c][b]])

    def resid_evac(L, nb, seqs, kind, same):
        def ev(c, bi):
            if same:
                T = nb * L
                stt(xT[:, c, 0:T], bank(bi)[:, 0:T], tab(kind, c, seqs[0]), xT[:, c, 0:T], ALU.mult, ALU.add,
                    [bankres[bi], r("TAB")] + gsel(g_xT, [c], range(nb)), gsel(g_xT, [c], range(nb)))
            else:
                for b in range(nb):
                    cs = slice(b * L, (b + 1) * L)
                    stt(xT[:, c, cs], bank(bi)[:, cs], tab(kind, c, seqs[b]), xT[:, c, cs], ALU.mult, ALU.add,
                        [bankres[bi], r("TAB"), g_xT[c][b]], [g_xT[c][b]])
        return ev

    def do_tile(kind, ti):
        prompt = (kind == "p")
        nb = 4
        L = 128 if prompt else 32
        T = nb * L
        seqs = [0, 0, 0, 0] if prompt else [1, 2, 3, 4]
        last = prompt and ti == NT - 1
        first = prompt and ti == 0
        BL = range(nb)
        CS = [slice(b * L, (b + 1) * L) for b in BL]

        ln_stage(L, nb, seqs, prompt, ti, "xload", None, [(uT, g_uT, 0, 1)])

        ckpt(2)
        def ev_qk(base):
            def ev(j, bi):
                c = base + j
                cpa(cb[:, c, 0:nb, 3:3 + L], bank(bi)[:, 0:T].rearrange("p (b t) -> p b t", b=nb),
                    [bankres[bi]], gsel(g_cb, [c], BL))
            return ev
        for pj in range(2):
            ws, wr_ = wnext()
            proj_a(ws, wr_, 4, uT, g_uT, T, ev_qk(pj * 4), [0, 1, 2, 3])
        for pj in range(2):
            ws, wr_ = wnext()
            for b in BL:
                bi = rot([4, 5, 6, 7])
                for kc in range(8):
                    mm(bank(bi)[0:L, :], uT[:, kc, CS[b]], ws[:, kc, :], kc == 0, kc == 7,
                       [wr_, g_uT[kc][b]], [bankres[bi]], sig=(kc == 7))
                cpa(vaug[0:L, b, pj * 2:pj * 2 + 2, 0:256], bank(bi)[0:L, :].rearrange("p (h v) -> p h v", h=2),
                    [bankres[bi]], [r_vaug[b]])

        def ev_o(base):
            def ev(j, bi):
                c = base + j
                act(bufB[:, c, 0:T], bank(bi)[:, 0:T], AF.Sigmoid, [bankres[bi]], gsel(g_B, [c], BL))
            return ev
        for pj in range(2):
            ws, wr_ = wnext()
            proj_a(ws, wr_, 4, uT, g_uT, T, ev_o(pj * 4), [0, 1, 2, 3])
        for gi in range(2):
            for kc in range(8):
                mm(bank(6 + gi)[0:4, 0:T], wg[:, kc, gi * 4:gi * 4 + 4], uT[:, kc, 0:T], kc == 0, kc == 7,
                   [r("wg")] + gsel(g_uT, [kc], BL), [bankres[6 + gi]], sig=(kc == 7))
        cp("act", G_ig[:, 0:T], bank(6)[0:4, 0:T], [bankres[6]], [expin_r[1]])
        cp("act", G_fg[:, 0:T], bank(7)[0:4, 0:T], [bankres[7]], [expin_r[0]])

        ckpt(3)
        for b in BL:
            if prompt:
                if b == 0:
                    if first:
                        S.op("dve", lambda e: e.memset(cb[:, :, 0, 0:3], 0.0), [], gsel(g_cb, ALLC, [0]))
                    else:
                        cp("dve", cb[:, :, 0, 0:3], cprev[:], [r("cprev")], gsel(g_cb, ALLC, [0]))
                else:
                    cp("dve", cb[:, :, b, 0:3], cb[:, :, b - 1, L:L + 3], gsel(g_cb, ALLC, [b - 1]), gsel(g_cb, ALLC, [b]))
            else:
                S.dma("act", cstage[:].rearrange("p j c -> p (j c)"),
                      dap(sconv, b * 3072, [[1, 128], [128, 24]]), writes=[r("cstage")],
                      allow_slow_non_contiguous=True)
                cp("dve", cb[:, :, b, 0:3], cstage[:].rearrange("p j c -> p c j"), [r("cstage")], gsel(g_cb, ALLC, [b]))
        for c in range(8):
            bi = rot([0, 1, 2, 3])
            for j in range(4):
                mm(bank(bi)[:, 0:T], diagc[:, c, j, :], cb[:, c, 0:nb, j:j + L], j == 0, j == 3,
                   [r("diagc")] + gsel(g_cb, [c], BL), [bankres[bi]], sig=(j == 3))
            act(bufC[:, c, 0:T], bank(bi)[:, 0:T], AF.Silu, [bankres[bi], r("vecT")], gsel(g_C, [c], BL),
                bias=vecT[:, 32 + c:33 + c])
        cp("dve", cprev[:], cb[:, :, nb - 1, L:L + 3], gsel(g_cb, ALLC, [nb - 1]), [r("cprev")])
        if last or not prompt:
            for b in ([nb - 1] if prompt else BL):
                cp("dve", cstage[:].rearrange("p j c -> p c j"), cb[:, :, b, L:L + 3], gsel(g_cb, ALLC, [b]), [r("cstage")])
                dst = dap(o_convp, 0, [[1, 128], [128, 24]]) if prompt else \
                    dap(o_convs, b * 3072, [[1, 128], [128, 24]])
                S.dma("pool", dst, cstage[:].rearrange("p j c -> p (j c)"), reads=[r("cstage")], writes=[r("o_conv")],
                      allow_slow_non_contiguous=True)

        ckpt(4)
        if not prompt:
            S.dma("act", mcol[:, 0:4], smT[:, :], writes=[r("mcol")])
        elif first:
            S.op("dve", lambda e: e.memset(mcol[:], 0.0), [], [r("mcol")])
            S.op("dve", lambda e: e.memset(C_sb[:], 0.0), [], r_Csb)
        for b in BL:
            cs = CS[b]
            rg = [r("G")]
            mprev = mcol[:, 0:1] if prompt else mcol[:, b:b + 1]
            act(G_l[:, 0:L], G_fg[:, cs], AF.Exp, [expin_r[0], r("bg")], rg, bias=bg[:, 1:2], scale=-1.0)
            act(G_l[:, 0:L], G_l[:, 0:L], AF.Ln, rg, rg, bias=1.0)
            S.op("dve", lambda e: e.tensor_tensor_scan(out=G_nb[:, 0:L], data0=ones4[:, 0:L], data1=G_l[:, 0:L],
                                                       initial=0.0, op0=ALU.mult, op1=ALU.add), rg + [r("ones4")], rg)
            stt(G_g[:, 0:L], G_ig[:, cs], bg[:, 0:1], G_nb[:, 0:L], ALU.add, ALU.add, [expin_r[1], r("bg")] + rg, rg)
            S.op("dve", lambda e: e.tensor_reduce(out=G_s[:, 0:1], in_=G_g[:, 0:L], axis=AX.X, op=ALU.max), rg, rg)
            tt("dve", G_s[:, 1:2], G_s[:, 0:1], mprev, ALU.max, rg + [r("mcol")], rg)
            ts("dve", G_s[:, 2:3], G_s[:, 1:2], -1.0, None, ALU.mult, None, rg, rg)
            act(G_s[:, 3:4], mprev, AF.Exp, rg + [r("mcol")], rg, bias=G_s[:, 2:3])
            act(G_wk[:, 0:L], G_g[:, 0:L], AF.Exp, rg, rg, bias=G_s[:, 2:3])
            act(G_thr[:, 0:L], G_nb[:, 0:L], AF.Exp, rg, rg, bias=G_s[:, 2:3])
            if prompt:
                tt("dve", mcol[:, 0:1], G_s[:, 1:2], G_nb[:, L - 1:L], ALU.subtract, rg + [r("mcol")], [r("mcol")])
            else:
                tt("dve", mout[:, b:b + 1], G_s[:, 1:2], G_nb[:, L - 1:L], ALU.subtract, rg, [r("mout")])
            ts("dve", G_d4[:], ident[0:4, 0:4], G_s[:, 3:4], None, ALU.mult, None, rg + [r("ident")], rg)
            gb_ = 5
            tr(bank(gb_)[0:L, 0:4], G_wk[:, 0:L], ident[0:4, 0:4], rg + [r("ident")], [bankres[gb_]])
            tr(bank(gb_)[0:L, 4:8], G_thr[:, 0:L], ident[0:4, 0:4], rg + [r("ident")], [bankres[gb_]])
            mm(bank(gb_)[:, 8:12], ones4[:, :], G_d4[:], True, True, rg + [r("ones4")], [bankres[gb_]])
            cp("dve", gsc[0:L, b, 0:8], bank(gb_)[0:L, 0:8], [bankres[gb_]], [r("gsc")])
            cp("dve", gsc[:, b, 8:12], bank(gb_)[:, 8:12], [bankres[gb_]], [r("gsc")])
            ts("dve", gsc[:, b, 12:16], gsc[:, b, 8:12], 128.0 ** -0.5, None, ALU.mult, None, [r("gsc")], [r("gsc")])

            if not prompt:
                stg, stg_r = nscr()
                S.dma("act", stg[:, :].rearrange("p (a d) -> p a d", a=8),
                      dap(sC, b * 4 * 256 * 128, [[128, 128], [128 * 128, 8], [1, 128]]), writes=[stg_r])
                for a in range(8):
                    tr(pair(0)[:, a * 128:(a + 1) * 128], stg[:, a * 128:(a + 1) * 128], ident[:, :],
                       [stg_r, r("ident")], [bankres[0], bankres[1]], sig=(a == 7))
                for h in range(4):
                    cp("dve" if h < 2 else "act", C_sb[:, h, 0:256], pair(0)[:, h * 256:(h + 1) * 256], [bankres[h // 2]], [r_Csb[h]])
                S.dma("act", C_sb[:, :, 256:257], dap(sn, b * 512, [[1, 128], [128, 4], [1, 1]]), writes=[r("Cn")],
                      reads=[], allow_slow_non_contiguous=True)
            rCn = [] if prompt else [r("Cn")]

            for h in range(4):
                mm(bank(0)[0:L, h * 128:h * 128 + L], bufC[:, 4 + h, cs], bufC[:, h, cs], True, True,
                   [g_C[4 + h][b], g_C[h][b]], [bankres[0]], sig=(h == 3))
            for h in range(4):
                stt(sTs[0:L, h, 0:L], bank(0)[0:L, h * 128:h * 128 + L], gsc[0:L, b, h:h + 1], trimask[0:L, 0:L],
                    ALU.mult, ALU.mult, [bankres[0], r("gsc"), r("trimask")], [r("sTs")])
            for h in range(4):
                tr(bankb(1)[0:L, h * 128:(h + 1) * 128], bufC[:, 4 + h, cs], identb[:, :], [g_C[4 + h][b], r("identb")],
                   [bankres[1]], sig=(h == 3))
            tt("dve", kw[0:L, :, :], bankb(1)[0:L, 0:512].rearrange("p (h d) -> p h d", h=4),
               gsc[0:L, b, 0:4].unsqueeze(2).broadcast_to([L, 4, 128]), ALU.mult, [bankres[1], r("gsc")], [r("kw")])
            for h in range(4):
                act(Cb[:, h, :], C_sb[:, h, :], AF.Identity, [r_Csb[h], r("gsc")] + rCn, [r_Cb[h]], scale=gsc[:, b, 12 + h:13 + h])
            for h in range(4):
                nbk = 2 + h
                mm(bank(nbk)[0:L, 0:257], sTs[0:L, h, 0:L], vaug[0:L, b, h, :], True, False,
                   [r("sTs"), r_vaug[b]], [bankres[nbk]], sig=False)
                mm(bank(nbk)[0:L, 0:257], bufC[:, h, cs], Cb[:, h, :], False, True,
                   [g_C[h][b], r_Cb[h]], [bankres[nbk]])
            for h in range(4):
                dbk = 6 + (h % 2)
                mm(bank(dbk)[:, 0:257], kw[0:L, h, :], vaug[0:L, b, h, :], True, True, [r("kw"), r_vaug[b]], [bankres[dbk]])
                stt(C_sb[:, h, :], C_sb[:, h, :], gsc[:, b, 8 + h:9 + h], bank(dbk)[:, 0:257], ALU.mult, ALU.add,
                    [r_Csb[h], r("gsc"), bankres[dbk]] + rCn, [r_Csb[h]])
            rh = [r("hsm")]
            for h in range(4):
                nbk = 2 + h
                S.op("dve", lambda e, h=h, nbk=nbk: e.bn_stats(out=st6[0:L, h, :], in_=bank(nbk)[0:L, 0:256]),
                     [bankres[nbk]], [r("st6")])
                S.op("dve", lambda e, h=h: e.bn_aggr(out=mv[0:L, h, :], in_=st6[0:L, h, :]), [r("st6")], [r("mv")])
                cp("dve", hsm[0:L, 0, h:h + 1], bank(nbk)[0:L, 256:257], [bankres[nbk]], rh)
            stt(hsm[0:L, 1, :], hsm[0:L, 0, :], -1.0, hsm[0:L, 0, :], ALU.mult, ALU.max, rh, rh)
            tt("dve", hsm[0:L, 1, :], hsm[0:L, 1, :], gsc[0:L, b, 4:8], ALU.max, rh + [r("gsc")], rh)
            recip(hsm[0:L, 2, :], hsm[0:L, 1, :], rh, rh)
            tt("dve", hsm[0:L, 3, :], hsm[0:L, 2, :], hsm[0:L, 2, :], ALU.mult, rh, rh)
            tt("dve", hsm[0:L, 3, :], hsm[0:L, 3, :], mv[0:L, :, 1], ALU.mult, rh + [r("mv")], rh)
            ts("dve", hsm[0:L, 3, :], hsm[0:L, 3, :], HN_EPS, None, ALU.add, None, rh, rh)
            act(hsm[0:L, 4, :], hsm[0:L, 3, :], AF.Sqrt, rh, rh)
            recip(hsm[0:L, 5, :], hsm[0:L, 4, :], rh, rh)
            tt("dve", hsm[0:L, 6, :], hsm[0:L, 2, :], hsm[0:L, 5, :], ALU.mult, rh, rh)
            stt(hsm[0:L, 7, :], mv[0:L, :, 0], -1.0, hsm[0:L, 6, :], ALU.mult, ALU.mult, rh + [r("mv")], rh)
            for h in range(4):
                nbk = 2 + h
                act(hn[0:L, h, :], bank(nbk)[0:L, 0:256], AF.Identity, [bankres[nbk]] + rh, [r("hn")],
                    bias=hsm[0:L, 7, h:h + 1], scale=hsm[0:L, 6, h:h + 1])
            for j in range(8):
                tr(bankb(1)[:, j * 128:j * 128 + L], hn[0:L, j // 2, (j % 2) * 128:(j % 2) * 128 + 128], identb[0:L, 0:L],
                   [r("hn"), r("identb")], [bankres[1]], sig=(j == 7))
            for j in range(8):
                stt(bufA[:, j, cs], bankb(1)[:, j * 128:j * 128 + L], vecT[:, 104 + j:105 + j], bufB[:, j, cs],
                    ALU.mult, ALU.mult, [bankres[1], r("vecT"), g_B[j][b]], [g_A[j][b]])

            if (last and b == nb - 1) or not prompt:
                for h in range(4):
                    for vh in range(2):
                        a = h * 2 + vh
                        tr(pair(0)[:, a * 128:(a + 1) * 128], C_sb[:, h, vh * 128:(vh + 1) * 128], ident[:, :],
                           [r_Csb[h], r("ident")] + rCn, [bankres[0], bankres[1]], sig=(a == 7))
                stg, stg_r = nscr()
                cp("dve", stg[:, :], pair(0)[:, :], [bankres[0], bankres[1]], [stg_r])
                dC = dap(o_Cp, 0, [[128, 128], [128 * 128, 8], [1, 128]]) if prompt else \
                    dap(o_Cs, b * 4 * 256 * 128, [[128, 128], [128 * 128, 8], [1, 128]])
                S.dma("pool", dC, stg[:, :].rearrange("p (a d) -> p a d", a=8), reads=[stg_r], writes=[r("o_C")])
                dn = dap(o_np, 0, [[1, 128], [128, 4], [1, 1]]) if prompt else dap(o_ns, b * 512, [[1, 128], [128, 4], [1, 1]])
                S.dma("pool", dn, C_sb[:, :, 256:257], reads=r_Csb + rCn, writes=[r("o_n")], allow_slow_non_contiguous=True)
                if prompt:
                    S.dma("pool", o_mp[:, :], mcol[:, 0:1], reads=[r("mcol")], writes=[r("o_m")])
                else:
                    S.dma("pool", o_ms[b], mout[:, b:b + 1], reads=[r("mout")], writes=[r("o_m")])

        ckpt(5)
        same = prompt
        for pj in range(2):
            ws, wr_ = wnext()
            evr = resid_evac(L, nb, seqs, 2, same)
            proj_a(ws, wr_, 4, bufA, g_A, T, (lambda j, bi, pj=pj, evr=evr: evr(pj * 4 + j, bi)), [0, 1, 2, 3])
        ln_stage(L, nb, seqs, prompt, ti, "ln", (gam(0, 0), bet(0, 0)), [(uT, g_uT, 3, 4)])

        def mlp(l, kind_s):
            for pj in range(8):
                ws, wr_ = wnext()

                def ev(j, bi, pj=pj):
                    c = pj * 4 + j
                    rt, rt_r = rot(list(zip(rtmp, rtmp_r)), "rtmp")
                    act(rt[:, 0:T], bank(bi)[:, 0:T], AF.Relu, [bankres[bi]], [rt_r])
                    tt(SQE, hT[:, c, 0:T], rt[:, 0:T], rt[:, 0:T], ALU.mult, [rt_r], [r_hT[c]])
                proj_a(ws, wr_, 4, uT, g_uT, T, ev, [0, 1, 2, 3, 4, 5, 6, 7])
            evr = resid_evac(L, nb, seqs, kind_s, same)
            for nh in range(2):
                bks = [nh * 4 + j for j in range(4)]
                for kg in range(4):
                    ws, wr_ = wnext()
                    for j in range(4):
                        for kc in range(8):
                            mm(bank(bks[j])[:, 0:T], ws[:, kc, j * 128:(j + 1) * 128], hT[:, kg * 8 + kc, 0:T],
                               kg == 0 and kc == 0, kg == 3 and kc == 7, [wr_, r_hT[kg * 8 + kc]], [bankres[bks[j]]],
                               sig=(kc == 7))
                for j in range(4):
                    evr(nh * 4 + j, bks[j])

        ckpt(6)
        mlp(0, 5)
        ckpt(7)
        ln_stage(L, nb, seqs, prompt, ti, "ln", (gam(0, 1), bet(0, 1)), [(uT, g_uT, 6, 7), (bufA, g_A, 8, 9)])

        ckpt(8)
        ws, wr_ = wnext()
        if prompt:
            kcol0 = (ti % 2) * 512
            ringb = [(ti % 2) * 4 + b for b in BL]
        else:
            kcol0 = 0
            ringb = [0, 1, 2, 3]

        def ev_k(j, bi):
            cpa(KTr[:, j, kcol0:kcol0 + T], bank(bi)[:, 0:T], [bankres[bi]],
                [r_KTr[x] for x in (ringb if prompt else [0])])
        proj_a(ws, wr_, 2, bufA, g_A, T, ev_k, [0, 1])
        for b in BL:
            bi = rot([2, 3])
            for kc in range(8):
                mm(bank(bi)[0:L, :], bufA[:, kc, CS[b]], ws[:, kc, :], kc == 0, kc == 7, [wr_, g_A[kc][b]], [bankres[bi]],
                   sig=(kc == 7))
            cpa(Vr[0:L, ringb[b], :, 0:64], bank(bi)[0:L, 256:512].rearrange("p (h d) -> p h d", h=4),
                [bankres[bi]], [r_Vr[ringb[b]]])
            if last or not prompt:
                stg, stg_r = nscr()
                cp("act", stg[0:L, 0:512], bank(bi)[0:L, :], [bankres[bi]], [stg_r])
                if prompt:
                    dk_, dv_ = o_kp[b * 128:(b + 1) * 128, :], o_vp[b * 128:(b + 1) * 128, :]
                else:
                    dk_, dv_ = o_ks[b * 32:(b + 1) * 32, :], o_vs[b * 32:(b + 1) * 32, :]
                S.dma("pool", dk_, stg[0:L, 0:256], reads=[stg_r], writes=[r("o_k")])
                S.dma("pool", dv_, stg[0:L, 256:512], reads=[stg_r], writes=[r("o_v")])

        def ev_q(base):
            def ev(j, bi):
                cpa(bufC[:, base + j, 0:T], bank(bi)[:, 0:T], [bankres[bi]], gsel(g_C, [base + j], BL))
            return ev
        for pj in range(2):
            ws, wr_ = wnext()
            proj_a(ws, wr_, 4, uT, g_uT, T, ev_q(pj * 4), [4, 5, 6, 7])

        ckpt(9)
        def keyblocks_for(b):
            kbs = []
            if prompt:
                Bg = ti * 4 + b
                for j in range(5):
                    KB = Bg - 4 + j
                    if KB < 0:
                        continue
                    pos = KB % 8
                    kd = {0: "mask0", 1: "plain", 2: "plain", 3: "b3", 4: "b4"}[j]
                    kbs.append((
                        (lambda kk, rows, pos=pos: KTr[rows, kk, pos * 128:(pos + 1) * 128]),
                        (lambda kh, pos=pos: Vr[:, pos, kh, :]), 128, kd, [r_KTr[pos], r_Vr[pos]]))
            else:
                rc = r_KTr[4:8] + r_Vr[4:8]
                for j in range(4):
                    kbs.append((
                        (lambda kk, rows, j=j: KTc[rows, kk, j * 128:(j + 1) * 128]),
                        (lambda kh, j=j: Vc[:, j, kh, :]), 128, "b3" if j == 3 else "plain", rc))
                kbs.append((
                    (lambda kk, rows, b=b: KTr[rows, kk, b * 32:(b + 1) * 32]),
                    (lambda kh, b=b: Vr[0:32, b, kh, :]), 32, "b4", [r_KTr[0], r_Vr[b]]))
            return kbs

        def load_cache(b):
            stg, stg_r = nscr()
            S.dma(XQ, stg[:, :].rearrange("p (j f) -> p j f", j=4),
                  dap(ck, b * 512 * 256, [[256, 128], [128 * 256, 4], [1, 256]]), writes=[stg_r])
            for j in range(4):
                for kk in range(2):
                    a = j * 2 + kk
                    tr(pair(0)[:, a * 128:(a + 1) * 128], stg[:, j * 256 + kk * 128:j * 256 + (kk + 1) * 128], ident[:, :],
                       [stg_r, r("ident")], [bankres[0], bankres[1]], sig=(a == 7))
            for kk in range(2):
                cp("dve", KTc[:, kk, :].rearrange("p (j s) -> p j s", j=4),
                   pair(0)[:, :].rearrange("p (j k s) -> p j k s", j=4, k=2)[:, :, kk, :],
                   [bankres[0], bankres[1]], r_KTr[4:8])
            stg2, stg2_r = nscr()
            S.dma(XQ, stg2[:, :].rearrange("p (j f) -> p j f", j=4),
                  dap(cv, b * 512 * 256, [[256, 128], [128 * 256, 4], [1, 256]]), writes=[stg2_r])
            cp("dve", Vc[:, :, :, 0:64], stg2[:, :].rearrange("p (j h d) -> p j h d", j=4, h=4), [stg2_r], r_Vr[4:8])

        uctr = [0]

        def emit_st(b, kh, kbs):
            cs = CS[b]
            Lq = L
            kk, e_ = kh // 2, kh % 2
            rows = slice(e_ * 64, (e_ + 1) * 64)
            uctr[0] += 1
            pts = []
            for (ktf, vf, nk, kd, rds) in kbs:
                sbk = rot([0, 1, 2, 3])
                sps = bank(sbk)[0:nk, 0:4 * Lq]
                mm(sps, ktf(kk, rows), bufC[rows, kk * 4:(kk + 1) * 4, cs], True, True,
                   rds + gsel(g_C, range(kk * 4, kk * 4 + 4), [b]), [bankres[sbk]])
                if kd == "mask0":
                    pt, pt_r = PT0[uctr[0] % 2], PT0_r[uctr[0] % 2]
                    s3 = sps.rearrange("p (g q) -> p g q", g=4)
                    p3 = pt[0:nk, 0:4 * Lq].rearrange("p (g q) -> p g q", g=4)
                    act(p3[:, :, 0:64], s3[:, :, 0:64], AF.Exp, [bankres[sbk]], [pt_r], scale=0.125)
                    act(p3[64:128, :, 64:128], s3[64:128, :, 64:128], AF.Exp, [bankres[sbk]], [pt_r], scale=0.125)
                else:
                    pt, pt_r = rot(list(zip(PT, PT_r)), "PT")
                    if kd == "plain":
                        act(pt[0:nk, 0:4 * Lq], sps, AF.Exp, [bankres[sbk]], [pt_r], scale=0.125)
                    else:
                        bt_ = bias3 if kd == "b3" else bias4
                        ei, ei_r = rot(list(zip(expin, expin_r)), "expin")
                        stt(ei[0:nk, 0:4 * Lq].rearrange("p (g q) -> p g q", g=4),
                            sps.rearrange("p (g q) -> p g q", g=4), 0.125,
                            bt_[0:nk, kh * 4:(kh + 1) * 4, 0:Lq], ALU.mult, ALU.add,
                            [bankres[sbk], r("bias3"), r("bias4")], [ei_r])
                        act(pt[0:nk, 0:4 * Lq], ei[0:nk, 0:4 * Lq], AF.Exp, [ei_r], [pt_r])
                pts.append((pt, pt_r))
            return pts

        def emit_pv(b, kh, kbs, pts):
            Lq = L
            obk = 4 + kh
            nkb = len(kbs)
            for g in range(4):
                for ki, (ktf, vf, nk, kd, rds) in enumerate(kbs):
                    pt, pt_r = pts[ki]
                    mm(bank(obk)[0:Lq, g * 65:(g + 1) * 65], pt[0:nk, g * Lq:(g + 1) * Lq], vf(kh)[0:nk, :],
                       ki == 0, ki == nkb - 1, [pt_r] + rds, [bankres[obk]], sig=(ki == nkb - 1))

        def emit_fin(b):
            cs = CS[b]
            Lq = L
            for kh in range(4):
                obk = 4 + kh
                o3 = bank(obk)[0:Lq, 0:260].rearrange("p (g d) -> p g d", g=4)
                recip(rden[0:Lq, kh * 4:(kh + 1) * 4], o3[:, :, 64], [bankres[obk]], [r("rden")])
                tt("dve", On[0:Lq, kh * 4:(kh + 1) * 4, :], o3[:, :, 0:64],
                   rden[0:Lq, kh * 4:(kh + 1) * 4].unsqueeze(2).broadcast_to([Lq, 4, 64]), ALU.mult,
                   [bankres[obk], r("rden")], [r("On")])
            for c in range(8):
                tr(bankb(0)[:, c * 128:c * 128 + Lq], On[0:Lq, 2 * c:2 * c + 2, :].rearrange("p h d -> p (h d)"),
                   identb[0:Lq, 0:Lq], [r("On"), r("identb")], [bankres[0]], sig=(c == 7))
            cpa(bufB[:, :, cs], bankb(0)[:, 0:1024].rearrange("p (c q) -> p c q", c=8)[:, :, 0:Lq], [bankres[0]],
                gsel(g_B, ALLC, [b]))

        if prompt:
            units = [(b, kh) for b in BL for kh in range(4)]
            kbl = {b: keyblocks_for(b) for b in BL}
            nxt = emit_st(units[0][0], units[0][1], kbl[units[0][0]])
            for ui, (b, kh) in enumerate(units):
                cur = nxt
                if ui + 1 < len(units):
                    b2, kh2 = units[ui + 1]
                    nxt = emit_st(b2, kh2, kbl[b2])
                emit_pv(b, kh, kbl[b], cur)
                if kh == 3:
                    emit_fin(b)
        else:
            for b in BL:
                load_cache(b)
                kbs = keyblocks_for(b)
                for kh in range(4):
                    pts = emit_st(b, kh, kbs)
                    emit_pv(b, kh, kbs, pts)
                emit_fin(b)

        ckpt(10)
        for pj in range(2):
            ws, wr_ = wnext()
            evr = resid_evac(L, nb, seqs, 10, same)
            proj_a(ws, wr_, 4, bufB, g_B, T, (lambda j, bi, pj=pj, evr=evr: evr(pj * 4 + j, bi)), [0, 1, 2, 3])
        ln_stage(L, nb, seqs, prompt, ti, "ln", (gam(1, 0), bet(1, 0)), [(uT, g_uT, 11, 12)])
        mlp(1, 13)
        ln_stage(L, nb, seqs, prompt, ti, "final", None, [])

    try:
        ckpt(1)
        for (kind, ti) in tiles:
            do_tile(kind, ti)
    except _Stop:
        pass

    print('sbuf bytes remaining', nc.sbuf_bytes_remaining, flush=True)
    S.final_drain("sp")
    S.emit(nc)
    st.close()
    return nc


_CACHE = {}


def _consts():
    ident = np.eye(128, dtype=np.float32)
    tri = (np.arange(128)[:, None] <= np.arange(128)[None, :]).astype(np.float32) * np.float32(128.0 ** -0.5)
    m4 = np.zeros((128, 128), np.float32)
    m4[64:, :64] = NEG
    return np.ascontiguousarray(np.stack([ident, tri, m4], axis=1))


def make_in_maps(inp, NT, ncores=8):
    f = lambda a: np.ascontiguousarray(np.asarray(a, dtype=np.float32))
    rel = f(inp["rel_bias_b"])[0]
    ext = np.concatenate([rel, np.repeat(rel[:, 256:257], 128, axis=1)], axis=1)
    rel_rev = np.ascontiguousarray(ext[:, ::-1])
    vec1 = np.concatenate([f(inp["conv_w_a"])[0].reshape(32, 128), f(inp["conv_b_a"])[0].reshape(8, 128),
                           f(inp["ln_g"]).reshape(32, 128), f(inp["ln_b"]).reshape(32, 128),
                           f(inp["mhn_g_a"])[0].reshape(8, 128)], axis=0)
    vec2 = np.concatenate([f(inp["b_ada"]).reshape(96, 128), f(inp["b_ada_kv"]).reshape(16, 128)], axis=0)
    lnf = np.stack([f(inp["ln_g"])[1, 1], f(inp["ln_b"])[1, 1]], axis=0)
    shared = {
        "w_in": f(inp["w_in_a"])[0], "w_out_a": f(inp["w_out_a"])[0],
        "w_up0": f(inp["w_up"])[0], "w_up1": f(inp["w_up"])[1],
        "w_down0": f(inp["w_down"])[0], "w_down1": f(inp["w_down"])[1],
        "w_kv": f(inp["w_kv"]), "w_q": f(inp["w_q_b"])[0], "w_out_b": f(inp["w_out_b"])[0],
        "w_ada": f(inp["w_ada"]), "w_ada_kv": f(inp["w_ada_kv"]),
        "vec1": np.ascontiguousarray(vec1), "vec2": np.ascontiguousarray(vec2),
        "b_if": f(inp["b_if_a"]).reshape(8, 1), "lnf": np.ascontiguousarray(lnf),
        "rel_rev": rel_rev, "cst": _consts(),
    }
    maps = []
    xp, xs = f(inp["x_prompt"]), f(inp["x_sample"])
    for i in range(ncores):
        s0 = 4 * i
        m = dict(shared)
        m["x_p"] = np.ascontiguousarray(xp[i, :NT * 512])
        m["x_s"] = np.ascontiguousarray(xs[s0:s0 + 4].reshape(128, 1024))
        m["c_all"] = np.ascontiguousarray(np.concatenate([f(inp["c_prompt"])[i:i + 1], f(inp["c_sample"])[s0:s0 + 4]], axis=0))
        m["sconv"] = np.ascontiguousarray(f(inp["state_conv"])[0, s0:s0 + 4])
        m["sC"] = np.ascontiguousarray(f(inp["state_C"])[0, s0:s0 + 4])
        m["sn"] = np.ascontiguousarray(f(inp["state_n"])[0, s0:s0 + 4])
        m["smT"] = np.ascontiguousarray(f(inp["state_m"])[0, s0:s0 + 4].T)
        m["ck"] = np.ascontiguousarray(f(inp["cache_k"])[s0:s0 + 4].reshape(4, 512, 256))
        m["cv"] = np.ascontiguousarray(f(inp["cache_v"])[s0:s0 + 4].reshape(4, 512, 256))
        maps.append(m)
    return maps


def assemble(results, NT, ncores=8):
    g = lambda k: np.stack([np.asarray(results[i][k], dtype=np.float32) for i in range(ncores)], axis=0)
    y_p = g("y_p")
    y_s = g("y_s").reshape(ncores * 4, 32, 1024)
    conv_p = g("o_convp")[None]
    C_p = g("o_Cp")[None]
    n_p = g("o_np")[None]
    m_p = g("o_mp").reshape(ncores, 4)[None]
    k_p = g("o_kp").reshape(ncores, 512, 4, 64)
    v_p = g("o_vp").reshape(ncores, 512, 4, 64)
    conv_s = g("o_convs").reshape(ncores * 4, 3, 1024)[None]
    C_s = g("o_Cs").reshape(ncores * 4, 4, 256, 128)[None]
    n_s = g("o_ns").reshape(ncores * 4, 4, 128)[None]
    m_s = g("o_ms").reshape(ncores * 4, 4)[None]
    k_s = g("o_ks").reshape(ncores * 4, 32, 4, 64)
    v_s = g("o_vs").reshape(ncores * 4, 32, 4, 64)
    return (y_p, y_s, conv_p, C_p, n_p, m_p, k_p, v_p, conv_s, C_s, n_s, m_s, k_s, v_s)


def kernel(**inputs):
    NT = 16
    if NT not in _CACHE:
        _CACHE[NT] = build(NT)
    nc = _CACHE[NT]
    in_maps = make_in_maps(inputs, NT)
    res = run_bass_kernel_spmd(nc, in_maps, core_ids=list(range(8)))
    return assemble(res.results, NT)
```

```python
import contextlib
import numpy as np
import concourse.bass as bass
import concourse.mybir as mybir
from concourse.bass_utils import run_bass_kernel_spmd

F32 = mybir.dt.float32
BF16 = mybir.dt.bfloat16
AF = mybir.ActivationFunctionType
ALU = mybir.AluOpType
AX = mybir.AxisListType

ALPHA = 4.0 ** 0.25
LN_EPS_P = 1e-5 / (ALPHA * ALPHA)
HN_EPS = 1e-6
NSLOT = 3
PDEPTH = 2
NEG = -30000.0
import os
XQ = os.environ.get("XQ", "act")
SQE = os.environ.get("SQE", "pool")


class Res:
    __slots__ = ("name", "w", "r", "sem", "excl")

    def __init__(self, name, excl=False):
        self.name = name
        self.w = None
        self.r = []
        self.sem = None
        self.excl = excl


class Sched:
    ENGS = ("pe", "act", "dve", "pool", "sp")

    def __init__(self):
        self.ops = {e: [] for e in self.ENGS}
        self.nsig = {e: 0 for e in self.ENGS}
        self.waited = {e: {} for e in self.ENGS}
        self.semkeys = ["E_" + e for e in self.ENGS]
        self.dma_sems = {}
        self.nres = 0

    def res(self, name=None):
        self.nres += 1
        return Res(name or f"r{self.nres}")

    def _need(self, eng, ev, waits, same_ok):
        if ev is None:
            return
        key, val, weng = ev
        if same_ok and weng == eng:
            return
        if self.waited[eng].get(key, 0) >= val:
            return
        if waits.get(key, 0) < val:
            waits[key] = val

    def _deps(self, eng, reads, writes):
        waits = {}
        for r in reads:
            self._need(eng, r.w, waits, eng == "pe")
            if r.excl:
                for ev in r.r:
                    self._need(eng, ev, waits, True)
        for w in writes:
            self._need(eng, w.w, waits, True)
            for ev in w.r:
                self._need(eng, ev, waits, True)
        for k, v in waits.items():
            self.waited[eng][k] = v
        return list(waits.items())

    def _record(self, ev, reads, writes):
        for r in reads:
            r.r.append(ev)
        for w in writes:
            w.w = ev
            w.r = []

    def op(self, eng, fn, reads=(), writes=(), sig=True):
        waits = self._deps(eng, reads, writes)
        key = "E_" + eng
        if sig:
            self.nsig[eng] += 1
            val = self.nsig[eng]
        else:
            val = self.nsig[eng] + 1
        self._record((key, val, eng), reads, writes)
        self.ops[eng].append((waits, fn, (key, 1) if sig else None))

    def dma(self, q, out, in_, reads=(), writes=(), **kw):
        waits = self._deps(q, reads, writes)
        tgt = writes[0]
        if tgt.sem is None:
            tgt.sem = f"D{len(self.dma_sems)}_{tgt.name}"
            self.semkeys.append(tgt.sem)
            self.dma_sems[tgt.sem] = 0
        self.dma_sems[tgt.sem] += 16
        self._record((tgt.sem, self.dma_sems[tgt.sem], "dma"), reads, writes)
        self.ops[q].append((waits, (lambda e, o=out, i=in_, k=kw: e.dma_start(out=o, in_=i, **k)),
                            (tgt.sem, 16)))

    def barrier(self):
        for e in self.ENGS:
            waits = {}
            for e2 in self.ENGS:
                k = "E_" + e2
                if self.nsig[e2] > 0 and e2 != e and self.waited[e].get(k, 0) < self.nsig[e2]:
                    waits[k] = self.nsig[e2]
            for k, v in self.dma_sems.items():
                if v > 0 and self.waited[e].get(k, 0) < v:
                    waits[k] = v
            for k, v in waits.items():
                self.waited[e][k] = v
            if waits:
                self.ops[e].append((list(waits.items()), None, None))

    def final_drain(self, eng="sp"):
        waits = {}
        for k, v in self.dma_sems.items():
            if v > 0 and self.waited[eng].get(k, 0) < v:
                waits[k] = v
        for e2 in self.ENGS:
            if self.nsig[e2] > 0 and e2 != eng:
                waits["E_" + e2] = self.nsig[e2]
        self.ops[eng].append((list(waits.items()), None, None))

    def emit(self, nc):
        with contextlib.ExitStack() as st:
            sems = {k: st.enter_context(nc.semaphore(k)) for k in self.semkeys}
            block = st.enter_context(nc.Block())

            def replay(engobj, name):
                for waits, fn, sig in self.ops[name]:
                    for k, v in waits:
                        engobj.wait_ge(sems[k], v)
                    if fn is None:
                        continue
                    ins = fn(engobj)
                    if sig is not None:
                        ins.then_inc(sems[sig[0]], sig[1])

            @block.sync
            def _(e):
                replay(e, "sp")

            @block.tensor
            def _(e):
                replay(e, "pe")

            @block.scalar
            def _(e):
                replay(e, "act")

            @block.vector
            def _(e):
                replay(e, "dve")

            @block.gpsimd
            def _(e):
                replay(e, "pool")


def panel_list():
    pl = []
    for j in range(6):
        pl.append(("w_in", 0, j * 512, 3080, None))
    for j in range(2):
        pl.append(("w_out_a", 0, j * 512, 1024, None))
    for j in range(8):
        pl.append(("w_up0", 0, j * 512, 4096, None))
    for nh in range(2):
        for kg in range(4):
            pl.append(("w_down0", kg * 1024, nh * 512, 1024, None))
    pl.append(("w_kv", 0, 0, 512, None))
    for j in range(2):
        pl.append(("w_q", 0, j * 512, 1024, "qperm"))
    for j in range(2):
        pl.append(("w_out_b", 0, j * 512, 1024, None))
    for j in range(8):
        pl.append(("w_up1", 0, j * 512, 4096, None))
    for nh in range(2):
        for kg in range(4):
            pl.append(("w_down1", kg * 1024, nh * 512, 1024, None))
    return pl


NPANEL = 45


class _Stop(Exception):
    pass


def build(NT, do_sample=True, stage=99):
    nc = bass.Bass("TRN2", target_bir_lowering=False)
    S = Sched()
    st = contextlib.ExitStack()
    SEQ = NT * 512

    def din(name, shape, dt=F32):
        return nc.dram_tensor(name, list(shape), dt, kind="ExternalInput").ap()

    def dout(name, shape, dt=F32):
        return nc.dram_tensor(name, list(shape), dt, kind="ExternalOutput").ap()

    def dap(t, offset, dims):
        return bass.AP(tensor=t.tensor, offset=offset, ap=[list(d) for d in dims])

    x_p = din("x_p", [SEQ, 1024])
    x_s = din("x_s", [128, 1024])
    c_all = din("c_all", [5, 1024])
    sconv = din("sconv", [4, 3, 1024])
    sC = din("sC", [4, 4, 256, 128])
    sn = din("sn", [4, 4, 128])
    smT = din("smT", [4, 4])
    ck = din("ck", [4, 512, 256])
    cv = din("cv", [4, 512, 256])
    W = {
        "w_in": din("w_in", [1024, 3080]),
        "w_out_a": din("w_out_a", [1024, 1024]),
        "w_up0": din("w_up0", [1024, 4096]),
        "w_up1": din("w_up1", [1024, 4096]),
        "w_down0": din("w_down0", [4096, 1024]),
        "w_down1": din("w_down1", [4096, 1024]),
        "w_kv": din("w_kv", [1024, 512]),
        "w_q": din("w_q", [1024, 1024]),
        "w_out_b": din("w_out_b", [1024, 1024]),
    }
    w_ada = din("w_ada", [2, 1024, 6144])
    w_ada_kv = din("w_ada_kv", [1024, 2048])
    vec1 = din("vec1", [112, 128])
    vec2 = din("vec2", [112, 128])
    b_if = din("b_if", [8, 1])
    lnf = din("lnf", [2, 1024])
    rel_rev = din("rel_rev", [16, 385])
    cst = din("cst", [128, 3, 128])

    y_p = dout("y_p", [SEQ, 1024])
    y_s = dout("y_s", [128, 1024])
    o_convp = dout("o_convp", [3, 1024])
    o_Cp = dout("o_Cp", [4, 256, 128])
    o_np = dout("o_np", [4, 128])
    o_mp = dout("o_mp", [4, 1])
    o_kp = dout("o_kp", [512, 256])
    o_vp = dout("o_vp", [512, 256])
    o_convs = dout("o_convs", [4, 3, 1024])
    o_Cs = dout("o_Cs", [4, 4, 256, 128])
    o_ns = dout("o_ns", [4, 4, 128])
    o_ms = dout("o_ms", [4, 4, 1])
    o_ks = dout("o_ks", [128, 256])
    o_vs = dout("o_vs", [128, 256])
    wsc = nc.dram_tensor("wsc", [NPANEL, 128, 8, 512], BF16).ap()

    def sb(name, shape, dt=F32):
        return st.enter_context(nc.sbuf_tensor(name, list(shape), dt))

    PP = [st.enter_context(nc.psum_tensor(f"pp{i}", [128, 1024], F32)) for i in range(4)]
    bankres = [Res(f"bank{i}", excl=True) for i in range(8)]

    def bank(i):
        return PP[i // 2][:, (i % 2) * 512:(i % 2) * 512 + 512]

    def bankb(i):
        return PP[i // 2][:, (i % 2) * 512:(i % 2) * 512 + 512].bitcast(BF16)

    def pair(i):
        return PP[i][:, :]

    wslot = [sb(f"wslot{i}", [128, 8, 512], BF16) for i in range(NSLOT)]
    wslot_r = [S.res(f"wslot{i}") for i in range(NSLOT)]
    xT = sb("xT", [128, 8, 512])
    scr = [sb(f"scr{i}", [128, 1024]) for i in range(3)]
    scr_r = [S.res(f"scr{i}") for i in range(3)]
    uT = sb("uT", [128, 8, 512], BF16)
    bufA = sb("bufA", [128, 8, 512], BF16)
    bufB = sb("bufB", [128, 8, 512], BF16)
    bufC = sb("bufC", [128, 8, 512], BF16)
    cb = sb("cb", [128, 8, 4, 131], BF16)
    vaug = sb("vaug", [128, 4, 4, 257], BF16)
    hT = sb("hT", [128, 32, 512], BF16)
    rtmp = [sb(f"rtmp{i}", [128, 512]) for i in range(2)]
    rtmp_r = [S.res(f"rtmp{i}") for i in range(2)]
    C_sb = sb("C_sb", [128, 4, 257])
    Cb = sb("Cb", [128, 4, 257], BF16)
    sTs = sb("sTs", [128, 4, 128], BF16)
    kw = sb("kw", [128, 4, 128], BF16)
    hn = sb("hn", [128, 4, 256], BF16)
    KTr = sb("KTr", [128, 2, 1024], BF16)
    Vr = sb("Vr", [128, 8, 4, 65], BF16)
    KTc = KTr[:, :, 512:1024]
    Vc = Vr[:, 4:8, :, :]
    NPT = 8
    PT = [sb(f"PT{i}", [128, 512], BF16) for i in range(NPT)]
    PT_r = [S.res(f"PT{i}") for i in range(NPT)]
    PT0 = [sb(f"PTm{i}", [128, 512], BF16) for i in range(2)]
    PT0_r = [S.res(f"PTm{i}") for i in range(2)]
    expin = [sb(f"expin{i}", [128, 512]) for i in range(2)]
    expin_r = [S.res(f"expin{i}") for i in range(2)]
    On = sb("On", [128, 16, 64], BF16)
    bias3 = sb("bias3", [128, 16, 128], BF16)
    bias4 = sb("bias4", [128, 16, 128], BF16)
    lnfbc = sb("lnfbc", [128, 2, 1024])
    ident = sb("ident", [128, 128])
    identb = sb("identb", [128, 128], BF16)
    trimask = sb("trimask", [128, 128])
    mask4 = sb("mask4", [128, 128])
    diagc = sb("diagc", [128, 8, 4, 128], BF16)
    vecT = sb("vecT", [128, 112])
    biasT = sb("biasT", [128, 112])
    TAB = sb("TAB", [128, 14, 8, 5])
    cT = sb("cT", [128, 8, 5])
    wg = sb("wg", [128, 8, 8], BF16)
    wg32 = sb("wg32", [128, 8, 8])
    bg = sb("bg", [4, 2])
    bg8 = sb("bg8", [8, 1])
    ones4 = sb("ones4", [4, 128])
    chv = sb("chv", [128, 16])
    cprev = sb("cprev", [128, 8, 3], BF16)
    cstage = sb("cstage", [128, 3, 8])
    mcol = sb("mcol", [4, 8])
    mout = sb("mout", [4, 4])
    G_g = sb("G_g", [4, 512])
    G_s = sb("G_s", [4, 8, 4])
    G_d4 = sb("G_d4", [4, 4, 4])
    gsc = sb("gsc", [128, 4, 16])
    st6 = sb("st6", [128, 4, 6])
    mv = sb("mv", [128, 4, 2])
    hsm = sb("hsm", [128, 8, 4])
    lst = sb("lst", [128, 4, 2, 6])
    lmv = sb("lmv", [128, 4, 8])
    rden = sb("rden", [128, 16])
    modT = scr[0][:, 0:560].rearrange("p (a s) -> p a s", a=112)
    mod1 = scr[1][:, 0:560].rearrange("p (a s) -> p a s", a=112)
    crow = scr[2][0:5, :]
    rowc = expin[0][0:5, :]
    G_ig = expin[1][0:4, :]
    G_fg = expin[0][0:4, :]
    G_l = rtmp[0][0:4, :]
    G_nb = rtmp[1][0:4, :]

    R = {"modT": scr_r[0], "mod1": scr_r[1], "crow": scr_r[2], "rowc": expin_r[0]}

    def r(name):
        if name not in R:
            R[name] = S.res(name)
        return R[name]

    def grid(name):
        return [[S.res(f"{name}_{c}_{b}") for b in range(4)] for c in range(8)]

    g_xT = grid("xT"); g_uT = grid("uT"); g_A = grid("bufA"); g_B = grid("bufB"); g_C = grid("bufC")
    g_cb = grid("cb")
    r_vaug = [S.res(f"vaug{b}") for b in range(4)]
    r_hT = [S.res(f"hT{c}") for c in range(32)]
    r_Csb = [S.res(f"Csb{h}") for h in range(4)]
    r_Cb = [S.res(f"Cb{h}") for h in range(4)]
    r_KTr = [S.res(f"KTr{i}") for i in range(8)]
    r_Vr = [S.res(f"Vr{i}") for i in range(8)]

    def gsel(g, cs, bs):
        return [g[c][b] for c in cs for b in bs]

    ALLC = list(range(8))

    def mm(out, lhsT, rhs, start, stop, rd, wr, sig=True):
        S.op("pe", lambda e: e.matmul(out, lhsT=lhsT, rhs=rhs, start=start, stop=stop), rd, wr, sig)

    def tr(out, in_, idn, rd, wr, sig=True):
        S.op("pe", lambda e: e.transpose(out, in_, idn), rd, wr, sig)

    def act(out, in_, func, rd, wr, bias=None, scale=None):
        kw_ = {}
        if bias is not None:
            kw_["bias"] = bias
        if scale is not None:
            kw_["scale"] = scale
        S.op("act", lambda e: e.activation(out=out, in_=in_, func=func, **kw_), rd, wr)

    def ts(eng, out, in0, s1, s2, op0, op1, rd, wr):
        if s2 is None:
            S.op(eng, lambda e: e.tensor_scalar(out=out, in0=in0, scalar1=s1, scalar2=None, op0=op0), rd, wr)
        else:
            S.op(eng, lambda e: e.tensor_scalar(out=out, in0=in0, scalar1=s1, scalar2=s2, op0=op0, op1=op1), rd, wr)

    def tt(eng, out, in0, in1, op, rd, wr):
        S.op(eng, lambda e: e.tensor_tensor(out=out, in0=in0, in1=in1, op=op), rd, wr)

    def stt(out, in0, scalar, in1, op0, op1, rd, wr):
        S.op("dve", lambda e: e.scalar_tensor_tensor(out=out, in0=in0, scalar=scalar, in1=in1, op0=op0, op1=op1), rd, wr)

    def cp(eng, out, in_, rd, wr):
        if eng == "act":
            if "i" in os.environ.get("DBG", ""):
                S.op("act", lambda e: e.activation(out=out, in_=in_, func=AF.Identity), rd, wr)
            else:
                S.op("act", lambda e: e.copy(out=out, in_=in_), rd, wr)
        else:
            S.op(eng, lambda e: e.tensor_copy(out=out, in_=in_), rd, wr)

    def recip(out, in_, rd, wr):
        S.op("dve", lambda e: e.reciprocal(out=out, in_=in_), rd, wr)

    affctr = [0]

    def aff(eng, out, in_, A, B, rd, wr):
        if eng == "dve":
            ts("dve", out, in_, A, B, ALU.mult, ALU.add, rd, wr)
        else:
            act(out, in_, AF.Identity, rd, wr, bias=B, scale=A)

    cpctr = [0]

    def cpa(out, in_, rd, wr):
        cpctr[0] += 1
        cp("dve" if cpctr[0] % 2 == 0 else "act", out, in_, rd, wr)

    S.dma("sp", ident[:], cst[:, 0, :], writes=[r("ident")])
    S.dma("sp", trimask[:], cst[:, 1, :], writes=[r("trimask")])
    S.dma("sp", mask4[:], cst[:, 2, :], writes=[r("mask4")])
    cp("dve", identb[:], ident[:], [r("ident")], [r("identb")])
    S.op("dve", lambda e: e.memset(ones4[:], 1.0), [], [r("ones4")])
    S.dma("sp", lnfbc[:, 0, :], dap(lnf, 0, [[0, 128], [1, 1024]]), writes=[r("lnfbc")])
    S.dma("sp", lnfbc[:, 1, :], dap(lnf, 1024, [[0, 128], [1, 1024]]), writes=[r("lnfbc")])

    S.dma("sp", scr[0][0:112, 0:128], vec1[:, :], writes=[scr_r[0]])
    S.dma("sp", scr[1][0:112, 0:128], vec2[:, :], writes=[scr_r[1]])
    tr(bank(0)[:, 0:112], scr[0][0:112, 0:128], ident[0:112, 0:112], [scr_r[0], r("ident")], [bankres[0]])
    tr(bank(1)[:, 0:112], scr[1][0:112, 0:128], ident[0:112, 0:112], [scr_r[1], r("ident")], [bankres[1]])
    cp("dve", vecT[:], bank(0)[:, 0:112], [bankres[0]], [r("vecT")])
    cp("dve", biasT[:], bank(1)[:, 0:112], [bankres[1]], [r("biasT")])

    for c in range(8):
        for j in range(4):
            ts("dve", diagc[:, c, j, :], ident[:], vecT[:, j * 8 + c:j * 8 + c + 1], None, ALU.mult, None,
               [r("ident"), r("vecT")], [r("diagc")])

    S.dma("sp", wg32[:], dap(W["w_in"], 3072, [[3080, 128], [128 * 3080, 8], [1, 8]]), writes=[r("wg32")])
    cp("dve", wg[:], wg32[:], [r("wg32")], [r("wg")])
    S.dma("sp", bg[:, 0:1], b_if[0:4, :], writes=[r("bg")])
    S.dma("sp", bg[:, 1:2], b_if[4:8, :], writes=[r("bg")])
    ts("dve", bg[:, 1:2], bg[:, 1:2], -1.0, None, ALU.mult, None, [r("bg")], [r("bg")])

    S.dma("sp", crow[:], c_all[:, :], writes=[r("crow")])
    act(crow[:], crow[:], AF.Silu, [r("crow")], [r("crow")])
    for kc in range(8):
        tr(bank(2)[:, kc * 8:kc * 8 + 5], crow[:, kc * 128:(kc + 1) * 128], ident[0:5, 0:5],
           [r("crow"), r("ident")], [bankres[2]])
    cp("dve", cT[:], bank(2)[:, 0:64].rearrange("p (k s) -> p k s", k=8)[:, :, 0:5], [bankres[2]], [r("cT")])

    hT32 = hT[:].rearrange("p a b -> p (a b)").bitcast(F32)
    stg32 = [hT32[:, i * 4096:(i + 1) * 4096].rearrange("p (k n) -> p k n", k=8) for i in range(2)]
    stg32_r = [S.res("stg32_0"), S.res("stg32_1")]
    npan = 0
    ada_srcs = []
    for l in range(2):
        for j in range(12):
            ada_srcs.append((w_ada, l * 1024 * 6144 + j * 512, 6144))
    for j in range(4):
        ada_srcs.append((w_ada_kv, j * 512, 2048))
    for pi, (wt, off, ncols) in enumerate(ada_srcs):
        sl = pi % 2
        S.dma("sp", stg32[sl], dap(wt, off, [[ncols, 128], [128 * ncols, 8], [1, 512]]), writes=[stg32_r[sl]])
        pb = 3 + (pi % 2)
        for kc in range(8):
            mm(bank(pb)[0:5, :], cT[:, kc, :], stg32[sl][:, kc, :], kc == 0, kc == 7,
               [r("cT"), stg32_r[sl]], [bankres[pb]], sig=(kc == 7))
        cp("act", rowc[:], bank(pb)[0:5, :], [bankres[pb]], [r("rowc")])
        tb = 5 + (pi % 2)
        for q in range(4):
            tr(bank(tb)[:, q * 8:q * 8 + 5], rowc[:, q * 128:(q + 1) * 128], ident[0:5, 0:5],
               [r("rowc"), r("ident")], [bankres[tb]])
        cp("dve", modT[:, pi * 4:pi * 4 + 4, :], bank(tb)[:, 0:32].rearrange("p (q s) -> p q s", q=4)[:, :, 0:5],
           [bankres[tb]], [r("modT")])
    tt("dve", modT[:], modT[:], biasT[:].unsqueeze(2).broadcast_to([128, 112, 5]), ALU.add,
       [r("modT"), r("biasT")], [r("modT")])
    ts("dve", mod1[:], modT[:], 1.0, None, ALU.add, None, [r("modT")], [r("mod1")])

    def mchunk(l, j):
        return l * 48 + j * 8

    def gam(l, i):
        o = 40 + (l * 2 + i) * 8
        return vecT[:, o:o + 8]

    def bet(l, i):
        o = 72 + (l * 2 + i) * 8
        return vecT[:, o:o + 8]

    def bc5(a):
        return a.unsqueeze(2).broadcast_to([128, 8, 5])

    rT = [r("modT"), r("mod1"), r("vecT")]
    cp("dve", TAB[:, 0], mod1[:, mchunk(0, 1):mchunk(0, 1) + 8, :], rT, [r("TAB")])
    cp("dve", TAB[:, 1], modT[:, mchunk(0, 0):mchunk(0, 0) + 8, :], rT, [r("TAB")])
    for kind, l, j in ((2, 0, 2), (5, 0, 5), (10, 1, 2), (13, 1, 5)):
        ts("dve", TAB[:, kind], mod1[:, mchunk(l, j):mchunk(l, j) + 8, :], 1.0 / ALPHA, None, ALU.mult, None, rT, [r("TAB")])

    def mkAB(kA, kB, g_, b_, sc_off, sh_off):
        tt("dve", TAB[:, kA], mod1[:, sc_off:sc_off + 8, :], bc5(g_), ALU.mult, rT + [r("TAB")], [r("TAB")])
        tt("dve", TAB[:, kB], mod1[:, sc_off:sc_off + 8, :], bc5(b_), ALU.mult, rT + [r("TAB")], [r("TAB")])
        tt("dve", TAB[:, kB], TAB[:, kB], modT[:, sh_off:sh_off + 8, :], ALU.add, rT + [r("TAB")], [r("TAB")])

    mkAB(3, 4, gam(0, 0), bet(0, 0), mchunk(0, 4), mchunk(0, 3))
    mkAB(6, 7, gam(0, 1), bet(0, 1), mchunk(1, 1), mchunk(1, 0))
    mkAB(8, 9, gam(0, 1), bet(0, 1), 96 + 8, 96)
    mkAB(11, 12, gam(1, 0), bet(1, 0), mchunk(1, 4), mchunk(1, 3))

    def tab(kind, c, seq):
        return TAB[:, kind, c, seq:seq + 1]

    bst = [hT32[:, 0:2048].rearrange("p (h q) -> p h q", h=16), hT32[:, 2048:4096].rearrange("p (h q) -> p h q", h=16)]
    S.barrier()
    S.dma("sp", bst[0], dap(rel_rev, 1, [[1, 128], [385, 16], [1, 128]]), writes=[stg32_r[0]])
    S.dma("sp", bst[1], dap(rel_rev, 129, [[1, 128], [385, 16], [1, 128]]), writes=[stg32_r[0]])
    S.dma("sp", chv[:], dap(rel_rev, 128, [[0, 128], [385, 16]]), writes=[r("chv")], allow_slow_non_contiguous=True)

    def flipped(a):
        return bass.AP(tensor=a.tensor, offset=a.offset + 127, ap=[list(a.ap[0]), list(a.ap[1]), [-1, 128]])

    chb = chv[:].unsqueeze(2).broadcast_to([128, 16, 128])
    tt("dve", bias3[:], flipped(bst[0]), chb, ALU.subtract, [stg32_r[0], r("chv")], [r("bias3")])
    tt("dve", bst[1], bst[1], flipped(chb) if False else chb, ALU.subtract, [stg32_r[0], r("chv")], [stg32_r[0]])
    m4b = bass.AP(tensor=mask4[:].tensor, offset=mask4[:].offset, ap=[list(mask4[:].ap[0]), [0, 16], [1, 128]])
    tt("dve", bias4[:], flipped(bst[1]), m4b, ALU.add, [stg32_r[0], r("mask4")], [r("bias4")])
    S.barrier()

    def ckpt(n):
        if stage <= n:
            raise _Stop()

    plist = panel_list()
    r_wsc = S.res("wsc")
    casteng = ["dve", "act", "pool"]
    for pi, (wn, row0, col0, ncols, kind) in enumerate(plist):
        sl = pi % 2
        ws = pi % NSLOT
        S.dma("sp", stg32[sl], dap(W[wn], row0 * ncols + col0, [[ncols, 128], [128 * ncols, 8], [1, 512]]),
              writes=[stg32_r[sl]])
        eng = casteng[pi % 3]
        if kind == "qperm":
            for k in range(2):
                o_ = wslot[ws][:].rearrange("p c (g k d) -> p c g k d", g=4, k=2)[:, :, :, k, :]
                i_ = stg32[sl].rearrange("p c (k g d) -> p c k g d", k=2, g=4)[:, :, k, :, :]
                cp("dve", o_, i_, [stg32_r[sl]], [wslot_r[ws]])
        else:
            cp(eng, wslot[ws][:], stg32[sl], [stg32_r[sl]], [wslot_r[ws]])
        S.dma("pool", wsc[pi], wslot[ws][:], reads=[wslot_r[ws]], writes=[r_wsc])
    S.barrier()

    S.op("dve", lambda e: e.memset(vaug[:], 1.0), [], r_vaug)
    S.op("dve", lambda e: e.memset(Vr[:], 1.0), [], r_Vr)
    S.op("dve", lambda e: e.memset(PT0[0][:], 0.0), [], [PT0_r[0]])
    S.op("dve", lambda e: e.memset(PT0[1][:], 0.0), [], [PT0_r[1]])
    S.op("dve", lambda e: e.memset(gsc[:], 0.0), [], [r("gsc")])

    tiles = []
    if do_sample:
        tiles.append(("s", 0))
    for ti in range(NT):
        tiles.append(("p", ti))
    uses = [pi for _ in tiles for pi in range(NPANEL)]
    wstate = {"next": 0, "cur": -1}

    def wnext():
        wstate["cur"] += 1
        i = wstate["cur"]
        while wstate["next"] <= min(i + PDEPTH, len(uses) - 1):
            n = wstate["next"]
            S.dma("sp", wslot[n % NSLOT][:], wsc[uses[n]], reads=[r_wsc], writes=[wslot_r[n % NSLOT]])
            wstate["next"] += 1
        return wslot[i % NSLOT], wslot_r[i % NSLOT]

    bctr = {}

    def rot(lst, key=None):
        key = key or "k%d_%s" % (len(lst), str(lst[0])[:24])
        bctr[key] = bctr.get(key, -1) + 1
        return lst[bctr[key] % len(lst)]

    scrctr = [0]

    def nscr():
        scrctr[0] += 1
        i = scrctr[0] % 3
        return scr[i], scr_r[i]

    def proj_a(ws, wr_, nchunk, rhs_buf, g_rhs, T, evac, banks):
        for j in range(nchunk):
            bi = rot(banks)
            for kc in range(8):
                mm(bank(bi)[:, 0:T], ws[:, kc, j * 128:(j + 1) * 128], rhs_buf[:, kc, 0:T], kc == 0, kc == 7,
                   [wr_] + gsel(g_rhs, [kc], range(4)), [bankres[bi]], sig=(kc == 7))
            evac(j, bi)

    def ln_block(L, b, cs, eps, targets, final_dst=None, nbk=4):
        pz = (b % 2)
        pbk = 2 + (b % 2)
        zp = pair(pz)
        zres = [bankres[2 * pz], bankres[2 * pz + 1]]
        for c in range(8):
            tr(zp[0:L, c * 128:(c + 1) * 128], xT[:, c, cs], ident[:, :], [g_xT[c][b], r("ident")], zres, sig=(c == 7))
        S.op("dve", lambda e: e.bn_stats(out=lst[0:L, 0, :], in_=zp[0:L, 0:512]), zres, [r("lst")])
        S.op("dve", lambda e: e.bn_stats(out=lst[0:L, 1, :], in_=zp[0:L, 512:1024]), zres, [r("lst")])
        S.op("dve", lambda e: e.bn_aggr(out=lmv[0:L, 0:2], in_=lst[0:L].rearrange("p a b -> p (a b)")), [r("lst")], [r("lmv")])
        ts("dve", lmv[0:L, 2:3], lmv[0:L, 1:2], eps, None, ALU.add, None, [r("lmv")], [r("lmv")])
        act(lmv[0:L, 3:4], lmv[0:L, 2:3], AF.Sqrt, [r("lmv")], [r("lmv")])
        recip(lmv[0:L, 4:5], lmv[0:L, 3:4], [r("lmv")], [r("lmv")])
        stt(lmv[0:L, 5:6], lmv[0:L, 0:1], -1.0, lmv[0:L, 4:5], ALU.mult, ALU.mult, [r("lmv")], [r("lmv")])
        xh, xh_r = nscr()
        act(xh[0:L, :], zp[0:L, :], AF.Identity, zres + [r("lmv")], [xh_r], bias=lmv[0:L, 5:6], scale=lmv[0:L, 4:5])
        return xh, xh_r

    def back_block(L, b, cs, xh, xh_r, gb, targets):
        pbk = 2 + (b % 2)
        bp = pair(pbk)
        bres = [bankres[2 * pbk], bankres[2 * pbk + 1]]
        for c in range(8):
            tr(bp[:, c * 128:c * 128 + L], xh[0:L, c * 128:(c + 1) * 128], ident[0:L, 0:L], [xh_r, r("ident")], bres,
               sig=(c == 7))
        for c in range(8):
            src = bp[:, c * 128:c * 128 + L]
            eng = "dve" if c < 4 else "act"
            br1 = [bres[c // 4]]
            if gb is None:
                cp(eng, xT[:, c, cs], src, br1, [g_xT[c][b]])
            else:
                aff(eng, xT[:, c, cs], src, gb[0][:, c:c + 1], gb[1][:, c:c + 1], br1 + [r("vecT")], [g_xT[c][b]])
            for (buf, g_, kA, kB, seq) in targets:
                aff(eng, buf[:, c, cs], src, tab(kA, c, seq), tab(kB, c, seq), br1 + [r("TAB")], [g_[c][b]])

    def ln_stage(L, nb, seqs, prompt, ti, mode, gb, targets):
        for g0 in (0, 2):
            grp = [g0, g0 + 1]
            xh = {}
            if mode == "xload":
                for b in grp:
                    xs, xs_r = nscr()
                    src = x_p[ti * 512 + b * 128: ti * 512 + (b + 1) * 128, :] if prompt else x_s[b * 32:(b + 1) * 32, :]
                    S.dma(XQ, xs[0:L, :], src, writes=[xs_r])
                    xh[b] = (xs, xs_r)
            else:
                zps = {}
                for b in grp:
                    cs = slice(b * L, (b + 1) * L)
                    zp = pair(b % 2)
                    zres = [bankres[2 * (b % 2)], bankres[2 * (b % 2) + 1]]
                    zps[b] = (zp, zres)
                    for c in range(8):
                        tr(zp[0:L, c * 128:(c + 1) * 128], xT[:, c, cs], ident[:, :], [g_xT[c][b], r("ident")], zres, sig=(c == 7))
                for b in grp:
                    zp, zres = zps[b]
                    rl = [r(f"lmv{b}")]
                    S.op("dve", lambda e, b=b, zp=zp: e.bn_stats(out=lst[0:L, b, 0, :], in_=zp[0:L, 0:512]), zres, rl)
                    S.op("dve", lambda e, b=b, zp=zp: e.bn_stats(out=lst[0:L, b, 1, :], in_=zp[0:L, 512:1024]), zres, rl)
                    S.op("dve", lambda e, b=b: e.bn_aggr(out=lmv[0:L, b, 0:2], in_=lst[0:L, b].rearrange("p a b -> p (a b)")), rl, rl)
                    ts("dve", lmv[0:L, b, 2:3], lmv[0:L, b, 1:2], LN_EPS_P, None, ALU.add, None, rl, rl)
                for b in grp:
                    rl = [r(f"lmv{b}")]
                    act(lmv[0:L, b, 3:4], lmv[0:L, b, 2:3], AF.Sqrt, rl, rl)
                for b in grp:
                    rl = [r(f"lmv{b}")]
                    recip(lmv[0:L, b, 4:5], lmv[0:L, b, 3:4], rl, rl)
                    stt(lmv[0:L, b, 5:6], lmv[0:L, b, 0:1], -1.0, lmv[0:L, b, 4:5], ALU.mult, ALU.mult, rl, rl)
                for b in grp:
                    zp, zres = zps[b]
                    xs, xs_r = nscr()
                    act(xs[0:L, :], zp[0:L, :], AF.Identity, zres + [r(f"lmv{b}")], [xs_r], bias=lmv[0:L, b, 5:6], scale=lmv[0:L, b, 4:5])
                    xh[b] = (xs, xs_r)
            if mode == "final":
                for b in grp:
                    xs, xs_r = xh[b]
                    tt("pool", xs[0:L, :], xs[0:L, :], lnfbc[0:L, 0, :], ALU.mult, [xs_r, r("lnfbc")], [xs_r])
                    tt("dve", xs[0:L, :], xs[0:L, :], lnfbc[0:L, 1, :], ALU.add, [xs_r, r("lnfbc")], [xs_r])
                    dst = y_p[ti * 512 + b * 128: ti * 512 + (b + 1) * 128, :] if prompt else y_s[b * 32:(b + 1) * 32, :]
                    S.dma("pool", dst, xs[0:L, :], reads=[xs_r], writes=[r("o_y")])
                continue
            for b in grp:
                xs, xs_r = xh[b]
                for c in range(8):
                    o_ = (c % 4) * 256 + (b % 2) * 128
                    tr(PP[2 + c // 4][:, o_:o_ + L], xs[0:L, c * 128:(c + 1) * 128], ident[0:L, 0:L], [xs_r, r("ident")],
                       [bankres[4 + c // 2]], sig=(c == 7))
            for c in range(8):
                bk = 4 + c // 2
                eng = "dve" if (c // 2) % 2 == 0 else "act"
                if prompt:
                    src = PP[2 + c // 4][:, (c % 4) * 256:(c % 4) * 256 + 256]
                    cs2 = slice(g0 * 128, g0 * 128 + 256)
                    wr_x = gsel(g_xT, [c], grp)
                    if gb is None:
                        cp(eng, xT[:, c, cs2], src, [bankres[bk]], wr_x)
                    else:
                        aff(eng, xT[:, c, cs2], src, gb[0][:, c:c + 1], gb[1][:, c:c + 1], [bankres[bk], r("vecT")], wr_x)
                    for (buf, g_, kA, kB) in targets:
                        aff(eng, buf[:, c, cs2], src, tab(kA, c, 0), tab(kB, c, 0), [bankres[bk], r("TAB")], gsel(g_, [c], grp))
                else:
                    for b in grp:
                        o_ = (c % 4) * 256 + (b % 2) * 128
                        src = PP[2 + c // 4][:, o_:o_ + L]
                        cs = slice(b * L, (b + 1) * L)
                        if gb is None:
                            cp(eng, xT[:, c, cs], src, [bankres[bk]], [g_xT[c][b]])
                        else:
                            aff(eng, xT[:, c, cs], src, gb[0][:, c:c + 1], gb[1][:, c:c + 1], [bankres[bk], r("vecT")], [g_xT[c][b]])
                        for (buf, g_, kA, kB) in targets:
                            aff(eng, buf[:, c, cs], src, tab(kA, c, seqs[b]), tab(kB, c, seqs[b]), [bankres[bk], r("TAB")], [g_[c][b]])

    def resid_evac(L, nb, seqs, kind, same):
        def ev(c, bi):
            if same:
                T = nb * L
                stt(xT[:, c, 0:T], bank(bi)[:, 0:T], tab(kind, c, seqs[0]), xT[:, c, 0:T], ALU.mult, ALU.add,
                    [bankres[bi], r("TAB")] + gsel(g_xT, [c], range(nb)), gsel(g_xT, [c], range(nb)))
            else:
                for b in range(nb):
                    cs = slice(b * L, (b + 1) * L)
                    stt(xT[:, c, cs], bank(bi)[:, cs], tab(kind, c, seqs[b]), xT[:, c, cs], ALU.mult, ALU.add,
                        [bankres[bi], r("TAB"), g_xT[c][b]], [g_xT[c][b]])
        return ev

    def do_tile(kind, ti):
        prompt = (kind == "p")
        nb = 4
        L = 128 if prompt else 32
        T = nb * L
        seqs = [0, 0, 0, 0] if prompt else [1, 2, 3, 4]
        last = prompt and ti == NT - 1
        first = prompt and ti == 0
        BL = range(nb)
        CS = [slice(b * L, (b + 1) * L) for b in BL]

        ln_stage(L, nb, seqs, prompt, ti, "xload", None, [(uT, g_uT, 0, 1)])

        ckpt(2)
        def ev_qk(base):
            def ev(j, bi):
                c = base + j
                cpa(cb[:, c, 0:nb, 3:3 + L], bank(bi)[:, 0:T].rearrange("p (b t) -> p b t", b=nb),
                    [bankres[bi]], gsel(g_cb, [c], BL))
            return ev
        for pj in range(2):
            ws, wr_ = wnext()
            proj_a(ws, wr_, 4, uT, g_uT, T, ev_qk(pj * 4), [0, 1, 2, 3])
        for pj in range(2):
            ws, wr_ = wnext()
            for b in BL:
                bi = rot([4, 5, 6, 7])
                for kc in range(8):
                    mm(bank(bi)[0:L, :], uT[:, kc, CS[b]], ws[:, kc, :], kc == 0, kc == 7,
                       [wr_, g_uT[kc][b]], [bankres[bi]], sig=(kc == 7))
                cpa(vaug[0:L, b, pj * 2:pj * 2 + 2, 0:256], bank(bi)[0:L, :].rearrange("p (h v) -> p h v", h=2),
                    [bankres[bi]], [r_vaug[b]])

        def ev_o(base):
            def ev(j, bi):
                c = base + j
                act(bufB[:, c, 0:T], bank(bi)[:, 0:T], AF.Sigmoid, [bankres[bi]], gsel(g_B, [c], BL))
            return ev
        for pj in range(2):
            ws, wr_ = wnext()
            proj_a(ws, wr_, 4, uT, g_uT, T, ev_o(pj * 4), [0, 1, 2, 3])
        for gi in range(2):
            for kc in range(8):
                mm(bank(6 + gi)[0:4, 0:T], wg[:, kc, gi * 4:gi * 4 + 4], uT[:, kc, 0:T], kc == 0, kc == 7,
                   [r("wg")] + gsel(g_uT, [kc], BL), [bankres[6 + gi]], sig=(kc == 7))
        cp("act", G_ig[:, 0:T], bank(6)[0:4, 0:T], [bankres[6]], [expin_r[1]])
        cp("act", G_fg[:, 0:T], bank(7)[0:4, 0:T], [bankres[7]], [expin_r[0]])
        rg = [r("G")]
        rl_, rn_ = rtmp_r[0], rtmp_r[1]
        if not prompt:
            S.dma(XQ, mcol[:, 0:4], smT[:, :], writes=[r("mcol")])
        elif first:
            S.op("dve", lambda e: e.memset(mcol[:], 0.0), [], [r("mcol")])
        act(G_l[:, 0:T], G_fg[:, 0:T], AF.Exp, [expin_r[0], r("bg")], [rl_], bias=bg[:, 1:2], scale=-1.0)
        act(G_l[:, 0:T], G_l[:, 0:T], AF.Ln, [rl_], [rl_], bias=1.0)
        for b in BL:
            S.op("dve", lambda e, b=b: e.tensor_tensor_scan(out=G_nb[:, CS[b]], data0=ones4[:, 0:L], data1=G_l[:, CS[b]],
                                                            initial=0.0, op0=ALU.mult, op1=ALU.add), [rl_, r("ones4")], [rn_])
        stt(G_g[:, 0:T], G_ig[:, 0:T], bg[:, 0:1], G_nb[:, 0:T], ALU.add, ALU.add, [expin_r[1], r("bg"), rn_], rg)
        S.op("dve", lambda e: e.tensor_reduce(out=G_s[:, 0, 0:nb], in_=G_g[:, 0:T].rearrange("p (b t) -> p b t", b=nb),
                                              axis=AX.X, op=ALU.max), rg, rg)
        for b in BL:
            if prompt:
                mp_ = mcol[:, 0:1] if b == 0 else G_s[:, 4, b - 1:b]
            else:
                mp_ = mcol[:, b:b + 1]
            cp("dve", G_s[:, 3, b:b + 1], mp_, rg + [r("mcol")], rg)
            tt("dve", G_s[:, 1, b:b + 1], G_s[:, 0, b:b + 1], mp_, ALU.max, rg + [r("mcol")], rg)
            tt("dve", G_s[:, 4, b:b + 1], G_s[:, 1, b:b + 1], G_nb[:, (b + 1) * L - 1:(b + 1) * L], ALU.subtract, rg + [rn_], rg)
        ts("dve", G_s[:, 2, 0:nb], G_s[:, 1, 0:nb], -1.0, None, ALU.mult, None, rg, rg)
        tt("dve", G_s[:, 5, 0:nb], G_s[:, 3, 0:nb], G_s[:, 1, 0:nb], ALU.subtract, rg, rg)
        act(G_s[:, 6, 0:nb], G_s[:, 5, 0:nb], AF.Exp, rg, rg)
        ngb = G_s[:, 2, 0:nb].unsqueeze(2).broadcast_to([4, nb, L])
        tt("dve", G_l[:, 0:T].rearrange("p (b t) -> p b t", b=nb), G_g[:, 0:T].rearrange("p (b t) -> p b t", b=nb), ngb,
           ALU.add, rg + [rl_], [rl_])
        act(G_l[:, 0:T], G_l[:, 0:T], AF.Exp, [rl_], [rl_])
        tt("dve", G_g[:, 0:T].rearrange("p (b t) -> p b t", b=nb), G_nb[:, 0:T].rearrange("p (b t) -> p b t", b=nb), ngb,
           ALU.add, rg + [rn_], rg)
        act(G_g[:, 0:T], G_g[:, 0:T], AF.Exp, rg, rg)
        for b in BL:
            ts("dve", G_d4[:, b, :], ident[0:4, 0:4], G_s[:, 6, b:b + 1], None, ALU.mult, None, rg + [r("ident")], rg)
        gb_ = 5
        for b in BL:
            tr(bank(gb_)[0:L, b * 16:b * 16 + 4], G_l[:, CS[b]], ident[0:4, 0:4], [rl_, r("ident")], [bankres[gb_]], sig=False)
            tr(bank(gb_)[0:L, b * 16 + 4:b * 16 + 8], G_g[:, CS[b]], ident[0:4, 0:4], rg + [r("ident")], [bankres[gb_]], sig=False)
            mm(bank(gb_)[:, b * 16 + 8:b * 16 + 12], ones4[:, :], G_d4[:, b, :], True, True, rg + [r("ones4")], [bankres[gb_]],
               sig=(b == nb - 1))
        cp("dve", gsc[:, :, 0:12], bank(gb_)[:, 0:64].rearrange("p (b f) -> p b f", b=4)[:, :, 0:12], [bankres[gb_]], [r("gsc")])
        ts("dve", gsc[:, :, 12:16], gsc[:, :, 8:12], 128.0 ** -0.5, None, ALU.mult, None, [r("gsc")], [r("gsc")])
        if prompt:
            cp("dve", mcol[:, 0:1], G_s[:, 4, nb - 1:nb], rg, [r("mcol")])
        else:
            cp("dve", mout[:, 0:nb], G_s[:, 4, 0:nb], rg, [r("mout")])

        ckpt(3)
        for b in BL:
            if prompt:
                if b == 0:
                    if first:
                        S.op("dve", lambda e: e.memset(cb[:, :, 0, 0:3], 0.0), [], gsel(g_cb, ALLC, [0]))
                    else:
                        cp("dve", cb[:, :, 0, 0:3], cprev[:], [r("cprev")], gsel(g_cb, ALLC, [0]))
                else:
                    cp("dve", cb[:, :, b, 0:3], cb[:, :, b - 1, L:L + 3], gsel(g_cb, ALLC, [b - 1]), gsel(g_cb, ALLC, [b]))
            else:
                S.dma("act", cstage[:].rearrange("p j c -> p (j c)"),
                      dap(sconv, b * 3072, [[1, 128], [128, 24]]), writes=[r("cstage")],
                      allow_slow_non_contiguous=True)
                cp("dve", cb[:, :, b, 0:3], cstage[:].rearrange("p j c -> p c j"), [r("cstage")], gsel(g_cb, ALLC, [b]))
        for c in range(8):
            bi = rot([0, 1, 2, 3])
            for j in range(4):
                mm(bank(bi)[:, 0:T], diagc[:, c, j, :], cb[:, c, 0:nb, j:j + L], j == 0, j == 3,
                   [r("diagc")] + gsel(g_cb, [c], BL), [bankres[bi]], sig=(j == 3))
            act(bufC[:, c, 0:T], bank(bi)[:, 0:T], AF.Silu, [bankres[bi], r("vecT")], gsel(g_C, [c], BL),
                bias=vecT[:, 32 + c:33 + c])
        cp("dve", cprev[:], cb[:, :, nb - 1, L:L + 3], gsel(g_cb, ALLC, [nb - 1]), [r("cprev")])
        if last or not prompt:
            for b in ([nb - 1] if prompt else BL):
                cp("dve", cstage[:].rearrange("p j c -> p c j"), cb[:, :, b, L:L + 3], gsel(g_cb, ALLC, [b]), [r("cstage")])
                dst = dap(o_convp, 0, [[1, 128], [128, 24]]) if prompt else \
                    dap(o_convs, b * 3072, [[1, 128], [128, 24]])
                S.dma("pool", dst, cstage[:].rearrange("p j c -> p (j c)"), reads=[r("cstage")], writes=[r("o_conv")],
                      allow_slow_non_contiguous=True)

        ckpt(4)
        if first:
            S.op("dve", lambda e: e.memset(C_sb[:], 0.0), [], r_Csb)
        for b in BL:
            cs = CS[b]
            if not prompt:
                stg, stg_r = nscr()
                S.dma("act", stg[:, :].rearrange("p (a d) -> p a d", a=8),
                      dap(sC, b * 4 * 256 * 128, [[128, 128], [128 * 128, 8], [1, 128]]), writes=[stg_r])
                for a in range(8):
                    tr(pair(0)[:, a * 128:(a + 1) * 128], stg[:, a * 128:(a + 1) * 128], ident[:, :],
                       [stg_r, r("ident")], [bankres[0], bankres[1]], sig=(a == 7))
                for h in range(4):
                    cp("dve" if h < 2 else "act", C_sb[:, h, 0:256], pair(0)[:, h * 256:(h + 1) * 256], [bankres[h // 2]], [r_Csb[h]])
                S.dma("act", C_sb[:, :, 256:257], dap(sn, b * 512, [[1, 128], [128, 4], [1, 1]]), writes=[r("Cn")],
                      reads=[], allow_slow_non_contiguous=True)
            rCn = [] if prompt else [r("Cn")]

            for h in range(4):
                mm(bank(0)[0:L, h * 128:h * 128 + L], bufC[:, 4 + h, cs], bufC[:, h, cs], True, True,
                   [g_C[4 + h][b], g_C[h][b]], [bankres[0]], sig=(h == 3))
            for h in range(4):
                stt(sTs[0:L, h, 0:L], bank(0)[0:L, h * 128:h * 128 + L], gsc[0:L, b, h:h + 1], trimask[0:L, 0:L],
                    ALU.mult, ALU.mult, [bankres[0], r("gsc"), r("trimask")], [r("sTs")])
            for h in range(4):
                tr(bankb(1)[0:L, h * 128:(h + 1) * 128], bufC[:, 4 + h, cs], identb[:, :], [g_C[4 + h][b], r("identb")],
                   [bankres[1]], sig=(h == 3))
            tt("dve", kw[0:L, :, :], bankb(1)[0:L, 0:512].rearrange("p (h d) -> p h d", h=4),
               gsc[0:L, b, 0:4].unsqueeze(2).broadcast_to([L, 4, 128]), ALU.mult, [bankres[1], r("gsc")], [r("kw")])
            for h in range(4):
                act(Cb[:, h, :], C_sb[:, h, :], AF.Identity, [r_Csb[h], r("gsc")] + rCn, [r_Cb[h]], scale=gsc[:, b, 12 + h:13 + h])
            for h in range(4):
                nbk = 2 + h
                mm(bank(nbk)[0:L, 0:257], sTs[0:L, h, 0:L], vaug[0:L, b, h, :], True, False,
                   [r("sTs"), r_vaug[b]], [bankres[nbk]], sig=False)
                mm(bank(nbk)[0:L, 0:257], bufC[:, h, cs], Cb[:, h, :], False, True,
                   [g_C[h][b], r_Cb[h]], [bankres[nbk]])
            for h in range(4):
                dbk = 6 + (h % 2)
                mm(bank(dbk)[:, 0:257], kw[0:L, h, :], vaug[0:L, b, h, :], True, True, [r("kw"), r_vaug[b]], [bankres[dbk]])
                stt(C_sb[:, h, :], C_sb[:, h, :], gsc[:, b, 8 + h:9 + h], bank(dbk)[:, 0:257], ALU.mult, ALU.add,
                    [r_Csb[h], r("gsc"), bankres[dbk]] + rCn, [r_Csb[h]])
            rh = [r("hsm")]
            for h in range(4):
                nbk = 2 + h
                S.op("dve", lambda e, h=h, nbk=nbk: e.bn_stats(out=st6[0:L, h, :], in_=bank(nbk)[0:L, 0:256]),
                     [bankres[nbk]], [r("st6")])
                S.op("dve", lambda e, h=h: e.bn_aggr(out=mv[0:L, h, :], in_=st6[0:L, h, :]), [r("st6")], [r("mv")])
                cp("dve", hsm[0:L, 0, h:h + 1], bank(nbk)[0:L, 256:257], [bankres[nbk]], rh)
            stt(hsm[0:L, 1, :], hsm[0:L, 0, :], -1.0, hsm[0:L, 0, :], ALU.mult, ALU.max, rh, rh)
            tt("dve", hsm[0:L, 1, :], hsm[0:L, 1, :], gsc[0:L, b, 4:8], ALU.max, rh + [r("gsc")], rh)
            recip(hsm[0:L, 2, :], hsm[0:L, 1, :], rh, rh)
            tt("dve", hsm[0:L, 3, :], hsm[0:L, 2, :], hsm[0:L, 2, :], ALU.mult, rh, rh)
            tt("dve", hsm[0:L, 3, :], hsm[0:L, 3, :], mv[0:L, :, 1], ALU.mult, rh + [r("mv")], rh)
            ts("dve", hsm[0:L, 3, :], hsm[0:L, 3, :], HN_EPS, None, ALU.add, None, rh, rh)
            act(hsm[0:L, 4, :], hsm[0:L, 3, :], AF.Sqrt, rh, rh)
            recip(hsm[0:L, 5, :], hsm[0:L, 4, :], rh, rh)
            tt("dve", hsm[0:L, 6, :], hsm[0:L, 2, :], hsm[0:L, 5, :], ALU.mult, rh, rh)
            stt(hsm[0:L, 7, :], mv[0:L, :, 0], -1.0, hsm[0:L, 6, :], ALU.mult, ALU.mult, rh + [r("mv")], rh)
            for h in range(4):
                nbk = 2 + h
                act(hn[0:L, h, :], bank(nbk)[0:L, 0:256], AF.Identity, [bankres[nbk]] + rh, [r("hn")],
                    bias=hsm[0:L, 7, h:h + 1], scale=hsm[0:L, 6, h:h + 1])
            for j in range(8):
                tr(bankb(1)[:, j * 128:j * 128 + L], hn[0:L, j // 2, (j % 2) * 128:(j % 2) * 128 + 128], identb[0:L, 0:L],
                   [r("hn"), r("identb")], [bankres[1]], sig=(j == 7))
            for j in range(8):
                stt(bufA[:, j, cs], bankb(1)[:, j * 128:j * 128 + L], vecT[:, 104 + j:105 + j], bufB[:, j, cs],
                    ALU.mult, ALU.mult, [bankres[1], r("vecT"), g_B[j][b]], [g_A[j][b]])

            if (last and b == nb - 1) or not prompt:
                for h in range(4):
                    for vh in range(2):
                        a = h * 2 + vh
                        tr(pair(0)[:, a * 128:(a + 1) * 128], C_sb[:, h, vh * 128:(vh + 1) * 128], ident[:, :],
                           [r_Csb[h], r("ident")] + rCn, [bankres[0], bankres[1]], sig=(a == 7))
                stg, stg_r = nscr()
                cp("dve", stg[:, :], pair(0)[:, :], [bankres[0], bankres[1]], [stg_r])
                dC = dap(o_Cp, 0, [[128, 128], [128 * 128, 8], [1, 128]]) if prompt else \
                    dap(o_Cs, b * 4 * 256 * 128, [[128, 128], [128 * 128, 8], [1, 128]])
                S.dma("pool", dC, stg[:, :].rearrange("p (a d) -> p a d", a=8), reads=[stg_r], writes=[r("o_C")])
                dn = dap(o_np, 0, [[1, 128], [128, 4], [1, 1]]) if prompt else dap(o_ns, b * 512, [[1, 128], [128, 4], [1, 1]])
                S.dma("pool", dn, C_sb[:, :, 256:257], reads=r_Csb + rCn, writes=[r("o_n")], allow_slow_non_contiguous=True)
                if prompt:
                    S.dma("pool", o_mp[:, :], mcol[:, 0:1], reads=[r("mcol")], writes=[r("o_m")])
                else:
                    S.dma("pool", o_ms[b], mout[:, b:b + 1], reads=[r("mout")], writes=[r("o_m")])

        ckpt(5)
        same = prompt
        for pj in range(2):
            ws, wr_ = wnext()
            evr = resid_evac(L, nb, seqs, 2, same)
            proj_a(ws, wr_, 4, bufA, g_A, T, (lambda j, bi, pj=pj, evr=evr: evr(pj * 4 + j, bi)), [0, 1, 2, 3])
        ln_stage(L, nb, seqs, prompt, ti, "ln", (gam(0, 0), bet(0, 0)), [(uT, g_uT, 3, 4)])

        def mlp(l, kind_s):
            for pj in range(8):
                ws, wr_ = wnext()

                def ev(j, bi, pj=pj):
                    c = pj * 4 + j
                    rt, rt_r = rot(list(zip(rtmp, rtmp_r)), "rtmp")
                    act(rt[:, 0:T], bank(bi)[:, 0:T], AF.Relu, [bankres[bi]], [rt_r])
                    tt(SQE, hT[:, c, 0:T], rt[:, 0:T], rt[:, 0:T], ALU.mult, [rt_r], [r_hT[c]])
                proj_a(ws, wr_, 4, uT, g_uT, T, ev, [0, 1, 2, 3, 4, 5, 6, 7])
            evr = resid_evac(L, nb, seqs, kind_s, same)
            for nh in range(2):
                bks = [nh * 4 + j for j in range(4)]
                for kg in range(4):
                    ws, wr_ = wnext()
                    for j in range(4):
                        for kc in range(8):
                            mm(bank(bks[j])[:, 0:T], ws[:, kc, j * 128:(j + 1) * 128], hT[:, kg * 8 + kc, 0:T],
                               kg == 0 and kc == 0, kg == 3 and kc == 7, [wr_, r_hT[kg * 8 + kc]], [bankres[bks[j]]],
                               sig=(kc == 7))
                for j in range(4):
                    evr(nh * 4 + j, bks[j])

        ckpt(6)
        mlp(0, 5)
        ckpt(7)
        ln_stage(L, nb, seqs, prompt, ti, "ln", (gam(0, 1), bet(0, 1)), [(uT, g_uT, 6, 7), (bufA, g_A, 8, 9)])

        ckpt(8)
        ws, wr_ = wnext()
        if prompt:
            kcol0 = (ti % 2) * 512
            ringb = [(ti % 2) * 4 + b for b in BL]
        else:
            kcol0 = 0
            ringb = [0, 1, 2, 3]

        def ev_k(j, bi):
            cpa(KTr[:, j, kcol0:kcol0 + T], bank(bi)[:, 0:T], [bankres[bi]],
                [r_KTr[x] for x in (ringb if prompt else [0])])
        proj_a(ws, wr_, 2, bufA, g_A, T, ev_k, [0, 1])
        for b in BL:
            bi = rot([2, 3])
            for kc in range(8):
                mm(bank(bi)[0:L, :], bufA[:, kc, CS[b]], ws[:, kc, :], kc == 0, kc == 7, [wr_, g_A[kc][b]], [bankres[bi]],
                   sig=(kc == 7))
            cpa(Vr[0:L, ringb[b], :, 0:64], bank(bi)[0:L, 256:512].rearrange("p (h d) -> p h d", h=4),
                [bankres[bi]], [r_Vr[ringb[b]]])
            if last or not prompt:
                stg, stg_r = nscr()
                cp("act", stg[0:L, 0:512], bank(bi)[0:L, :], [bankres[bi]], [stg_r])
                if prompt:
                    dk_, dv_ = o_kp[b * 128:(b + 1) * 128, :], o_vp[b * 128:(b + 1) * 128, :]
                else:
                    dk_, dv_ = o_ks[b * 32:(b + 1) * 32, :], o_vs[b * 32:(b + 1) * 32, :]
                S.dma("pool", dk_, stg[0:L, 0:256], reads=[stg_r], writes=[r("o_k")])
                S.dma("pool", dv_, stg[0:L, 256:512], reads=[stg_r], writes=[r("o_v")])

        def ev_q(base):
            def ev(j, bi):
                cpa(bufC[:, base + j, 0:T], bank(bi)[:, 0:T], [bankres[bi]], gsel(g_C, [base + j], BL))
            return ev
        for pj in range(2):
            ws, wr_ = wnext()
            proj_a(ws, wr_, 4, uT, g_uT, T, ev_q(pj * 4), [4, 5, 6, 7])

        ckpt(9)
        def keyblocks_for(b):
            kbs = []
            if prompt:
                Bg = ti * 4 + b
                for j in range(5):
                    KB = Bg - 4 + j
                    if KB < 0:
                        continue
                    pos = KB % 8
                    kd = {0: "mask0", 1: "plain", 2: "plain", 3: "b3", 4: "b4"}[j]
                    kbs.append((
                        (lambda kk, rows, pos=pos: KTr[rows, kk, pos * 128:(pos + 1) * 128]),
                        (lambda kh, pos=pos: Vr[:, pos, kh, :]), 128, kd, [r_KTr[pos], r_Vr[pos]]))
            else:
                rc = r_KTr[4:8] + r_Vr[4:8]
                for j in range(4):
                    kbs.append((
                        (lambda kk, rows, j=j: KTc[rows, kk, j * 128:(j + 1) * 128]),
                        (lambda kh, j=j: Vc[:, j, kh, :]), 128, "b3" if j == 3 else "plain", rc))
                kbs.append((
                    (lambda kk, rows, b=b: KTr[rows, kk, b * 32:(b + 1) * 32]),
                    (lambda kh, b=b: Vr[0:32, b, kh, :]), 32, "b4", [r_KTr[0], r_Vr[b]]))
            return kbs

        def load_cache(b):
            stg, stg_r = nscr()
            S.dma(XQ, stg[:, :].rearrange("p (j f) -> p j f", j=4),
                  dap(ck, b * 512 * 256, [[256, 128], [128 * 256, 4], [1, 256]]), writes=[stg_r])
            for j in range(4):
                for kk in range(2):
                    a = j * 2 + kk
                    tr(pair(0)[:, a * 128:(a + 1) * 128], stg[:, j * 256 + kk * 128:j * 256 + (kk + 1) * 128], ident[:, :],
                       [stg_r, r("ident")], [bankres[0], bankres[1]], sig=(a == 7))
            for kk in range(2):
                cp("dve", KTc[:, kk, :].rearrange("p (j s) -> p j s", j=4),
                   pair(0)[:, :].rearrange("p (j k s) -> p j k s", j=4, k=2)[:, :, kk, :],
                   [bankres[0], bankres[1]], r_KTr[4:8])
            stg2, stg2_r = nscr()
            S.dma(XQ, stg2[:, :].rearrange("p (j f) -> p j f", j=4),
                  dap(cv, b * 512 * 256, [[256, 128], [128 * 256, 4], [1, 256]]), writes=[stg2_r])
            cp("dve", Vc[:, :, :, 0:64], stg2[:, :].rearrange("p (j h d) -> p j h d", j=4, h=4), [stg2_r], r_Vr[4:8])

        uctr = [0]

        def emit_st(b, kh, kbs):
            cs = CS[b]
            Lq = L
            kk, e_ = kh // 2, kh % 2
            rows = slice(e_ * 64, (e_ + 1) * 64)
            uctr[0] += 1
            pts = []
            for (ktf, vf, nk, kd, rds) in kbs:
                sbk = rot([0, 1, 2, 3])
                sps = bank(sbk)[0:nk, 0:4 * Lq]
                mm(sps, ktf(kk, rows), bufC[rows, kk * 4:(kk + 1) * 4, cs], True, True,
                   rds + gsel(g_C, range(kk * 4, kk * 4 + 4), [b]), [bankres[sbk]])
                if kd == "mask0":
                    pt, pt_r = PT0[uctr[0] % 2], PT0_r[uctr[0] % 2]
                    s3 = sps.rearrange("p (g q) -> p g q", g=4)
                    p3 = pt[0:nk, 0:4 * Lq].rearrange("p (g q) -> p g q", g=4)
                    act(p3[:, :, 0:64], s3[:, :, 0:64], AF.Exp, [bankres[sbk]], [pt_r], scale=0.125)
                    act(p3[64:128, :, 64:128], s3[64:128, :, 64:128], AF.Exp, [bankres[sbk]], [pt_r], scale=0.125)
                else:
                    pt, pt_r = rot(list(zip(PT, PT_r)), "PT")
                    if kd == "plain":
                        act(pt[0:nk, 0:4 * Lq], sps, AF.Exp, [bankres[sbk]], [pt_r], scale=0.125)
                    else:
                        bt_ = bias3 if kd == "b3" else bias4
                        ei, ei_r = rot(list(zip(expin, expin_r)), "expin")
                        stt(ei[0:nk, 0:4 * Lq].rearrange("p (g q) -> p g q", g=4),
                            sps.rearrange("p (g q) -> p g q", g=4), 0.125,
                            bt_[0:nk, kh * 4:(kh + 1) * 4, 0:Lq], ALU.mult, ALU.add,
                            [bankres[sbk], r("bias3"), r("bias4")], [ei_r])
                        act(pt[0:nk, 0:4 * Lq], ei[0:nk, 0:4 * Lq], AF.Exp, [ei_r], [pt_r])
                pts.append((pt, pt_r))
            return pts

        def emit_pv(b, kh, kbs, pts):
            Lq = L
            obk = 4 + kh
            nkb = len(kbs)
            for g in range(4):
                for ki, (ktf, vf, nk, kd, rds) in enumerate(kbs):
                    pt, pt_r = pts[ki]
                    mm(bank(obk)[0:Lq, g * 65:(g + 1) * 65], pt[0:nk, g * Lq:(g + 1) * Lq], vf(kh)[0:nk, :],
                       ki == 0, ki == nkb - 1, [pt_r] + rds, [bankres[obk]], sig=(ki == nkb - 1))

        def emit_fin(b):
            cs = CS[b]
            Lq = L
            for kh in range(4):
                obk = 4 + kh
                o3 = bank(obk)[0:Lq, 0:260].rearrange("p (g d) -> p g d", g=4)
                recip(rden[0:Lq, kh * 4:(kh + 1) * 4], o3[:, :, 64], [bankres[obk]], [r("rden")])
                tt("dve", On[0:Lq, kh * 4:(kh + 1) * 4, :], o3[:, :, 0:64],
                   rden[0:Lq, kh * 4:(kh + 1) * 4].unsqueeze(2).broadcast_to([Lq, 4, 64]), ALU.mult,
                   [bankres[obk], r("rden")], [r("On")])
            for c in range(8):
                tr(bankb(0)[:, c * 128:c * 128 + Lq], On[0:Lq, 2 * c:2 * c + 2, :].rearrange("p h d -> p (h d)"),
                   identb[0:Lq, 0:Lq], [r("On"), r("identb")], [bankres[0]], sig=(c == 7))
            cpa(bufB[:, :, cs], bankb(0)[:, 0:1024].rearrange("p (c q) -> p c q", c=8)[:, :, 0:Lq], [bankres[0]],
                gsel(g_B, ALLC, [b]))

        if prompt:
            units = [(b, kh) for b in BL for kh in range(4)]
            kbl = {b: keyblocks_for(b) for b in BL}
            nxt = emit_st(units[0][0], units[0][1], kbl[units[0][0]])
            for ui, (b, kh) in enumerate(units):
                cur = nxt
                if ui + 1 < len(units):
                    b2, kh2 = units[ui + 1]
                    nxt = emit_st(b2, kh2, kbl[b2])
                emit_pv(b, kh, kbl[b], cur)
                if kh == 3:
                    emit_fin(b)
        else:
            for b in BL:
                load_cache(b)
                kbs = keyblocks_for(b)
                for kh in range(4):
                    pts = emit_st(b, kh, kbs)
                    emit_pv(b, kh, kbs, pts)
                emit_fin(b)

        ckpt(10)
        for pj in range(2):
            ws, wr_ = wnext()
            evr = resid_evac(L, nb, seqs, 10, same)
            proj_a(ws, wr_, 4, bufB, g_B, T, (lambda j, bi, pj=pj, evr=evr: evr(pj * 4 + j, bi)), [0, 1, 2, 3])
        ln_stage(L, nb, seqs, prompt, ti, "ln", (gam(1, 0), bet(1, 0)), [(uT, g_uT, 11, 12)])
        mlp(1, 13)
        ln_stage(L, nb, seqs, prompt, ti, "final", None, [])

    try:
        ckpt(1)
        for (kind, ti) in tiles:
            do_tile(kind, ti)
    except _Stop:
        pass

    print('sbuf bytes remaining', nc.sbuf_bytes_remaining, flush=True)
    S.final_drain("sp")
    S.emit(nc)
    st.close()
    return nc


_CACHE = {}


def _consts():
    ident = np.eye(128, dtype=np.float32)
    tri = (np.arange(128)[:, None] <= np.arange(128)[None, :]).astype(np.float32) * np.float32(128.0 ** -0.5)
    m4 = np.zeros((128, 128), np.float32)
    m4[64:, :64] = NEG
    return np.ascontiguousarray(np.stack([ident, tri, m4], axis=1))


def make_in_maps(inp, NT, ncores=8):
    f = lambda a: np.ascontiguousarray(np.asarray(a, dtype=np.float32))
    rel = f(inp["rel_bias_b"])[0]
    ext = np.concatenate([rel, np.repeat(rel[:, 256:257], 128, axis=1)], axis=1)
    rel_rev = np.ascontiguousarray(ext[:, ::-1])
    vec1 = np.concatenate([f(inp["conv_w_a"])[0].reshape(32, 128), f(inp["conv_b_a"])[0].reshape(8, 128),
                           f(inp["ln_g"]).reshape(32, 128), f(inp["ln_b"]).reshape(32, 128),
                           f(inp["mhn_g_a"])[0].reshape(8, 128)], axis=0)
    vec2 = np.concatenate([f(inp["b_ada"]).reshape(96, 128), f(inp["b_ada_kv"]).reshape(16, 128)], axis=0)
    lnf = np.stack([f(inp["ln_g"])[1, 1], f(inp["ln_b"])[1, 1]], axis=0)
    shared = {
        "w_in": f(inp["w_in_a"])[0], "w_out_a": f(inp["w_out_a"])[0],
        "w_up0": f(inp["w_up"])[0], "w_up1": f(inp["w_up"])[1],
        "w_down0": f(inp["w_down"])[0], "w_down1": f(inp["w_down"])[1],
        "w_kv": f(inp["w_kv"]), "w_q": f(inp["w_q_b"])[0], "w_out_b": f(inp["w_out_b"])[0],
        "w_ada": f(inp["w_ada"]), "w_ada_kv": f(inp["w_ada_kv"]),
        "vec1": np.ascontiguousarray(vec1), "vec2": np.ascontiguousarray(vec2),
        "b_if": f(inp["b_if_a"]).reshape(8, 1), "lnf": np.ascontiguousarray(lnf),
        "rel_rev": rel_rev, "cst": _consts(),
    }
    maps = []
    xp, xs = f(inp["x_prompt"]), f(inp["x_sample"])
    for i in range(ncores):
        s0 = 4 * i
        m = dict(shared)
        m["x_p"] = np.ascontiguousarray(xp[i, :NT * 512])
        m["x_s"] = np.ascontiguousarray(xs[s0:s0 + 4].reshape(128, 1024))
        m["c_all"] = np.ascontiguousarray(np.concatenate([f(inp["c_prompt"])[i:i + 1], f(inp["c_sample"])[s0:s0 + 4]], axis=0))
        m["sconv"] = np.ascontiguousarray(f(inp["state_conv"])[0, s0:s0 + 4])
        m["sC"] = np.ascontiguousarray(f(inp["state_C"])[0, s0:s0 + 4])
        m["sn"] = np.ascontiguousarray(f(inp["state_n"])[0, s0:s0 + 4])
        m["smT"] = np.ascontiguousarray(f(inp["state_m"])[0, s0:s0 + 4].T)
        m["ck"] = np.ascontiguousarray(f(inp["cache_k"])[s0:s0 + 4].reshape(4, 512, 256))
        m["cv"] = np.ascontiguousarray(f(inp["cache_v"])[s0:s0 + 4].reshape(4, 512, 256))
        maps.append(m)
    return maps


def assemble(results, NT, ncores=8):
    g = lambda k: np.stack([np.asarray(results[i][k], dtype=np.float32) for i in range(ncores)], axis=0)
    y_p = g("y_p")
    y_s = g("y_s").reshape(ncores * 4, 32, 1024)
    conv_p = g("o_convp")[None]
    C_p = g("o_Cp")[None]
    n_p = g("o_np")[None]
    m_p = g("o_mp").reshape(ncores, 4)[None]
    k_p = g("o_kp").reshape(ncores, 512, 4, 64)
    v_p = g("o_vp").reshape(ncores, 512, 4, 64)
    conv_s = g("o_convs").reshape(ncores * 4, 3, 1024)[None]
    C_s = g("o_Cs").reshape(ncores * 4, 4, 256, 128)[None]
    n_s = g("o_ns").reshape(ncores * 4, 4, 128)[None]
    m_s = g("o_ms").reshape(ncores * 4, 4)[None]
    k_s = g("o_ks").reshape(ncores * 4, 32, 4, 64)
    v_s = g("o_vs").reshape(ncores * 4, 32, 4, 64)
    return (y_p, y_s, conv_p, C_p, n_p, m_p, k_p, v_p, conv_s, C_s, n_s, m_s, k_s, v_s)


def kernel(**inputs):
    NT = 16
    if NT not in _CACHE:
        _CACHE[NT] = build(NT)
    nc = _CACHE[NT]
    in_maps = make_in_maps(inputs, NT)
    res = run_bass_kernel_spmd(nc, in_maps, core_ids=list(range(8)))
    return assemble(res.results, NT)
```

```python
import contextlib
import numpy as np
import concourse.bass as bass
import concourse.mybir as mybir
from concourse.bass_utils import run_bass_kernel_spmd

F32 = mybir.dt.float32
BF16 = mybir.dt.bfloat16
AF = mybir.ActivationFunctionType
ALU = mybir.AluOpType
AX = mybir.AxisListType

ALPHA = 4.0 ** 0.25
LN_EPS_P = 1e-5 / (ALPHA * ALPHA)
HN_EPS = 1e-6
NSLOT = 3
PDEPTH = 2
NEG = -30000.0
import os
XQ = os.environ.get("XQ", "act")
SQE = os.environ.get("SQE", "pool")


class Res:
    __slots__ = ("name", "w", "r", "sem", "excl")

    def __init__(self, name, excl=False):
        self.name = name
        self.w = None
        self.r = []
        self.sem = None
        self.excl = excl


class Sched:
    ENGS = ("pe", "act", "dve", "pool", "sp")

    def __init__(self):
        self.ops = {e: [] for e in self.ENGS}
        self.nsig = {e: 0 for e in self.ENGS}
        self.waited = {e: {} for e in self.ENGS}
        self.semkeys = ["E_" + e for e in self.ENGS]
        self.dma_sems = {}
        self.nres = 0

    def res(self, name=None):
        self.nres += 1
        return Res(name or f"r{self.nres}")

    def _need(self, eng, ev, waits, same_ok):
        if ev is None:
            return
        key, val, weng = ev
        if same_ok and weng == eng:
            return
        if self.waited[eng].get(key, 0) >= val:
            return
        if waits.get(key, 0) < val:
            waits[key] = val

    def _deps(self, eng, reads, writes):
        waits = {}
        for r in reads:
            self._need(eng, r.w, waits, eng == "pe")
            if r.excl:
                for ev in r.r:
                    self._need(eng, ev, waits, True)
        for w in writes:
            self._need(eng, w.w, waits, True)
            for ev in w.r:
                self._need(eng, ev, waits, True)
        for k, v in waits.items():
            self.waited[eng][k] = v
        return list(waits.items())

    def _record(self, ev, reads, writes):
        for r in reads:
            r.r.append(ev)
        for w in writes:
            w.w = ev
            w.r = []

    def op(self, eng, fn, reads=(), writes=(), sig=True):
        waits = self._deps(eng, reads, writes)
        key = "E_" + eng
        if sig:
            self.nsig[eng] += 1
            val = self.nsig[eng]
        else:
            val = self.nsig[eng] + 1
        self._record((key, val, eng), reads, writes)
        self.ops[eng].append((waits, fn, (key, 1) if sig else None))

    def dma(self, q, out, in_, reads=(), writes=(), **kw):
        waits = self._deps(q, reads, writes)
        tgt = writes[0]
        if tgt.sem is None:
            tgt.sem = f"D{len(self.dma_sems)}_{tgt.name}"
            self.semkeys.append(tgt.sem)
            self.dma_sems[tgt.sem] = 0
        self.dma_sems[tgt.sem] += 16
        self._record((tgt.sem, self.dma_sems[tgt.sem], "dma"), reads, writes)
        self.ops[q].append((waits, (lambda e, o=out, i=in_, k=kw: e.dma_start(out=o, in_=i, **k)),
                            (tgt.sem, 16)))

    def barrier(self):
        for e in self.ENGS:
            waits = {}
            for e2 in self.ENGS:
                k = "E_" + e2
                if self.nsig[e2] > 0 and e2 != e and self.waited[e].get(k, 0) < self.nsig[e2]:
                    waits[k] = self.nsig[e2]
            for k, v in self.dma_sems.items():
                if v > 0 and self.waited[e].get(k, 0) < v:
                    waits[k] = v
            for k, v in waits.items():
                self.waited[e][k] = v
            if waits:
                self.ops[e].append((list(waits.items()), None, None))

    def final_drain(self, eng="sp"):
        waits = {}
        for k, v in self.dma_sems.items():
            if v > 0 and self.waited[eng].get(k, 0) < v:
                waits[k] = v
        for e2 in self.ENGS:
            if self.nsig[e2] > 0 and e2 != eng:
                waits["E_" + e2] = self.nsig[e2]
        self.ops[eng].append((list(waits.items()), None, None))

    def emit(self, nc):
        with contextlib.ExitStack() as st:
            sems = {k: st.enter_context(nc.semaphore(k)) for k in self.semkeys}
            block = st.enter_context(nc.Block())

            def replay(engobj, name):
                for waits, fn, sig in self.ops[name]:
                    for k, v in waits:
                        engobj.wait_ge(sems[k], v)
                    if fn is None:
                        continue
                    ins = fn(engobj)
                    if sig is not None:
                        ins.then_inc(sems[sig[0]], sig[1])

            @block.sync
            def _(e):
                replay(e, "sp")

            @block.tensor
            def _(e):
                replay(e, "pe")

            @block.scalar
            def _(e):
                replay(e, "act")

            @block.vector
            def _(e):
                replay(e, "dve")

            @block.gpsimd
            def _(e):
                replay(e, "pool")


def panel_list():
    pl = []
    for j in range(6):
        pl.append(("w_in", 0, j * 512, 3080, None))
    for j in range(2):
        pl.append(("w_out_a", 0, j * 512, 1024, None))
    for j in range(8):
        pl.append(("w_up0", 0, j * 512, 4096, None))
    for nh in range(2):
        for kg in range(4):
            pl.append(("w_down0", kg * 1024, nh * 512, 1024, None))
    pl.append(("w_kv", 0, 0, 512, None))
    for j in range(2):
        pl.append(("w_q", 0, j * 512, 1024, "qperm"))
    for j in range(2):
        pl.append(("w_out_b", 0, j * 512, 1024, None))
    for j in range(8):
        pl.append(("w_up1", 0, j * 512, 4096, None))
    for nh in range(2):
        for kg in range(4):
            pl.append(("w_down1", kg * 1024, nh * 512, 1024, None))
    return pl


NPANEL = 45


class _Stop(Exception):
    pass


def build(NT, do_sample=True, stage=99):
    nc = bass.Bass("TRN2", target_bir_lowering=False)
    S = Sched()
    st = contextlib.ExitStack()
    SEQ = NT * 512

    def din(name, shape, dt=F32):
        return nc.dram_tensor(name, list(shape), dt, kind="ExternalInput").ap()

    def dout(name, shape, dt=F32):
        return nc.dram_tensor(name, list(shape), dt, kind="ExternalOutput").ap()

    def dap(t, offset, dims):
        return bass.AP(tensor=t.tensor, offset=offset, ap=[list(d) for d in dims])

    x_p = din("x_p", [SEQ, 1024])
    x_s = din("x_s", [128, 1024])
    c_all = din("c_all", [5, 1024])
    sconv = din("sconv", [4, 3, 1024])
    sC = din("sC", [4, 4, 256, 128])
    sn = din("sn", [4, 4, 128])
    smT = din("smT", [4, 4])
    ck = din("ck", [4, 512, 256])
    cv = din("cv", [4, 512, 256])
    W = {
        "w_in": din("w_in", [1024, 3080]),
        "w_out_a": din("w_out_a", [1024, 1024]),
        "w_up0": din("w_up0", [1024, 4096]),
        "w_up1": din("w_up1", [1024, 4096]),
        "w_down0": din("w_down0", [4096, 1024]),
        "w_down1": din("w_down1", [4096, 1024]),
        "w_kv": din("w_kv", [1024, 512]),
        "w_q": din("w_q", [1024, 1024]),
        "w_out_b": din("w_out_b", [1024, 1024]),
    }
    w_ada = din("w_ada", [2, 1024, 6144])
    w_ada_kv = din("w_ada_kv", [1024, 2048])
    vec1 = din("vec1", [112, 128])
    vec2 = din("vec2", [112, 128])
    b_if = din("b_if", [8, 1])
    lnf = din("lnf", [2, 1024])
    rel_rev = din("rel_rev", [16, 385])
    cst = din("cst", [128, 3, 128])

    y_p = dout("y_p", [SEQ, 1024])
    y_s = dout("y_s", [128, 1024])
    o_convp = dout("o_convp", [3, 1024])
    o_Cp = dout("o_Cp", [4, 256, 128])
    o_np = dout("o_np", [4, 128])
    o_mp = dout("o_mp", [4, 1])
    o_kp = dout("o_kp", [512, 256])
    o_vp = dout("o_vp", [512, 256])
    o_convs = dout("o_convs", [4, 3, 1024])
    o_Cs = dout("o_Cs", [4, 4, 256, 128])
    o_ns = dout("o_ns", [4, 4, 128])
    o_ms = dout("o_ms", [4, 4, 1])
    o_ks = dout("o_ks", [128, 256])
    o_vs = dout("o_vs", [128, 256])
    wsc = nc.dram_tensor("wsc", [NPANEL, 128, 8, 512], BF16).ap()

    def sb(name, shape, dt=F32):
        return st.enter_context(nc.sbuf_tensor(name, list(shape), dt))

    PP = [st.enter_context(nc.psum_tensor(f"pp{i}", [128, 1024], F32)) for i in range(4)]
    bankres = [Res(f"bank{i}", excl=True) for i in range(8)]

    def bank(i):
        return PP[i // 2][:, (i % 2) * 512:(i % 2) * 512 + 512]

    def bankb(i):
        return PP[i // 2][:, (i % 2) * 512:(i % 2) * 512 + 512].bitcast(BF16)

    def pair(i):
        return PP[i][:, :]

    wslot = [sb(f"wslot{i}", [128, 8, 512], BF16) for i in range(NSLOT)]
    wslot_r = [S.res(f"wslot{i}") for i in range(NSLOT)]
    xT = sb("xT", [128, 8, 512])
    scr = [sb(f"scr{i}", [128, 1024]) for i in range(4)]
    scr_r = [S.res(f"scr{i}") for i in range(4)]
    uT = sb("uT", [128, 8, 512], BF16)
    bufA = sb("bufA", [128, 8, 512], BF16)
    bufB = sb("bufB", [128, 8, 512], BF16)
    bufC = sb("bufC", [128, 8, 512], BF16)
    cb = sb("cb", [128, 8, 4, 131], BF16)
    vaug = sb("vaug", [128, 4, 4, 257], BF16)
    hT = sb("hT", [128, 32, 512], BF16)
    rtmp = [sb(f"rtmp{i}", [128, 512]) for i in range(2)]
    rtmp_r = [S.res(f"rtmp{i}") for i in range(2)]
    C_sb = sb("C_sb", [128, 4, 257])
    Cb = sb("Cb", [128, 4, 257], BF16)
    sTs = sb("sTs", [128, 4, 128], BF16)
    kw = sb("kw", [128, 4, 128], BF16)
    hn = sb("hn", [128, 4, 256], BF16)
    KTr = sb("KTr", [128, 2, 1024], BF16)
    Vr = sb("Vr", [128, 8, 4, 65], BF16)
    KTc = KTr[:, :, 512:1024]
    Vc = Vr[:, 4:8, :, :]
    NPT = 8
    PT = [sb(f"PT{i}", [128, 512], BF16) for i in range(NPT)]
    PT_r = [S.res(f"PT{i}") for i in range(NPT)]
    PT0 = [sb(f"PTm{i}", [128, 512], BF16) for i in range(2)]
    PT0_r = [S.res(f"PTm{i}") for i in range(2)]
    expin = [sb(f"expin{i}", [128, 512]) for i in range(2)]
    expin_r = [S.res(f"expin{i}") for i in range(2)]
    On = sb("On", [128, 16, 64], BF16)
    bias3 = sb("bias3", [128, 16, 128], BF16)
    bias4 = sb("bias4", [128, 16, 128], BF16)
    lnfbc = sb("lnfbc", [128, 2, 1024])
    ident = sb("ident", [128, 128])
    identb = sb("identb", [128, 128], BF16)
    trimask = sb("trimask", [128, 128])
    mask4 = sb("mask4", [128, 128])
    diagc = sb("diagc", [128, 8, 4, 128], BF16)
    vecT = sb("vecT", [128, 112])
    biasT = sb("biasT", [128, 112])
    TAB = sb("TAB", [128, 14, 8, 5])
    cT = sb("cT", [128, 8, 5])
    wg = sb("wg", [128, 8, 8], BF16)
    wg32 = sb("wg32", [128, 8, 8])
    bg = sb("bg", [4, 2])
    bg8 = sb("bg8", [8, 1])
    ones4 = sb("ones4", [4, 128])
    chv = sb("chv", [128, 16])
    cprev = sb("cprev", [128, 8, 3], BF16)
    cstage = sb("cstage", [128, 3, 8])
    mcol = sb("mcol", [4, 8])
    mout = sb("mout", [4, 4])
    G_g = sb("G_g", [4, 512])
    G_s = sb("G_s", [4, 8, 4])
    G_d4 = sb("G_d4", [4, 4, 4])
    gsc = sb("gsc", [128, 4, 16])
    st6 = sb("st6", [128, 4, 6])
    mv = sb("mv", [128, 4, 2])
    hsm = sb("hsm", [128, 8, 4])
    lst = sb("lst", [128, 4, 2, 6])
    lmv = sb("lmv", [128, 4, 8])
    rden = sb("rden", [128, 16])
    epsc = sb("epsc", [128, 1])
    modT = scr[0][:, 0:560].rearrange("p (a s) -> p a s", a=112)
    mod1 = scr[1][:, 0:560].rearrange("p (a s) -> p a s", a=112)
    crow = scr[2][0:5, :]
    rowc = expin[0][0:5, :]
    G_ig = expin[1][0:4, :]
    G_fg = expin[0][0:4, :]
    G_l = rtmp[0][0:4, :]
    G_nb = rtmp[1][0:4, :]

    R = {"modT": scr_r[0], "mod1": scr_r[1], "crow": scr_r[2], "rowc": expin_r[0]}

    def r(name):
        if name not in R:
            R[name] = S.res(name)
        return R[name]

    def grid(name):
        return [[S.res(f"{name}_{c}_{b}") for b in range(4)] for c in range(8)]

    g_xT = grid("xT"); g_uT = grid("uT"); g_A = grid("bufA"); g_B = grid("bufB"); g_C = grid("bufC")
    g_cb = grid("cb")
    r_vaug = [S.res(f"vaug{b}") for b in range(4)]
    r_hT = [S.res(f"hT{c}") for c in range(32)]
    r_Csb = [S.res(f"Csb{h}") for h in range(4)]
    r_Cb = [S.res(f"Cb{h}") for h in range(4)]
    r_KTr = [S.res(f"KTr{i}") for i in range(8)]
    r_Vr = [S.res(f"Vr{i}") for i in range(8)]

    def gsel(g, cs, bs):
        return [g[c][b] for c in cs for b in bs]

    ALLC = list(range(8))

    def mm(out, lhsT, rhs, start, stop, rd, wr, sig=True):
        S.op("pe", lambda e: e.matmul(out, lhsT=lhsT, rhs=rhs, start=start, stop=stop), rd, wr, sig)

    def tr(out, in_, idn, rd, wr, sig=True):
        S.op("pe", lambda e: e.transpose(out, in_, idn), rd, wr, sig)

    def act(out, in_, func, rd, wr, bias=None, scale=None):
        kw_ = {}
        if bias is not None:
            kw_["bias"] = bias
        if scale is not None:
            kw_["scale"] = scale
        S.op("act", lambda e: e.activation(out=out, in_=in_, func=func, **kw_), rd, wr)

    def ts(eng, out, in0, s1, s2, op0, op1, rd, wr):
        if s2 is None:
            S.op(eng, lambda e: e.tensor_scalar(out=out, in0=in0, scalar1=s1, scalar2=None, op0=op0), rd, wr)
        else:
            S.op(eng, lambda e: e.tensor_scalar(out=out, in0=in0, scalar1=s1, scalar2=s2, op0=op0, op1=op1), rd, wr)

    def tt(eng, out, in0, in1, op, rd, wr):
        S.op(eng, lambda e: e.tensor_tensor(out=out, in0=in0, in1=in1, op=op), rd, wr)

    def stt(out, in0, scalar, in1, op0, op1, rd, wr):
        S.op("dve", lambda e: e.scalar_tensor_tensor(out=out, in0=in0, scalar=scalar, in1=in1, op0=op0, op1=op1), rd, wr)

    def cp(eng, out, in_, rd, wr):
        if eng == "act":
            if "i" in os.environ.get("DBG", ""):
                S.op("act", lambda e: e.activation(out=out, in_=in_, func=AF.Identity), rd, wr)
            else:
                S.op("act", lambda e: e.copy(out=out, in_=in_), rd, wr)
        else:
            S.op(eng, lambda e: e.tensor_copy(out=out, in_=in_), rd, wr)

    def recip(out, in_, rd, wr):
        S.op("dve", lambda e: e.reciprocal(out=out, in_=in_), rd, wr)

    affctr = [0]

    def aff(eng, out, in_, A, B, rd, wr):
        if eng == "dve":
            ts("dve", out, in_, A, B, ALU.mult, ALU.add, rd, wr)
        else:
            act(out, in_, AF.Identity, rd, wr, bias=B, scale=A)

    cpctr = [0]

    def cpa(out, in_, rd, wr):
        cpctr[0] += 1
        cp("dve" if cpctr[0] % 2 == 0 else "act", out, in_, rd, wr)

    S.dma("sp", ident[:], cst[:, 0, :], writes=[r("ident")])
    S.dma("sp", trimask[:], cst[:, 1, :], writes=[r("trimask")])
    S.dma("sp", mask4[:], cst[:, 2, :], writes=[r("mask4")])
    cp("dve", identb[:], ident[:], [r("ident")], [r("identb")])
    S.op("dve", lambda e: e.memset(ones4[:], 1.0), [], [r("ones4")])
    S.op("dve", lambda e: e.memset(epsc[:], LN_EPS_P), [], [r("epsc")])
    S.dma("sp", lnfbc[:, 0, :], dap(lnf, 0, [[0, 128], [1, 1024]]), writes=[r("lnfbc")])
    S.dma("sp", lnfbc[:, 1, :], dap(lnf, 1024, [[0, 128], [1, 1024]]), writes=[r("lnfbc")])

    S.dma("sp", scr[0][0:112, 0:128], vec1[:, :], writes=[scr_r[0]])
    S.dma("sp", scr[1][0:112, 0:128], vec2[:, :], writes=[scr_r[1]])
    tr(bank(0)[:, 0:112], scr[0][0:112, 0:128], ident[0:112, 0:112], [scr_r[0], r("ident")], [bankres[0]])
    tr(bank(1)[:, 0:112], scr[1][0:112, 0:128], ident[0:112, 0:112], [scr_r[1], r("ident")], [bankres[1]])
    cp("dve", vecT[:], bank(0)[:, 0:112], [bankres[0]], [r("vecT")])
    cp("dve", biasT[:], bank(1)[:, 0:112], [bankres[1]], [r("biasT")])

    for c in range(8):
        for j in range(4):
            ts("dve", diagc[:, c, j, :], ident[:], vecT[:, j * 8 + c:j * 8 + c + 1], None, ALU.mult, None,
               [r("ident"), r("vecT")], [r("diagc")])

    S.dma("sp", wg32[:], dap(W["w_in"], 3072, [[3080, 128], [128 * 3080, 8], [1, 8]]), writes=[r("wg32")])
    cp("dve", wg[:], wg32[:], [r("wg32")], [r("wg")])
    S.dma("sp", bg[:, 0:1], b_if[0:4, :], writes=[r("bg")])
    S.dma("sp", bg[:, 1:2], b_if[4:8, :], writes=[r("bg")])
    ts("dve", bg[:, 1:2], bg[:, 1:2], -1.0, None, ALU.mult, None, [r("bg")], [r("bg")])

    S.dma("sp", crow[:], c_all[:, :], writes=[r("crow")])
    act(crow[:], crow[:], AF.Silu, [r("crow")], [r("crow")])
    for kc in range(8):
        tr(bank(2)[:, kc * 8:kc * 8 + 5], crow[:, kc * 128:(kc + 1) * 128], ident[0:5, 0:5],
           [r("crow"), r("ident")], [bankres[2]])
    cp("dve", cT[:], bank(2)[:, 0:64].rearrange("p (k s) -> p k s", k=8)[:, :, 0:5], [bankres[2]], [r("cT")])

    hT32 = hT[:].rearrange("p a b -> p (a b)").bitcast(F32)
    stg32 = [hT32[:, i * 4096:(i + 1) * 4096].rearrange("p (k n) -> p k n", k=8) for i in range(2)]
    stg32_r = [S.res("stg32_0"), S.res("stg32_1")]
    npan = 0
    ada_srcs = []
    for l in range(2):
        for j in range(12):
            ada_srcs.append((w_ada, l * 1024 * 6144 + j * 512, 6144))
    for j in range(4):
        ada_srcs.append((w_ada_kv, j * 512, 2048))
    for pi, (wt, off, ncols) in enumerate(ada_srcs):
        sl = pi % 2
        S.dma("sp", stg32[sl], dap(wt, off, [[ncols, 128], [128 * ncols, 8], [1, 512]]), writes=[stg32_r[sl]])
        pb = 3 + (pi % 2)
        for kc in range(8):
            mm(bank(pb)[0:5, :], cT[:, kc, :], stg32[sl][:, kc, :], kc == 0, kc == 7,
               [r("cT"), stg32_r[sl]], [bankres[pb]], sig=(kc == 7))
        cp("act", rowc[:], bank(pb)[0:5, :], [bankres[pb]], [r("rowc")])
        tb = 5 + (pi % 2)
        for q in range(4):
            tr(bank(tb)[:, q * 8:q * 8 + 5], rowc[:, q * 128:(q + 1) * 128], ident[0:5, 0:5],
               [r("rowc"), r("ident")], [bankres[tb]])
        cp("dve", modT[:, pi * 4:pi * 4 + 4, :], bank(tb)[:, 0:32].rearrange("p (q s) -> p q s", q=4)[:, :, 0:5],
           [bankres[tb]], [r("modT")])
    tt("dve", modT[:], modT[:], biasT[:].unsqueeze(2).broadcast_to([128, 112, 5]), ALU.add,
       [r("modT"), r("biasT")], [r("modT")])
    ts("dve", mod1[:], modT[:], 1.0, None, ALU.add, None, [r("modT")], [r("mod1")])

    def mchunk(l, j):
        return l * 48 + j * 8

    def gam(l, i):
        o = 40 + (l * 2 + i) * 8
        return vecT[:, o:o + 8]

    def bet(l, i):
        o = 72 + (l * 2 + i) * 8
        return vecT[:, o:o + 8]

    def bc5(a):
        return a.unsqueeze(2).broadcast_to([128, 8, 5])

    rT = [r("modT"), r("mod1"), r("vecT")]
    cp("dve", TAB[:, 0], mod1[:, mchunk(0, 1):mchunk(0, 1) + 8, :], rT, [r("TAB")])
    cp("dve", TAB[:, 1], modT[:, mchunk(0, 0):mchunk(0, 0) + 8, :], rT, [r("TAB")])
    for kind, l, j in ((2, 0, 2), (5, 0, 5), (10, 1, 2), (13, 1, 5)):
        ts("dve", TAB[:, kind], mod1[:, mchunk(l, j):mchunk(l, j) + 8, :], 1.0 / ALPHA, None, ALU.mult, None, rT, [r("TAB")])

    def mkAB(kA, kB, g_, b_, sc_off, sh_off):
        tt("dve", TAB[:, kA], mod1[:, sc_off:sc_off + 8, :], bc5(g_), ALU.mult, rT + [r("TAB")], [r("TAB")])
        tt("dve", TAB[:, kB], mod1[:, sc_off:sc_off + 8, :], bc5(b_), ALU.mult, rT + [r("TAB")], [r("TAB")])
        tt("dve", TAB[:, kB], TAB[:, kB], modT[:, sh_off:sh_off + 8, :], ALU.add, rT + [r("TAB")], [r("TAB")])

    mkAB(3, 4, gam(0, 0), bet(0, 0), mchunk(0, 4), mchunk(0, 3))
    mkAB(6, 7, gam(0, 1), bet(0, 1), mchunk(1, 1), mchunk(1, 0))
    mkAB(8, 9, gam(0, 1), bet(0, 1), 96 + 8, 96)
    mkAB(11, 12, gam(1, 0), bet(1, 0), mchunk(1, 4), mchunk(1, 3))

    def tab(kind, c, seq):
        return TAB[:, kind, c, seq:seq + 1]

    bst = [hT32[:, 0:2048].rearrange("p (h q) -> p h q", h=16), hT32[:, 2048:4096].rearrange("p (h q) -> p h q", h=16)]
    S.barrier()
    S.dma("sp", bst[0], dap(rel_rev, 1, [[1, 128], [385, 16], [1, 128]]), writes=[stg32_r[0]])
    S.dma("sp", bst[1], dap(rel_rev, 129, [[1, 128], [385, 16], [1, 128]]), writes=[stg32_r[0]])
    S.dma("sp", chv[:], dap(rel_rev, 128, [[0, 128], [385, 16]]), writes=[r("chv")], allow_slow_non_contiguous=True)

    def flipped(a):
        return bass.AP(tensor=a.tensor, offset=a.offset + 127, ap=[list(a.ap[0]), list(a.ap[1]), [-1, 128]])

    chb = chv[:].unsqueeze(2).broadcast_to([128, 16, 128])
    tt("dve", bias3[:], flipped(bst[0]), chb, ALU.subtract, [stg32_r[0], r("chv")], [r("bias3")])
    tt("dve", bst[1], bst[1], flipped(chb) if False else chb, ALU.subtract, [stg32_r[0], r("chv")], [stg32_r[0]])
    m4b = bass.AP(tensor=mask4[:].tensor, offset=mask4[:].offset, ap=[list(mask4[:].ap[0]), [0, 16], [1, 128]])
    tt("dve", bias4[:], flipped(bst[1]), m4b, ALU.add, [stg32_r[0], r("mask4")], [r("bias4")])
    S.barrier()

    def ckpt(n):
        if stage <= n:
            raise _Stop()

    plist = panel_list()
    r_wsc = S.res("wsc")
    casteng = ["dve", "act", "pool"]
    for pi, (wn, row0, col0, ncols, kind) in enumerate(plist):
        sl = pi % 2
        ws = pi % NSLOT
        S.dma("sp", stg32[sl], dap(W[wn], row0 * ncols + col0, [[ncols, 128], [128 * ncols, 8], [1, 512]]),
              writes=[stg32_r[sl]])
        eng = casteng[pi % 3]
        if kind == "qperm":
            for k in range(2):
                o_ = wslot[ws][:].rearrange("p c (g k d) -> p c g k d", g=4, k=2)[:, :, :, k, :]
                i_ = stg32[sl].rearrange("p c (k g d) -> p c k g d", k=2, g=4)[:, :, k, :, :]
                cp("dve", o_, i_, [stg32_r[sl]], [wslot_r[ws]])
        else:
            cp(eng, wslot[ws][:], stg32[sl], [stg32_r[sl]], [wslot_r[ws]])
        S.dma("pool", wsc[pi], wslot[ws][:], reads=[wslot_r[ws]], writes=[r_wsc])
    S.barrier()

    S.op("dve", lambda e: e.memset(vaug[:], 1.0), [], r_vaug)
    S.op("dve", lambda e: e.memset(Vr[:], 1.0), [], r_Vr)
    S.op("dve", lambda e: e.memset(PT0[0][:], 0.0), [], [PT0_r[0]])
    S.op("dve", lambda e: e.memset(PT0[1][:], 0.0), [], [PT0_r[1]])
    S.op("dve", lambda e: e.memset(gsc[:], 0.0), [], [r("gsc")])

    tiles = []
    if do_sample:
        tiles.append(("s", 0))
    for ti in range(NT):
        tiles.append(("p", ti))
    uses = [pi for _ in tiles for pi in range(NPANEL)]
    wstate = {"next": 0, "cur": -1}

    def wnext():
        wstate["cur"] += 1
        i = wstate["cur"]
        while wstate["next"] <= min(i + PDEPTH, len(uses) - 1):
            n = wstate["next"]
            S.dma("sp", wslot[n % NSLOT][:], wsc[uses[n]], reads=[r_wsc], writes=[wslot_r[n % NSLOT]])
            wstate["next"] += 1
        return wslot[i % NSLOT], wslot_r[i % NSLOT]

    bctr = {}

    def rot(lst, key=None):
        key = key or "k%d_%s" % (len(lst), str(lst[0])[:24])
        bctr[key] = bctr.get(key, -1) + 1
        return lst[bctr[key] % len(lst)]

    scrctr = [0]

    def nscr():
        scrctr[0] += 1
        i = scrctr[0] % 4
        return scr[i], scr_r[i]

    def proj_a(ws, wr_, nchunk, rhs_buf, g_rhs, T, evac, banks):
        for j in range(nchunk):
            bi = rot(banks)
            for kc in range(8):
                mm(bank(bi)[:, 0:T], ws[:, kc, j * 128:(j + 1) * 128], rhs_buf[:, kc, 0:T], kc == 0, kc == 7,
                   [wr_] + gsel(g_rhs, [kc], range(4)), [bankres[bi]], sig=(kc == 7))
            evac(j, bi)

    def ln_block(L, b, cs, eps, targets, final_dst=None, nbk=4):
        pz = (b % 2)
        pbk = 2 + (b % 2)
        zp = pair(pz)
        zres = [bankres[2 * pz], bankres[2 * pz + 1]]
        for c in range(8):
            tr(zp[0:L, c * 128:(c + 1) * 128], xT[:, c, cs], ident[:, :], [g_xT[c][b], r("ident")], zres, sig=(c == 7))
        S.op("dve", lambda e: e.bn_stats(out=lst[0:L, 0, :], in_=zp[0:L, 0:512]), zres, [r("lst")])
        S.op("dve", lambda e: e.bn_stats(out=lst[0:L, 1, :], in_=zp[0:L, 512:1024]), zres, [r("lst")])
        S.op("dve", lambda e: e.bn_aggr(out=lmv[0:L, 0:2], in_=lst[0:L].rearrange("p a b -> p (a b)")), [r("lst")], [r("lmv")])
        ts("dve", lmv[0:L, 2:3], lmv[0:L, 1:2], eps, None, ALU.add, None, [r("lmv")], [r("lmv")])
        act(lmv[0:L, 3:4], lmv[0:L, 2:3], AF.Sqrt, [r("lmv")], [r("lmv")])
        recip(lmv[0:L, 4:5], lmv[0:L, 3:4], [r("lmv")], [r("lmv")])
        stt(lmv[0:L, 5:6], lmv[0:L, 0:1], -1.0, lmv[0:L, 4:5], ALU.mult, ALU.mult, [r("lmv")], [r("lmv")])
        xh, xh_r = nscr()
        act(xh[0:L, :], zp[0:L, :], AF.Identity, zres + [r("lmv")], [xh_r], bias=lmv[0:L, 5:6], scale=lmv[0:L, 4:5])
        return xh, xh_r

    def back_block(L, b, cs, xh, xh_r, gb, targets):
        pbk = 2 + (b % 2)
        bp = pair(pbk)
        bres = [bankres[2 * pbk], bankres[2 * pbk + 1]]
        for c in range(8):
            tr(bp[:, c * 128:c * 128 + L], xh[0:L, c * 128:(c + 1) * 128], ident[0:L, 0:L], [xh_r, r("ident")], bres,
               sig=(c == 7))
        for c in range(8):
            src = bp[:, c * 128:c * 128 + L]
            eng = "dve" if c < 4 else "act"
            br1 = [bres[c // 4]]
            if gb is None:
                cp(eng, xT[:, c, cs], src, br1, [g_xT[c][b]])
            else:
                aff(eng, xT[:, c, cs], src, gb[0][:, c:c + 1], gb[1][:, c:c + 1], br1 + [r("vecT")], [g_xT[c][b]])
            for (buf, g_, kA, kB, seq) in targets:
                aff(eng, buf[:, c, cs], src, tab(kA, c, seq), tab(kB, c, seq), br1 + [r("TAB")], [g_[c][b]])

    def ln_stage(L, nb, seqs, prompt, ti, mode, gb, targets):
        BLK = [0, 1, 2, 3]
        xh = {}
        if mode == "xload":
            for b in BLK:
                xs, xs_r = nscr()
                src = x_p[ti * 512 + b * 128: ti * 512 + (b + 1) * 128, :] if prompt else x_s[b * 32:(b + 1) * 32, :]
                S.dma(XQ, xs[0:L, :], src, writes=[xs_r])
                xh[b] = (xs, xs_r)
        else:
            zps = {}
            for b in BLK:
                cs = slice(b * L, (b + 1) * L)
                zp = pair(b)
                zres = [bankres[2 * b], bankres[2 * b + 1]]
                zps[b] = (zp, zres)
                for c in range(8):
                    tr(zp[0:L, c * 128:(c + 1) * 128], xT[:, c, cs], ident[:, :], [g_xT[c][b], r("ident")], zres, sig=(c == 7))
            for b in BLK:
                zp, zres = zps[b]
                rl = [r(f"lmv{b}")]
                S.op("dve", lambda e, b=b, zp=zp: e.bn_stats(out=lst[0:L, b, 0, :], in_=zp[0:L, 0:512]), zres, rl)
                S.op("dve", lambda e, b=b, zp=zp: e.bn_stats(out=lst[0:L, b, 1, :], in_=zp[0:L, 512:1024]), zres, rl)
                S.op("dve", lambda e, b=b: e.bn_aggr(out=lmv[0:L, b, 0:2], in_=lst[0:L, b].rearrange("p a b -> p (a b)")), rl, rl)
            for b in BLK:
                rl = [r(f"lmv{b}")]
                act(lmv[0:L, b, 3:4], lmv[0:L, b, 1:2], AF.Sqrt, rl + [r("epsc")], rl, bias=epsc[0:L, 0:1])
            for b in BLK:
                rl = [r(f"lmv{b}")]
                recip(lmv[0:L, b, 4:5], lmv[0:L, b, 3:4], rl, rl)
                stt(lmv[0:L, b, 5:6], lmv[0:L, b, 0:1], -1.0, lmv[0:L, b, 4:5], ALU.mult, ALU.mult, rl, rl)
            for b in BLK:
                zp, zres = zps[b]
                xs, xs_r = nscr()
                act(xs[0:L, :], zp[0:L, :], AF.Identity, zres + [r(f"lmv{b}")], [xs_r], bias=lmv[0:L, b, 5:6], scale=lmv[0:L, b, 4:5])
                xh[b] = (xs, xs_r)
        if mode == "final":
            for b in BLK:
                xs, xs_r = xh[b]
                tt("pool", xs[0:L, :], xs[0:L, :], lnfbc[0:L, 0, :], ALU.mult, [xs_r, r("lnfbc")], [xs_r])
                tt("dve", xs[0:L, :], xs[0:L, :], lnfbc[0:L, 1, :], ALU.add, [xs_r, r("lnfbc")], [xs_r])
                dst = y_p[ti * 512 + b * 128: ti * 512 + (b + 1) * 128, :] if prompt else y_s[b * 32:(b + 1) * 32, :]
                S.dma("pool", dst, xs[0:L, :], reads=[xs_r], writes=[r("o_y")])
            return
        for b in BLK:
            xs, xs_r = xh[b]
            g = b // 2
            for c in range(8):
                o_ = (c % 4) * 256 + (b % 2) * 128
                tr(PP[2 * g + c // 4][:, o_:o_ + L], xs[0:L, c * 128:(c + 1) * 128], ident[0:L, 0:L], [xs_r, r("ident")],
                   [bankres[4 * g + c // 2]], sig=(c == 7))
        for g in range(2):
            grp = [2 * g, 2 * g + 1]
            for c in range(8):
                bk = 4 * g + c // 2
                eng = "dve" if (c // 2) % 2 == 0 else "act"
                if prompt:
                    src = PP[2 * g + c // 4][:, (c % 4) * 256:(c % 4) * 256 + 256]
                    cs2 = slice(g * 256, g * 256 + 256)
                    wr_x = gsel(g_xT, [c], grp)
                    if gb is None:
                        cp(eng, xT[:, c, cs2], src, [bankres[bk]], wr_x)
                    else:
                        aff(eng, xT[:, c, cs2], src, gb[0][:, c:c + 1], gb[1][:, c:c + 1], [bankres[bk], r("vecT")], wr_x)
                    for (buf, g_, kA, kB) in targets:
                        aff(eng, buf[:, c, cs2], src, tab(kA, c, 0), tab(kB, c, 0), [bankres[bk], r("TAB")], gsel(g_, [c], grp))
                else:
                    for b in grp:
                        o_ = (c % 4) * 256 + (b % 2) * 128
                        src = PP[2 * g + c // 4][:, o_:o_ + L]
                        cs = slice(b * L, (b + 1) * L)
                        if gb is None:
                            cp(eng, xT[:, c, cs], src, [bankres[bk]], [g_xT[c][b]])
                        else:
                            aff(eng, xT[:, c, cs], src, gb[0][:, c:c + 1], gb[1][:, c:c + 1], [bankres[bk], r("vecT")], [g_xT[c][b]])
                        for (buf, g_, kA, kB) in targets:
                            aff(eng, buf[:, c, cs], src, tab(kA, c, seqs[b]), tab(kB, c, seqs[b]), [bankres[bk], r("TAB")], [g_[c][b]])

    def resid_evac(L, nb, seqs, kind, same):
        def ev(c, bi):
            if same:
                T = nb * L
                stt(xT[:, c, 0:T], bank(bi)[:, 0:T], tab(kind, c, seqs[0]), xT[:, c, 0:T], ALU.mult, ALU.add,
                    [bankres[bi], r("TAB")] + gsel(g_xT, [c], range(nb)), gsel(g_xT, [c], range(nb)))
            else:
                for b in range(nb):
                    cs = slice(b * L, (b + 1) * L)
                    stt(xT[:, c, cs], bank(bi)[:, cs], tab(kind, c, seqs[b]), xT[:, c, cs], ALU.mult, ALU.add,
                        [bankres[bi], r("TAB"), g_xT[c][b]], [g_xT[c][b]])
        return ev

    def do_tile(kind, ti):
        prompt = (kind == "p")
        nb = 4
        L = 128 if prompt else 32
        T = nb * L
        seqs = [0, 0, 0, 0] if prompt else [1, 2, 3, 4]
        last = prompt and ti == NT - 1
        first = prompt and ti == 0
        BL = range(nb)
        CS = [slice(b * L, (b + 1) * L) for b in BL]

        ln_stage(L, nb, seqs, prompt, ti, "xload", None, [(uT, g_uT, 0, 1)])

        ckpt(2)
        def ev_qk(base):
            def ev(j, bi):
                c = base + j
                cpa(cb[:, c, 0:nb, 3:3 + L], bank(bi)[:, 0:T].rearrange("p (b t) -> p b t", b=nb),
                    [bankres[bi]], gsel(g_cb, [c], BL))
            return ev
        for pj in range(2):
            ws, wr_ = wnext()
            proj_a(ws, wr_, 4, uT, g_uT, T, ev_qk(pj * 4), [0, 1, 2, 3])
        for pj in range(2):
            ws, wr_ = wnext()
            for b in BL:
                bi = rot([4, 5, 6, 7])
                for kc in range(8):
                    mm(bank(bi)[0:L, :], uT[:, kc, CS[b]], ws[:, kc, :], kc == 0, kc == 7,
                       [wr_, g_uT[kc][b]], [bankres[bi]], sig=(kc == 7))
                cpa(vaug[0:L, b, pj * 2:pj * 2 + 2, 0:256], bank(bi)[0:L, :].rearrange("p (h v) -> p h v", h=2),
                    [bankres[bi]], [r_vaug[b]])

        def ev_o(base):
            def ev(j, bi):
                c = base + j
                act(bufB[:, c, 0:T], bank(bi)[:, 0:T], AF.Sigmoid, [bankres[bi]], gsel(g_B, [c], BL))
            return ev
        for pj in range(2):
            ws, wr_ = wnext()
            proj_a(ws, wr_, 4, uT, g_uT, T, ev_o(pj * 4), [0, 1, 2, 3])
        for gi in range(2):
            for kc in range(8):
                mm(bank(6 + gi)[0:4, 0:T], wg[:, kc, gi * 4:gi * 4 + 4], uT[:, kc, 0:T], kc == 0, kc == 7,
                   [r("wg")] + gsel(g_uT, [kc], BL), [bankres[6 + gi]], sig=(kc == 7))
        cp("act", G_ig[:, 0:T], bank(6)[0:4, 0:T], [bankres[6]], [expin_r[1]])
        cp("act", G_fg[:, 0:T], bank(7)[0:4, 0:T], [bankres[7]], [expin_r[0]])
        rg = [r("G")]
        rl_, rn_ = rtmp_r[0], rtmp_r[1]
        if not prompt:
            S.dma(XQ, mcol[:, 0:4], smT[:, :], writes=[r("mcol")])
        elif first:
            S.op("dve", lambda e: e.memset(mcol[:], 0.0), [], [r("mcol")])
        act(G_l[:, 0:T], G_fg[:, 0:T], AF.Exp, [expin_r[0], r("bg")], [rl_], bias=bg[:, 1:2], scale=-1.0)
        act(G_l[:, 0:T], G_l[:, 0:T], AF.Ln, [rl_], [rl_], bias=1.0)
        for b in BL:
            S.op("dve", lambda e, b=b: e.tensor_tensor_scan(out=G_nb[:, CS[b]], data0=ones4[:, 0:L], data1=G_l[:, CS[b]],
                                                            initial=0.0, op0=ALU.mult, op1=ALU.add), [rl_, r("ones4")], [rn_])
        stt(G_g[:, 0:T], G_ig[:, 0:T], bg[:, 0:1], G_nb[:, 0:T], ALU.add, ALU.add, [expin_r[1], r("bg"), rn_], rg)
        S.op("dve", lambda e: e.tensor_reduce(out=G_s[:, 0, 0:nb], in_=G_g[:, 0:T].rearrange("p (b t) -> p b t", b=nb),
                                              axis=AX.X, op=ALU.max), rg, rg)
        for b in BL:
            if prompt:
                mp_ = mcol[:, 0:1] if b == 0 else G_s[:, 4, b - 1:b]
            else:
                mp_ = mcol[:, b:b + 1]
            cp("dve", G_s[:, 3, b:b + 1], mp_, rg + [r("mcol")], rg)
            tt("dve", G_s[:, 1, b:b + 1], G_s[:, 0, b:b + 1], mp_, ALU.max, rg + [r("mcol")], rg)
            tt("dve", G_s[:, 4, b:b + 1], G_s[:, 1, b:b + 1], G_nb[:, (b + 1) * L - 1:(b + 1) * L], ALU.subtract, rg + [rn_], rg)
        ts("dve", G_s[:, 2, 0:nb], G_s[:, 1, 0:nb], -1.0, None, ALU.mult, None, rg, rg)
        tt("dve", G_s[:, 5, 0:nb], G_s[:, 3, 0:nb], G_s[:, 1, 0:nb], ALU.subtract, rg, rg)
        act(G_s[:, 6, 0:nb], G_s[:, 5, 0:nb], AF.Exp, rg, rg)
        ngb = G_s[:, 2, 0:nb].unsqueeze(2).broadcast_to([4, nb, L])
        tt("dve", G_l[:, 0:T].rearrange("p (b t) -> p b t", b=nb), G_g[:, 0:T].rearrange("p (b t) -> p b t", b=nb), ngb,
           ALU.add, rg + [rl_], [rl_])
        act(G_l[:, 0:T], G_l[:, 0:T], AF.Exp, [rl_], [rl_])
        tt("dve", G_g[:, 0:T].rearrange("p (b t) -> p b t", b=nb), G_nb[:, 0:T].rearrange("p (b t) -> p b t", b=nb), ngb,
           ALU.add, rg + [rn_], rg)
        act(G_g[:, 0:T], G_g[:, 0:T], AF.Exp, rg, rg)
        for b in BL:
            ts("dve", G_d4[:, b, :], ident[0:4, 0:4], G_s[:, 6, b:b + 1], None, ALU.mult, None, rg + [r("ident")], rg)
        gb_ = 5
        for b in BL:
            tr(bank(gb_)[0:L, b * 16:b * 16 + 4], G_l[:, CS[b]], ident[0:4, 0:4], [rl_, r("ident")], [bankres[gb_]], sig=False)
            tr(bank(gb_)[0:L, b * 16 + 4:b * 16 + 8], G_g[:, CS[b]], ident[0:4, 0:4], rg + [r("ident")], [bankres[gb_]], sig=False)
            mm(bank(gb_)[:, b * 16 + 8:b * 16 + 12], ones4[:, :], G_d4[:, b, :], True, True, rg + [r("ones4")], [bankres[gb_]],
               sig=(b == nb - 1))
        cp("dve", gsc[:, :, 0:12], bank(gb_)[:, 0:64].rearrange("p (b f) -> p b f", b=4)[:, :, 0:12], [bankres[gb_]], [r("gsc")])
        ts("dve", gsc[:, :, 12:16], gsc[:, :, 8:12], 128.0 ** -0.5, None, ALU.mult, None, [r("gsc")], [r("gsc")])
        if prompt:
            cp("dve", mcol[:, 0:1], G_s[:, 4, nb - 1:nb], rg, [r("mcol")])
        else:
            cp("dve", mout[:, 0:nb], G_s[:, 4, 0:nb], rg, [r("mout")])

        ckpt(3)
        for b in BL:
            if prompt:
                if b == 0:
                    if first:
                        S.op("dve", lambda e: e.memset(cb[:, :, 0, 0:3], 0.0), [], gsel(g_cb, ALLC, [0]))
                    else:
                        cp("dve", cb[:, :, 0, 0:3], cprev[:], [r("cprev")], gsel(g_cb, ALLC, [0]))
                else:
                    cp("dve", cb[:, :, b, 0:3], cb[:, :, b - 1, L:L + 3], gsel(g_cb, ALLC, [b - 1]), gsel(g_cb, ALLC, [b]))
            else:
                S.dma("act", cstage[:].rearrange("p j c -> p (j c)"),
                      dap(sconv, b * 3072, [[1, 128], [128, 24]]), writes=[r("cstage")],
                      allow_slow_non_contiguous=True)
                cp("dve", cb[:, :, b, 0:3], cstage[:].rearrange("p j c -> p c j"), [r("cstage")], gsel(g_cb, ALLC, [b]))
        for c in range(8):
            bi = rot([0, 1, 2, 3])
            for j in range(4):
                mm(bank(bi)[:, 0:T], diagc[:, c, j, :], cb[:, c, 0:nb, j:j + L], j == 0, j == 3,
                   [r("diagc")] + gsel(g_cb, [c], BL), [bankres[bi]], sig=(j == 3))
            act(bufC[:, c, 0:T], bank(bi)[:, 0:T], AF.Silu, [bankres[bi], r("vecT")], gsel(g_C, [c], BL),
                bias=vecT[:, 32 + c:33 + c])
        cp("dve", cprev[:], cb[:, :, nb - 1, L:L + 3], gsel(g_cb, ALLC, [nb - 1]), [r("cprev")])
        if last or not prompt:
            for b in ([nb - 1] if prompt else BL):
                cp("dve", cstage[:].rearrange("p j c -> p c j"), cb[:, :, b, L:L + 3], gsel(g_cb, ALLC, [b]), [r("cstage")])
                dst = dap(o_convp, 0, [[1, 128], [128, 24]]) if prompt else \
                    dap(o_convs, b * 3072, [[1, 128], [128, 24]])
                S.dma("pool", dst, cstage[:].rearrange("p j c -> p (j c)"), reads=[r("cstage")], writes=[r("o_conv")],
                      allow_slow_non_contiguous=True)

        ckpt(4)
        if first:
            S.op("dve", lambda e: e.memset(C_sb[:], 0.0), [], r_Csb)
        for b in BL:
            cs = CS[b]
            if not prompt:
                stg, stg_r = nscr()
                S.dma("act", stg[:, :].rearrange("p (a d) -> p a d", a=8),
                      dap(sC, b * 4 * 256 * 128, [[128, 128], [128 * 128, 8], [1, 128]]), writes=[stg_r])
                for a in range(8):
                    tr(pair(0)[:, a * 128:(a + 1) * 128], stg[:, a * 128:(a + 1) * 128], ident[:, :],
                       [stg_r, r("ident")], [bankres[0], bankres[1]], sig=(a == 7))
                for h in range(4):
                    cp("dve" if h < 2 else "act", C_sb[:, h, 0:256], pair(0)[:, h * 256:(h + 1) * 256], [bankres[h // 2]], [r_Csb[h]])
                S.dma("act", C_sb[:, :, 256:257], dap(sn, b * 512, [[1, 128], [128, 4], [1, 1]]), writes=[r("Cn")],
                      reads=[], allow_slow_non_contiguous=True)
            rCn = [] if prompt else [r("Cn")]

            for h in range(4):
                mm(bank(0)[0:L, h * 128:h * 128 + L], bufC[:, 4 + h, cs], bufC[:, h, cs], True, True,
                   [g_C[4 + h][b], g_C[h][b]], [bankres[0]], sig=(h == 3))
            for h in range(4):
                stt(sTs[0:L, h, 0:L], bank(0)[0:L, h * 128:h * 128 + L], gsc[0:L, b, h:h + 1], trimask[0:L, 0:L],
                    ALU.mult, ALU.mult, [bankres[0], r("gsc"), r("trimask")], [r("sTs")])
            for h in range(4):
                tr(bankb(1)[0:L, h * 128:(h + 1) * 128], bufC[:, 4 + h, cs], identb[:, :], [g_C[4 + h][b], r("identb")],
                   [bankres[1]], sig=(h == 3))
            tt("dve", kw[0:L, :, :], bankb(1)[0:L, 0:512].rearrange("p (h d) -> p h d", h=4),
               gsc[0:L, b, 0:4].unsqueeze(2).broadcast_to([L, 4, 128]), ALU.mult, [bankres[1], r("gsc")], [r("kw")])
            for h in range(4):
                act(Cb[:, h, :], C_sb[:, h, :], AF.Identity, [r_Csb[h], r("gsc")] + rCn, [r_Cb[h]], scale=gsc[:, b, 12 + h:13 + h])
            for h in range(4):
                nbk = 2 + h
                mm(bank(nbk)[0:L, 0:257], sTs[0:L, h, 0:L], vaug[0:L, b, h, :], True, False,
                   [r("sTs"), r_vaug[b]], [bankres[nbk]], sig=False)
                mm(bank(nbk)[0:L, 0:257], bufC[:, h, cs], Cb[:, h, :], False, True,
                   [g_C[h][b], r_Cb[h]], [bankres[nbk]])
            for h in range(4):
                dbk = 6 + (h % 2)
                mm(bank(dbk)[:, 0:257], kw[0:L, h, :], vaug[0:L, b, h, :], True, True, [r("kw"), r_vaug[b]], [bankres[dbk]])
                stt(C_sb[:, h, :], C_sb[:, h, :], gsc[:, b, 8 + h:9 + h], bank(dbk)[:, 0:257], ALU.mult, ALU.add,
                    [r_Csb[h], r("gsc"), bankres[dbk]] + rCn, [r_Csb[h]])
            rh = [r("hsm")]
            for h in range(4):
                nbk = 2 + h
                S.op("dve", lambda e, h=h, nbk=nbk: e.bn_stats(out=st6[0:L, h, :], in_=bank(nbk)[0:L, 0:256]),
                     [bankres[nbk]], [r("st6")])
                S.op("dve", lambda e, h=h: e.bn_aggr(out=mv[0:L, h, :], in_=st6[0:L, h, :]), [r("st6")], [r("mv")])
                cp("dve", hsm[0:L, 0, h:h + 1], bank(nbk)[0:L, 256:257], [bankres[nbk]], rh)
            stt(hsm[0:L, 1, :], hsm[0:L, 0, :], -1.0, hsm[0:L, 0, :], ALU.mult, ALU.max, rh, rh)
            tt("dve", hsm[0:L, 1, :], hsm[0:L, 1, :], gsc[0:L, b, 4:8], ALU.max, rh + [r("gsc")], rh)
            recip(hsm[0:L, 2, :], hsm[0:L, 1, :], rh, rh)
            tt("dve", hsm[0:L, 3, :], hsm[0:L, 2, :], hsm[0:L, 2, :], ALU.mult, rh, rh)
            tt("dve", hsm[0:L, 3, :], hsm[0:L, 3, :], mv[0:L, :, 1], ALU.mult, rh + [r("mv")], rh)
            ts("dve", hsm[0:L, 3, :], hsm[0:L, 3, :], HN_EPS, None, ALU.add, None, rh, rh)
            act(hsm[0:L, 4, :], hsm[0:L, 3, :], AF.Sqrt, rh, rh)
            recip(hsm[0:L, 5, :], hsm[0:L, 4, :], rh, rh)
            tt("dve", hsm[0:L, 6, :], hsm[0:L, 2, :], hsm[0:L, 5, :], ALU.mult, rh, rh)
            stt(hsm[0:L, 7, :], mv[0:L, :, 0], -1.0, hsm[0:L, 6, :], ALU.mult, ALU.mult, rh + [r("mv")], rh)
            for h in range(4):
                nbk = 2 + h
                act(hn[0:L, h, :], bank(nbk)[0:L, 0:256], AF.Identity, [bankres[nbk]] + rh, [r("hn")],
                    bias=hsm[0:L, 7, h:h + 1], scale=hsm[0:L, 6, h:h + 1])
            for j in range(8):
                tr(bankb(1)[:, j * 128:j * 128 + L], hn[0:L, j // 2, (j % 2) * 128:(j % 2) * 128 + 128], identb[0:L, 0:L],
                   [r("hn"), r("identb")], [bankres[1]], sig=(j == 7))
            for j in range(8):
                stt(bufA[:, j, cs], bankb(1)[:, j * 128:j * 128 + L], vecT[:, 104 + j:105 + j], bufB[:, j, cs],
                    ALU.mult, ALU.mult, [bankres[1], r("vecT"), g_B[j][b]], [g_A[j][b]])

            if (last and b == nb - 1) or not prompt:
                for h in range(4):
                    for vh in range(2):
                        a = h * 2 + vh
                        tr(pair(0)[:, a * 128:(a + 1) * 128], C_sb[:, h, vh * 128:(vh + 1) * 128], ident[:, :],
                           [r_Csb[h], r("ident")] + rCn, [bankres[0], bankres[1]], sig=(a == 7))
                stg, stg_r = nscr()
                cp("dve", stg[:, :], pair(0)[:, :], [bankres[0], bankres[1]], [stg_r])
                dC = dap(o_Cp, 0, [[128, 128], [128 * 128, 8], [1, 128]]) if prompt else \
                    dap(o_Cs, b * 4 * 256 * 128, [[128, 128], [128 * 128, 8], [1, 128]])
                S.dma("pool", dC, stg[:, :].rearrange("p (a d) -> p a d", a=8), reads=[stg_r], writes=[r("o_C")])
                dn = dap(o_np, 0, [[1, 128], [128, 4], [1, 1]]) if prompt else dap(o_ns, b * 512, [[1, 128], [128, 4], [1, 1]])
                S.dma("pool", dn, C_sb[:, :, 256:257], reads=r_Csb + rCn, writes=[r("o_n")], allow_slow_non_contiguous=True)
                if prompt:
                    S.dma("pool", o_mp[:, :], mcol[:, 0:1], reads=[r("mcol")], writes=[r("o_m")])
                else:
                    S.dma("pool", o_ms[b], mout[:, b:b + 1], reads=[r("mout")], writes=[r("o_m")])

        ckpt(5)
        same = prompt
        for pj in range(2):
            ws, wr_ = wnext()
            evr = resid_evac(L, nb, seqs, 2, same)
            proj_a(ws, wr_, 4, bufA, g_A, T, (lambda j, bi, pj=pj, evr=evr: evr(pj * 4 + j, bi)), [0, 1, 2, 3])
        ln_stage(L, nb, seqs, prompt, ti, "ln", (gam(0, 0), bet(0, 0)), [(uT, g_uT, 3, 4)])

        def mlp(l, kind_s):
            for pj in range(8):
                ws, wr_ = wnext()

                def ev(j, bi, pj=pj):
                    c = pj * 4 + j
                    rt, rt_r = rot(list(zip(rtmp, rtmp_r)), "rtmp")
                    act(rt[:, 0:T], bank(bi)[:, 0:T], AF.Relu, [bankres[bi]], [rt_r])
                    tt(SQE, hT[:, c, 0:T], rt[:, 0:T], rt[:, 0:T], ALU.mult, [rt_r], [r_hT[c]])
                proj_a(ws, wr_, 4, uT, g_uT, T, ev, [0, 1, 2, 3, 4, 5, 6, 7])
            evr = resid_evac(L, nb, seqs, kind_s, same)
            for nh in range(2):
                bks = [nh * 4 + j for j in range(4)]
                for kg in range(4):
                    ws, wr_ = wnext()
                    for j in range(4):
                        for kc in range(8):
                            mm(bank(bks[j])[:, 0:T], ws[:, kc, j * 128:(j + 1) * 128], hT[:, kg * 8 + kc, 0:T],
                               kg == 0 and kc == 0, kg == 3 and kc == 7, [wr_, r_hT[kg * 8 + kc]], [bankres[bks[j]]],
                               sig=(kc == 7))
                for j in range(4):
                    evr(nh * 4 + j, bks[j])

        ckpt(6)
        mlp(0, 5)
        ckpt(7)
        ln_stage(L, nb, seqs, prompt, ti, "ln", (gam(0, 1), bet(0, 1)), [(uT, g_uT, 6, 7), (bufA, g_A, 8, 9)])

        ckpt(8)
        ws, wr_ = wnext()
        if prompt:
            kcol0 = (ti % 2) * 512
            ringb = [(ti % 2) * 4 + b for b in BL]
        else:
            kcol0 = 0
            ringb = [0, 1, 2, 3]

        def ev_k(j, bi):
            cpa(KTr[:, j, kcol0:kcol0 + T], bank(bi)[:, 0:T], [bankres[bi]],
                [r_KTr[x] for x in (ringb if prompt else [0])])
        proj_a(ws, wr_, 2, bufA, g_A, T, ev_k, [0, 1])
        for b in BL:
            bi = rot([2, 3])
            for kc in range(8):
                mm(bank(bi)[0:L, :], bufA[:, kc, CS[b]], ws[:, kc, :], kc == 0, kc == 7, [wr_, g_A[kc][b]], [bankres[bi]],
                   sig=(kc == 7))
            cpa(Vr[0:L, ringb[b], :, 0:64], bank(bi)[0:L, 256:512].rearrange("p (h d) -> p h d", h=4),
                [bankres[bi]], [r_Vr[ringb[b]]])
            if last or not prompt:
                stg, stg_r = nscr()
                cp("act", stg[0:L, 0:512], bank(bi)[0:L, :], [bankres[bi]], [stg_r])
                if prompt:
                    dk_, dv_ = o_kp[b * 128:(b + 1) * 128, :], o_vp[b * 128:(b + 1) * 128, :]
                else:
                    dk_, dv_ = o_ks[b * 32:(b + 1) * 32, :], o_vs[b * 32:(b + 1) * 32, :]
                S.dma("pool", dk_, stg[0:L, 0:256], reads=[stg_r], writes=[r("o_k")])
                S.dma("pool", dv_, stg[0:L, 256:512], reads=[stg_r], writes=[r("o_v")])

        def ev_q(base):
            def ev(j, bi):
                cpa(bufC[:, base + j, 0:T], bank(bi)[:, 0:T], [bankres[bi]], gsel(g_C, [base + j], BL))
            return ev
        for pj in range(2):
            ws, wr_ = wnext()
            proj_a(ws, wr_, 4, uT, g_uT, T, ev_q(pj * 4), [4, 5, 6, 7])

        ckpt(9)
        def keyblocks_for(b):
            kbs = []
            if prompt:
                Bg = ti * 4 + b
                for j in range(5):
                    KB = Bg - 4 + j
                    if KB < 0:
                        continue
                    pos = KB % 8
                    kd = {0: "mask0", 1: "plain", 2: "plain", 3: "b3", 4: "b4"}[j]
                    kbs.append((
                        (lambda kk, rows, pos=pos: KTr[rows, kk, pos * 128:(pos + 1) * 128]),
                        (lambda kh, pos=pos: Vr[:, pos, kh, :]), 128, kd, [r_KTr[pos], r_Vr[pos]]))
            else:
                rc = r_KTr[4:8] + r_Vr[4:8]
                for j in range(4):
                    kbs.append((
                        (lambda kk, rows, j=j: KTc[rows, kk, j * 128:(j + 1) * 128]),
                        (lambda kh, j=j: Vc[:, j, kh, :]), 128, "b3" if j == 3 else "plain", rc))
                kbs.append((
                    (lambda kk, rows, b=b: KTr[rows, kk, b * 32:(b + 1) * 32]),
                    (lambda kh, b=b: Vr[0:32, b, kh, :]), 32, "b4", [r_KTr[0], r_Vr[b]]))
            return kbs

        def load_cache(b):
            stg, stg_r = nscr()
            S.dma(XQ, stg[:, :].rearrange("p (j f) -> p j f", j=4),
                  dap(ck, b * 512 * 256, [[256, 128], [128 * 256, 4], [1, 256]]), writes=[stg_r])
            for j in range(4):
                for kk in range(2):
                    a = j * 2 + kk
                    tr(pair(0)[:, a * 128:(a + 1) * 128], stg[:, j * 256 + kk * 128:j * 256 + (kk + 1) * 128], ident[:, :],
                       [stg_r, r("ident")], [bankres[0], bankres[1]], sig=(a == 7))
            for kk in range(2):
                cp("dve", KTc[:, kk, :].rearrange("p (j s) -> p j s", j=4),
                   pair(0)[:, :].rearrange("p (j k s) -> p j k s", j=4, k=2)[:, :, kk, :],
                   [bankres[0], bankres[1]], r_KTr[4:8])
            stg2, stg2_r = nscr()
            S.dma(XQ, stg2[:, :].rearrange("p (j f) -> p j f", j=4),
                  dap(cv, b * 512 * 256, [[256, 128], [128 * 256, 4], [1, 256]]), writes=[stg2_r])
            cp("dve", Vc[:, :, :, 0:64], stg2[:, :].rearrange("p (j h d) -> p j h d", j=4, h=4), [stg2_r], r_Vr[4:8])

        uctr = [0]

        def emit_st(b, kh, kbs):
            cs = CS[b]
            Lq = L
            kk, e_ = kh // 2, kh % 2
            rows = slice(e_ * 64, (e_ + 1) * 64)
            uctr[0] += 1
            pts = []
            for (ktf, vf, nk, kd, rds) in kbs:
                sbk = rot([0, 1, 2, 3])
                sps = bank(sbk)[0:nk, 0:4 * Lq]
                mm(sps, ktf(kk, rows), bufC[rows, kk * 4:(kk + 1) * 4, cs], True, True,
                   rds + gsel(g_C, range(kk * 4, kk * 4 + 4), [b]), [bankres[sbk]])
                if kd == "mask0":
                    pt, pt_r = PT0[uctr[0] % 2], PT0_r[uctr[0] % 2]
                    s3 = sps.rearrange("p (g q) -> p g q", g=4)
                    p3 = pt[0:nk, 0:4 * Lq].rearrange("p (g q) -> p g q", g=4)
                    act(p3[:, :, 0:64], s3[:, :, 0:64], AF.Exp, [bankres[sbk]], [pt_r], scale=0.125)
                    act(p3[64:128, :, 64:128], s3[64:128, :, 64:128], AF.Exp, [bankres[sbk]], [pt_r], scale=0.125)
                else:
                    pt, pt_r = rot(list(zip(PT, PT_r)), "PT")
                    if kd == "plain":
                        act(pt[0:nk, 0:4 * Lq], sps, AF.Exp, [bankres[sbk]], [pt_r], scale=0.125)
                    else:
                        bt_ = bias3 if kd == "b3" else bias4
                        ei, ei_r = rot(list(zip(expin, expin_r)), "expin")
                        stt(ei[0:nk, 0:4 * Lq].rearrange("p (g q) -> p g q", g=4),
                            sps.rearrange("p (g q) -> p g q", g=4), 0.125,
                            bt_[0:nk, kh * 4:(kh + 1) * 4, 0:Lq], ALU.mult, ALU.add,
                            [bankres[sbk], r("bias3"), r("bias4")], [ei_r])
                        act(pt[0:nk, 0:4 * Lq], ei[0:nk, 0:4 * Lq], AF.Exp, [ei_r], [pt_r])
                pts.append((pt, pt_r))
            return pts

        def emit_pv(b, kh, kbs, pts):
            Lq = L
            obk = 4 + kh
            nkb = len(kbs)
            for g in range(4):
                for ki, (ktf, vf, nk, kd, rds) in enumerate(kbs):
                    pt, pt_r = pts[ki]
                    mm(bank(obk)[0:Lq, g * 65:(g + 1) * 65], pt[0:nk, g * Lq:(g + 1) * Lq], vf(kh)[0:nk, :],
                       ki == 0, ki == nkb - 1, [pt_r] + rds, [bankres[obk]], sig=(ki == nkb - 1))

        def emit_fin(b):
            cs = CS[b]
            Lq = L
            for kh in range(4):
                obk = 4 + kh
                o3 = bank(obk)[0:Lq, 0:260].rearrange("p (g d) -> p g d", g=4)
                recip(rden[0:Lq, kh * 4:(kh + 1) * 4], o3[:, :, 64], [bankres[obk]], [r("rden")])
                tt("dve", On[0:Lq, kh * 4:(kh + 1) * 4, :], o3[:, :, 0:64],
                   rden[0:Lq, kh * 4:(kh + 1) * 4].unsqueeze(2).broadcast_to([Lq, 4, 64]), ALU.mult,
                   [bankres[obk], r("rden")], [r("On")])
            for c in range(8):
                tr(bankb(0)[:, c * 128:c * 128 + Lq], On[0:Lq, 2 * c:2 * c + 2, :].rearrange("p h d -> p (h d)"),
                   identb[0:Lq, 0:Lq], [r("On"), r("identb")], [bankres[0]], sig=(c == 7))
            cpa(bufB[:, :, cs], bankb(0)[:, 0:1024].rearrange("p (c q) -> p c q", c=8)[:, :, 0:Lq], [bankres[0]],
                gsel(g_B, ALLC, [b]))

        if prompt:
            units = [(b, kh) for b in BL for kh in range(4)]
            kbl = {b: keyblocks_for(b) for b in BL}
            nxt = emit_st(units[0][0], units[0][1], kbl[units[0][0]])
            for ui, (b, kh) in enumerate(units):
                cur = nxt
                if ui + 1 < len(units):
                    b2, kh2 = units[ui + 1]
                    nxt = emit_st(b2, kh2, kbl[b2])
                emit_pv(b, kh, kbl[b], cur)
                if kh == 3:
                    emit_fin(b)
        else:
            for b in BL:
                load_cache(b)
                kbs = keyblocks_for(b)
                for kh in range(4):
                    pts = emit_st(b, kh, kbs)
                    emit_pv(b, kh, kbs, pts)
                emit_fin(b)

        ckpt(10)
        for pj in range(2):
            ws, wr_ = wnext()
            evr = resid_evac(L, nb, seqs, 10, same)
            proj_a(ws, wr_, 4, bufB, g_B, T, (lambda j, bi, pj=pj, evr=evr: evr(pj * 4 + j, bi)), [0, 1, 2, 3])
        ln_stage(L, nb, seqs, prompt, ti, "ln", (gam(1, 0), bet(1, 0)), [(uT, g_uT, 11, 12)])
        mlp(1, 13)
        ln_stage(L, nb, seqs, prompt, ti, "final", None, [])

    try:
        ckpt(1)
        for (kind, ti) in tiles:
            do_tile(kind, ti)
    except _Stop:
        pass

    print('sbuf bytes remaining', nc.sbuf_bytes_remaining, flush=True)
    S.final_drain("sp")
    S.emit(nc)
    st.close()
    return nc


_CACHE = {}


def _consts():
    ident = np.eye(128, dtype=np.float32)
    tri = (np.arange(128)[:, None] <= np.arange(128)[None, :]).astype(np.float32) * np.float32(128.0 ** -0.5)
    m4 = np.zeros((128, 128), np.float32)
    m4[64:, :64] = NEG
    return np.ascontiguousarray(np.stack([ident, tri, m4], axis=1))


def make_in_maps(inp, NT, ncores=8):
    f = lambda a: np.ascontiguousarray(np.asarray(a, dtype=np.float32))
    rel = f(inp["rel_bias_b"])[0]
    ext = np.concatenate([rel, np.repeat(rel[:, 256:257], 128, axis=1)], axis=1)
    rel_rev = np.ascontiguousarray(ext[:, ::-1])
    vec1 = np.concatenate([f(inp["conv_w_a"])[0].reshape(32, 128), f(inp["conv_b_a"])[0].reshape(8, 128),
                           f(inp["ln_g"]).reshape(32, 128), f(inp["ln_b"]).reshape(32, 128),
                           f(inp["mhn_g_a"])[0].reshape(8, 128)], axis=0)
    vec2 = np.concatenate([f(inp["b_ada"]).reshape(96, 128), f(inp["b_ada_kv"]).reshape(16, 128)], axis=0)
    lnf = np.stack([f(inp["ln_g"])[1, 1], f(inp["ln_b"])[1, 1]], axis=0)
    shared = {
        "w_in": f(inp["w_in_a"])[0], "w_out_a": f(inp["w_out_a"])[0],
        "w_up0": f(inp["w_up"])[0], "w_up1": f(inp["w_up"])[1],
        "w_down0": f(inp["w_down"])[0], "w_down1": f(inp["w_down"])[1],
        "w_kv": f(inp["w_kv"]), "w_q": f(inp["w_q_b"])[0], "w_out_b": f(inp["w_out_b"])[0],
        "w_ada": f(inp["w_ada"]), "w_ada_kv": f(inp["w_ada_kv"]),
        "vec1": np.ascontiguousarray(vec1), "vec2": np.ascontiguousarray(vec2),
        "b_if": f(inp["b_if_a"]).reshape(8, 1), "lnf": np.ascontiguousarray(lnf),
        "rel_rev": rel_rev, "cst": _consts(),
    }
    maps = []
    xp, xs = f(inp["x_prompt"]), f(inp["x_sample"])
    for i in range(ncores):
        s0 = 4 * i
        m = dict(shared)
        m["x_p"] = np.ascontiguousarray(xp[i, :NT * 512])
        m["x_s"] = np.ascontiguousarray(xs[s0:s0 + 4].reshape(128, 1024))
        m["c_all"] = np.ascontiguousarray(np.concatenate([f(inp["c_prompt"])[i:i + 1], f(inp["c_sample"])[s0:s0 + 4]], axis=0))
        m["sconv"] = np.ascontiguousarray(f(inp["state_conv"])[0, s0:s0 + 4])
        m["sC"] = np.ascontiguousarray(f(inp["state_C"])[0, s0:s0 + 4])
        m["sn"] = np.ascontiguousarray(f(inp["state_n"])[0, s0:s0 + 4])
        m["smT"] = np.ascontiguousarray(f(inp["state_m"])[0, s0:s0 + 4].T)
        m["ck"] = np.ascontiguousarray(f(inp["cache_k"])[s0:s0 + 4].reshape(4, 512, 256))
        m["cv"] = np.ascontiguousarray(f(inp["cache_v"])[s0:s0 + 4].reshape(4, 512, 256))
        maps.append(m)
    return maps


def assemble(results, NT, ncores=8):
    g = lambda k: np.stack([np.asarray(results[i][k], dtype=np.float32) for i in range(ncores)], axis=0)
    y_p = g("y_p")
    y_s = g("y_s").reshape(ncores * 4, 32, 1024)
    conv_p = g("o_convp")[None]
    C_p = g("o_Cp")[None]
    n_p = g("o_np")[None]
    m_p = g("o_mp").reshape(ncores, 4)[None]
    k_p = g("o_kp").reshape(ncores, 512, 4, 64)
    v_p = g("o_vp").reshape(ncores, 512, 4, 64)
    conv_s = g("o_convs").reshape(ncores * 4, 3, 1024)[None]
    C_s = g("o_Cs").reshape(ncores * 4, 4, 256, 128)[None]
    n_s = g("o_ns").reshape(ncores * 4, 4, 128)[None]
    m_s = g("o_ms").reshape(ncores * 4, 4)[None]
    k_s = g("o_ks").reshape(ncores * 4, 32, 4, 64)
    v_s = g("o_vs").reshape(ncores * 4, 32, 4, 64)
    return (y_p, y_s, conv_p, C_p, n_p, m_p, k_p, v_p, conv_s, C_s, n_s, m_s, k_s, v_s)


def kernel(**inputs):
    NT = 16
    if NT not in _CACHE:
        _CACHE[NT] = build(NT)
    nc = _CACHE[NT]
    in_maps = make_in_maps(inputs, NT)
    res = run_bass_kernel_spmd(nc, in_maps, core_ids=list(range(8)))
    return assemble(res.results, NT)
```

```python
import contextlib
import numpy as np
import concourse.bass as bass
import concourse.mybir as mybir
from concourse.bass_utils import run_bass_kernel_spmd

F32 = mybir.dt.float32
BF16 = mybir.dt.bfloat16
AF = mybir.ActivationFunctionType
ALU = mybir.AluOpType
AX = mybir.AxisListType

ALPHA = 4.0 ** 0.25
LN_EPS_P = 1e-5 / (ALPHA * ALPHA)
HN_EPS = 1e-6
NSLOT = 3
PDEPTH = 2
NEG = -30000.0
import os
XQ = os.environ.get("XQ", "act")
SQE = os.environ.get("SQE", "pool")


class Res:
    __slots__ = ("name", "w", "r", "sem", "excl")

    def __init__(self, name, excl=False):
        self.name = name
        self.w = None
        self.r = []
        self.sem = None
        self.excl = excl


class Sched:
    ENGS = ("pe", "act", "dve", "pool", "sp")

    def __init__(self):
        self.ops = {e: [] for e in self.ENGS}
        self.nsig = {e: 0 for e in self.ENGS}
        self.waited = {e: {} for e in self.ENGS}
        self.semkeys = ["E_" + e for e in self.ENGS]
        self.dma_sems = {}
        self.nres = 0

    def res(self, name=None):
        self.nres += 1
        return Res(name or f"r{self.nres}")

    def _need(self, eng, ev, waits, same_ok):
        if ev is None:
            return
        key, val, weng = ev
        if same_ok and weng == eng:
            return
        if self.waited[eng].get(key, 0) >= val:
            return
        if waits.get(key, 0) < val:
            waits[key] = val

    def _deps(self, eng, reads, writes):
        waits = {}
        for r in reads:
            self._need(eng, r.w, waits, eng == "pe")
            if r.excl:
                for ev in r.r:
                    self._need(eng, ev, waits, True)
        for w in writes:
            self._need(eng, w.w, waits, True)
            for ev in w.r:
                self._need(eng, ev, waits, True)
        for k, v in waits.items():
            self.waited[eng][k] = v
        return list(waits.items())

    def _record(self, ev, reads, writes):
        for r in reads:
            r.r.append(ev)
        for w in writes:
            w.w = ev
            w.r = []

    def op(self, eng, fn, reads=(), writes=(), sig=True):
        waits = self._deps(eng, reads, writes)
        key = "E_" + eng
        if sig:
            self.nsig[eng] += 1
            val = self.nsig[eng]
        else:
            val = self.nsig[eng] + 1
        self._record((key, val, eng), reads, writes)
        self.ops[eng].append((waits, fn, (key, 1) if sig else None))

    def dma(self, q, out, in_, reads=(), writes=(), **kw):
        waits = self._deps(q, reads, writes)
        tgt = writes[0]
        if tgt.sem is None:
            tgt.sem = f"D{len(self.dma_sems)}_{tgt.name}"
            self.semkeys.append(tgt.sem)
            self.dma_sems[tgt.sem] = 0
        self.dma_sems[tgt.sem] += 16
        self._record((tgt.sem, self.dma_sems[tgt.sem], "dma"), reads, writes)
        self.ops[q].append((waits, (lambda e, o=out, i=in_, k=kw: e.dma_start(out=o, in_=i, **k)),
                            (tgt.sem, 16)))

    def barrier(self):
        for e in self.ENGS:
            waits = {}
            for e2 in self.ENGS:
                k = "E_" + e2
                if self.nsig[e2] > 0 and e2 != e and self.waited[e].get(k, 0) < self.nsig[e2]:
                    waits[k] = self.nsig[e2]
            for k, v in self.dma_sems.items():
                if v > 0 and self.waited[e].get(k, 0) < v:
                    waits[k] = v
            for k, v in waits.items():
                self.waited[e][k] = v
            if waits:
                self.ops[e].append((list(waits.items()), None, None))

    def final_drain(self, eng="sp"):
        waits = {}
        for k, v in self.dma_sems.items():
            if v > 0 and self.waited[eng].get(k, 0) < v:
                waits[k] = v
        for e2 in self.ENGS:
            if self.nsig[e2] > 0 and e2 != eng:
                waits["E_" + e2] = self.nsig[e2]
        self.ops[eng].append((list(waits.items()), None, None))

    def emit(self, nc):
        with contextlib.ExitStack() as st:
            sems = {k: st.enter_context(nc.semaphore(k)) for k in self.semkeys}
            block = st.enter_context(nc.Block())

            def replay(engobj, name):
                for waits, fn, sig in self.ops[name]:
                    for k, v in waits:
                        engobj.wait_ge(sems[k], v)
                    if fn is None:
                        continue
                    ins = fn(engobj)
                    if sig is not None:
                        ins.then_inc(sems[sig[0]], sig[1])

            @block.sync
            def _(e):
                replay(e, "sp")

            @block.tensor
            def _(e):
                replay(e, "pe")

            @block.scalar
            def _(e):
                replay(e, "act")

            @block.vector
            def _(e):
                replay(e, "dve")

            @block.gpsimd
            def _(e):
                replay(e, "pool")


def panel_list():
    pl = []
    for j in range(6):
        pl.append(("w_in", 0, j * 512, 3080, None))
    for j in range(2):
        pl.append(("w_out_a", 0, j * 512, 1024, None))
    for j in range(8):
        pl.append(("w_up0", 0, j * 512, 4096, None))
    for nh in range(2):
        for kg in range(4):
            pl.append(("w_down0", kg * 1024, nh * 512, 1024, None))
    pl.append(("w_kv", 0, 0, 512, None))
    for j in range(2):
        pl.append(("w_q", 0, j * 512, 1024, "qperm"))
    for j in range(2):
        pl.append(("w_out_b", 0, j * 512, 1024, None))
    for j in range(8):
        pl.append(("w_up1", 0, j * 512, 4096, None))
    for nh in range(2):
        for kg in range(4):
            pl.append(("w_down1", kg * 1024, nh * 512, 1024, None))
    return pl


NPANEL = 45


class _Stop(Exception):
    pass


def build(NT, do_sample=True, stage=99):
    nc = bass.Bass("TRN2", target_bir_lowering=False)
    S = Sched()
    st = contextlib.ExitStack()
    SEQ = NT * 512

    def din(name, shape, dt=F32):
        return nc.dram_tensor(name, list(shape), dt, kind="ExternalInput").ap()

    def dout(name, shape, dt=F32):
        return nc.dram_tensor(name, list(shape), dt, kind="ExternalOutput").ap()

    def dap(t, offset, dims):
        return bass.AP(tensor=t.tensor, offset=offset, ap=[list(d) for d in dims])

    x_p = din("x_p", [SEQ, 1024])
    x_s = din("x_s", [128, 1024])
    c_all = din("c_all", [5, 1024])
    sconv = din("sconv", [4, 3, 1024])
    sC = din("sC", [4, 4, 256, 128])
    sn = din("sn", [4, 4, 128])
    smT = din("smT", [4, 4])
    ck = din("ck", [4, 512, 256])
    cv = din("cv", [4, 512, 256])
    W = {
        "w_in": din("w_in", [1024, 3080]),
        "w_out_a": din("w_out_a", [1024, 1024]),
        "w_up0": din("w_up0", [1024, 4096]),
        "w_up1": din("w_up1", [1024, 4096]),
        "w_down0": din("w_down0", [4096, 1024]),
        "w_down1": din("w_down1", [4096, 1024]),
        "w_kv": din("w_kv", [1024, 512]),
        "w_q": din("w_q", [1024, 1024]),
        "w_out_b": din("w_out_b", [1024, 1024]),
    }
    w_ada = din("w_ada", [2, 1024, 6144])
    w_ada_kv = din("w_ada_kv", [1024, 2048])
    vec1 = din("vec1", [112, 128])
    vec2 = din("vec2", [112, 128])
    b_if = din("b_if", [8, 1])
    lnf = din("lnf", [2, 1024])
    rel_rev = din("rel_rev", [16, 385])
    cst = din("cst", [128, 3, 128])

    y_p = dout("y_p", [SEQ, 1024])
    y_s = dout("y_s", [128, 1024])
    o_convp = dout("o_convp", [3, 1024])
    o_Cp = dout("o_Cp", [4, 256, 128])
    o_np = dout("o_np", [4, 128])
    o_mp = dout("o_mp", [4, 1])
    o_kp = dout("o_kp", [512, 256])
    o_vp = dout("o_vp", [512, 256])
    o_convs = dout("o_convs", [4, 3, 1024])
    o_Cs = dout("o_Cs", [4, 4, 256, 128])
    o_ns = dout("o_ns", [4, 4, 128])
    o_ms = dout("o_ms", [4, 4, 1])
    o_ks = dout("o_ks", [128, 256])
    o_vs = dout("o_vs", [128, 256])
    wsc = nc.dram_tensor("wsc", [NPANEL, 128, 8, 512], BF16).ap()

    def sb(name, shape, dt=F32):
        return st.enter_context(nc.sbuf_tensor(name, list(shape), dt))

    PP = [st.enter_context(nc.psum_tensor(f"pp{i}", [128, 1024], F32)) for i in range(4)]
    bankres = [Res(f"bank{i}", excl=True) for i in range(8)]

    def bank(i):
        return PP[i // 2][:, (i % 2) * 512:(i % 2) * 512 + 512]

    def bankb(i):
        return PP[i // 2][:, (i % 2) * 512:(i % 2) * 512 + 512].bitcast(BF16)

    def pair(i):
        return PP[i][:, :]

    wslot = [sb(f"wslot{i}", [128, 8, 512], BF16) for i in range(NSLOT)]
    wslot_r = [S.res(f"wslot{i}") for i in range(NSLOT)]
    xT = sb("xT", [128, 8, 512])
    scr = [sb(f"scr{i}", [128, 1024]) for i in range(4)]
    scr_r = [S.res(f"scr{i}") for i in range(4)]
    uT = sb("uT", [128, 8, 512], BF16)
    bufA = sb("bufA", [128, 8, 512], BF16)
    bufB = sb("bufB", [128, 8, 512], BF16)
    bufC = sb("bufC", [128, 8, 512], BF16)
    cb = sb("cb", [128, 8, 4, 131], BF16)
    vaug = sb("vaug", [128, 4, 4, 257], BF16)
    hT = sb("hT", [128, 32, 512], BF16)
    rtmp = [sb(f"rtmp{i}", [128, 512]) for i in range(2)]
    rtmp_r = [S.res(f"rtmp{i}") for i in range(2)]
    C_sb = sb("C_sb", [128, 4, 257])
    Cb = sb("Cb", [128, 4, 257], BF16)
    sTs = sb("sTs", [128, 4, 128], BF16)
    kw = sb("kw", [128, 4, 128], BF16)
    hn = sb("hn", [128, 4, 256], BF16)
    KTr = sb("KTr", [128, 2, 1024], BF16)
    Vr = sb("Vr", [128, 8, 4, 65], BF16)
    KTc = KTr[:, :, 512:1024]
    Vc = Vr[:, 4:8, :, :]
    NPT = 8
    PT = [sb(f"PT{i}", [128, 512], BF16) for i in range(NPT)]
    PT_r = [S.res(f"PT{i}") for i in range(NPT)]
    PT0 = [sb(f"PTm{i}", [128, 512], BF16) for i in range(2)]
    PT0_r = [S.res(f"PTm{i}") for i in range(2)]
    expin = [sb(f"expin{i}", [128, 512]) for i in range(2)]
    expin_r = [S.res(f"expin{i}") for i in range(2)]
    On = sb("On", [128, 16, 64], BF16)
    bias3 = sb("bias3", [128, 16, 128], BF16)
    bias4 = sb("bias4", [128, 16, 128], BF16)
    lnfbc = sb("lnfbc", [128, 2, 1024])
    ident = sb("ident", [128, 128])
    identb = sb("identb", [128, 128], BF16)
    trimask = sb("trimask", [128, 128])
    mask4 = sb("mask4", [128, 128])
    diagc = sb("diagc", [128, 8, 4, 128], BF16)
    vecT = sb("vecT", [128, 112])
    biasT = sb("biasT", [128, 112])
    TAB = sb("TAB", [128, 14, 8, 5])
    cT = sb("cT", [128, 8, 5])
    wg = sb("wg", [128, 8, 8], BF16)
    wg32 = sb("wg32", [128, 8, 8])
    bg = sb("bg", [4, 2])
    bg8 = sb("bg8", [8, 1])
    ones4 = sb("ones4", [4, 128])
    chv = sb("chv", [128, 16])
    cprev = sb("cprev", [128, 8, 3], BF16)
    cstage = sb("cstage", [128, 3, 8])
    mcol = sb("mcol", [4, 8])
    mout = sb("mout", [4, 4])
    G_g = sb("G_g", [4, 512])
    G_s = sb("G_s", [4, 8, 4])
    G_d4 = sb("G_d4", [4, 4, 4])
    gsc = sb("gsc", [128, 4, 16])
    st6 = sb("st6", [128, 4, 6])
    mv = sb("mv", [128, 4, 2])
    hsm = sb("hsm", [128, 8, 4])
    lst = sb("lst", [128, 4, 2, 6])
    lmv = sb("lmv", [128, 4, 8])
    rden = sb("rden", [128, 16])
    epsc = sb("epsc", [128, 1])
    modT = scr[0][:, 0:560].rearrange("p (a s) -> p a s", a=112)
    mod1 = scr[1][:, 0:560].rearrange("p (a s) -> p a s", a=112)
    crow = scr[2][0:5, :]
    rowc = expin[0][0:5, :]
    G_ig = expin[1][0:4, :]
    G_fg = expin[0][0:4, :]
    G_l = rtmp[0][0:4, :]
    G_nb = rtmp[1][0:4, :]

    R = {"modT": scr_r[0], "mod1": scr_r[1], "crow": scr_r[2], "rowc": expin_r[0]}

    def r(name):
        if name not in R:
            R[name] = S.res(name)
        return R[name]

    def grid(name):
        return [[S.res(f"{name}_{c}_{b}") for b in range(4)] for c in range(8)]

    g_xT = grid("xT"); g_uT = grid("uT"); g_A = grid("bufA"); g_B = grid("bufB"); g_C = grid("bufC")
    g_cb = grid("cb")
    r_vaug = [S.res(f"vaug{b}") for b in range(4)]
    r_hT = [S.res(f"hT{c}") for c in range(32)]
    r_Csb = [S.res(f"Csb{h}") for h in range(4)]
    r_Cb = [S.res(f"Cb{h}") for h in range(4)]
    r_KTr = [S.res(f"KTr{i}") for i in range(8)]
    r_Vr = [S.res(f"Vr{i}") for i in range(8)]

    def gsel(g, cs, bs):
        return [g[c][b] for c in cs for b in bs]

    ALLC = list(range(8))

    def mm(out, lhsT, rhs, start, stop, rd, wr, sig=True):
        S.op("pe", lambda e: e.matmul(out, lhsT=lhsT, rhs=rhs, start=start, stop=stop), rd, wr, sig)

    def tr(out, in_, idn, rd, wr, sig=True):
        S.op("pe", lambda e: e.transpose(out, in_, idn), rd, wr, sig)

    def act(out, in_, func, rd, wr, bias=None, scale=None):
        kw_ = {}
        if bias is not None:
            kw_["bias"] = bias
        if scale is not None:
            kw_["scale"] = scale
        S.op("act", lambda e: e.activation(out=out, in_=in_, func=func, **kw_), rd, wr)

    def ts(eng, out, in0, s1, s2, op0, op1, rd, wr):
        if s2 is None:
            S.op(eng, lambda e: e.tensor_scalar(out=out, in0=in0, scalar1=s1, scalar2=None, op0=op0), rd, wr)
        else:
            S.op(eng, lambda e: e.tensor_scalar(out=out, in0=in0, scalar1=s1, scalar2=s2, op0=op0, op1=op1), rd, wr)

    def tt(eng, out, in0, in1, op, rd, wr):
        S.op(eng, lambda e: e.tensor_tensor(out=out, in0=in0, in1=in1, op=op), rd, wr)

    def stt(out, in0, scalar, in1, op0, op1, rd, wr):
        S.op("dve", lambda e: e.scalar_tensor_tensor(out=out, in0=in0, scalar=scalar, in1=in1, op0=op0, op1=op1), rd, wr)

    def cp(eng, out, in_, rd, wr):
        if eng == "act":
            if "i" in os.environ.get("DBG", ""):
                S.op("act", lambda e: e.activation(out=out, in_=in_, func=AF.Identity), rd, wr)
            else:
                S.op("act", lambda e: e.copy(out=out, in_=in_), rd, wr)
        else:
            S.op(eng, lambda e: e.tensor_copy(out=out, in_=in_), rd, wr)

    def recip(out, in_, rd, wr):
        S.op("dve", lambda e: e.reciprocal(out=out, in_=in_), rd, wr)

    affctr = [0]

    def aff(eng, out, in_, A, B, rd, wr):
        if eng == "dve":
            ts("dve", out, in_, A, B, ALU.mult, ALU.add, rd, wr)
        else:
            act(out, in_, AF.Identity, rd, wr, bias=B, scale=A)

    cpctr = [0]

    def cpa(out, in_, rd, wr):
        cpctr[0] += 1
        cp("dve" if cpctr[0] % 2 == 0 else "act", out, in_, rd, wr)

    S.dma("sp", ident[:], cst[:, 0, :], writes=[r("ident")])
    S.dma("sp", trimask[:], cst[:, 1, :], writes=[r("trimask")])
    S.dma("sp", mask4[:], cst[:, 2, :], writes=[r("mask4")])
    cp("dve", identb[:], ident[:], [r("ident")], [r("identb")])
    S.op("dve", lambda e: e.memset(ones4[:], 1.0), [], [r("ones4")])
    S.op("dve", lambda e: e.memset(epsc[:], LN_EPS_P), [], [r("epsc")])
    S.dma("sp", lnfbc[:, 0, :], dap(lnf, 0, [[0, 128], [1, 1024]]), writes=[r("lnfbc")])
    S.dma("sp", lnfbc[:, 1, :], dap(lnf, 1024, [[0, 128], [1, 1024]]), writes=[r("lnfbc")])

    S.dma("sp", scr[0][0:112, 0:128], vec1[:, :], writes=[scr_r[0]])
    S.dma("sp", scr[1][0:112, 0:128], vec2[:, :], writes=[scr_r[1]])
    tr(bank(0)[:, 0:112], scr[0][0:112, 0:128], ident[0:112, 0:112], [scr_r[0], r("ident")], [bankres[0]])
    tr(bank(1)[:, 0:112], scr[1][0:112, 0:128], ident[0:112, 0:112], [scr_r[1], r("ident")], [bankres[1]])
    cp("dve", vecT[:], bank(0)[:, 0:112], [bankres[0]], [r("vecT")])
    cp("dve", biasT[:], bank(1)[:, 0:112], [bankres[1]], [r("biasT")])

    for c in range(8):
        for j in range(4):
            ts("dve", diagc[:, c, j, :], ident[:], vecT[:, j * 8 + c:j * 8 + c + 1], None, ALU.mult, None,
               [r("ident"), r("vecT")], [r("diagc")])

    S.dma("sp", wg32[:], dap(W["w_in"], 3072, [[3080, 128], [128 * 3080, 8], [1, 8]]), writes=[r("wg32")])
    cp("dve", wg[:], wg32[:], [r("wg32")], [r("wg")])
    S.dma("sp", bg[:, 0:1], b_if[0:4, :], writes=[r("bg")])
    S.dma("sp", bg[:, 1:2], b_if[4:8, :], writes=[r("bg")])
    ts("dve", bg[:, 1:2], bg[:, 1:2], -1.0, None, ALU.mult, None, [r("bg")], [r("bg")])

    S.dma("sp", crow[:], c_all[:, :], writes=[r("crow")])
    act(crow[:], crow[:], AF.Silu, [r("crow")], [r("crow")])
    for kc in range(8):
        tr(bank(2)[:, kc * 8:kc * 8 + 5], crow[:, kc * 128:(kc + 1) * 128], ident[0:5, 0:5],
           [r("crow"), r("ident")], [bankres[2]])
    cp("dve", cT[:], bank(2)[:, 0:64].rearrange("p (k s) -> p k s", k=8)[:, :, 0:5], [bankres[2]], [r("cT")])

    hT32 = hT[:].rearrange("p a b -> p (a b)").bitcast(F32)
    stg32 = [hT32[:, i * 4096:(i + 1) * 4096].rearrange("p (k n) -> p k n", k=8) for i in range(2)]
    stg32_r = [S.res("stg32_0"), S.res("stg32_1")]
    npan = 0
    ada_srcs = []
    for l in range(2):
        for j in range(12):
            ada_srcs.append((w_ada, l * 1024 * 6144 + j * 512, 6144))
    for j in range(4):
        ada_srcs.append((w_ada_kv, j * 512, 2048))
    for pi, (wt, off, ncols) in enumerate(ada_srcs):
        sl = pi % 2
        S.dma("sp", stg32[sl], dap(wt, off, [[ncols, 128], [128 * ncols, 8], [1, 512]]), writes=[stg32_r[sl]])
        pb = 3 + (pi % 2)
        for kc in range(8):
            mm(bank(pb)[0:5, :], cT[:, kc, :], stg32[sl][:, kc, :], kc == 0, kc == 7,
               [r("cT"), stg32_r[sl]], [bankres[pb]], sig=(kc == 7))
        cp("act", rowc[:], bank(pb)[0:5, :], [bankres[pb]], [r("rowc")])
        tb = 5 + (pi % 2)
        for q in range(4):
            tr(bank(tb)[:, q * 8:q * 8 + 5], rowc[:, q * 128:(q + 1) * 128], ident[0:5, 0:5],
               [r("rowc"), r("ident")], [bankres[tb]])
        cp("dve", modT[:, pi * 4:pi * 4 + 4, :], bank(tb)[:, 0:32].rearrange("p (q s) -> p q s", q=4)[:, :, 0:5],
           [bankres[tb]], [r("modT")])
    tt("dve", modT[:], modT[:], biasT[:].unsqueeze(2).broadcast_to([128, 112, 5]), ALU.add,
       [r("modT"), r("biasT")], [r("modT")])
    ts("dve", mod1[:], modT[:], 1.0, None, ALU.add, None, [r("modT")], [r("mod1")])

    def mchunk(l, j):
        return l * 48 + j * 8

    def gam(l, i):
        o = 40 + (l * 2 + i) * 8
        return vecT[:, o:o + 8]

    def bet(l, i):
        o = 72 + (l * 2 + i) * 8
        return vecT[:, o:o + 8]

    def bc5(a):
        return a.unsqueeze(2).broadcast_to([128, 8, 5])

    rT = [r("modT"), r("mod1"), r("vecT")]
    cp("dve", TAB[:, 0], mod1[:, mchunk(0, 1):mchunk(0, 1) + 8, :], rT, [r("TAB")])
    cp("dve", TAB[:, 1], modT[:, mchunk(0, 0):mchunk(0, 0) + 8, :], rT, [r("TAB")])
    for kind, l, j in ((2, 0, 2), (5, 0, 5), (10, 1, 2), (13, 1, 5)):
        ts("dve", TAB[:, kind], mod1[:, mchunk(l, j):mchunk(l, j) + 8, :], 1.0 / ALPHA, None, ALU.mult, None, rT, [r("TAB")])

    def mkAB(kA, kB, g_, b_, sc_off, sh_off):
        tt("dve", TAB[:, kA], mod1[:, sc_off:sc_off + 8, :], bc5(g_), ALU.mult, rT + [r("TAB")], [r("TAB")])
        tt("dve", TAB[:, kB], mod1[:, sc_off:sc_off + 8, :], bc5(b_), ALU.mult, rT + [r("TAB")], [r("TAB")])
        tt("dve", TAB[:, kB], TAB[:, kB], modT[:, sh_off:sh_off + 8, :], ALU.add, rT + [r("TAB")], [r("TAB")])

    mkAB(3, 4, gam(0, 0), bet(0, 0), mchunk(0, 4), mchunk(0, 3))
    mkAB(6, 7, gam(0, 1), bet(0, 1), mchunk(1, 1), mchunk(1, 0))
    mkAB(8, 9, gam(0, 1), bet(0, 1), 96 + 8, 96)
    mkAB(11, 12, gam(1, 0), bet(1, 0), mchunk(1, 4), mchunk(1, 3))

    def tab(kind, c, seq):
        return TAB[:, kind, c, seq:seq + 1]

    bst = [hT32[:, 0:2048].rearrange("p (h q) -> p h q", h=16), hT32[:, 2048:4096].rearrange("p (h q) -> p h q", h=16)]
    S.barrier()
    S.dma("sp", bst[0], dap(rel_rev, 1, [[1, 128], [385, 16], [1, 128]]), writes=[stg32_r[0]])
    S.dma("sp", bst[1], dap(rel_rev, 129, [[1, 128], [385, 16], [1, 128]]), writes=[stg32_r[0]])
    S.dma("sp", chv[:], dap(rel_rev, 128, [[0, 128], [385, 16]]), writes=[r("chv")], allow_slow_non_contiguous=True)

    def flipped(a):
        return bass.AP(tensor=a.tensor, offset=a.offset + 127, ap=[list(a.ap[0]), list(a.ap[1]), [-1, 128]])

    chb = chv[:].unsqueeze(2).broadcast_to([128, 16, 128])
    tt("dve", bias3[:], flipped(bst[0]), chb, ALU.subtract, [stg32_r[0], r("chv")], [r("bias3")])
    tt("dve", bst[1], bst[1], flipped(chb) if False else chb, ALU.subtract, [stg32_r[0], r("chv")], [stg32_r[0]])
    m4b = bass.AP(tensor=mask4[:].tensor, offset=mask4[:].offset, ap=[list(mask4[:].ap[0]), [0, 16], [1, 128]])
    tt("dve", bias4[:], flipped(bst[1]), m4b, ALU.add, [stg32_r[0], r("mask4")], [r("bias4")])
    S.barrier()

    def ckpt(n):
        if stage <= n:
            raise _Stop()

    plist = panel_list()
    r_wsc = S.res("wsc")
    casteng = ["dve", "act", "pool"]
    for pi, (wn, row0, col0, ncols, kind) in enumerate(plist):
        sl = pi % 2
        ws = pi % NSLOT
        S.dma("sp", stg32[sl], dap(W[wn], row0 * ncols + col0, [[ncols, 128], [128 * ncols, 8], [1, 512]]),
              writes=[stg32_r[sl]])
        eng = casteng[pi % 3]
        if kind == "qperm":
            for k in range(2):
                o_ = wslot[ws][:].rearrange("p c (g k d) -> p c g k d", g=4, k=2)[:, :, :, k, :]
                i_ = stg32[sl].rearrange("p c (k g d) -> p c k g d", k=2, g=4)[:, :, k, :, :]
                cp("dve", o_, i_, [stg32_r[sl]], [wslot_r[ws]])
        else:
            cp(eng, wslot[ws][:], stg32[sl], [stg32_r[sl]], [wslot_r[ws]])
        S.dma("pool", wsc[pi], wslot[ws][:], reads=[wslot_r[ws]], writes=[r_wsc])
    S.barrier()

    S.op("dve", lambda e: e.memset(vaug[:], 1.0), [], r_vaug)
    S.op("dve", lambda e: e.memset(Vr[:], 1.0), [], r_Vr)
    S.op("dve", lambda e: e.memset(PT0[0][:], 0.0), [], [PT0_r[0]])
    S.op("dve", lambda e: e.memset(PT0[1][:], 0.0), [], [PT0_r[1]])
    S.op("dve", lambda e: e.memset(gsc[:], 0.0), [], [r("gsc")])

    tiles = []
    if do_sample:
        tiles.append(("s", 0))
    for ti in range(NT):
        tiles.append(("p", ti))
    uses = [pi for _ in tiles for pi in range(NPANEL)]
    wstate = {"next": 0, "cur": -1}

    def wnext():
        wstate["cur"] += 1
        i = wstate["cur"]
        while wstate["next"] <= min(i + PDEPTH, len(uses) - 1):
            n = wstate["next"]
            S.dma("sp", wslot[n % NSLOT][:], wsc[uses[n]], reads=[r_wsc], writes=[wslot_r[n % NSLOT]])
            wstate["next"] += 1
        return wslot[i % NSLOT], wslot_r[i % NSLOT]

    bctr = {}

    def rot(lst, key=None):
        key = key or "k%d_%s" % (len(lst), str(lst[0])[:24])
        bctr[key] = bctr.get(key, -1) + 1
        return lst[bctr[key] % len(lst)]

    scrctr = [0]

    def nscr():
        scrctr[0] += 1
        i = scrctr[0] % 4
        return scr[i], scr_r[i]

    def proj_a(ws, wr_, nchunk, rhs_buf, g_rhs, T, evac, banks):
        for j in range(nchunk):
            bi = rot(banks)
            for kc in range(8):
                mm(bank(bi)[:, 0:T], ws[:, kc, j * 128:(j + 1) * 128], rhs_buf[:, kc, 0:T], kc == 0, kc == 7,
                   [wr_] + gsel(g_rhs, [kc], range(4)), [bankres[bi]], sig=(kc == 7))
            evac(j, bi)

    def ln_block(L, b, cs, eps, targets, final_dst=None, nbk=4):
        pz = (b % 2)
        pbk = 2 + (b % 2)
        zp = pair(pz)
        zres = [bankres[2 * pz], bankres[2 * pz + 1]]
        for c in range(8):
            tr(zp[0:L, c * 128:(c + 1) * 128], xT[:, c, cs], ident[:, :], [g_xT[c][b], r("ident")], zres, sig=(c == 7))
        S.op("dve", lambda e: e.bn_stats(out=lst[0:L, 0, :], in_=zp[0:L, 0:512]), zres, [r("lst")])
        S.op("dve", lambda e: e.bn_stats(out=lst[0:L, 1, :], in_=zp[0:L, 512:1024]), zres, [r("lst")])
        S.op("dve", lambda e: e.bn_aggr(out=lmv[0:L, 0:2], in_=lst[0:L].rearrange("p a b -> p (a b)")), [r("lst")], [r("lmv")])
        ts("dve", lmv[0:L, 2:3], lmv[0:L, 1:2], eps, None, ALU.add, None, [r("lmv")], [r("lmv")])
        act(lmv[0:L, 3:4], lmv[0:L, 2:3], AF.Sqrt, [r("lmv")], [r("lmv")])
        recip(lmv[0:L, 4:5], lmv[0:L, 3:4], [r("lmv")], [r("lmv")])
        stt(lmv[0:L, 5:6], lmv[0:L, 0:1], -1.0, lmv[0:L, 4:5], ALU.mult, ALU.mult, [r("lmv")], [r("lmv")])
        xh, xh_r = nscr()
        act(xh[0:L, :], zp[0:L, :], AF.Identity, zres + [r("lmv")], [xh_r], bias=lmv[0:L, 5:6], scale=lmv[0:L, 4:5])
        return xh, xh_r

    def back_block(L, b, cs, xh, xh_r, gb, targets):
        pbk = 2 + (b % 2)
        bp = pair(pbk)
        bres = [bankres[2 * pbk], bankres[2 * pbk + 1]]
        for c in range(8):
            tr(bp[:, c * 128:c * 128 + L], xh[0:L, c * 128:(c + 1) * 128], ident[0:L, 0:L], [xh_r, r("ident")], bres,
               sig=(c == 7))
        for c in range(8):
            src = bp[:, c * 128:c * 128 + L]
            eng = "dve" if c < 4 else "act"
            br1 = [bres[c // 4]]
            if gb is None:
                cp(eng, xT[:, c, cs], src, br1, [g_xT[c][b]])
            else:
                aff(eng, xT[:, c, cs], src, gb[0][:, c:c + 1], gb[1][:, c:c + 1], br1 + [r("vecT")], [g_xT[c][b]])
            for (buf, g_, kA, kB, seq) in targets:
                aff(eng, buf[:, c, cs], src, tab(kA, c, seq), tab(kB, c, seq), br1 + [r("TAB")], [g_[c][b]])

    def ln_stage(L, nb, seqs, prompt, ti, mode, gb, targets):
        BLK = [0, 1, 2, 3]
        xh = {}
        if mode == "xload":
            for b in BLK:
                xs, xs_r = nscr()
                src = x_p[ti * 512 + b * 128: ti * 512 + (b + 1) * 128, :] if prompt else x_s[b * 32:(b + 1) * 32, :]
                S.dma(XQ, xs[0:L, :], src, writes=[xs_r])
                xh[b] = (xs, xs_r)
        else:
            zps = {}
            for b in BLK:
                cs = slice(b * L, (b + 1) * L)
                zp = pair(b)
                zres = [bankres[2 * b], bankres[2 * b + 1]]
                zps[b] = (zp, zres)
                for c in range(8):
                    tr(zp[0:L, c * 128:(c + 1) * 128], xT[:, c, cs], ident[:, :], [g_xT[c][b], r("ident")], zres, sig=(c == 7))
            for b in BLK:
                zp, zres = zps[b]
                rl = [r(f"lmv{b}")]
                S.op("dve", lambda e, b=b, zp=zp: e.bn_stats(out=lst[0:L, b, 0, :], in_=zp[0:L, 0:512]), zres, rl)
                S.op("dve", lambda e, b=b, zp=zp: e.bn_stats(out=lst[0:L, b, 1, :], in_=zp[0:L, 512:1024]), zres, rl)
                S.op("dve", lambda e, b=b: e.bn_aggr(out=lmv[0:L, b, 0:2], in_=lst[0:L, b].rearrange("p a b -> p (a b)")), rl, rl)
            for b in BLK:
                rl = [r(f"lmv{b}")]
                act(lmv[0:L, b, 3:4], lmv[0:L, b, 1:2], AF.Sqrt, rl + [r("epsc")], rl, bias=epsc[0:L, 0:1])
            for b in BLK:
                rl = [r(f"lmv{b}")]
                recip(lmv[0:L, b, 4:5], lmv[0:L, b, 3:4], rl, rl)
                stt(lmv[0:L, b, 5:6], lmv[0:L, b, 0:1], -1.0, lmv[0:L, b, 4:5], ALU.mult, ALU.mult, rl, rl)
            for b in BLK:
                zp, zres = zps[b]
                xs, xs_r = nscr()
                act(xs[0:L, :], zp[0:L, :], AF.Identity, zres + [r(f"lmv{b}")], [xs_r], bias=lmv[0:L, b, 5:6], scale=lmv[0:L, b, 4:5])
                xh[b] = (xs, xs_r)
        if mode == "final":
            for b in BLK:
                xs, xs_r = xh[b]
                tt("dve" if b % 2 == 0 else "pool", xs[0:L, :], xs[0:L, :], lnfbc[0:L, 0, :], ALU.mult, [xs_r, r("lnfbc")], [xs_r])
            for b in BLK:
                xs, xs_r = xh[b]
                tt("dve", xs[0:L, :], xs[0:L, :], lnfbc[0:L, 1, :], ALU.add, [xs_r, r("lnfbc")], [xs_r])
            for b in BLK:
                xs, xs_r = xh[b]
                dst = y_p[ti * 512 + b * 128: ti * 512 + (b + 1) * 128, :] if prompt else y_s[b * 32:(b + 1) * 32, :]
                S.dma("sp", dst, xs[0:L, :], reads=[xs_r], writes=[r(f"o_y{b}")])
            return
        for b in BLK:
            xs, xs_r = xh[b]
            g = b // 2
            for c in range(8):
                o_ = (c % 4) * 256 + (b % 2) * 128
                tr(PP[2 * g + c // 4][:, o_:o_ + L], xs[0:L, c * 128:(c + 1) * 128], ident[0:L, 0:L], [xs_r, r("ident")],
                   [bankres[4 * g + c // 2]], sig=(c == 7))
        for pas in (0, 1):
            for g in range(2):
                grp = [2 * g, 2 * g + 1]
                for c in range(8):
                    bk = 4 * g + c // 2
                    eng = "dve" if (c // 2) % 2 == 0 else "act"
                    if prompt:
                        src = PP[2 * g + c // 4][:, (c % 4) * 256:(c % 4) * 256 + 256]
                        cs2 = slice(g * 256, g * 256 + 256)
                        if pas == 1:
                            wr_x = gsel(g_xT, [c], grp)
                            if gb is None:
                                cp(eng, xT[:, c, cs2], src, [bankres[bk]], wr_x)
                            else:
                                aff(eng, xT[:, c, cs2], src, gb[0][:, c:c + 1], gb[1][:, c:c + 1], [bankres[bk], r("vecT")], wr_x)
                        else:
                            for (buf, g_, kA, kB) in targets:
                                aff(eng, buf[:, c, cs2], src, tab(kA, c, 0), tab(kB, c, 0), [bankres[bk], r("TAB")], gsel(g_, [c], grp))
                    else:
                        for b in grp:
                            o_ = (c % 4) * 256 + (b % 2) * 128
                            src = PP[2 * g + c // 4][:, o_:o_ + L]
                            cs = slice(b * L, (b + 1) * L)
                            if pas == 1:
                                if gb is None:
                                    cp(eng, xT[:, c, cs], src, [bankres[bk]], [g_xT[c][b]])
                                else:
                                    aff(eng, xT[:, c, cs], src, gb[0][:, c:c + 1], gb[1][:, c:c + 1], [bankres[bk], r("vecT")], [g_xT[c][b]])
                            else:
                                for (buf, g_, kA, kB) in targets:
                                    aff(eng, buf[:, c, cs], src, tab(kA, c, seqs[b]), tab(kB, c, seqs[b]), [bankres[bk], r("TAB")], [g_[c][b]])

    def resid_evac(L, nb, seqs, kind, same):
        def ev(c, bi):
            if same:
                T = nb * L
                stt(xT[:, c, 0:T], bank(bi)[:, 0:T], tab(kind, c, seqs[0]), xT[:, c, 0:T], ALU.mult, ALU.add,
                    [bankres[bi], r("TAB")] + gsel(g_xT, [c], range(nb)), gsel(g_xT, [c], range(nb)))
            else:
                for b in range(nb):
                    cs = slice(b * L, (b + 1) * L)
                    stt(xT[:, c, cs], bank(bi)[:, cs], tab(kind, c, seqs[b]), xT[:, c, cs], ALU.mult, ALU.add,
                        [bankres[bi], r("TAB"), g_xT[c][b]], [g_xT[c][b]])
        return ev

    def do_tile(kind, ti):
        prompt = (kind == "p")
        nb = 4
        L = 128 if prompt else 32
        T = nb * L
        seqs = [0, 0, 0, 0] if prompt else [1, 2, 3, 4]
        last = prompt and ti == NT - 1
        first = prompt and ti == 0
        BL = range(nb)
        CS = [slice(b * L, (b + 1) * L) for b in BL]

        ln_stage(L, nb, seqs, prompt, ti, "xload", None, [(uT, g_uT, 0, 1)])

        ckpt(2)
        def ev_qk(base):
            def ev(j, bi):
                c = base + j
                cpa(cb[:, c, 0:nb, 3:3 + L], bank(bi)[:, 0:T].rearrange("p (b t) -> p b t", b=nb),
                    [bankres[bi]], gsel(g_cb, [c], BL))
            return ev
        rg = [r("G")]
        rl_, rn_ = rtmp_r[0], rtmp_r[1]

        def gates_a():
            for gi in range(2):
                for kc in range(8):
                    mm(bank(6 + gi)[0:4, 0:T], wg[:, kc, gi * 4:gi * 4 + 4], uT[:, kc, 0:T], kc == 0, kc == 7,
                       [r("wg")] + gsel(g_uT, [kc], BL), [bankres[6 + gi]], sig=(kc == 7))
            cp("act", G_ig[:, 0:T], bank(6)[0:4, 0:T], [bankres[6]], [expin_r[1]])
            cp("act", G_fg[:, 0:T], bank(7)[0:4, 0:T], [bankres[7]], [expin_r[0]])
            rg = [r("G")]
            rl_, rn_ = rtmp_r[0], rtmp_r[1]
            if not prompt:
                S.dma(XQ, mcol[:, 0:4], smT[:, :], writes=[r("mcol")])
            elif first:
                S.op("dve", lambda e: e.memset(mcol[:], 0.0), [], [r("mcol")])
            act(G_l[:, 0:T], G_fg[:, 0:T], AF.Exp, [expin_r[0], r("bg")], [rl_], bias=bg[:, 1:2], scale=-1.0)
            act(G_l[:, 0:T], G_l[:, 0:T], AF.Ln, [rl_], [rl_], bias=1.0)
            for b in BL:
                S.op("dve", lambda e, b=b: e.tensor_tensor_scan(out=G_nb[:, CS[b]], data0=ones4[:, 0:L], data1=G_l[:, CS[b]],
                                                                initial=0.0, op0=ALU.mult, op1=ALU.add), [rl_, r("ones4")], [rn_])
            stt(G_g[:, 0:T], G_ig[:, 0:T], bg[:, 0:1], G_nb[:, 0:T], ALU.add, ALU.add, [expin_r[1], r("bg"), rn_], rg)
            S.op("dve", lambda e: e.tensor_reduce(out=G_s[:, 0, 0:nb], in_=G_g[:, 0:T].rearrange("p (b t) -> p b t", b=nb),
                                                  axis=AX.X, op=ALU.max), rg, rg)
            for b in BL:
                if prompt:
                    mp_ = mcol[:, 0:1] if b == 0 else G_s[:, 4, b - 1:b]
                else:
                    mp_ = mcol[:, b:b + 1]
                cp("dve", G_s[:, 3, b:b + 1], mp_, rg + [r("mcol")], rg)
                tt("dve", G_s[:, 1, b:b + 1], G_s[:, 0, b:b + 1], mp_, ALU.max, rg + [r("mcol")], rg)
                tt("dve", G_s[:, 4, b:b + 1], G_s[:, 1, b:b + 1], G_nb[:, (b + 1) * L - 1:(b + 1) * L], ALU.subtract, rg + [rn_], rg)
            ts("dve", G_s[:, 2, 0:nb], G_s[:, 1, 0:nb], -1.0, None, ALU.mult, None, rg, rg)
            tt("dve", G_s[:, 5, 0:nb], G_s[:, 3, 0:nb], G_s[:, 1, 0:nb], ALU.subtract, rg, rg)

        def gates_b():
            act(G_s[:, 6, 0:nb], G_s[:, 5, 0:nb], AF.Exp, rg, rg)
            ngb = G_s[:, 2, 0:nb].unsqueeze(2).broadcast_to([4, nb, L])
            tt("dve", G_l[:, 0:T].rearrange("p (b t) -> p b t", b=nb), G_g[:, 0:T].rearrange("p (b t) -> p b t", b=nb), ngb,
               ALU.add, rg + [rl_], [rl_])
            act(G_l[:, 0:T], G_l[:, 0:T], AF.Exp, [rl_], [rl_])
            tt("dve", G_g[:, 0:T].rearrange("p (b t) -> p b t", b=nb), G_nb[:, 0:T].rearrange("p (b t) -> p b t", b=nb), ngb,
               ALU.add, rg + [rn_], rg)
            act(G_g[:, 0:T], G_g[:, 0:T], AF.Exp, rg, rg)
            for b in BL:
                ts("dve", G_d4[:, b, :], ident[0:4, 0:4], G_s[:, 6, b:b + 1], None, ALU.mult, None, rg + [r("ident")], rg)

        def gates_c():
            gb_ = 5
            for b in BL:
                tr(bank(gb_)[0:L, b * 16:b * 16 + 4], G_l[:, CS[b]], ident[0:4, 0:4], [rl_, r("ident")], [bankres[gb_]], sig=False)
                tr(bank(gb_)[0:L, b * 16 + 4:b * 16 + 8], G_g[:, CS[b]], ident[0:4, 0:4], rg + [r("ident")], [bankres[gb_]], sig=False)
                mm(bank(gb_)[:, b * 16 + 8:b * 16 + 12], ones4[:, :], G_d4[:, b, :], True, True, rg + [r("ones4")], [bankres[gb_]],
                   sig=(b == nb - 1))
            cp("dve", gsc[:, :, 0:12], bank(gb_)[:, 0:64].rearrange("p (b f) -> p b f", b=4)[:, :, 0:12], [bankres[gb_]], [r("gsc")])
            ts("dve", gsc[:, :, 12:16], gsc[:, :, 8:12], 128.0 ** -0.5, None, ALU.mult, None, [r("gsc")], [r("gsc")])
            if prompt:
                cp("dve", mcol[:, 0:1], G_s[:, 4, nb - 1:nb], rg, [r("mcol")])
            else:
                cp("dve", mout[:, 0:nb], G_s[:, 4, 0:nb], rg, [r("mout")])


        gates_a()
        for pj in range(2):
            ws, wr_ = wnext()
            proj_a(ws, wr_, 4, uT, g_uT, T, ev_qk(pj * 4), [0, 1, 2, 3])
            if pj == 0:
                gates_b()
        ckpt(3)
        for b in BL:
            if prompt:
                if b == 0:
                    if first:
                        S.op("dve", lambda e: e.memset(cb[:, :, 0, 0:3], 0.0), [], gsel(g_cb, ALLC, [0]))
                    else:
                        cp("dve", cb[:, :, 0, 0:3], cprev[:], [r("cprev")], gsel(g_cb, ALLC, [0]))
                else:
                    cp("dve", cb[:, :, b, 0:3], cb[:, :, b - 1, L:L + 3], gsel(g_cb, ALLC, [b - 1]), gsel(g_cb, ALLC, [b]))
            else:
                S.dma("act", cstage[:].rearrange("p j c -> p (j c)"),
                      dap(sconv, b * 3072, [[1, 128], [128, 24]]), writes=[r("cstage")],
                      allow_slow_non_contiguous=True)
                cp("dve", cb[:, :, b, 0:3], cstage[:].rearrange("p j c -> p c j"), [r("cstage")], gsel(g_cb, ALLC, [b]))
        for c in range(8):
            bi = rot([0, 1, 2, 3])
            for j in range(4):
                mm(bank(bi)[:, 0:T], diagc[:, c, j, :], cb[:, c, 0:nb, j:j + L], j == 0, j == 3,
                   [r("diagc")] + gsel(g_cb, [c], BL), [bankres[bi]], sig=(j == 3))
            act(bufC[:, c, 0:T], bank(bi)[:, 0:T], AF.Silu, [bankres[bi], r("vecT")], gsel(g_C, [c], BL),
                bias=vecT[:, 32 + c:33 + c])
        cp("dve", cprev[:], cb[:, :, nb - 1, L:L + 3], gsel(g_cb, ALLC, [nb - 1]), [r("cprev")])
        if last or not prompt:
            for b in ([nb - 1] if prompt else BL):
                cp("dve", cstage[:].rearrange("p j c -> p c j"), cb[:, :, b, L:L + 3], gsel(g_cb, ALLC, [b]), [r("cstage")])
                dst = dap(o_convp, 0, [[1, 128], [128, 24]]) if prompt else \
                    dap(o_convs, b * 3072, [[1, 128], [128, 24]])
                S.dma("pool", dst, cstage[:].rearrange("p j c -> p (j c)"), reads=[r("cstage")], writes=[r("o_conv")],
                      allow_slow_non_contiguous=True)

        gates_c()
        for pj in range(2):
            ws, wr_ = wnext()
            for b in BL:
                bi = rot([4, 5, 6, 7])
                for kc in range(8):
                    mm(bank(bi)[0:L, :], uT[:, kc, CS[b]], ws[:, kc, :], kc == 0, kc == 7,
                       [wr_, g_uT[kc][b]], [bankres[bi]], sig=(kc == 7))
                cpa(vaug[0:L, b, pj * 2:pj * 2 + 2, 0:256], bank(bi)[0:L, :].rearrange("p (h v) -> p h v", h=2),
                    [bankres[bi]], [r_vaug[b]])

        def ev_o(base):
            def ev(j, bi):
                c = base + j
                act(bufB[:, c, 0:T], bank(bi)[:, 0:T], AF.Sigmoid, [bankres[bi]], gsel(g_B, [c], BL))
            return ev
        for pj in range(2):
            ws, wr_ = wnext()
            proj_a(ws, wr_, 4, uT, g_uT, T, ev_o(pj * 4), [0, 1, 2, 3])
        ckpt(4)
        if first:
            S.op("dve", lambda e: e.memset(C_sb[:], 0.0), [], r_Csb)

        def part1(b, rCn):
            cs = CS[b]
            for h in range(4):
                mm(bank(0)[0:L, h * 128:h * 128 + L], bufC[:, 4 + h, cs], bufC[:, h, cs], True, True,
                   [g_C[4 + h][b], g_C[h][b]], [bankres[0]], sig=(h == 3))
            for h in range(4):
                stt(sTs[0:L, h, 0:L], bank(0)[0:L, h * 128:h * 128 + L], gsc[0:L, b, h:h + 1], trimask[0:L, 0:L],
                    ALU.mult, ALU.mult, [bankres[0], r("gsc"), r("trimask")], [r("sTs")])
            for h in range(4):
                tr(bankb(1)[0:L, h * 128:(h + 1) * 128], bufC[:, 4 + h, cs], identb[:, :], [g_C[4 + h][b], r("identb")],
                   [bankres[1]], sig=(h == 3))
            tt("dve", kw[0:L, :, :], bankb(1)[0:L, 0:512].rearrange("p (h d) -> p h d", h=4),
               gsc[0:L, b, 0:4].unsqueeze(2).broadcast_to([L, 4, 128]), ALU.mult, [bankres[1], r("gsc")], [r("kw")])
            for h in range(4):
                act(Cb[:, h, :], C_sb[:, h, :], AF.Identity, [r_Csb[h], r("gsc")] + rCn, [r_Cb[h]], scale=gsc[:, b, 12 + h:13 + h])

        for b in BL:
            cs = CS[b]
            if not prompt:
                stg, stg_r = nscr()
                S.dma("act", stg[:, :].rearrange("p (a d) -> p a d", a=8),
                      dap(sC, b * 4 * 256 * 128, [[128, 128], [128 * 128, 8], [1, 128]]), writes=[stg_r])
                for a in range(8):
                    tr(pair(0)[:, a * 128:(a + 1) * 128], stg[:, a * 128:(a + 1) * 128], ident[:, :],
                       [stg_r, r("ident")], [bankres[0], bankres[1]], sig=(a == 7))
                for h in range(4):
                    cp("dve" if h < 2 else "act", C_sb[:, h, 0:256], pair(0)[:, h * 256:(h + 1) * 256], [bankres[h // 2]], [r_Csb[h]])
                S.dma("act", C_sb[:, :, 256:257], dap(sn, b * 512, [[1, 128], [128, 4], [1, 1]]), writes=[r("Cn")],
                      reads=[], allow_slow_non_contiguous=True)
            rCn = [] if prompt else [r("Cn")]

            if (not prompt) or b == 0:
                part1(b, rCn)
            for h in range(4):
                nbk = 2 + h
                mm(bank(nbk)[0:L, 0:257], sTs[0:L, h, 0:L], vaug[0:L, b, h, :], True, False,
                   [r("sTs"), r_vaug[b]], [bankres[nbk]], sig=False)
                mm(bank(nbk)[0:L, 0:257], bufC[:, h, cs], Cb[:, h, :], False, True,
                   [g_C[h][b], r_Cb[h]], [bankres[nbk]])
            for h in range(4):
                dbk = 6 + (h % 2)
                mm(bank(dbk)[:, 0:257], kw[0:L, h, :], vaug[0:L, b, h, :], True, True, [r("kw"), r_vaug[b]], [bankres[dbk]])
                stt(C_sb[:, h, :], C_sb[:, h, :], gsc[:, b, 8 + h:9 + h], bank(dbk)[:, 0:257], ALU.mult, ALU.add,
                    [r_Csb[h], r("gsc"), bankres[dbk]] + rCn, [r_Csb[h]])
            rh = [r("hsm")]
            for h in range(4):
                nbk = 2 + h
                S.op("dve", lambda e, h=h, nbk=nbk: e.bn_stats(out=st6[0:L, h, :], in_=bank(nbk)[0:L, 0:256]),
                     [bankres[nbk]], [r("st6")])
                S.op("dve", lambda e, h=h: e.bn_aggr(out=mv[0:L, h, :], in_=st6[0:L, h, :]), [r("st6")], [r("mv")])
                cp("dve", hsm[0:L, 0, h:h + 1], bank(nbk)[0:L, 256:257], [bankres[nbk]], rh)
            stt(hsm[0:L, 1, :], hsm[0:L, 0, :], -1.0, hsm[0:L, 0, :], ALU.mult, ALU.max, rh, rh)
            tt("dve", hsm[0:L, 1, :], hsm[0:L, 1, :], gsc[0:L, b, 4:8], ALU.max, rh + [r("gsc")], rh)
            recip(hsm[0:L, 2, :], hsm[0:L, 1, :], rh, rh)
            tt("dve", hsm[0:L, 3, :], hsm[0:L, 2, :], hsm[0:L, 2, :], ALU.mult, rh, rh)
            tt("dve", hsm[0:L, 3, :], hsm[0:L, 3, :], mv[0:L, :, 1], ALU.mult, rh + [r("mv")], rh)
            ts("dve", hsm[0:L, 3, :], hsm[0:L, 3, :], HN_EPS, None, ALU.add, None, rh, rh)
            act(hsm[0:L, 4, :], hsm[0:L, 3, :], AF.Sqrt, rh, rh)
            recip(hsm[0:L, 5, :], hsm[0:L, 4, :], rh, rh)
            tt("dve", hsm[0:L, 6, :], hsm[0:L, 2, :], hsm[0:L, 5, :], ALU.mult, rh, rh)
            stt(hsm[0:L, 7, :], mv[0:L, :, 0], -1.0, hsm[0:L, 6, :], ALU.mult, ALU.mult, rh + [r("mv")], rh)
            for h in range(4):
                nbk = 2 + h
                act(hn[0:L, h, :], bank(nbk)[0:L, 0:256], AF.Identity, [bankres[nbk]] + rh, [r("hn")],
                    bias=hsm[0:L, 7, h:h + 1], scale=hsm[0:L, 6, h:h + 1])
            if prompt and b + 1 < nb:
                part1(b + 1, rCn)
            for j in range(8):
                tr(bankb(1)[:, j * 128:j * 128 + L], hn[0:L, j // 2, (j % 2) * 128:(j % 2) * 128 + 128], identb[0:L, 0:L],
                   [r("hn"), r("identb")], [bankres[1]], sig=(j == 7))
            for j in range(8):
                stt(bufA[:, j, cs], bankb(1)[:, j * 128:j * 128 + L], vecT[:, 104 + j:105 + j], bufB[:, j, cs],
                    ALU.mult, ALU.mult, [bankres[1], r("vecT"), g_B[j][b]], [g_A[j][b]])

            if (last and b == nb - 1) or not prompt:
                for h in range(4):
                    for vh in range(2):
                        a = h * 2 + vh
                        tr(pair(0)[:, a * 128:(a + 1) * 128], C_sb[:, h, vh * 128:(vh + 1) * 128], ident[:, :],
                           [r_Csb[h], r("ident")] + rCn, [bankres[0], bankres[1]], sig=(a == 7))
                stg, stg_r = nscr()
                cp("dve", stg[:, :], pair(0)[:, :], [bankres[0], bankres[1]], [stg_r])
                dC = dap(o_Cp, 0, [[128, 128], [128 * 128, 8], [1, 128]]) if prompt else \
                    dap(o_Cs, b * 4 * 256 * 128, [[128, 128], [128 * 128, 8], [1, 128]])
                S.dma("pool", dC, stg[:, :].rearrange("p (a d) -> p a d", a=8), reads=[stg_r], writes=[r("o_C")])
                dn = dap(o_np, 0, [[1, 128], [128, 4], [1, 1]]) if prompt else dap(o_ns, b * 512, [[1, 128], [128, 4], [1, 1]])
                S.dma("pool", dn, C_sb[:, :, 256:257], reads=r_Csb + rCn, writes=[r("o_n")], allow_slow_non_contiguous=True)
                if prompt:
                    S.dma("pool", o_mp[:, :], mcol[:, 0:1], reads=[r("mcol")], writes=[r("o_m")])
                else:
                    S.dma("pool", o_ms[b], mout[:, b:b + 1], reads=[r("mout")], writes=[r("o_m")])

        ckpt(5)
        same = prompt
        for pj in range(2):
            ws, wr_ = wnext()
            evr = resid_evac(L, nb, seqs, 2, same)
            proj_a(ws, wr_, 4, bufA, g_A, T, (lambda j, bi, pj=pj, evr=evr: evr(pj * 4 + j, bi)), [0, 1, 2, 3])
        ln_stage(L, nb, seqs, prompt, ti, "ln", (gam(0, 0), bet(0, 0)), [(uT, g_uT, 3, 4)])

        def mlp(l, kind_s):
            for pj in range(8):
                ws, wr_ = wnext()

                def ev(j, bi, pj=pj):
                    c = pj * 4 + j
                    rt, rt_r = rot(list(zip(rtmp, rtmp_r)), "rtmp")
                    act(rt[:, 0:T], bank(bi)[:, 0:T], AF.Relu, [bankres[bi]], [rt_r])
                    tt(SQE, hT[:, c, 0:T], rt[:, 0:T], rt[:, 0:T], ALU.mult, [rt_r], [r_hT[c]])
                proj_a(ws, wr_, 4, uT, g_uT, T, ev, [0, 1, 2, 3, 4, 5, 6, 7])
            evr = resid_evac(L, nb, seqs, kind_s, same)
            for nh in range(2):
                bks = [nh * 4 + j for j in range(4)]
                for kg in range(4):
                    ws, wr_ = wnext()
                    for j in range(4):
                        for kc in range(8):
                            mm(bank(bks[j])[:, 0:T], ws[:, kc, j * 128:(j + 1) * 128], hT[:, kg * 8 + kc, 0:T],
                               kg == 0 and kc == 0, kg == 3 and kc == 7, [wr_, r_hT[kg * 8 + kc]], [bankres[bks[j]]],
                               sig=(kc == 7))
                for j in range(4):
                    evr(nh * 4 + j, bks[j])

        ckpt(6)
        mlp(0, 5)
        ckpt(7)
        ln_stage(L, nb, seqs, prompt, ti, "ln", (gam(0, 1), bet(0, 1)), [(uT, g_uT, 6, 7), (bufA, g_A, 8, 9)])

        ckpt(8)
        ws, wr_ = wnext()
        if prompt:
            kcol0 = (ti % 2) * 512
            ringb = [(ti % 2) * 4 + b for b in BL]
        else:
            kcol0 = 0
            ringb = [0, 1, 2, 3]

        def ev_k(j, bi):
            cpa(KTr[:, j, kcol0:kcol0 + T], bank(bi)[:, 0:T], [bankres[bi]],
                [r_KTr[x] for x in (ringb if prompt else [0])])
        proj_a(ws, wr_, 2, bufA, g_A, T, ev_k, [0, 1])
        for b in BL:
            bi = rot([2, 3])
            for kc in range(8):
                mm(bank(bi)[0:L, :], bufA[:, kc, CS[b]], ws[:, kc, :], kc == 0, kc == 7, [wr_, g_A[kc][b]], [bankres[bi]],
                   sig=(kc == 7))
            cpa(Vr[0:L, ringb[b], :, 0:64], bank(bi)[0:L, 256:512].rearrange("p (h d) -> p h d", h=4),
                [bankres[bi]], [r_Vr[ringb[b]]])
            if last or not prompt:
                stg, stg_r = nscr()
                cp("act", stg[0:L, 0:512], bank(bi)[0:L, :], [bankres[bi]], [stg_r])
                if prompt:
                    dk_, dv_ = o_kp[b * 128:(b + 1) * 128, :], o_vp[b * 128:(b + 1) * 128, :]
                else:
                    dk_, dv_ = o_ks[b * 32:(b + 1) * 32, :], o_vs[b * 32:(b + 1) * 32, :]
                S.dma("pool", dk_, stg[0:L, 0:256], reads=[stg_r], writes=[r("o_k")])
                S.dma("pool", dv_, stg[0:L, 256:512], reads=[stg_r], writes=[r("o_v")])

        def ev_q(base):
            def ev(j, bi):
                cpa(bufC[:, base + j, 0:T], bank(bi)[:, 0:T], [bankres[bi]], gsel(g_C, [base + j], BL))
            return ev
        for pj in range(2):
            ws, wr_ = wnext()
            proj_a(ws, wr_, 4, uT, g_uT, T, ev_q(pj * 4), [4, 5, 6, 7])

        ckpt(9)
        def keyblocks_for(b):
            kbs = []
            if prompt:
                Bg = ti * 4 + b
                for j in range(5):
                    KB = Bg - 4 + j
                    if KB < 0:
                        continue
                    pos = KB % 8
                    kd = {0: "mask0", 1: "plain", 2: "plain", 3: "b3", 4: "b4"}[j]
                    kbs.append((
                        (lambda kk, rows, pos=pos: KTr[rows, kk, pos * 128:(pos + 1) * 128]),
                        (lambda kh, pos=pos: Vr[:, pos, kh, :]), 128, kd, [r_KTr[pos], r_Vr[pos]]))
            else:
                rc = r_KTr[4:8] + r_Vr[4:8]
                for j in range(4):
                    kbs.append((
                        (lambda kk, rows, j=j: KTc[rows, kk, j * 128:(j + 1) * 128]),
                        (lambda kh, j=j: Vc[:, j, kh, :]), 128, "b3" if j == 3 else "plain", rc))
                kbs.append((
                    (lambda kk, rows, b=b: KTr[rows, kk, b * 32:(b + 1) * 32]),
                    (lambda kh, b=b: Vr[0:32, b, kh, :]), 32, "b4", [r_KTr[0], r_Vr[b]]))
            return kbs

        def load_cache(b):
            stg, stg_r = nscr()
            S.dma(XQ, stg[:, :].rearrange("p (j f) -> p j f", j=4),
                  dap(ck, b * 512 * 256, [[256, 128], [128 * 256, 4], [1, 256]]), writes=[stg_r])
            for j in range(4):
                for kk in range(2):
                    a = j * 2 + kk
                    tr(pair(0)[:, a * 128:(a + 1) * 128], stg[:, j * 256 + kk * 128:j * 256 + (kk + 1) * 128], ident[:, :],
                       [stg_r, r("ident")], [bankres[0], bankres[1]], sig=(a == 7))
            for kk in range(2):
                cp("dve", KTc[:, kk, :].rearrange("p (j s) -> p j s", j=4),
                   pair(0)[:, :].rearrange("p (j k s) -> p j k s", j=4, k=2)[:, :, kk, :],
                   [bankres[0], bankres[1]], r_KTr[4:8])
            stg2, stg2_r = nscr()
            S.dma(XQ, stg2[:, :].rearrange("p (j f) -> p j f", j=4),
                  dap(cv, b * 512 * 256, [[256, 128], [128 * 256, 4], [1, 256]]), writes=[stg2_r])
            cp("dve", Vc[:, :, :, 0:64], stg2[:, :].rearrange("p (j h d) -> p j h d", j=4, h=4), [stg2_r], r_Vr[4:8])

        uctr = [0]

        def emit_st(b, kh, kbs):
            cs = CS[b]
            Lq = L
            kk, e_ = kh // 2, kh % 2
            rows = slice(e_ * 64, (e_ + 1) * 64)
            uctr[0] += 1
            pts = []
            for (ktf, vf, nk, kd, rds) in kbs:
                sbk = rot([0, 1, 2, 3])
                sps = bank(sbk)[0:nk, 0:4 * Lq]
                mm(sps, ktf(kk, rows), bufC[rows, kk * 4:(kk + 1) * 4, cs], True, True,
                   rds + gsel(g_C, range(kk * 4, kk * 4 + 4), [b]), [bankres[sbk]])
                if kd == "mask0":
                    pt, pt_r = PT0[uctr[0] % 2], PT0_r[uctr[0] % 2]
                    s3 = sps.rearrange("p (g q) -> p g q", g=4)
                    p3 = pt[0:nk, 0:4 * Lq].rearrange("p (g q) -> p g q", g=4)
                    act(p3[:, :, 0:64], s3[:, :, 0:64], AF.Exp, [bankres[sbk]], [pt_r], scale=0.125)
                    act(p3[64:128, :, 64:128], s3[64:128, :, 64:128], AF.Exp, [bankres[sbk]], [pt_r], scale=0.125)
                else:
                    pt, pt_r = rot(list(zip(PT, PT_r)), "PT")
                    if kd == "plain":
                        act(pt[0:nk, 0:4 * Lq], sps, AF.Exp, [bankres[sbk]], [pt_r], scale=0.125)
                    else:
                        bt_ = bias3 if kd == "b3" else bias4
                        ei, ei_r = rot(list(zip(expin, expin_r)), "expin")
                        stt(ei[0:nk, 0:4 * Lq].rearrange("p (g q) -> p g q", g=4),
                            sps.rearrange("p (g q) -> p g q", g=4), 0.125,
                            bt_[0:nk, kh * 4:(kh + 1) * 4, 0:Lq], ALU.mult, ALU.add,
                            [bankres[sbk], r("bias3"), r("bias4")], [ei_r])
                        act(pt[0:nk, 0:4 * Lq], ei[0:nk, 0:4 * Lq], AF.Exp, [ei_r], [pt_r])
                pts.append((pt, pt_r))
            return pts

        def emit_pv(b, kh, kbs, pts):
            Lq = L
            obk = 4 + kh
            nkb = len(kbs)
            for g in range(4):
                for ki, (ktf, vf, nk, kd, rds) in enumerate(kbs):
                    pt, pt_r = pts[ki]
                    mm(bank(obk)[0:Lq, g * 65:(g + 1) * 65], pt[0:nk, g * Lq:(g + 1) * Lq], vf(kh)[0:nk, :],
                       ki == 0, ki == nkb - 1, [pt_r] + rds, [bankres[obk]], sig=(ki == nkb - 1))

        def emit_fin(b):
            cs = CS[b]
            Lq = L
            for kh in range(4):
                obk = 4 + kh
                o3 = bank(obk)[0:Lq, 0:260].rearrange("p (g d) -> p g d", g=4)
                recip(rden[0:Lq, kh * 4:(kh + 1) * 4], o3[:, :, 64], [bankres[obk]], [r("rden")])
                tt("dve", On[0:Lq, kh * 4:(kh + 1) * 4, :], o3[:, :, 0:64],
                   rden[0:Lq, kh * 4:(kh + 1) * 4].unsqueeze(2).broadcast_to([Lq, 4, 64]), ALU.mult,
                   [bankres[obk], r("rden")], [r("On")])
            for c in range(8):
                tr(bankb(0)[:, c * 128:c * 128 + Lq], On[0:Lq, 2 * c:2 * c + 2, :].rearrange("p h d -> p (h d)"),
                   identb[0:Lq, 0:Lq], [r("On"), r("identb")], [bankres[0]], sig=(c == 7))
            cpa(bufB[:, :, cs], bankb(0)[:, 0:1024].rearrange("p (c q) -> p c q", c=8)[:, :, 0:Lq], [bankres[0]],
                gsel(g_B, ALLC, [b]))

        if prompt:
            units = [(b, kh) for b in BL for kh in range(4)]
            kbl = {b: keyblocks_for(b) for b in BL}
            nxt = emit_st(units[0][0], units[0][1], kbl[units[0][0]])
            for ui, (b, kh) in enumerate(units):
                cur = nxt
                if ui + 1 < len(units):
                    b2, kh2 = units[ui + 1]
                    nxt = emit_st(b2, kh2, kbl[b2])
                emit_pv(b, kh, kbl[b], cur)
                if kh == 3:
                    emit_fin(b)
        else:
            for b in BL:
                load_cache(b)
                kbs = keyblocks_for(b)
                for kh in range(4):
                    pts = emit_st(b, kh, kbs)
                    emit_pv(b, kh, kbs, pts)
                emit_fin(b)

        ckpt(10)
        for pj in range(2):
            ws, wr_ = wnext()
            evr = resid_evac(L, nb, seqs, 10, same)
            proj_a(ws, wr_, 4, bufB, g_B, T, (lambda j, bi, pj=pj, evr=evr: evr(pj * 4 + j, bi)), [0, 1, 2, 3])
        ln_stage(L, nb, seqs, prompt, ti, "ln", (gam(1, 0), bet(1, 0)), [(uT, g_uT, 11, 12)])
        mlp(1, 13)
        ln_stage(L, nb, seqs, prompt, ti, "final", None, [])

    try:
        ckpt(1)
        for (kind, ti) in tiles:
            do_tile(kind, ti)
    except _Stop:
        pass

    print('sbuf bytes remaining', nc.sbuf_bytes_remaining, flush=True)
    S.final_drain("sp")
    S.emit(nc)
    st.close()
    return nc


_CACHE = {}


def _consts():
    ident = np.eye(128, dtype=np.float32)
    tri = (np.arange(128)[:, None] <= np.arange(128)[None, :]).astype(np.float32) * np.float32(128.0 ** -0.5)
    m4 = np.zeros((128, 128), np.float32)
    m4[64:, :64] = NEG
    return np.ascontiguousarray(np.stack([ident, tri, m4], axis=1))


def make_in_maps(inp, NT, ncores=8):
    f = lambda a: np.ascontiguousarray(np.asarray(a, dtype=np.float32))
    rel = f(inp["rel_bias_b"])[0]
    ext = np.concatenate([rel, np.repeat(rel[:, 256:257], 128, axis=1)], axis=1)
    rel_rev = np.ascontiguousarray(ext[:, ::-1])
    vec1 = np.concatenate([f(inp["conv_w_a"])[0].reshape(32, 128), f(inp["conv_b_a"])[0].reshape(8, 128),
                           f(inp["ln_g"]).reshape(32, 128), f(inp["ln_b"]).reshape(32, 128),
                           f(inp["mhn_g_a"])[0].reshape(8, 128)], axis=0)
    vec2 = np.concatenate([f(inp["b_ada"]).reshape(96, 128), f(inp["b_ada_kv"]).reshape(16, 128)], axis=0)
    lnf = np.stack([f(inp["ln_g"])[1, 1], f(inp["ln_b"])[1, 1]], axis=0)
    shared = {
        "w_in": f(inp["w_in_a"])[0], "w_out_a": f(inp["w_out_a"])[0],
        "w_up0": f(inp["w_up"])[0], "w_up1": f(inp["w_up"])[1],
        "w_down0": f(inp["w_down"])[0], "w_down1": f(inp["w_down"])[1],
        "w_kv": f(inp["w_kv"]), "w_q": f(inp["w_q_b"])[0], "w_out_b": f(inp["w_out_b"])[0],
        "w_ada": f(inp["w_ada"]), "w_ada_kv": f(inp["w_ada_kv"]),
        "vec1": np.ascontiguousarray(vec1), "vec2": np.ascontiguousarray(vec2),
        "b_if": f(inp["b_if_a"]).reshape(8, 1), "lnf": np.ascontiguousarray(lnf),
        "rel_rev": rel_rev, "cst": _consts(),
    }
    maps = []
    xp, xs = f(inp["x_prompt"]), f(inp["x_sample"])
    for i in range(ncores):
        s0 = 4 * i
        m = dict(shared)
        m["x_p"] = np.ascontiguousarray(xp[i, :NT * 512])
        m["x_s"] = np.ascontiguousarray(xs[s0:s0 + 4].reshape(128, 1024))
        m["c_all"] = np.ascontiguousarray(np.concatenate([f(inp["c_prompt"])[i:i + 1], f(inp["c_sample"])[s0:s0 + 4]], axis=0))
        m["sconv"] = np.ascontiguousarray(f(inp["state_conv"])[0, s0:s0 + 4])
        m["sC"] = np.ascontiguousarray(f(inp["state_C"])[0, s0:s0 + 4])
        m["sn"] = np.ascontiguousarray(f(inp["state_n"])[0, s0:s0 + 4])
        m["smT"] = np.ascontiguousarray(f(inp["state_m"])[0, s0:s0 + 4].T)
        m["ck"] = np.ascontiguousarray(f(inp["cache_k"])[s0:s0 + 4].reshape(4, 512, 256))
        m["cv"] = np.ascontiguousarray(f(inp["cache_v"])[s0:s0 + 4].reshape(4, 512, 256))
        maps.append(m)
    return maps


def assemble(results, NT, ncores=8):
    g = lambda k: np.stack([np.asarray(results[i][k], dtype=np.float32) for i in range(ncores)], axis=0)
    y_p = g("y_p")
    y_s = g("y_s").reshape(ncores * 4, 32, 1024)
    conv_p = g("o_convp")[None]
    C_p = g("o_Cp")[None]
    n_p = g("o_np")[None]
    m_p = g("o_mp").reshape(ncores, 4)[None]
    k_p = g("o_kp").reshape(ncores, 512, 4, 64)
    v_p = g("o_vp").reshape(ncores, 512, 4, 64)
    conv_s = g("o_convs").reshape(ncores * 4, 3, 1024)[None]
    C_s = g("o_Cs").reshape(ncores * 4, 4, 256, 128)[None]
    n_s = g("o_ns").reshape(ncores * 4, 4, 128)[None]
    m_s = g("o_ms").reshape(ncores * 4, 4)[None]
    k_s = g("o_ks").reshape(ncores * 4, 32, 4, 64)
    v_s = g("o_vs").reshape(ncores * 4, 32, 4, 64)
    return (y_p, y_s, conv_p, C_p, n_p, m_p, k_p, v_p, conv_s, C_s, n_s, m_s, k_s, v_s)


def kernel(**inputs):
    NT = 16
    if NT not in _CACHE:
        _CACHE[NT] = build(NT)
    nc = _CACHE[NT]
    in_maps = make_in_maps(inputs, NT)
    res = run_bass_kernel_spmd(nc, in_maps, core_ids=list(range(8)))
    return assemble(res.results, NT)
```

```python
import contextlib
import numpy as np
import concourse.bass as bass
import concourse.mybir as mybir
from concourse.bass_utils import run_bass_kernel_spmd

F32 = mybir.dt.float32
BF16 = mybir.dt.bfloat16
AF = mybir.ActivationFunctionType
ALU = mybir.AluOpType
AX = mybir.AxisListType

ALPHA = 4.0 ** 0.25
LN_EPS_P = 1e-5 / (ALPHA * ALPHA)
HN_EPS = 1e-6
NSLOT = 3
PDEPTH = 2
NEG = -30000.0
import os
XQ = os.environ.get("XQ", "act")
SQE = os.environ.get("SQE", "pool")


class Res:
    __slots__ = ("name", "w", "r", "sem", "excl")

    def __init__(self, name, excl=False):
        self.name = name
        self.w = None
        self.r = []
        self.sem = None
        self.excl = excl


class Sched:
    ENGS = ("pe", "act", "dve", "pool", "sp")

    def __init__(self):
        self.ops = {e: [] for e in self.ENGS}
        self.nsig = {e: 0 for e in self.ENGS}
        self.waited = {e: {} for e in self.ENGS}
        self.semkeys = ["E_" + e for e in self.ENGS]
        self.dma_sems = {}
        self.nres = 0

    def res(self, name=None):
        self.nres += 1
        return Res(name or f"r{self.nres}")

    def _need(self, eng, ev, waits, same_ok):
        if ev is None:
            return
        key, val, weng = ev
        if same_ok and weng == eng:
            return
        if self.waited[eng].get(key, 0) >= val:
            return
        if waits.get(key, 0) < val:
            waits[key] = val

    def _deps(self, eng, reads, writes):
        waits = {}
        for r in reads:
            self._need(eng, r.w, waits, eng == "pe")
            if r.excl:
                for ev in r.r:
                    self._need(eng, ev, waits, True)
        for w in writes:
            self._need(eng, w.w, waits, True)
            for ev in w.r:
                self._need(eng, ev, waits, True)
        for k, v in waits.items():
            self.waited[eng][k] = v
        return list(waits.items())

    def _record(self, ev, reads, writes):
        for r in reads:
            r.r.append(ev)
        for w in writes:
            w.w = ev
            w.r = []

    def op(self, eng, fn, reads=(), writes=(), sig=True):
        waits = self._deps(eng, reads, writes)
        key = "E_" + eng
        if sig:
            self.nsig[eng] += 1
            val = self.nsig[eng]
        else:
            val = self.nsig[eng] + 1
        self._record((key, val, eng), reads, writes)
        self.ops[eng].append((waits, fn, (key, 1) if sig else None))

    def dma(self, q, out, in_, reads=(), writes=(), **kw):
        waits = self._deps(q, reads, writes)
        tgt = writes[0]
        if tgt.sem is None:
            tgt.sem = f"D{len(self.dma_sems)}_{tgt.name}"
            self.semkeys.append(tgt.sem)
            self.dma_sems[tgt.sem] = 0
        self.dma_sems[tgt.sem] += 16
        self._record((tgt.sem, self.dma_sems[tgt.sem], "dma"), reads, writes)
        self.ops[q].append((waits, (lambda e, o=out, i=in_, k=kw: e.dma_start(out=o, in_=i, **k)),
                            (tgt.sem, 16)))

    def barrier(self):
        for e in self.ENGS:
            waits = {}
            for e2 in self.ENGS:
                k = "E_" + e2
                if self.nsig[e2] > 0 and e2 != e and self.waited[e].get(k, 0) < self.nsig[e2]:
                    waits[k] = self.nsig[e2]
            for k, v in self.dma_sems.items():
                if v > 0 and self.waited[e].get(k, 0) < v:
                    waits[k] = v
            for k, v in waits.items():
                self.waited[e][k] = v
            if waits:
                self.ops[e].append((list(waits.items()), None, None))

    def final_drain(self, eng="sp"):
        waits = {}
        for k, v in self.dma_sems.items():
            if v > 0 and self.waited[eng].get(k, 0) < v:
                waits[k] = v
        for e2 in self.ENGS:
            if self.nsig[e2] > 0 and e2 != eng:
                waits["E_" + e2] = self.nsig[e2]
        self.ops[eng].append((list(waits.items()), None, None))

    def emit(self, nc):
        with contextlib.ExitStack() as st:
            sems = {k: st.enter_context(nc.semaphore(k)) for k in self.semkeys}
            block = st.enter_context(nc.Block())

            def replay(engobj, name):
                for waits, fn, sig in self.ops[name]:
                    for k, v in waits:
                        engobj.wait_ge(sems[k], v)
                    if fn is None:
                        continue
                    ins = fn(engobj)
                    if sig is not None:
                        ins.then_inc(sems[sig[0]], sig[1])

            @block.sync
            def _(e):
                replay(e, "sp")

            @block.tensor
            def _(e):
                replay(e, "pe")

            @block.scalar
            def _(e):
                replay(e, "act")

            @block.vector
            def _(e):
                replay(e, "dve")

            @block.gpsimd
            def _(e):
                replay(e, "pool")


def panel_list():
    pl = []
    for j in range(6):
        pl.append(("w_in", 0, j * 512, 3080, None))
    for j in range(2):
        pl.append(("w_out_a", 0, j * 512, 1024, None))
    for j in range(8):
        pl.append(("w_up0", 0, j * 512, 4096, None))
    for nh in range(2):
        for kg in range(4):
            pl.append(("w_down0", kg * 1024, nh * 512, 1024, None))
    pl.append(("w_kv", 0, 0, 512, None))
    for j in range(2):
        pl.append(("w_q", 0, j * 512, 1024, "qperm"))
    for j in range(2):
        pl.append(("w_out_b", 0, j * 512, 1024, None))
    for j in range(8):
        pl.append(("w_up1", 0, j * 512, 4096, None))
    for nh in range(2):
        for kg in range(4):
            pl.append(("w_down1", kg * 1024, nh * 512, 1024, None))
    return pl


NPANEL = 45


class _Stop(Exception):
    pass


def build(NT, do_sample=True, stage=99):
    nc = bass.Bass("TRN2", target_bir_lowering=False)
    S = Sched()
    st = contextlib.ExitStack()
    SEQ = NT * 512

    def din(name, shape, dt=F32):
        return nc.dram_tensor(name, list(shape), dt, kind="ExternalInput").ap()

    def dout(name, shape, dt=F32):
        return nc.dram_tensor(name, list(shape), dt, kind="ExternalOutput").ap()

    def dap(t, offset, dims):
        return bass.AP(tensor=t.tensor, offset=offset, ap=[list(d) for d in dims])

    x_p = din("x_p", [SEQ, 1024])
    x_s = din("x_s", [128, 1024])
    c_all = din("c_all", [5, 1024])
    sconv = din("sconv", [4, 3, 1024])
    sC = din("sC", [4, 4, 256, 128])
    sn = din("sn", [4, 4, 128])
    smT = din("smT", [4, 4])
    ck = din("ck", [4, 512, 256])
    cv = din("cv", [4, 512, 256])
    W = {
        "w_in": din("w_in", [1024, 3080]),
        "w_out_a": din("w_out_a", [1024, 1024]),
        "w_up0": din("w_up0", [1024, 4096]),
        "w_up1": din("w_up1", [1024, 4096]),
        "w_down0": din("w_down0", [4096, 1024]),
        "w_down1": din("w_down1", [4096, 1024]),
        "w_kv": din("w_kv", [1024, 512]),
        "w_q": din("w_q", [1024, 1024]),
        "w_out_b": din("w_out_b", [1024, 1024]),
    }
    w_ada = din("w_ada", [2, 1024, 6144])
    w_ada_kv = din("w_ada_kv", [1024, 2048])
    vec1 = din("vec1", [112, 128])
    vec2 = din("vec2", [112, 128])
    b_if = din("b_if", [8, 1])
    lnf = din("lnf", [2, 1024])
    rel_rev = din("rel_rev", [16, 385])
    cst = din("cst", [128, 3, 128])

    y_p = dout("y_p", [SEQ, 1024])
    y_s = dout("y_s", [128, 1024])
    o_convp = dout("o_convp", [3, 1024])
    o_Cp = dout("o_Cp", [4, 256, 128])
    o_np = dout("o_np", [4, 128])
    o_mp = dout("o_mp", [4, 1])
    o_kp = dout("o_kp", [512, 256])
    o_vp = dout("o_vp", [512, 256])
    o_convs = dout("o_convs", [4, 3, 1024])
    o_Cs = dout("o_Cs", [4, 4, 256, 128])
    o_ns = dout("o_ns", [4, 4, 128])
    o_ms = dout("o_ms", [4, 4, 1])
    o_ks = dout("o_ks", [128, 256])
    o_vs = dout("o_vs", [128, 256])
    wsc = nc.dram_tensor("wsc", [NPANEL, 128, 8, 512], BF16).ap()

    def sb(name, shape, dt=F32):
        return st.enter_context(nc.sbuf_tensor(name, list(shape), dt))

    PP = [st.enter_context(nc.psum_tensor(f"pp{i}", [128, 1024], F32)) for i in range(4)]
    bankres = [Res(f"bank{i}", excl=True) for i in range(8)]

    def bank(i):
        return PP[i // 2][:, (i % 2) * 512:(i % 2) * 512 + 512]

    def bankb(i):
        return PP[i // 2][:, (i % 2) * 512:(i % 2) * 512 + 512].bitcast(BF16)

    def pair(i):
        return PP[i][:, :]

    wslot = [sb(f"wslot{i}", [128, 8, 512], BF16) for i in range(NSLOT)]
    wslot_r = [S.res(f"wslot{i}") for i in range(NSLOT)]
    xT = sb("xT", [128, 8, 512])
    scr = [sb(f"scr{i}", [128, 1024]) for i in range(4)]
    scr_r = [S.res(f"scr{i}") for i in range(4)]
    uT = sb("uT", [128, 8, 512], BF16)
    bufA = sb("bufA", [128, 8, 512], BF16)
    bufB = sb("bufB", [128, 8, 512], BF16)
    bufC = sb("bufC", [128, 8, 512], BF16)
    cb = sb("cb", [128, 8, 4, 131], BF16)
    vaug = sb("vaug", [128, 4, 4, 257], BF16)
    hT = sb("hT", [128, 32, 512], BF16)
    rtmp = [sb(f"rtmp{i}", [128, 512]) for i in range(2)]
    rtmp_r = [S.res(f"rtmp{i}") for i in range(2)]
    C_sb = sb("C_sb", [128, 4, 257])
    Cb = sb("Cb", [128, 4, 257], BF16)
    sTs = sb("sTs", [128, 4, 128], BF16)
    kw = sb("kw", [128, 4, 128], BF16)
    hn = sb("hn", [128, 4, 256], BF16)
    KTr = sb("KTr", [128, 2, 1024], BF16)
    Vr = sb("Vr", [128, 8, 4, 65], BF16)
    KTc = KTr[:, :, 512:1024]
    Vc = Vr[:, 4:8, :, :]
    NPT = 8
    PT = [sb(f"PT{i}", [128, 512], BF16) for i in range(NPT)]
    PT_r = [S.res(f"PT{i}") for i in range(NPT)]
    PT0 = [sb(f"PTm{i}", [128, 512], BF16) for i in range(2)]
    PT0_r = [S.res(f"PTm{i}") for i in range(2)]
    expin = [sb(f"expin{i}", [128, 512]) for i in range(2)]
    expin_r = [S.res(f"expin{i}") for i in range(2)]
    On = sb("On", [128, 16, 64], BF16)
    bias3 = sb("bias3", [128, 16, 128], BF16)
    bias4 = sb("bias4", [128, 16, 128], BF16)
    lnfbc = sb("lnfbc", [128, 2, 1024])
    ident = sb("ident", [128, 128])
    identb = sb("identb", [128, 128], BF16)
    trimask = sb("trimask", [128, 128])
    mask4 = sb("mask4", [128, 128])
    diagc = sb("diagc", [128, 8, 4, 128], BF16)
    vecT = sb("vecT", [128, 112])
    biasT = sb("biasT", [128, 112])
    TAB = sb("TAB", [128, 14, 8, 5])
    cT = sb("cT", [128, 8, 5])
    wg = sb("wg", [128, 8, 8], BF16)
    wg32 = sb("wg32", [128, 8, 8])
    bg = sb("bg", [4, 2])
    bg8 = sb("bg8", [8, 1])
    ones4 = sb("ones4", [4, 128])
    chv = sb("chv", [128, 16])
    cprev = sb("cprev", [128, 8, 3], BF16)
    cstage = sb("cstage", [128, 3, 8])
    mcol = sb("mcol", [4, 8])
    mout = sb("mout", [4, 4])
    G_g = sb("G_g", [4, 512])
    G_s = sb("G_s", [4, 8, 4])
    G_d4 = sb("G_d4", [4, 4, 4])
    gsc = sb("gsc", [128, 4, 16])
    st6 = sb("st6", [128, 4, 6])
    mv = sb("mv", [128, 4, 2])
    hsm = sb("hsm", [128, 8, 4])
    lst = sb("lst", [128, 4, 2, 6])
    lmv = sb("lmv", [128, 4, 8])
    rden = sb("rden", [128, 16])
    epsc = sb("epsc", [128, 1])
    modT = scr[0][:, 0:560].rearrange("p (a s) -> p a s", a=112)
    mod1 = scr[1][:, 0:560].rearrange("p (a s) -> p a s", a=112)
    crow = scr[2][0:5, :]
    rowc = expin[0][0:5, :]
    G_ig = expin[1][0:4, :]
    G_fg = expin[0][0:4, :]
    G_l = rtmp[0][0:4, :]
    G_nb = rtmp[1][0:4, :]

    R = {"modT": scr_r[0], "mod1": scr_r[1], "crow": scr_r[2], "rowc": expin_r[0]}

    def r(name):
        if name not in R:
            R[name] = S.res(name)
        return R[name]

    def grid(name):
        return [[S.res(f"{name}_{c}_{b}") for b in range(4)] for c in range(8)]

    g_xT = grid("xT"); g_uT = grid("uT"); g_A = grid("bufA"); g_B = grid("bufB"); g_C = grid("bufC")
    g_cb = grid("cb")
    r_vaug = [S.res(f"vaug{b}") for b in range(4)]
    r_hT = [S.res(f"hT{c}") for c in range(32)]
    r_Csb = [S.res(f"Csb{h}") for h in range(4)]
    r_Cb = [S.res(f"Cb{h}") for h in range(4)]
    r_KTr = [S.res(f"KTr{i}") for i in range(8)]
    r_Vr = [S.res(f"Vr{i}") for i in range(8)]

    def gsel(g, cs, bs):
        return [g[c][b] for c in cs for b in bs]

    ALLC = list(range(8))

    def mm(out, lhsT, rhs, start, stop, rd, wr, sig=True):
        S.op("pe", lambda e: e.matmul(out, lhsT=lhsT, rhs=rhs, start=start, stop=stop), rd, wr, sig)

    def tr(out, in_, idn, rd, wr, sig=True):
        S.op("pe", lambda e: e.transpose(out, in_, idn), rd, wr, sig)

    def act(out, in_, func, rd, wr, bias=None, scale=None):
        kw_ = {}
        if bias is not None:
            kw_["bias"] = bias
        if scale is not None:
            kw_["scale"] = scale
        S.op("act", lambda e: e.activation(out=out, in_=in_, func=func, **kw_), rd, wr)

    def ts(eng, out, in0, s1, s2, op0, op1, rd, wr):
        if s2 is None:
            S.op(eng, lambda e: e.tensor_scalar(out=out, in0=in0, scalar1=s1, scalar2=None, op0=op0), rd, wr)
        else:
            S.op(eng, lambda e: e.tensor_scalar(out=out, in0=in0, scalar1=s1, scalar2=s2, op0=op0, op1=op1), rd, wr)

    def tt(eng, out, in0, in1, op, rd, wr):
        S.op(eng, lambda e: e.tensor_tensor(out=out, in0=in0, in1=in1, op=op), rd, wr)

    def stt(out, in0, scalar, in1, op0, op1, rd, wr):
        S.op("dve", lambda e: e.scalar_tensor_tensor(out=out, in0=in0, scalar=scalar, in1=in1, op0=op0, op1=op1), rd, wr)

    def cp(eng, out, in_, rd, wr):
        if eng == "act":
            if "i" in os.environ.get("DBG", ""):
                S.op("act", lambda e: e.activation(out=out, in_=in_, func=AF.Identity), rd, wr)
            else:
                S.op("act", lambda e: e.copy(out=out, in_=in_), rd, wr)
        else:
            S.op(eng, lambda e: e.tensor_copy(out=out, in_=in_), rd, wr)

    def recip(out, in_, rd, wr):
        S.op("dve", lambda e: e.reciprocal(out=out, in_=in_), rd, wr)

    affctr = [0]

    def aff(eng, out, in_, A, B, rd, wr):
        if eng == "dve":
            ts("dve", out, in_, A, B, ALU.mult, ALU.add, rd, wr)
        else:
            act(out, in_, AF.Identity, rd, wr, bias=B, scale=A)

    cpctr = [0]

    def cpa(out, in_, rd, wr):
        cpctr[0] += 1
        cp("dve" if cpctr[0] % 2 == 0 else "act", out, in_, rd, wr)

    S.dma("sp", ident[:], cst[:, 0, :], writes=[r("ident")])
    S.dma("sp", trimask[:], cst[:, 1, :], writes=[r("trimask")])
    S.dma("sp", mask4[:], cst[:, 2, :], writes=[r("mask4")])
    cp("dve", identb[:], ident[:], [r("ident")], [r("identb")])
    S.op("dve", lambda e: e.memset(ones4[:], 1.0), [], [r("ones4")])
    S.op("dve", lambda e: e.memset(epsc[:], LN_EPS_P), [], [r("epsc")])
    S.dma("sp", lnfbc[:, 0, :], dap(lnf, 0, [[0, 128], [1, 1024]]), writes=[r("lnfbc")])
    S.dma("sp", lnfbc[:, 1, :], dap(lnf, 1024, [[0, 128], [1, 1024]]), writes=[r("lnfbc")])

    S.dma("sp", scr[0][0:112, 0:128], vec1[:, :], writes=[scr_r[0]])
    S.dma("sp", scr[1][0:112, 0:128], vec2[:, :], writes=[scr_r[1]])
    tr(bank(0)[:, 0:112], scr[0][0:112, 0:128], ident[0:112, 0:112], [scr_r[0], r("ident")], [bankres[0]])
    tr(bank(1)[:, 0:112], scr[1][0:112, 0:128], ident[0:112, 0:112], [scr_r[1], r("ident")], [bankres[1]])
    cp("dve", vecT[:], bank(0)[:, 0:112], [bankres[0]], [r("vecT")])
    cp("dve", biasT[:], bank(1)[:, 0:112], [bankres[1]], [r("biasT")])

    for c in range(8):
        for j in range(4):
            ts("dve", diagc[:, c, j, :], ident[:], vecT[:, j * 8 + c:j * 8 + c + 1], None, ALU.mult, None,
               [r("ident"), r("vecT")], [r("diagc")])

    S.dma("sp", wg32[:], dap(W["w_in"], 3072, [[3080, 128], [128 * 3080, 8], [1, 8]]), writes=[r("wg32")])
    cp("dve", wg[:], wg32[:], [r("wg32")], [r("wg")])
    S.dma("sp", bg[:, 0:1], b_if[0:4, :], writes=[r("bg")])
    S.dma("sp", bg[:, 1:2], b_if[4:8, :], writes=[r("bg")])
    ts("dve", bg[:, 1:2], bg[:, 1:2], -1.0, None, ALU.mult, None, [r("bg")], [r("bg")])

    S.dma("sp", crow[:], c_all[:, :], writes=[r("crow")])
    act(crow[:], crow[:], AF.Silu, [r("crow")], [r("crow")])
    for kc in range(8):
        tr(bank(2)[:, kc * 8:kc * 8 + 5], crow[:, kc * 128:(kc + 1) * 128], ident[0:5, 0:5],
           [r("crow"), r("ident")], [bankres[2]])
    cp("dve", cT[:], bank(2)[:, 0:64].rearrange("p (k s) -> p k s", k=8)[:, :, 0:5], [bankres[2]], [r("cT")])

    hT32 = hT[:].rearrange("p a b -> p (a b)").bitcast(F32)
    stg32 = [hT32[:, i * 4096:(i + 1) * 4096].rearrange("p (k n) -> p k n", k=8) for i in range(2)]
    stg32_r = [S.res("stg32_0"), S.res("stg32_1")]
    npan = 0
    ada_srcs = []
    for l in range(2):
        for j in range(12):
            ada_srcs.append((w_ada, l * 1024 * 6144 + j * 512, 6144))
    for j in range(4):
        ada_srcs.append((w_ada_kv, j * 512, 2048))
    for pi, (wt, off, ncols) in enumerate(ada_srcs):
        sl = pi % 2
        S.dma("sp", stg32[sl], dap(wt, off, [[ncols, 128], [128 * ncols, 8], [1, 512]]), writes=[stg32_r[sl]])
        pb = 3 + (pi % 2)
        for kc in range(8):
            mm(bank(pb)[0:5, :], cT[:, kc, :], stg32[sl][:, kc, :], kc == 0, kc == 7,
               [r("cT"), stg32_r[sl]], [bankres[pb]], sig=(kc == 7))
        cp("act", rowc[:], bank(pb)[0:5, :], [bankres[pb]], [r("rowc")])
        tb = 5 + (pi % 2)
        for q in range(4):
            tr(bank(tb)[:, q * 8:q * 8 + 5], rowc[:, q * 128:(q + 1) * 128], ident[0:5, 0:5],
               [r("rowc"), r("ident")], [bankres[tb]])
        cp("dve", modT[:, pi * 4:pi * 4 + 4, :], bank(tb)[:, 0:32].rearrange("p (q s) -> p q s", q=4)[:, :, 0:5],
           [bankres[tb]], [r("modT")])
    tt("dve", modT[:], modT[:], biasT[:].unsqueeze(2).broadcast_to([128, 112, 5]), ALU.add,
       [r("modT"), r("biasT")], [r("modT")])
    ts("dve", mod1[:], modT[:], 1.0, None, ALU.add, None, [r("modT")], [r("mod1")])

    def mchunk(l, j):
        return l * 48 + j * 8

    def gam(l, i):
        o = 40 + (l * 2 + i) * 8
        return vecT[:, o:o + 8]

    def bet(l, i):
        o = 72 + (l * 2 + i) * 8
        return vecT[:, o:o + 8]

    def bc5(a):
        return a.unsqueeze(2).broadcast_to([128, 8, 5])

    rT = [r("modT"), r("mod1"), r("vecT")]
    cp("dve", TAB[:, 0], mod1[:, mchunk(0, 1):mchunk(0, 1) + 8, :], rT, [r("TAB")])
    cp("dve", TAB[:, 1], modT[:, mchunk(0, 0):mchunk(0, 0) + 8, :], rT, [r("TAB")])
    for kind, l, j in ((2, 0, 2), (5, 0, 5), (10, 1, 2), (13, 1, 5)):
        ts("dve", TAB[:, kind], mod1[:, mchunk(l, j):mchunk(l, j) + 8, :], 1.0 / ALPHA, None, ALU.mult, None, rT, [r("TAB")])

    def mkAB(kA, kB, g_, b_, sc_off, sh_off):
        tt("dve", TAB[:, kA], mod1[:, sc_off:sc_off + 8, :], bc5(g_), ALU.mult, rT + [r("TAB")], [r("TAB")])
        tt("dve", TAB[:, kB], mod1[:, sc_off:sc_off + 8, :], bc5(b_), ALU.mult, rT + [r("TAB")], [r("TAB")])
        tt("dve", TAB[:, kB], TAB[:, kB], modT[:, sh_off:sh_off + 8, :], ALU.add, rT + [r("TAB")], [r("TAB")])

    mkAB(3, 4, gam(0, 0), bet(0, 0), mchunk(0, 4), mchunk(0, 3))
    mkAB(6, 7, gam(0, 1), bet(0, 1), mchunk(1, 1), mchunk(1, 0))
    mkAB(8, 9, gam(0, 1), bet(0, 1), 96 + 8, 96)
    mkAB(11, 12, gam(1, 0), bet(1, 0), mchunk(1, 4), mchunk(1, 3))

    def tab(kind, c, seq):
        return TAB[:, kind, c, seq:seq + 1]

    bst = [hT32[:, 0:2048].rearrange("p (h q) -> p h q", h=16), hT32[:, 2048:4096].rearrange("p (h q) -> p h q", h=16)]
    S.barrier()
    S.dma("sp", bst[0], dap(rel_rev, 1, [[1, 128], [385, 16], [1, 128]]), writes=[stg32_r[0]])
    S.dma("sp", bst[1], dap(rel_rev, 129, [[1, 128], [385, 16], [1, 128]]), writes=[stg32_r[0]])
    S.dma("sp", chv[:], dap(rel_rev, 128, [[0, 128], [385, 16]]), writes=[r("chv")], allow_slow_non_contiguous=True)

    def flipped(a):
        return bass.AP(tensor=a.tensor, offset=a.offset + 127, ap=[list(a.ap[0]), list(a.ap[1]), [-1, 128]])

    chb = chv[:].unsqueeze(2).broadcast_to([128, 16, 128])
    tt("dve", bias3[:], flipped(bst[0]), chb, ALU.subtract, [stg32_r[0], r("chv")], [r("bias3")])
    tt("dve", bst[1], bst[1], flipped(chb) if False else chb, ALU.subtract, [stg32_r[0], r("chv")], [stg32_r[0]])
    m4b = bass.AP(tensor=mask4[:].tensor, offset=mask4[:].offset, ap=[list(mask4[:].ap[0]), [0, 16], [1, 128]])
    tt("dve", bias4[:], flipped(bst[1]), m4b, ALU.add, [stg32_r[0], r("mask4")], [r("bias4")])
    S.barrier()

    def ckpt(n):
        if stage <= n:
            raise _Stop()

    plist = panel_list()
    r_wsc = S.res("wsc")
    casteng = ["dve", "act", "pool"]
    for pi, (wn, row0, col0, ncols, kind) in enumerate(plist):
        sl = pi % 2
        ws = pi % NSLOT
        S.dma("sp", stg32[sl], dap(W[wn], row0 * ncols + col0, [[ncols, 128], [128 * ncols, 8], [1, 512]]),
              writes=[stg32_r[sl]])
        eng = casteng[pi % 3]
        if kind == "qperm":
            for k in range(2):
                o_ = wslot[ws][:].rearrange("p c (g k d) -> p c g k d", g=4, k=2)[:, :, :, k, :]
                i_ = stg32[sl].rearrange("p c (k g d) -> p c k g d", k=2, g=4)[:, :, k, :, :]
                cp("dve", o_, i_, [stg32_r[sl]], [wslot_r[ws]])
        else:
            cp(eng, wslot[ws][:], stg32[sl], [stg32_r[sl]], [wslot_r[ws]])
        S.dma("pool", wsc[pi], wslot[ws][:], reads=[wslot_r[ws]], writes=[r_wsc])
    S.barrier()

    S.op("dve", lambda e: e.memset(vaug[:], 1.0), [], r_vaug)
    S.op("dve", lambda e: e.memset(Vr[:], 1.0), [], r_Vr)
    S.op("dve", lambda e: e.memset(PT0[0][:], 0.0), [], [PT0_r[0]])
    S.op("dve", lambda e: e.memset(PT0[1][:], 0.0), [], [PT0_r[1]])
    S.op("dve", lambda e: e.memset(gsc[:], 0.0), [], [r("gsc")])

    tiles = []
    if do_sample:
        tiles.append(("s", 0))
    for ti in range(NT):
        tiles.append(("p", ti))
    uses = [pi for _ in tiles for pi in range(NPANEL)]
    wstate = {"next": 0, "cur": -1}

    def wnext():
        wstate["cur"] += 1
        i = wstate["cur"]
        while wstate["next"] <= min(i + PDEPTH, len(uses) - 1):
            n = wstate["next"]
            S.dma("sp", wslot[n % NSLOT][:], wsc[uses[n]], reads=[r_wsc], writes=[wslot_r[n % NSLOT]])
            wstate["next"] += 1
        return wslot[i % NSLOT], wslot_r[i % NSLOT]

    bctr = {}

    def rot(lst, key=None):
        key = key or "k%d_%s" % (len(lst), str(lst[0])[:24])
        bctr[key] = bctr.get(key, -1) + 1
        return lst[bctr[key] % len(lst)]

    scrctr = [0]

    def nscr():
        scrctr[0] += 1
        i = scrctr[0] % 4
        return scr[i], scr_r[i]

    def proj_a(ws, wr_, nchunk, rhs_buf, g_rhs, T, evac, banks):
        for j in range(nchunk):
            bi = rot(banks)
            for kc in range(8):
                mm(bank(bi)[:, 0:T], ws[:, kc, j * 128:(j + 1) * 128], rhs_buf[:, kc, 0:T], kc == 0, kc == 7,
                   [wr_] + gsel(g_rhs, [kc], range(4)), [bankres[bi]], sig=(kc == 7))
            evac(j, bi)

    def ln_block(L, b, cs, eps, targets, final_dst=None, nbk=4):
        pz = (b % 2)
        pbk = 2 + (b % 2)
        zp = pair(pz)
        zres = [bankres[2 * pz], bankres[2 * pz + 1]]
        for c in range(8):
            tr(zp[0:L, c * 128:(c + 1) * 128], xT[:, c, cs], ident[:, :], [g_xT[c][b], r("ident")], zres, sig=(c == 7))
        S.op("dve", lambda e: e.bn_stats(out=lst[0:L, 0, :], in_=zp[0:L, 0:512]), zres, [r("lst")])
        S.op("dve", lambda e: e.bn_stats(out=lst[0:L, 1, :], in_=zp[0:L, 512:1024]), zres, [r("lst")])
        S.op("dve", lambda e: e.bn_aggr(out=lmv[0:L, 0:2], in_=lst[0:L].rearrange("p a b -> p (a b)")), [r("lst")], [r("lmv")])
        ts("dve", lmv[0:L, 2:3], lmv[0:L, 1:2], eps, None, ALU.add, None, [r("lmv")], [r("lmv")])
        act(lmv[0:L, 3:4], lmv[0:L, 2:3], AF.Sqrt, [r("lmv")], [r("lmv")])
        recip(lmv[0:L, 4:5], lmv[0:L, 3:4], [r("lmv")], [r("lmv")])
        stt(lmv[0:L, 5:6], lmv[0:L, 0:1], -1.0, lmv[0:L, 4:5], ALU.mult, ALU.mult, [r("lmv")], [r("lmv")])
        xh, xh_r = nscr()
        act(xh[0:L, :], zp[0:L, :], AF.Identity, zres + [r("lmv")], [xh_r], bias=lmv[0:L, 5:6], scale=lmv[0:L, 4:5])
        return xh, xh_r

    def back_block(L, b, cs, xh, xh_r, gb, targets):
        pbk = 2 + (b % 2)
        bp = pair(pbk)
        bres = [bankres[2 * pbk], bankres[2 * pbk + 1]]
        for c in range(8):
            tr(bp[:, c * 128:c * 128 + L], xh[0:L, c * 128:(c + 1) * 128], ident[0:L, 0:L], [xh_r, r("ident")], bres,
               sig=(c == 7))
        for c in range(8):
            src = bp[:, c * 128:c * 128 + L]
            eng = "dve" if c < 4 else "act"
            br1 = [bres[c // 4]]
            if gb is None:
                cp(eng, xT[:, c, cs], src, br1, [g_xT[c][b]])
            else:
                aff(eng, xT[:, c, cs], src, gb[0][:, c:c + 1], gb[1][:, c:c + 1], br1 + [r("vecT")], [g_xT[c][b]])
            for (buf, g_, kA, kB, seq) in targets:
                aff(eng, buf[:, c, cs], src, tab(kA, c, seq), tab(kB, c, seq), br1 + [r("TAB")], [g_[c][b]])

    def ln_stage(L, nb, seqs, prompt, ti, mode, gb, targets):
        BLK = [0, 1, 2, 3]
        xh = {}
        if mode == "xload":
            for b in BLK:
                xh[b] = (hT32[:, b * 1024:(b + 1) * 1024], None)
        else:
            zps = {}
            for b in BLK:
                cs = slice(b * L, (b + 1) * L)
                zp = pair(b)
                zres = [bankres[2 * b], bankres[2 * b + 1]]
                zps[b] = (zp, zres)
                for c in range(8):
                    tr(zp[0:L, c * 128:(c + 1) * 128], xT[:, c, cs], ident[:, :], [g_xT[c][b], r("ident")], zres, sig=(c == 7))
            for b in BLK:
                zp, zres = zps[b]
                rl = [r(f"lmv{b}")]
                S.op("dve", lambda e, b=b, zp=zp: e.bn_stats(out=lst[0:L, b, 0, :], in_=zp[0:L, 0:512]), zres, rl)
                S.op("dve", lambda e, b=b, zp=zp: e.bn_stats(out=lst[0:L, b, 1, :], in_=zp[0:L, 512:1024]), zres, rl)
                S.op("dve", lambda e, b=b: e.bn_aggr(out=lmv[0:L, b, 0:2], in_=lst[0:L, b].rearrange("p a b -> p (a b)")), rl, rl)
            for b in BLK:
                rl = [r(f"lmv{b}")]
                act(lmv[0:L, b, 3:4], lmv[0:L, b, 1:2], AF.Sqrt, rl + [r("epsc")], rl, bias=epsc[0:L, 0:1])
            for b in BLK:
                rl = [r(f"lmv{b}")]
                recip(lmv[0:L, b, 4:5], lmv[0:L, b, 3:4], rl, rl)
                stt(lmv[0:L, b, 5:6], lmv[0:L, b, 0:1], -1.0, lmv[0:L, b, 4:5], ALU.mult, ALU.mult, rl, rl)
            for b in BLK:
                zp, zres = zps[b]
                xs, xs_r = nscr()
                act(xs[0:L, :], zp[0:L, :], AF.Identity, zres + [r(f"lmv{b}")], [xs_r], bias=lmv[0:L, b, 5:6], scale=lmv[0:L, b, 4:5])
                xh[b] = (xs, xs_r)
        if mode == "final":
            for b in BLK:
                xs, xs_r = xh[b]
                tt("dve" if b % 2 == 0 else "pool", xs[0:L, :], xs[0:L, :], lnfbc[0:L, 0, :], ALU.mult, [xs_r, r("lnfbc")], [xs_r])
            for b in BLK:
                xs, xs_r = xh[b]
                tt("dve", xs[0:L, :], xs[0:L, :], lnfbc[0:L, 1, :], ALU.add, [xs_r, r("lnfbc")], [xs_r])
            for b in BLK:
                xs, xs_r = xh[b]
                dst = y_p[ti * 512 + b * 128: ti * 512 + (b + 1) * 128, :] if prompt else y_s[b * 32:(b + 1) * 32, :]
                S.dma("sp", dst, xs[0:L, :], reads=[xs_r], writes=[r(f"o_y{b}")])
            return
        for b in BLK:
            xs, xs_r = xh[b]
            xs_rl = r_hT[4 * b:4 * b + 4] if xs_r is None else [xs_r]
            g = b // 2
            for c in range(8):
                o_ = (c % 4) * 256 + (b % 2) * 128
                tr(PP[2 * g + c // 4][:, o_:o_ + L], xs[0:L, c * 128:(c + 1) * 128], ident[0:L, 0:L], xs_rl + [r("ident")],
                   [bankres[4 * g + c // 2]], sig=(c == 7))
        for pas in (0, 1):
            for g in range(2):
                grp = [2 * g, 2 * g + 1]
                for c in range(8):
                    bk = 4 * g + c // 2
                    eng = "dve" if (c // 2) % 2 == 0 else "act"
                    if prompt:
                        src = PP[2 * g + c // 4][:, (c % 4) * 256:(c % 4) * 256 + 256]
                        cs2 = slice(g * 256, g * 256 + 256)
                        if pas == 1:
                            wr_x = gsel(g_xT, [c], grp)
                            if gb is None:
                                cp(eng, xT[:, c, cs2], src, [bankres[bk]], wr_x)
                            else:
                                aff(eng, xT[:, c, cs2], src, gb[0][:, c:c + 1], gb[1][:, c:c + 1], [bankres[bk], r("vecT")], wr_x)
                        else:
                            for (buf, g_, kA, kB) in targets:
                                aff(eng, buf[:, c, cs2], src, tab(kA, c, 0), tab(kB, c, 0), [bankres[bk], r("TAB")], gsel(g_, [c], grp))
                    else:
                        for b in grp:
                            o_ = (c % 4) * 256 + (b % 2) * 128
                            src = PP[2 * g + c // 4][:, o_:o_ + L]
                            cs = slice(b * L, (b + 1) * L)
                            if pas == 1:
                                if gb is None:
                                    cp(eng, xT[:, c, cs], src, [bankres[bk]], [g_xT[c][b]])
                                else:
                                    aff(eng, xT[:, c, cs], src, gb[0][:, c:c + 1], gb[1][:, c:c + 1], [bankres[bk], r("vecT")], [g_xT[c][b]])
                            else:
                                for (buf, g_, kA, kB) in targets:
                                    aff(eng, buf[:, c, cs], src, tab(kA, c, seqs[b]), tab(kB, c, seqs[b]), [bankres[bk], r("TAB")], [g_[c][b]])

    def resid_evac(L, nb, seqs, kind, same):
        def ev(c, bi):
            if same:
                T = nb * L
                stt(xT[:, c, 0:T], bank(bi)[:, 0:T], tab(kind, c, seqs[0]), xT[:, c, 0:T], ALU.mult, ALU.add,
                    [bankres[bi], r("TAB")] + gsel(g_xT, [c], range(nb)), gsel(g_xT, [c], range(nb)))
            else:
                for b in range(nb):
                    cs = slice(b * L, (b + 1) * L)
                    stt(xT[:, c, cs], bank(bi)[:, cs], tab(kind, c, seqs[b]), xT[:, c, cs], ALU.mult, ALU.add,
                        [bankres[bi], r("TAB"), g_xT[c][b]], [g_xT[c][b]])
        return ev

    def xprefetch(kind, ti):
        for b in range(4):
            if kind == "p":
                src, L_ = x_p[ti * 512 + b * 128: ti * 512 + (b + 1) * 128, :], 128
            else:
                src, L_ = x_s[b * 32:(b + 1) * 32, :], 32
            S.dma(XQ, hT32[0:L_, b * 1024:(b + 1) * 1024], src, writes=r_hT[4 * b:4 * b + 4])

    def do_tile(kind, ti, nxt_tile=None):
        prompt = (kind == "p")
        nb = 4
        L = 128 if prompt else 32
        T = nb * L
        seqs = [0, 0, 0, 0] if prompt else [1, 2, 3, 4]
        last = prompt and ti == NT - 1
        first = prompt and ti == 0
        BL = range(nb)
        CS = [slice(b * L, (b + 1) * L) for b in BL]

        ln_stage(L, nb, seqs, prompt, ti, "xload", None, [(uT, g_uT, 0, 1)])

        ckpt(2)
        def ev_qk(base):
            def ev(j, bi):
                c = base + j
                cpa(cb[:, c, 0:nb, 3:3 + L], bank(bi)[:, 0:T].rearrange("p (b t) -> p b t", b=nb),
                    [bankres[bi]], gsel(g_cb, [c], BL))
            return ev
        rg = [r("G")]
        rl_, rn_ = rtmp_r[0], rtmp_r[1]

        def gates_a():
            for gi in range(2):
                for kc in range(8):
                    mm(bank(6 + gi)[0:4, 0:T], wg[:, kc, gi * 4:gi * 4 + 4], uT[:, kc, 0:T], kc == 0, kc == 7,
                       [r("wg")] + gsel(g_uT, [kc], BL), [bankres[6 + gi]], sig=(kc == 7))
            cp("act", G_ig[:, 0:T], bank(6)[0:4, 0:T], [bankres[6]], [expin_r[1]])
            cp("act", G_fg[:, 0:T], bank(7)[0:4, 0:T], [bankres[7]], [expin_r[0]])
            rg = [r("G")]
            rl_, rn_ = rtmp_r[0], rtmp_r[1]
            if not prompt:
                S.dma(XQ, mcol[:, 0:4], smT[:, :], writes=[r("mcol")])
            elif first:
                S.op("dve", lambda e: e.memset(mcol[:], 0.0), [], [r("mcol")])
            act(G_l[:, 0:T], G_fg[:, 0:T], AF.Exp, [expin_r[0], r("bg")], [rl_], bias=bg[:, 1:2], scale=-1.0)
            act(G_l[:, 0:T], G_l[:, 0:T], AF.Ln, [rl_], [rl_], bias=1.0)
            for b in BL:
                S.op("dve", lambda e, b=b: e.tensor_tensor_scan(out=G_nb[:, CS[b]], data0=ones4[:, 0:L], data1=G_l[:, CS[b]],
                                                                initial=0.0, op0=ALU.mult, op1=ALU.add), [rl_, r("ones4")], [rn_])
            stt(G_g[:, 0:T], G_ig[:, 0:T], bg[:, 0:1], G_nb[:, 0:T], ALU.add, ALU.add, [expin_r[1], r("bg"), rn_], rg)
            S.op("dve", lambda e: e.tensor_reduce(out=G_s[:, 0, 0:nb], in_=G_g[:, 0:T].rearrange("p (b t) -> p b t", b=nb),
                                                  axis=AX.X, op=ALU.max), rg, rg)
            for b in BL:
                if prompt:
                    mp_ = mcol[:, 0:1] if b == 0 else G_s[:, 4, b - 1:b]
                else:
                    mp_ = mcol[:, b:b + 1]
                cp("dve", G_s[:, 3, b:b + 1], mp_, rg + [r("mcol")], rg)
                tt("dve", G_s[:, 1, b:b + 1], G_s[:, 0, b:b + 1], mp_, ALU.max, rg + [r("mcol")], rg)
                tt("dve", G_s[:, 4, b:b + 1], G_s[:, 1, b:b + 1], G_nb[:, (b + 1) * L - 1:(b + 1) * L], ALU.subtract, rg + [rn_], rg)
            ts("dve", G_s[:, 2, 0:nb], G_s[:, 1, 0:nb], -1.0, None, ALU.mult, None, rg, rg)
            tt("dve", G_s[:, 5, 0:nb], G_s[:, 3, 0:nb], G_s[:, 1, 0:nb], ALU.subtract, rg, rg)

        def gates_b():
            act(G_s[:, 6, 0:nb], G_s[:, 5, 0:nb], AF.Exp, rg, rg)
            ngb = G_s[:, 2, 0:nb].unsqueeze(2).broadcast_to([4, nb, L])
            tt("dve", G_l[:, 0:T].rearrange("p (b t) -> p b t", b=nb), G_g[:, 0:T].rearrange("p (b t) -> p b t", b=nb), ngb,
               ALU.add, rg + [rl_], [rl_])
            act(G_l[:, 0:T], G_l[:, 0:T], AF.Exp, [rl_], [rl_])
            tt("dve", G_g[:, 0:T].rearrange("p (b t) -> p b t", b=nb), G_nb[:, 0:T].rearrange("p (b t) -> p b t", b=nb), ngb,
               ALU.add, rg + [rn_], rg)
            act(G_g[:, 0:T], G_g[:, 0:T], AF.Exp, rg, rg)
            for b in BL:
                ts("dve", G_d4[:, b, :], ident[0:4, 0:4], G_s[:, 6, b:b + 1], None, ALU.mult, None, rg + [r("ident")], rg)

        def gates_c():
            gb_ = 5
            for b in BL:
                tr(bank(gb_)[0:L, b * 16:b * 16 + 4], G_l[:, CS[b]], ident[0:4, 0:4], [rl_, r("ident")], [bankres[gb_]], sig=False)
                tr(bank(gb_)[0:L, b * 16 + 4:b * 16 + 8], G_g[:, CS[b]], ident[0:4, 0:4], rg + [r("ident")], [bankres[gb_]], sig=False)
                mm(bank(gb_)[:, b * 16 + 8:b * 16 + 12], ones4[:, :], G_d4[:, b, :], True, True, rg + [r("ones4")], [bankres[gb_]],
                   sig=(b == nb - 1))
            cp("dve", gsc[:, :, 0:12], bank(gb_)[:, 0:64].rearrange("p (b f) -> p b f", b=4)[:, :, 0:12], [bankres[gb_]], [r("gsc")])
            ts("dve", gsc[:, :, 12:16], gsc[:, :, 8:12], 128.0 ** -0.5, None, ALU.mult, None, [r("gsc")], [r("gsc")])
            if prompt:
                cp("dve", mcol[:, 0:1], G_s[:, 4, nb - 1:nb], rg, [r("mcol")])
            else:
                cp("dve", mout[:, 0:nb], G_s[:, 4, 0:nb], rg, [r("mout")])


        gates_a()
        for pj in range(2):
            ws, wr_ = wnext()
            proj_a(ws, wr_, 4, uT, g_uT, T, ev_qk(pj * 4), [0, 1, 2, 3])
            if pj == 0:
                gates_b()
        ckpt(3)
        for b in BL:
            if prompt:
                if b == 0:
                    if first:
                        S.op("dve", lambda e: e.memset(cb[:, :, 0, 0:3], 0.0), [], gsel(g_cb, ALLC, [0]))
                    else:
                        cp("dve", cb[:, :, 0, 0:3], cprev[:], [r("cprev")], gsel(g_cb, ALLC, [0]))
                else:
                    cp("dve", cb[:, :, b, 0:3], cb[:, :, b - 1, L:L + 3], gsel(g_cb, ALLC, [b - 1]), gsel(g_cb, ALLC, [b]))
            else:
                S.dma("act", cstage[:].rearrange("p j c -> p (j c)"),
                      dap(sconv, b * 3072, [[1, 128], [128, 24]]), writes=[r("cstage")],
                      allow_slow_non_contiguous=True)
                cp("dve", cb[:, :, b, 0:3], cstage[:].rearrange("p j c -> p c j"), [r("cstage")], gsel(g_cb, ALLC, [b]))
        for c in range(8):
            bi = rot([0, 1, 2, 3])
            for j in range(4):
                mm(bank(bi)[:, 0:T], diagc[:, c, j, :], cb[:, c, 0:nb, j:j + L], j == 0, j == 3,
                   [r("diagc")] + gsel(g_cb, [c], BL), [bankres[bi]], sig=(j == 3))
            act(bufC[:, c, 0:T], bank(bi)[:, 0:T], AF.Silu, [bankres[bi], r("vecT")], gsel(g_C, [c], BL),
                bias=vecT[:, 32 + c:33 + c])
        cp("dve", cprev[:], cb[:, :, nb - 1, L:L + 3], gsel(g_cb, ALLC, [nb - 1]), [r("cprev")])
        if last or not prompt:
            for b in ([nb - 1] if prompt else BL):
                cp("dve", cstage[:].rearrange("p j c -> p c j"), cb[:, :, b, L:L + 3], gsel(g_cb, ALLC, [b]), [r("cstage")])
                dst = dap(o_convp, 0, [[1, 128], [128, 24]]) if prompt else \
                    dap(o_convs, b * 3072, [[1, 128], [128, 24]])
                S.dma("pool", dst, cstage[:].rearrange("p j c -> p (j c)"), reads=[r("cstage")], writes=[r("o_conv")],
                      allow_slow_non_contiguous=True)

        gates_c()
        for pj in range(2):
            ws, wr_ = wnext()
            for b in BL:
                bi = rot([4, 5, 6, 7])
                for kc in range(8):
                    mm(bank(bi)[0:L, :], uT[:, kc, CS[b]], ws[:, kc, :], kc == 0, kc == 7,
                       [wr_, g_uT[kc][b]], [bankres[bi]], sig=(kc == 7))
                cpa(vaug[0:L, b, pj * 2:pj * 2 + 2, 0:256], bank(bi)[0:L, :].rearrange("p (h v) -> p h v", h=2),
                    [bankres[bi]], [r_vaug[b]])

        def ev_o(base):
            def ev(j, bi):
                c = base + j
                act(bufB[:, c, 0:T], bank(bi)[:, 0:T], AF.Sigmoid, [bankres[bi]], gsel(g_B, [c], BL))
            return ev
        for pj in range(2):
            ws, wr_ = wnext()
            proj_a(ws, wr_, 4, uT, g_uT, T, ev_o(pj * 4), [0, 1, 2, 3])
        ckpt(4)
        if first:
            S.op("dve", lambda e: e.memset(C_sb[:], 0.0), [], r_Csb)

        def part1(b, rCn):
            cs = CS[b]
            for h in range(4):
                mm(bank(0)[0:L, h * 128:h * 128 + L], bufC[:, 4 + h, cs], bufC[:, h, cs], True, True,
                   [g_C[4 + h][b], g_C[h][b]], [bankres[0]], sig=(h == 3))
            for h in range(4):
                stt(sTs[0:L, h, 0:L], bank(0)[0:L, h * 128:h * 128 + L], gsc[0:L, b, h:h + 1], trimask[0:L, 0:L],
                    ALU.mult, ALU.mult, [bankres[0], r("gsc"), r("trimask")], [r("sTs")])
            for h in range(4):
                tr(bankb(1)[0:L, h * 128:(h + 1) * 128], bufC[:, 4 + h, cs], identb[:, :], [g_C[4 + h][b], r("identb")],
                   [bankres[1]], sig=(h == 3))
            tt("dve", kw[0:L, :, :], bankb(1)[0:L, 0:512].rearrange("p (h d) -> p h d", h=4),
               gsc[0:L, b, 0:4].unsqueeze(2).broadcast_to([L, 4, 128]), ALU.mult, [bankres[1], r("gsc")], [r("kw")])
            for h in range(4):
                act(Cb[:, h, :], C_sb[:, h, :], AF.Identity, [r_Csb[h], r("gsc")] + rCn, [r_Cb[h]], scale=gsc[:, b, 12 + h:13 + h])

        for b in BL:
            cs = CS[b]
            if not prompt:
                stg, stg_r = nscr()
                S.dma("act", stg[:, :].rearrange("p (a d) -> p a d", a=8),
                      dap(sC, b * 4 * 256 * 128, [[128, 128], [128 * 128, 8], [1, 128]]), writes=[stg_r])
                for a in range(8):
                    tr(pair(0)[:, a * 128:(a + 1) * 128], stg[:, a * 128:(a + 1) * 128], ident[:, :],
                       [stg_r, r("ident")], [bankres[0], bankres[1]], sig=(a == 7))
                for h in range(4):
                    cp("dve" if h < 2 else "act", C_sb[:, h, 0:256], pair(0)[:, h * 256:(h + 1) * 256], [bankres[h // 2]], [r_Csb[h]])
                S.dma("act", C_sb[:, :, 256:257], dap(sn, b * 512, [[1, 128], [128, 4], [1, 1]]), writes=[r("Cn")],
                      reads=[], allow_slow_non_contiguous=True)
            rCn = [] if prompt else [r("Cn")]

            if (not prompt) or b == 0:
                part1(b, rCn)
            for h in range(4):
                nbk = 2 + h
                mm(bank(nbk)[0:L, 0:257], sTs[0:L, h, 0:L], vaug[0:L, b, h, :], True, False,
                   [r("sTs"), r_vaug[b]], [bankres[nbk]], sig=False)
                mm(bank(nbk)[0:L, 0:257], bufC[:, h, cs], Cb[:, h, :], False, True,
                   [g_C[h][b], r_Cb[h]], [bankres[nbk]])
            for h in range(4):
                dbk = 6 + (h % 2)
                mm(bank(dbk)[:, 0:257], kw[0:L, h, :], vaug[0:L, b, h, :], True, True, [r("kw"), r_vaug[b]], [bankres[dbk]])
                stt(C_sb[:, h, :], C_sb[:, h, :], gsc[:, b, 8 + h:9 + h], bank(dbk)[:, 0:257], ALU.mult, ALU.add,
                    [r_Csb[h], r("gsc"), bankres[dbk]] + rCn, [r_Csb[h]])
            rh = [r("hsm")]
            for h in range(4):
                nbk = 2 + h
                S.op("dve", lambda e, h=h, nbk=nbk: e.bn_stats(out=st6[0:L, h, :], in_=bank(nbk)[0:L, 0:256]),
                     [bankres[nbk]], [r("st6")])
                S.op("dve", lambda e, h=h: e.bn_aggr(out=mv[0:L, h, :], in_=st6[0:L, h, :]), [r("st6")], [r("mv")])
                cp("dve", hsm[0:L, 0, h:h + 1], bank(nbk)[0:L, 256:257], [bankres[nbk]], rh)
            stt(hsm[0:L, 1, :], hsm[0:L, 0, :], -1.0, hsm[0:L, 0, :], ALU.mult, ALU.max, rh, rh)
            tt("dve", hsm[0:L, 1, :], hsm[0:L, 1, :], gsc[0:L, b, 4:8], ALU.max, rh + [r("gsc")], rh)
            recip(hsm[0:L, 2, :], hsm[0:L, 1, :], rh, rh)
            tt("dve", hsm[0:L, 3, :], hsm[0:L, 2, :], hsm[0:L, 2, :], ALU.mult, rh, rh)
            tt("dve", hsm[0:L, 3, :], hsm[0:L, 3, :], mv[0:L, :, 1], ALU.mult, rh + [r("mv")], rh)
            ts("dve", hsm[0:L, 3, :], hsm[0:L, 3, :], HN_EPS, None, ALU.add, None, rh, rh)
            act(hsm[0:L, 4, :], hsm[0:L, 3, :], AF.Sqrt, rh, rh)
            recip(hsm[0:L, 5, :], hsm[0:L, 4, :], rh, rh)
            tt("dve", hsm[0:L, 6, :], hsm[0:L, 2, :], hsm[0:L, 5, :], ALU.mult, rh, rh)
            stt(hsm[0:L, 7, :], mv[0:L, :, 0], -1.0, hsm[0:L, 6, :], ALU.mult, ALU.mult, rh + [r("mv")], rh)
            for h in range(4):
                nbk = 2 + h
                act(hn[0:L, h, :], bank(nbk)[0:L, 0:256], AF.Identity, [bankres[nbk]] + rh, [r("hn")],
                    bias=hsm[0:L, 7, h:h + 1], scale=hsm[0:L, 6, h:h + 1])
            if prompt and b + 1 < nb:
                part1(b + 1, rCn)
            for j in range(8):
                tr(bankb(1)[:, j * 128:j * 128 + L], hn[0:L, j // 2, (j % 2) * 128:(j % 2) * 128 + 128], identb[0:L, 0:L],
                   [r("hn"), r("identb")], [bankres[1]], sig=(j == 7))
            for j in range(8):
                stt(bufA[:, j, cs], bankb(1)[:, j * 128:j * 128 + L], vecT[:, 104 + j:105 + j], bufB[:, j, cs],
                    ALU.mult, ALU.mult, [bankres[1], r("vecT"), g_B[j][b]], [g_A[j][b]])

            if (last and b == nb - 1) or not prompt:
                for h in range(4):
                    for vh in range(2):
                        a = h * 2 + vh
                        tr(pair(0)[:, a * 128:(a + 1) * 128], C_sb[:, h, vh * 128:(vh + 1) * 128], ident[:, :],
                           [r_Csb[h], r("ident")] + rCn, [bankres[0], bankres[1]], sig=(a == 7))
                stg, stg_r = nscr()
                cp("dve", stg[:, :], pair(0)[:, :], [bankres[0], bankres[1]], [stg_r])
                dC = dap(o_Cp, 0, [[128, 128], [128 * 128, 8], [1, 128]]) if prompt else \
                    dap(o_Cs, b * 4 * 256 * 128, [[128, 128], [128 * 128, 8], [1, 128]])
                S.dma("pool", dC, stg[:, :].rearrange("p (a d) -> p a d", a=8), reads=[stg_r], writes=[r("o_C")])
                dn = dap(o_np, 0, [[1, 128], [128, 4], [1, 1]]) if prompt else dap(o_ns, b * 512, [[1, 128], [128, 4], [1, 1]])
                S.dma("pool", dn, C_sb[:, :, 256:257], reads=r_Csb + rCn, writes=[r("o_n")], allow_slow_non_contiguous=True)
                if prompt:
                    S.dma("pool", o_mp[:, :], mcol[:, 0:1], reads=[r("mcol")], writes=[r("o_m")])
                else:
                    S.dma("pool", o_ms[b], mout[:, b:b + 1], reads=[r("mout")], writes=[r("o_m")])

        ckpt(5)
        same = prompt
        for pj in range(2):
            ws, wr_ = wnext()
            evr = resid_evac(L, nb, seqs, 2, same)
            proj_a(ws, wr_, 4, bufA, g_A, T, (lambda j, bi, pj=pj, evr=evr: evr(pj * 4 + j, bi)), [0, 1, 2, 3])
        ln_stage(L, nb, seqs, prompt, ti, "ln", (gam(0, 0), bet(0, 0)), [(uT, g_uT, 3, 4)])

        def mlp(l, kind_s):
            for pj in range(8):
                ws, wr_ = wnext()

                def ev(j, bi, pj=pj):
                    c = pj * 4 + j
                    rt, rt_r = rot(list(zip(rtmp, rtmp_r)), "rtmp")
                    act(rt[:, 0:T], bank(bi)[:, 0:T], AF.Relu, [bankres[bi]], [rt_r])
                    tt(SQE, hT[:, c, 0:T], rt[:, 0:T], rt[:, 0:T], ALU.mult, [rt_r], [r_hT[c]])
                proj_a(ws, wr_, 4, uT, g_uT, T, ev, [0, 1, 2, 3, 4, 5, 6, 7])
            evr = resid_evac(L, nb, seqs, kind_s, same)
            for nh in range(2):
                bks = [nh * 4 + j for j in range(4)]
                for kg in range(4):
                    ws, wr_ = wnext()
                    for j in range(4):
                        for kc in range(8):
                            mm(bank(bks[j])[:, 0:T], ws[:, kc, j * 128:(j + 1) * 128], hT[:, kg * 8 + kc, 0:T],
                               kg == 0 and kc == 0, kg == 3 and kc == 7, [wr_, r_hT[kg * 8 + kc]], [bankres[bks[j]]],
                               sig=(kc == 7))
                for j in range(4):
                    evr(nh * 4 + j, bks[j])

        ckpt(6)
        mlp(0, 5)
        ckpt(7)
        ln_stage(L, nb, seqs, prompt, ti, "ln", (gam(0, 1), bet(0, 1)), [(uT, g_uT, 6, 7), (bufA, g_A, 8, 9)])

        ckpt(8)
        ws, wr_ = wnext()
        if prompt:
            kcol0 = (ti % 2) * 512
            ringb = [(ti % 2) * 4 + b for b in BL]
        else:
            kcol0 = 0
            ringb = [0, 1, 2, 3]

        def ev_k(j, bi):
            cpa(KTr[:, j, kcol0:kcol0 + T], bank(bi)[:, 0:T], [bankres[bi]],
                [r_KTr[x] for x in (ringb if prompt else [0])])
        proj_a(ws, wr_, 2, bufA, g_A, T, ev_k, [0, 1])
        for b in BL:
            bi = rot([2, 3])
            for kc in range(8):
                mm(bank(bi)[0:L, :], bufA[:, kc, CS[b]], ws[:, kc, :], kc == 0, kc == 7, [wr_, g_A[kc][b]], [bankres[bi]],
                   sig=(kc == 7))
            cpa(Vr[0:L, ringb[b], :, 0:64], bank(bi)[0:L, 256:512].rearrange("p (h d) -> p h d", h=4),
                [bankres[bi]], [r_Vr[ringb[b]]])
            if last or not prompt:
                stg, stg_r = nscr()
                cp("act", stg[0:L, 0:512], bank(bi)[0:L, :], [bankres[bi]], [stg_r])
                if prompt:
                    dk_, dv_ = o_kp[b * 128:(b + 1) * 128, :], o_vp[b * 128:(b + 1) * 128, :]
                else:
                    dk_, dv_ = o_ks[b * 32:(b + 1) * 32, :], o_vs[b * 32:(b + 1) * 32, :]
                S.dma("pool", dk_, stg[0:L, 0:256], reads=[stg_r], writes=[r("o_k")])
                S.dma("pool", dv_, stg[0:L, 256:512], reads=[stg_r], writes=[r("o_v")])

        def ev_q(base):
            def ev(j, bi):
                cpa(bufC[:, base + j, 0:T], bank(bi)[:, 0:T], [bankres[bi]], gsel(g_C, [base + j], BL))
            return ev
        for pj in range(2):
            ws, wr_ = wnext()
            proj_a(ws, wr_, 4, uT, g_uT, T, ev_q(pj * 4), [4, 5, 6, 7])

        ckpt(9)
        def keyblocks_for(b):
            kbs = []
            if prompt:
                Bg = ti * 4 + b
                for j in range(5):
                    KB = Bg - 4 + j
                    if KB < 0:
                        continue
                    pos = KB % 8
                    kd = {0: "mask0", 1: "plain", 2: "plain", 3: "b3", 4: "b4"}[j]
                    kbs.append((
                        (lambda kk, rows, pos=pos: KTr[rows, kk, pos * 128:(pos + 1) * 128]),
                        (lambda kh, pos=pos: Vr[:, pos, kh, :]), 128, kd, [r_KTr[pos], r_Vr[pos]]))
            else:
                rc = r_KTr[4:8] + r_Vr[4:8]
                for j in range(4):
                    kbs.append((
                        (lambda kk, rows, j=j: KTc[rows, kk, j * 128:(j + 1) * 128]),
                        (lambda kh, j=j: Vc[:, j, kh, :]), 128, "b3" if j == 3 else "plain", rc))
                kbs.append((
                    (lambda kk, rows, b=b: KTr[rows, kk, b * 32:(b + 1) * 32]),
                    (lambda kh, b=b: Vr[0:32, b, kh, :]), 32, "b4", [r_KTr[0], r_Vr[b]]))
            return kbs

        def load_cache(b):
            stg, stg_r = nscr()
            S.dma(XQ, stg[:, :].rearrange("p (j f) -> p j f", j=4),
                  dap(ck, b * 512 * 256, [[256, 128], [128 * 256, 4], [1, 256]]), writes=[stg_r])
            for j in range(4):
                for kk in range(2):
                    a = j * 2 + kk
                    tr(pair(0)[:, a * 128:(a + 1) * 128], stg[:, j * 256 + kk * 128:j * 256 + (kk + 1) * 128], ident[:, :],
                       [stg_r, r("ident")], [bankres[0], bankres[1]], sig=(a == 7))
            for kk in range(2):
                cp("dve", KTc[:, kk, :].rearrange("p (j s) -> p j s", j=4),
                   pair(0)[:, :].rearrange("p (j k s) -> p j k s", j=4, k=2)[:, :, kk, :],
                   [bankres[0], bankres[1]], r_KTr[4:8])
            stg2, stg2_r = nscr()
            S.dma(XQ, stg2[:, :].rearrange("p (j f) -> p j f", j=4),
                  dap(cv, b * 512 * 256, [[256, 128], [128 * 256, 4], [1, 256]]), writes=[stg2_r])
            cp("dve", Vc[:, :, :, 0:64], stg2[:, :].rearrange("p (j h d) -> p j h d", j=4, h=4), [stg2_r], r_Vr[4:8])

        uctr = [0]

        def emit_st(b, kh, kbs):
            cs = CS[b]
            Lq = L
            kk, e_ = kh // 2, kh % 2
            rows = slice(e_ * 64, (e_ + 1) * 64)
            uctr[0] += 1
            pts = []
            for (ktf, vf, nk, kd, rds) in kbs:
                sbk = rot([0, 1, 2, 3])
                sps = bank(sbk)[0:nk, 0:4 * Lq]
                mm(sps, ktf(kk, rows), bufC[rows, kk * 4:(kk + 1) * 4, cs], True, True,
                   rds + gsel(g_C, range(kk * 4, kk * 4 + 4), [b]), [bankres[sbk]])
                if kd == "mask0":
                    pt, pt_r = PT0[uctr[0] % 2], PT0_r[uctr[0] % 2]
                    s3 = sps.rearrange("p (g q) -> p g q", g=4)
                    p3 = pt[0:nk, 0:4 * Lq].rearrange("p (g q) -> p g q", g=4)
                    act(p3[:, :, 0:64], s3[:, :, 0:64], AF.Exp, [bankres[sbk]], [pt_r], scale=0.125)
                    act(p3[64:128, :, 64:128], s3[64:128, :, 64:128], AF.Exp, [bankres[sbk]], [pt_r], scale=0.125)
                else:
                    pt, pt_r = rot(list(zip(PT, PT_r)), "PT")
                    if kd == "plain":
                        act(pt[0:nk, 0:4 * Lq], sps, AF.Exp, [bankres[sbk]], [pt_r], scale=0.125)
                    else:
                        bt_ = bias3 if kd == "b3" else bias4
                        ei, ei_r = rot(list(zip(expin, expin_r)), "expin")
                        stt(ei[0:nk, 0:4 * Lq].rearrange("p (g q) -> p g q", g=4),
                            sps.rearrange("p (g q) -> p g q", g=4), 0.125,
                            bt_[0:nk, kh * 4:(kh + 1) * 4, 0:Lq], ALU.mult, ALU.add,
                            [bankres[sbk], r("bias3"), r("bias4")], [ei_r])
                        act(pt[0:nk, 0:4 * Lq], ei[0:nk, 0:4 * Lq], AF.Exp, [ei_r], [pt_r])
                pts.append((pt, pt_r))
            return pts

        def emit_pv(b, kh, kbs, pts):
            Lq = L
            obk = 4 + kh
            nkb = len(kbs)
            for g in range(4):
                for ki, (ktf, vf, nk, kd, rds) in enumerate(kbs):
                    pt, pt_r = pts[ki]
                    mm(bank(obk)[0:Lq, g * 65:(g + 1) * 65], pt[0:nk, g * Lq:(g + 1) * Lq], vf(kh)[0:nk, :],
                       ki == 0, ki == nkb - 1, [pt_r] + rds, [bankres[obk]], sig=(ki == nkb - 1))

        def emit_fin(b):
            cs = CS[b]
            Lq = L
            for kh in range(4):
                obk = 4 + kh
                o3 = bank(obk)[0:Lq, 0:260].rearrange("p (g d) -> p g d", g=4)
                recip(rden[0:Lq, kh * 4:(kh + 1) * 4], o3[:, :, 64], [bankres[obk]], [r("rden")])
                tt("dve", On[0:Lq, kh * 4:(kh + 1) * 4, :], o3[:, :, 0:64],
                   rden[0:Lq, kh * 4:(kh + 1) * 4].unsqueeze(2).broadcast_to([Lq, 4, 64]), ALU.mult,
                   [bankres[obk], r("rden")], [r("On")])
            for c in range(8):
                tr(bankb(0)[:, c * 128:c * 128 + Lq], On[0:Lq, 2 * c:2 * c + 2, :].rearrange("p h d -> p (h d)"),
                   identb[0:Lq, 0:Lq], [r("On"), r("identb")], [bankres[0]], sig=(c == 7))
            cpa(bufB[:, :, cs], bankb(0)[:, 0:1024].rearrange("p (c q) -> p c q", c=8)[:, :, 0:Lq], [bankres[0]],
                gsel(g_B, ALLC, [b]))

        if prompt:
            units = [(b, kh) for b in BL for kh in range(4)]
            kbl = {b: keyblocks_for(b) for b in BL}
            nxt = emit_st(units[0][0], units[0][1], kbl[units[0][0]])
            for ui, (b, kh) in enumerate(units):
                cur = nxt
                if ui + 1 < len(units):
                    b2, kh2 = units[ui + 1]
                    nxt = emit_st(b2, kh2, kbl[b2])
                emit_pv(b, kh, kbl[b], cur)
                if kh == 3:
                    emit_fin(b)
        else:
            for b in BL:
                load_cache(b)
                kbs = keyblocks_for(b)
                for kh in range(4):
                    pts = emit_st(b, kh, kbs)
                    emit_pv(b, kh, kbs, pts)
                emit_fin(b)

        ckpt(10)
        for pj in range(2):
            ws, wr_ = wnext()
            evr = resid_evac(L, nb, seqs, 10, same)
            proj_a(ws, wr_, 4, bufB, g_B, T, (lambda j, bi, pj=pj, evr=evr: evr(pj * 4 + j, bi)), [0, 1, 2, 3])
        ln_stage(L, nb, seqs, prompt, ti, "ln", (gam(1, 0), bet(1, 0)), [(uT, g_uT, 11, 12)])
        mlp(1, 13)
        if nxt_tile is not None:
            xprefetch(*nxt_tile)
        ln_stage(L, nb, seqs, prompt, ti, "final", None, [])

    try:
        ckpt(1)
        xprefetch(*tiles[0])
        for i_, (kind, ti) in enumerate(tiles):
            do_tile(kind, ti, tiles[i_ + 1] if i_ + 1 < len(tiles) else None)
    except _Stop:
        pass

    print('sbuf bytes remaining', nc.sbuf_bytes_remaining, flush=True)
    S.final_drain("sp")
    S.emit(nc)
    st.close()
    return nc


_CACHE = {}


def _consts():
    ident = np.eye(128, dtype=np.float32)
    tri = (np.arange(128)[:, None] <= np.arange(128)[None, :]).astype(np.float32) * np.float32(128.0 ** -0.5)
    m4 = np.zeros((128, 128), np.float32)
    m4[64:, :64] = NEG
    return np.ascontiguousarray(np.stack([ident, tri, m4], axis=1))


def make_in_maps(inp, NT, ncores=8):
    f = lambda a: np.ascontiguousarray(np.asarray(a, dtype=np.float32))
    rel = f(inp["rel_bias_b"])[0]
    ext = np.concatenate([rel, np.repeat(rel[:, 256:257], 128, axis=1)], axis=1)
    rel_rev = np.ascontiguousarray(ext[:, ::-1])
    vec1 = np.concatenate([f(inp["conv_w_a"])[0].reshape(32, 128), f(inp["conv_b_a"])[0].reshape(8, 128),
                           f(inp["ln_g"]).reshape(32, 128), f(inp["ln_b"]).reshape(32, 128),
                           f(inp["mhn_g_a"])[0].reshape(8, 128)], axis=0)
    vec2 = np.concatenate([f(inp["b_ada"]).reshape(96, 128), f(inp["b_ada_kv"]).reshape(16, 128)], axis=0)
    lnf = np.stack([f(inp["ln_g"])[1, 1], f(inp["ln_b"])[1, 1]], axis=0)
    shared = {
        "w_in": f(inp["w_in_a"])[0], "w_out_a": f(inp["w_out_a"])[0],
        "w_up0": f(inp["w_up"])[0], "w_up1": f(inp["w_up"])[1],
        "w_down0": f(inp["w_down"])[0], "w_down1": f(inp["w_down"])[1],
        "w_kv": f(inp["w_kv"]), "w_q": f(inp["w_q_b"])[0], "w_out_b": f(inp["w_out_b"])[0],
        "w_ada": f(inp["w_ada"]), "w_ada_kv": f(inp["w_ada_kv"]),
        "vec1": np.ascontiguousarray(vec1), "vec2": np.ascontiguousarray(vec2),
        "b_if": f(inp["b_if_a"]).reshape(8, 1), "lnf": np.ascontiguousarray(lnf),
        "rel_rev": rel_rev, "cst": _consts(),
    }
    maps = []
    xp, xs = f(inp["x_prompt"]), f(inp["x_sample"])
    for i in range(ncores):
        s0 = 4 * i
        m = dict(shared)
        m["x_p"] = np.ascontiguousarray(xp[i, :NT * 512])
        m["x_s"] = np.ascontiguousarray(xs[s0:s0 + 4].reshape(128, 1024))
        m["c_all"] = np.ascontiguousarray(np.concatenate([f(inp["c_prompt"])[i:i + 1], f(inp["c_sample"])[s0:s0 + 4]], axis=0))
        m["sconv"] = np.ascontiguousarray(f(inp["state_conv"])[0, s0:s0 + 4])
        m["sC"] = np.ascontiguousarray(f(inp["state_C"])[0, s0:s0 + 4])
        m["sn"] = np.ascontiguousarray(f(inp["state_n"])[0, s0:s0 + 4])
        m["smT"] = np.ascontiguousarray(f(inp["state_m"])[0, s0:s0 + 4].T)
        m["ck"] = np.ascontiguousarray(f(inp["cache_k"])[s0:s0 + 4].reshape(4, 512, 256))
        m["cv"] = np.ascontiguousarray(f(inp["cache_v"])[s0:s0 + 4].reshape(4, 512, 256))
        maps.append(m)
    return maps


def assemble(results, NT, ncores=8):
    g = lambda k: np.stack([np.asarray(results[i][k], dtype=np.float32) for i in range(ncores)], axis=0)
    y_p = g("y_p")
    y_s = g("y_s").reshape(ncores * 4, 32, 1024)
    conv_p = g("o_convp")[None]
    C_p = g("o_Cp")[None]
    n_p = g("o_np")[None]
    m_p = g("o_mp").reshape(ncores, 4)[None]
    k_p = g("o_kp").reshape(ncores, 512, 4, 64)
    v_p = g("o_vp").reshape(ncores, 512, 4, 64)
    conv_s = g("o_convs").reshape(ncores * 4, 3, 1024)[None]
    C_s = g("o_Cs").reshape(ncores * 4, 4, 256, 128)[None]
    n_s = g("o_ns").reshape(ncores * 4, 4, 128)[None]
    m_s = g("o_ms").reshape(ncores * 4, 4)[None]
    k_s = g("o_ks").reshape(ncores * 4, 32, 4, 64)
    v_s = g("o_vs").reshape(ncores * 4, 32, 4, 64)
    return (y_p, y_s, conv_p, C_p, n_p, m_p, k_p, v_p, conv_s, C_s, n_s, m_s, k_s, v_s)


def kernel(**inputs):
    NT = 16
    if NT not in _CACHE:
        _CACHE[NT] = build(NT)
    nc = _CACHE[NT]
    in_maps = make_in_maps(inputs, NT)
    res = run_bass_kernel_spmd(nc, in_maps, core_ids=list(range(8)))
    return assemble(res.results, NT)
```
